# Optimizing a Trainium2 kernel written in Bass

```python
import math
import jax, jax.numpy as jnp
from jax import lax
import numpy as np

D_MODEL = 1024
BATCH = 4
SEQ = 8192
DEPTH = 1

CONV_A_WIDTH = 512
CONV_A_K = 3
DN_HEADS = 8
DN_DK = 128
DN_DV = 128
DN_CONV_K = 5
CHUNK = 64
DN_QK = DN_HEADS * DN_DK
DN_VW = DN_HEADS * DN_DV
N_BRANCH = 2
D_FF = 2816
N_MOD = 9
EPS = 1e-6

SPLIT_SIZES = (
    CONV_A_WIDTH,
    CONV_A_WIDTH,
    CONV_A_WIDTH,
    2 * DN_QK + DN_VW,
    DN_VW,
    4 * DN_HEADS,
    N_BRANCH * D_MODEL,
)
IN_COLS = sum(SPLIT_SIZES)
SPLIT_OFFSETS = tuple(int(v) for v in np.cumsum(SPLIT_SIZES)[:-1])

kernel_name = "bidir_hybrid_conv_gdn_macaron_adaln"


def _rmsnorm(x, w):
    xf = x.astype(jnp.float32)
    y = xf * lax.rsqrt(jnp.mean(xf * xf, axis=-1, keepdims=True) + EPS)
    return (y * w.astype(jnp.float32)).astype(x.dtype)


def _l2norm(x):
    return x * lax.rsqrt(jnp.sum(x * x, axis=-1, keepdims=True) + EPS)


def _modulate(x, shift, scale):
    return x * (1.0 + scale) + shift


def _swiglu(x, w_up, w_down):
    a, b = jnp.split(x @ w_up, 2, axis=-1)
    return (jax.nn.silu(a) * b) @ w_down


def _dwconv(x, w):
    k_taps = w.shape[0]
    pad = k_taps // 2
    s = x.shape[1]
    xp = jnp.pad(x, ((0, 0), (pad, pad), (0, 0)))
    y = xp[:, 0:s] * w[0]
    for i in range(1, k_taps):
        y = y + xp[:, i:i + s] * w[i]
    return y


def _chunk_gated_delta(q, k, v, beta, g):
    bn, s, h, dk = q.shape
    dv = v.shape[-1]
    n = s // CHUNK

    def to_chunks(t):
        t = t.reshape((bn, n, CHUNK, h) + t.shape[3:])
        return jnp.moveaxis(t, 3, 1)

    q, k, v, beta, g = (to_chunks(t) for t in (q, k, v, beta, g))
    g = jnp.cumsum(g, axis=-1)
    idx = jnp.arange(CHUNK)
    incl = idx[:, None] >= idx[None, :]
    strict = idx[:, None] > idx[None, :]
    decay = jnp.exp(jnp.where(incl, g[..., :, None] - g[..., None, :], -jnp.inf))
    k_beta = k * beta[..., None]
    a_mat = jnp.where(strict, jnp.einsum('bhnid,bhnjd->bhnij', k_beta, k) * decay, 0.0)
    a_mat = a_mat + jnp.eye(CHUNK, dtype=q.dtype)
    rhs = jnp.concatenate([v * beta[..., None], k_beta * jnp.exp(g)[..., None]], axis=-1)
    sol = lax.linalg.triangular_solve(a_mat, rhs, left_side=True, lower=True)
    u, w = sol[..., :dv], sol[..., dv:]
    attn = jnp.einsum('bhnid,bhnjd->bhnij', q, k) * decay

    xs = tuple(jnp.moveaxis(t, 2, 0) for t in (q, k, u, w, g, attn))

    def step(state, inp):
        q_c, k_c, u_c, w_c, g_c, attn_c = inp
        v_new = u_c - jnp.einsum('bhcd,bhde->bhce', w_c, state)
        o_c = (jnp.einsum('bhcd,bhde->bhce', q_c * jnp.exp(g_c)[..., None], state)
               + jnp.einsum('bhij,bhje->bhie', attn_c, v_new))
        g_last = g_c[..., -1:]
        state = (state * jnp.exp(g_last)[..., None]
                 + jnp.einsum('bhcd,bhce->bhde', k_c * jnp.exp(g_last - g_c)[..., None], v_new))
        return state, o_c

    state0 = jnp.zeros((bn, h, dk, dv), q.dtype)
    _, o = lax.scan(step, state0, xs)
    return o.transpose(1, 0, 3, 2, 4).reshape(bn, s, h, dv)


def _hybrid_mixer(u, w_in, conv_a, conv_dn, a_log_fwd, dt_bias_fwd, a_log_bwd, dt_bias_bwd,
                  dn_norm, w_a_out, w_b_out, w_out):
    bn, s, _ = u.shape
    f32 = jnp.float32
    proj = u @ w_in
    ca_b, ca_c, ca_v, dn_qkv, dn_z, dn_ba, gate_logits = jnp.split(proj, SPLIT_OFFSETS, axis=-1)

    y_a = ca_b * _dwconv(ca_c * ca_v, conv_a)

    qkv = jax.nn.silu(_dwconv(dn_qkv, conv_dn))
    q, k, v = jnp.split(qkv, [DN_QK, 2 * DN_QK], axis=-1)
    q = (_l2norm(q.reshape(bn, s, DN_HEADS, DN_DK).astype(f32)) * (DN_DK ** -0.5))
    k = _l2norm(k.reshape(bn, s, DN_HEADS, DN_DK).astype(f32))
    v = v.reshape(bn, s, DN_HEADS, DN_DV).astype(f32)
    b_f, b_b, al_f, al_b = jnp.split(dn_ba.astype(f32), 4, axis=-1)
    beta_f, beta_b = jax.nn.sigmoid(b_f), jax.nn.sigmoid(b_b)
    g_f = -jnp.exp(a_log_fwd.astype(f32)) * jax.nn.softplus(al_f + dt_bias_fwd.astype(f32))
    g_b = -jnp.exp(a_log_bwd.astype(f32)) * jax.nn.softplus(al_b + dt_bias_bwd.astype(f32))
    o_fwd = _chunk_gated_delta(q, k, v, beta_f, g_f)
    rev = lambda t: t[:, ::-1]
    o_bwd = rev(_chunk_gated_delta(rev(q), rev(k), rev(v), rev(beta_b), rev(g_b)))
    o = _rmsnorm(o_fwd + o_bwd, dn_norm) * jax.nn.silu(dn_z.reshape(bn, s, DN_HEADS, DN_DV).astype(f32))
    y_b = o.reshape(bn, s, DN_VW).astype(u.dtype)

    gate_a, gate_b = jnp.split(gate_logits, N_BRANCH, axis=-1)
    merged = jax.nn.sigmoid(gate_a) * (y_a @ w_a_out) + jax.nn.sigmoid(gate_b) * (y_b @ w_b_out)
    return merged @ w_out


def setup_inputs(seed: int = 0) -> dict:
    key = jax.random.key(seed)
    ks = jax.random.split(key, 24)
    L = DEPTH
    f32 = jnp.float32

    def nrm(k, shape, fan_in):
        return jax.random.normal(k, shape, f32) * (fan_in ** -0.5)

    def gain(k, shape):
        return 1.0 + 0.02 * jax.random.normal(k, shape, f32)

    def a_log(k):
        return jnp.log(jax.random.uniform(k, (L, DN_HEADS), f32, minval=1.0, maxval=16.0))

    def dt_bias(k):
        dt = jnp.exp(jax.random.uniform(k, (L, DN_HEADS), f32,
                                        minval=math.log(1e-3), maxval=math.log(1e-1)))
        return dt + jnp.log(-jnp.expm1(-dt))

    return {
        "x": jax.random.normal(ks[0], (BATCH, SEQ, D_MODEL), f32),
        "c": jax.random.normal(ks[1], (BATCH, D_MODEL), f32),
        "w_ada": nrm(ks[2], (L, D_MODEL, N_MOD * D_MODEL), D_MODEL),
        "b_ada": 0.02 * jax.random.normal(ks[3], (L, N_MOD * D_MODEL), f32),
        "norm_ffn1": gain(ks[4], (L, D_MODEL)),
        "w_ffn1_up": nrm(ks[5], (L, D_MODEL, 2 * D_FF), D_MODEL),
        "w_ffn1_down": nrm(ks[6], (L, D_FF, D_MODEL), D_FF),
        "norm_mix": gain(ks[7], (L, D_MODEL)),
        "w_in": nrm(ks[8], (L, D_MODEL, IN_COLS), D_MODEL),
        "conv_a": nrm(ks[9], (L, CONV_A_K, CONV_A_WIDTH), CONV_A_K),
        "conv_dn": nrm(ks[10], (L, DN_CONV_K, 2 * DN_QK + DN_VW), DN_CONV_K),
        "a_log_fwd": a_log(ks[11]),
        "dt_bias_fwd": dt_bias(ks[12]),
        "a_log_bwd": a_log(ks[13]),
        "dt_bias_bwd": dt_bias(ks[14]),
        "dn_norm": gain(ks[15], (L, DN_DV)),
        "w_a_out": nrm(ks[16], (L, CONV_A_WIDTH, D_MODEL), CONV_A_WIDTH),
        "w_b_out": nrm(ks[17], (L, DN_VW, D_MODEL), DN_VW),
        "w_out": nrm(ks[18], (L, D_MODEL, D_MODEL), D_MODEL),
        "norm_ffn2": gain(ks[19], (L, D_MODEL)),
        "w_ffn2_up": nrm(ks[20], (L, D_MODEL, 2 * D_FF), D_MODEL),
        "w_ffn2_down": nrm(ks[21], (L, D_FF, D_MODEL), D_FF),
        "norm_final": gain(ks[22], (D_MODEL,)),
    }


def reference(x, c, w_ada, b_ada, norm_ffn1, w_ffn1_up, w_ffn1_down, norm_mix, w_in, conv_a,
              conv_dn, a_log_fwd, dt_bias_fwd, a_log_bwd, dt_bias_bwd, dn_norm, w_a_out, w_b_out,
              w_out, norm_ffn2, w_ffn2_up, w_ffn2_down, norm_final):
    h = x
    c_act = jax.nn.silu(c)
    for l in range(DEPTH):
        mod = c_act @ w_ada[l] + b_ada[l]
        sh1, sc1, g1, sh2, sc2, g2, sh3, sc3, g3 = [m[:, None, :] for m in jnp.split(mod, N_MOD, axis=-1)]
        u = _modulate(_rmsnorm(h, norm_ffn1[l]), sh1, sc1)
        h = h + 0.5 * g1 * _swiglu(u, w_ffn1_up[l], w_ffn1_down[l])
        u = _modulate(_rmsnorm(h, norm_mix[l]), sh2, sc2)
        h = h + g2 * _hybrid_mixer(u, w_in[l], conv_a[l], conv_dn[l], a_log_fwd[l], dt_bias_fwd[l],
                                   a_log_bwd[l], dt_bias_bwd[l], dn_norm[l], w_a_out[l],
                                   w_b_out[l], w_out[l])
        u = _modulate(_rmsnorm(h, norm_ffn2[l]), sh3, sc3)
        h = h + 0.5 * g3 * _swiglu(u, w_ffn2_up[l], w_ffn2_down[l])
    return _rmsnorm(h, norm_final)
```

```python
import numpy as np
from contextlib import ExitStack
import concourse.bass as bass
import concourse.mybir as mybir
from concourse.bass_utils import run_bass_kernel_spmd

F32 = mybir.dt.float32
BF16 = mybir.dt.bfloat16
AF = mybir.ActivationFunctionType
ALU = mybir.AluOpType
AX = mybir.AxisListType

D = 1024
DFF = 2816
NJ = DFF // 128
H = 8
C = 64
NT = 512
EPS = 1e-6
NR = 4
NEG = -30000.0
DEBUG = False
SAME_ENGINE_SYNC = True
DBG = {}

NB_ADA = 36
NB_UP = 22
NB_IN = 26
NB_Z = 4
OFF_ADA = 0
OFF_UP1 = OFF_ADA + NB_ADA
OFF_IN = OFF_UP1 + NB_UP
OFF_Z = OFF_IN + NB_IN
OFF_UP2 = OFF_Z + NB_Z
NB_TOT = OFF_UP2 + NB_UP
NPC = 52
NCONV = 32


class Tick:
    __slots__ = ("kind", "key", "val")

    def __init__(self, kind, key, val):
        self.kind, self.key, self.val = kind, key, val


class Tr:
    def __init__(self, nc, es):
        self.nc = nc
        self.es = es
        self.E = {"pe": nc.tensor, "act": nc.scalar, "dve": nc.vector, "pool": nc.gpsimd, "sp": nc.sync}
        self.csem = {e: es.enter_context(nc.semaphore("c_" + e)) for e in ("pe", "act", "dve", "pool")}
        self.ccnt = {e: 0 for e in self.csem}
        self.pend = {e: [] for e in self.csem}
        self.dsem = {}
        self.dcnt = {}
        self.seen = {e: {} for e in self.E}
        self.st = {}

    def _wait(self, eng, t, same_ok=True):
        if t is None:
            return
        if t.kind == "c":
            if t.key == eng and (t.val is None or (same_ok and not SAME_ENGINE_SYNC)):
                return
            assert t.val is not None, "pending tick waited on"
            sem, v, k = self.csem[t.key], t.val, ("c", t.key)
        else:
            sem, v, k = self.dsem[t.key], self.dcnt[t.key], ("d", t.key)
        if self.seen[eng].get(k, 0) >= v:
            return
        self.seen[eng][k] = v
        self.E[eng].wait_ge(sem, v)

    def _deps(self, eng, r, w, same_ok):
        for k in r:
            s = self.st.get(k)
            if s:
                self._wait(eng, s[0], same_ok)
        for k in w:
            s = self.st.get(k)
            if s:
                self._wait(eng, s[0], same_ok)
                for t in s[1].values():
                    self._wait(eng, t, same_ok)

    def _upd(self, t, r, w):
        for k in r:
            self.st.setdefault(k, [None, {}])[1][(t.kind, t.key)] = t
        for k in w:
            self.st[k] = [t, {}]

    def op(self, eng, fn, r=(), w=(), inc=True, strict=False):
        self._deps(eng, r, w, not strict)
        ins = fn()
        t = Tick("c", eng, None)
        self.pend[eng].append(t)
        if inc:
            self.ccnt[eng] += 1
            ins.then_inc(self.csem[eng], 1)
            for p in self.pend[eng]:
                p.val = self.ccnt[eng]
            self.pend[eng] = []
        self._upd(t, r, w)
        return ins

    def dma(self, q, out, in_, r=(), w=(), sem=None):
        self._deps(q, r, w, False)
        if sem not in self.dsem:
            self.dsem[sem] = self.es.enter_context(self.nc.semaphore("d%d" % len(self.dsem)))
            self.dcnt[sem] = 0
        ins = self.E[q].dma_start(out=out, in_=in_)
        self.dcnt[sem] += 16
        ins.then_inc(self.dsem[sem], 16)
        t = Tick("d", sem, self.dcnt[sem])
        self._upd(t, r, w)
        return ins

    def cc(self, ins_fn, r=(), w=(), sem=None):
        self._deps("pool", r, w, False)
        if sem not in self.dsem:
            self.dsem[sem] = self.es.enter_context(self.nc.semaphore("d%d" % len(self.dsem)))
            self.dcnt[sem] = 0
        ins = ins_fn()
        self.dcnt[sem] += 1
        ins.then_inc(self.dsem[sem], 1)
        t = Tick("d", sem, self.dcnt[sem])
        self._upd(t, r, w)

    def barrier(self):
        for e in self.csem:
            assert not self.pend[e]
        for e in self.E:
            for f in self.csem:
                if f != e and self.ccnt[f] > 0:
                    self._wait(e, Tick("c", f, self.ccnt[f]))
            for k in self.dsem:
                if self.dcnt[k] > 0:
                    self._wait(e, Tick("d", k, self.dcnt[k]))
        self.st = {}

    def finish(self):
        for k in self.dsem:
            if self.dcnt[k] > 0:
                self._wait("sp", Tick("d", k, self.dcnt[k]))


class Ring:
    def __init__(self, tr, nc, slots, wr):
        self.tr, self.nc, self.slots, self.wr = tr, nc, slots, wr
        self.plan = []
        self.issued = 0
        self.pos = 0

    def add(self, blocks):
        self.plan.extend(blocks)

    def _issue(self):
        b = self.plan[self.issued]
        s = self.issued % NR
        self.tr.dma("pool", self.slots[s][:], self.wr[b], w=[("ring", s)], sem=("ring", s))
        self.issued += 1

    def get(self):
        while self.issued < min(len(self.plan), self.pos + NR):
            self._issue()
        s = self.pos % NR
        self.pos += 1
        return s, self.slots[s]


def build_program(T):
    ntile = T // NT
    nchunk = T // C
    nc = bass.Bass("TRN2", target_bir_lowering=False)

    def din(name, shape, dt=F32):
        return nc.dram_tensor(name, list(shape), dt, kind="ExternalInput").ap()

    xT = din("xT", [D, T])
    wr = din("wr", [NB_TOT, 128, 2048])
    wdn1 = din("wdn1", [DFF, D])
    wdn2 = din("wdn2", [DFF, D])
    wba = din("wba", [D, 32])
    waout = din("waout", [512, D])
    wbout = din("wbout", [D, D])
    wout = din("wout", [D, D])
    vecs = din("vecs", [128, 120])
    convw = din("convw", [128, 24 * 5 + 4 * 3])
    tokp = din("tokp", [128, 32 + 128])
    cst = din("cst", [128, 4096])
    cst2 = din("cst2", [64, 2048])
    sel = din("sel", [128, 2])
    outT = nc.dram_tensor("outT", [D, T], F32, kind="ExternalOutput").ap()

    def dscr(name, shape, dt=F32):
        if DEBUG:
            return nc.dram_tensor(name, list(shape), dt, kind="ExternalOutput").ap()
        return nc.dram_tensor(name, list(shape), dt).ap()

    H1T = dscr("H1T", [D, T])
    PT = dscr("PT", [NPC * 128, T + 4])
    ZS = dscr("ZS", [T, D])
    BAt = dscr("BAt", [T, 32])
    QKVT = dscr("QKVT", [3 * D, T], BF16)
    PAT = dscr("PAT", [D, T])
    OF = dscr("OF", [T, D])
    YBT = dscr("YBT", [D, T], BF16)
    hx_in = nc.dram_tensor("hx_in", [NCONV * 128, 2], F32)
    hx_out = nc.dram_tensor("hx_out", [2 * NCONV * 128, 2], F32)
    sx_in = nc.dram_tensor("sx_in", [128, H * 128], F32)
    sx_out = nc.dram_tensor("sx_out", [256, H * 128], F32)
    PAIRS = [[0, 1], [2, 3], [4, 5], [6, 7]]

    with ExitStack() as es:
        tr = Tr(nc, es)

        sbn = [0]

        def sb(name, shape, dt=F32, stack=None):
            sbn[0] += 1
            return (stack or es).enter_context(nc.sbuf_tensor("%s_%d" % (name, sbn[0]), list(shape), dt))

        ps = [es.enter_context(nc.psum_tensor("ps%d" % i, [128, 512], F32)) for i in range(8)]
        psi = [0]

        def nps():
            for _ in range(8):
                i = psi[0] % 8
                psi[0] += 1
                s_ = tr.st.get(("ps", i))
                if s_ is None or s_[0] is None or len(s_[1]) > 0:
                    return i, ps[i]
            raise RuntimeError("all PSUM banks hold unconsumed results")

        ring = Ring(tr, nc, None, wr)

        def ring_phase(stack, blocks):
            assert ring.issued == len(ring.plan) and ring.pos == len(ring.plan)
            ring.slots = [sb("ring", [128, 2048], BF16, stack) for i in range(NR)]
            ring.add(blocks)

        vec = sb("vec", [128, 120])
        cw = sb("cw", [128, 132])
        tp = sb("tp", [128, 160])
        cs = sb("cs", [128, 4096])
        csb = sb("csb", [128, 1024], BF16)
        selt = sb("selt", [128, 2])
        modT = sb("modT", [128, 72])
        der = sb("der", [128, 64])
        cact = sb("cact", [128, 8], BF16)
        negA = sb("negA", [128, 16])
        tr.dma("sp", vec[:], vecs, w=["vec"], sem="c0")
        tr.dma("sp", cw[:], convw, w=["cw"], sem="c0")
        tr.dma("sp", tp[:], tokp, w=["tp"], sem="c0")
        tr.dma("sp", cs[:], cst, w=["cs"], sem="c0")
        tr.dma("sp", selt[:], sel, w=["selt"], sem="c0")
        ONES = cs[:, 0:128]
        ID64 = cs[0:64, 128:192]
        LTRI = [cs[0:64, 192:256], cs[0:64, 256:320]]
        CAPS = [cs[0:64, 320:832], cs[0:64, 1344:1856]]
        CAPI = [cs[0:64, 832:1344], cs[0:64, 1856:2368]]
        I8 = cs[0:64, 2368:2880]
        E8 = cs[0:8, 2880:3392]
        tr.op("dve", lambda: nc.vector.tensor_copy(out=csb[:, 0:256], in_=cs[:, 0:256]), r=["cs"], w=["csb"])
        tr.op("dve", lambda: nc.vector.tensor_copy(out=csb[:, 256:384], in_=cs[:, 3392:3520]), r=["cs"], w=["csb"])
        ONESB = csb[:, 0:128]
        ID64B = csb[0:64, 128:192]
        ID128B = csb[:, 256:384]

        tr.op("act", lambda: nc.scalar.activation(out=cact[:], in_=vec[:, 0:8], func=AF.Silu), r=["vec"], w=["cact"])
        st0 = ExitStack()
        ring_phase(st0, list(range(OFF_ADA, OFF_ADA + NB_ADA)))
        bi, pb = nps()
        for blk in range(NB_ADA):
            s, slot = ring.get()
            sv = slot[:].rearrange("p (k c) -> p k c", k=8)
            for cc in range(2):
                j = blk * 2 + cc
                for k in range(8):
                    tr.op("pe", lambda k=k, j=j, cc=cc: nc.tensor.matmul(
                        pb[:, j:j + 1], sv[:, k, cc * 128:(cc + 1) * 128], cact[:, k:k + 1],
                        start=(k == 0), stop=(k == 7)),
                        r=[("ring", s), "cact"], w=[("ps", bi)], inc=(k == 7 and cc == 1))
        tr.op("dve", lambda: nc.vector.tensor_tensor(out=modT[:], in0=pb[:, 0:72], in1=vec[:, 8:80], op=ALU.add),
              r=[("ps", bi), "vec"], w=["modT"])
        for s in range(3):
            tr.op("dve", lambda s=s: nc.vector.scalar_tensor_tensor(
                out=der[:, s * 8:(s + 1) * 8], in0=modT[:, (3 * s + 1) * 8:(3 * s + 2) * 8], scalar=1.0,
                in1=vec[:, 80 + s * 8:88 + s * 8], op0=ALU.add, op1=ALU.mult), r=["modT", "vec"], w=["der"])
            gsc = 1.0 if s == 1 else 0.5
            tr.op("dve", lambda s=s, gsc=gsc: nc.vector.tensor_scalar(
                out=der[:, 24 + s * 8:32 + s * 8], in0=modT[:, (3 * s + 2) * 8:(3 * s + 3) * 8],
                scalar1=gsc, scalar2=None, op0=ALU.mult), r=["modT"], w=["der"])
        NF = vec[:, 104:112]
        tr.op("act", lambda: nc.scalar.activation(out=negA[:], in_=tp[:, 0:16], func=AF.Exp), r=["tp"], w=["negA"])
        tr.op("dve", lambda: nc.vector.tensor_scalar(out=negA[:], in0=negA[:], scalar1=-1.0, scalar2=None,
                                                     op0=ALU.mult), r=["negA"], w=["negA"])

        tr.barrier()
        st0.close()

        def load_tile(dst, src, t0, key, n=NT, coff=0):
            for i in range(8):
                tr.dma("sp", dst[:, i, coff:coff + n], src[i * 128:(i + 1) * 128, t0:t0 + n],
                       w=[(key, i)], sem=(key,))

        def norm_mod(P, xt, xkey, u, s_idx):
            bi, pss = nps()
            for i in range(8):
                sq = P["sq"][i % 2]
                tr.op("act", lambda i=i, sq=sq: nc.scalar.activation(out=sq[:], in_=xt[:, i, :], func=AF.Square),
                      r=[(xkey, i)], w=[("sq", i % 2)])
                tr.op("pe", lambda i=i, sq=sq: nc.tensor.matmul(pss[:], ONES, sq[:], start=(i == 0), stop=(i == 7)),
                      r=[("sq", i % 2), "cs"], w=[("ps", bi)], inc=True)
            rstd = P["rstd"]
            tr.op("act", lambda: nc.scalar.activation(out=rstd[:], in_=pss[:], func=AF.Sqrt, bias=P["eps"][:, 0:1],
                                                      scale=1.0 / D), r=[("ps", bi), "eps"], w=["rstd"])
            tr.op("dve", lambda: nc.vector.reciprocal(out=rstd[:], in_=rstd[:]), r=["rstd"], w=["rstd"])
            for i in range(8):
                tt = P["tt"][i % 2]
                tr.op("dve", lambda i=i, tt=tt: nc.vector.scalar_tensor_tensor(
                    out=tt[:], in0=xt[:, i, :], scalar=der[:, s_idx * 8 + i:s_idx * 8 + i + 1], in1=rstd[:],
                    op0=ALU.mult, op1=ALU.mult), r=[(xkey, i), "rstd", "der"], w=[("tt", i % 2)])
                tr.op("act", lambda i=i, tt=tt: nc.scalar.activation(
                    out=u[:, i, :], in_=tt[:], func=AF.Identity,
                    bias=modT[:, 3 * s_idx * 8 + i:3 * s_idx * 8 + i + 1], scale=1.0),
                    r=[("tt", i % 2), "modT"], w=[("u", i)])

        def ffn_body(P, xt, xkey, s_idx, wdn, final, dst, t0):
            u, hid = P["u"], P["hid"]
            norm_mod(P, xt, xkey, u, s_idx)
            for j in range(NJ):
                s, slot = ring.get()
                sv = slot[:].rearrange("p (k c) -> p k c", k=8)
                ba_, pa = nps()
                bb_, pbb = nps()
                for part, (bix, pt) in enumerate(((ba_, pa), (bb_, pbb))):
                    for k in range(8):
                        tr.op("pe", lambda k=k, part=part, pt=pt: nc.tensor.matmul(
                            pt[:], sv[:, k, part * 128:(part + 1) * 128], u[:, k, :], start=(k == 0), stop=(k == 7)),
                            r=[("ring", s), ("u", k)], w=[("ps", bix)], inc=(k == 7))
                sl = P["s"][j % 2]
                tr.op("act", lambda sl=sl, pa=pa: nc.scalar.activation(out=sl[:], in_=pa[:], func=AF.Silu),
                      r=[("ps", ba_)], w=[("s", j % 2)])
                tr.op("dve", lambda sl=sl, pbb=pbb, j=j: nc.vector.tensor_tensor(
                    out=hid[:, j, :], in0=pbb[:], in1=sl[:], op=ALU.mult),
                    r=[("ps", bb_), ("s", j % 2)], w=[("hid", j)])
            for m in range(8):
                bd, pd = nps()
                for j in range(NJ):
                    tr.op("pe", lambda j=j, m=m, pd=pd: nc.tensor.matmul(
                        pd[:], wdn[:, j, m * 128:(m + 1) * 128], hid[:, j, :], start=(j == 0), stop=(j == NJ - 1)),
                        r=["wdn", ("hid", j)], w=[("ps", bd)], inc=(j == NJ - 1))
                tr.op("dve", lambda m=m, pd=pd: nc.vector.scalar_tensor_tensor(
                    out=xt[:, m, :], in0=pd[:], scalar=der[:, 24 + s_idx * 8 + m:24 + s_idx * 8 + m + 1],
                    in1=xt[:, m, :], op0=ALU.mult, op1=ALU.add), r=[("ps", bd), "der"], w=[(xkey, m)])
            if final:
                bi, pss = nps()
                for i in range(8):
                    sq = P["sq"][i % 2]
                    tr.op("act", lambda i=i, sq=sq: nc.scalar.activation(out=sq[:], in_=xt[:, i, :], func=AF.Square),
                          r=[(xkey, i)], w=[("sq", i % 2)])
                    tr.op("pe", lambda i=i, sq=sq: nc.tensor.matmul(pss[:], ONES, sq[:], start=(i == 0),
                                                                    stop=(i == 7)),
                          r=[("sq", i % 2), "cs"], w=[("ps", bi)], inc=True)
                rstd = P["rstd"]
                tr.op("act", lambda: nc.scalar.activation(out=rstd[:], in_=pss[:], func=AF.Sqrt,
                                                          bias=P["eps"][:, 0:1], scale=1.0 / D),
                      r=[("ps", bi), "eps"], w=["rstd"])
                tr.op("dve", lambda: nc.vector.reciprocal(out=rstd[:], in_=rstd[:]), r=["rstd"], w=["rstd"])
                for i in range(8):
                    tr.op("dve", lambda i=i: nc.vector.scalar_tensor_tensor(
                        out=xt[:, i, :], in0=xt[:, i, :], scalar=NF[:, i:i + 1], in1=rstd[:],
                        op0=ALU.mult, op1=ALU.mult), r=[(xkey, i), "rstd", "vec"], w=[(xkey, i)])
            for i in range(8):
                tr.dma("sp", dst[i * 128:(i + 1) * 128, t0:t0 + NT], xt[:, i, :], r=[(xkey, i)], sem=(xkey, "st"))

        def ffn_pool(stack, wdn_src, nxt=2):
            P = {}
            P["wdn"] = sb("wdn", [128, NJ, D], BF16, stack)
            P["xt"] = [sb("xt%d" % i, [128, 8, NT], F32, stack) for i in range(nxt)]
            P["sq"] = [sb("sq%d" % i, [128, NT], F32, stack) for i in range(2)]
            P["tt"] = [sb("tt%d" % i, [128, NT], F32, stack) for i in range(2)]
            P["s"] = [sb("s%d" % i, [128, NT], F32, stack) for i in range(2)]
            P["rstd"] = sb("rstd", [128, NT], F32, stack)
            P["u"] = sb("u", [128, 8, NT], BF16, stack)
            P["hid"] = sb("hid", [128, NJ, NT], BF16, stack)
            P["eps"] = sb("eps", [128, 1], F32, stack)
            tr.op("dve", lambda: nc.vector.memset(P["eps"][:], EPS), w=["eps"])
            wv = wdn_src.rearrange("(j p) n -> p j n", p=128)
            for jj in range(0, NJ, 2):
                tr.dma("pool", P["wdn"][:, jj:jj + 2, :], wv[:, jj:jj + 2, :], w=["wdn"], sem="wdn")
            return P

        with ExitStack() as st1:
            ring_phase(st1, [b for _ in range(ntile) for b in range(OFF_UP1, OFF_UP1 + NB_UP)])
            P = ffn_pool(st1, wdn1)
            for n in range(ntile):
                xt = P["xt"][n % 2]
                xkey = "xt%d" % (n % 2)
                load_tile(xt, xT, n * NT, xkey)
                ffn_body(P, xt, xkey, 0, P["wdn"], False, H1T, n * NT)
            tr.barrier()

        with ExitStack() as st2:
            ring_phase(st2, [b for _ in range(ntile) for b in range(OFF_IN, OFF_IN + NB_IN + NB_Z)])
            xts = [sb("xt%d" % i, [128, 8, NT], F32, st2) for i in range(2)]
            P = {"sq": [sb("sq%d" % i, [128, NT], F32, st2) for i in range(2)],
                 "tt": [sb("tt%d" % i, [128, NT], F32, st2) for i in range(2)],
                 "rstd": sb("rstd", [128, NT], F32, st2), "eps": sb("eps", [128, 1], F32, st2)}
            tr.op("dve", lambda: nc.vector.memset(P["eps"][:], EPS), w=["eps"])
            u = sb("u", [128, 8, NT], BF16, st2)
            wbas = sb("wbas", [128, 8, 32], BF16, st2)
            stg = [sb("stg%d" % i, [128, NT], F32, st2) for i in range(4)]
            zero = sb("zero", [128, NCONV, 2], F32, st2)
            tr.dma("pool", wbas[:], wba.rearrange("(k p) n -> p k n", p=128), w=["wbas"], sem="wbas")
            tr.op("dve", lambda: nc.vector.memset(zero[:], 0.0), w=["zero"])
            tr.dma("sp", PT[0:NCONV * 128, 0:2].rearrange("(c p) n -> p c n", p=128), zero[:], r=["zero"], sem="zero")
            sti = 0
            for n in range(ntile):
                xt = xts[n % 2]
                xkey = "xt%d" % (n % 2)
                load_tile(xt, H1T, n * NT, xkey)
                norm_mod(P, xt, xkey, u, 1)
                for blk in range(NB_IN):
                    s, slot = ring.get()
                    sv = slot[:].rearrange("p (k c) -> p k c", k=8)
                    for cc in range(2):
                        ch = blk * 2 + cc
                        bi, pt = nps()
                        for k in range(8):
                            tr.op("pe", lambda k=k, cc=cc, pt=pt: nc.tensor.matmul(
                                pt[:], sv[:, k, cc * 128:(cc + 1) * 128], u[:, k, :], start=(k == 0), stop=(k == 7)),
                                r=[("ring", s), ("u", k)], w=[("ps", bi)], inc=(k == 7))
                        sg = stg[sti % 4]
                        sk = ("stg", sti % 4)
                        sti += 1
                        tr.op("act", lambda sg=sg, pt=pt: nc.scalar.copy(out=sg[:], in_=pt[:]), r=[("ps", bi)], w=[sk])
                        tr.dma("sp", PT[ch * 128:(ch + 1) * 128, 2 + n * NT:2 + (n + 1) * NT], sg[:], r=[sk], sem=sk)
                for zb in range(NB_Z):
                    s, slot = ring.get()
                    sv = slot[:].rearrange("p (k c) -> p k c", k=8)
                    for tb in range(0, 4, 2):
                        bi, pt = nps()
                        for t2 in range(2):
                            for k in range(8):
                                tr.op("pe", lambda k=k, t2=t2, tb=tb, pt=pt: nc.tensor.matmul(
                                    pt[:, t2 * 256:(t2 + 1) * 256], u[:, k, (tb + t2) * 128:(tb + t2 + 1) * 128],
                                    sv[:, k, :], start=(k == 0), stop=(k == 7)),
                                    r=[("ring", s), ("u", k)], w=[("ps", bi)], inc=(k == 7))
                        sg = stg[sti % 4]
                        sk = ("stg", sti % 4)
                        sti += 1
                        tr.op("act", lambda sg=sg, pt=pt: nc.scalar.activation(out=sg[:], in_=pt[:], func=AF.Silu),
                              r=[("ps", bi)], w=[sk])
                        for t2 in range(2):
                            r0 = n * NT + (tb + t2) * 128
                            tr.dma("sp", ZS[r0:r0 + 128, zb * 256:(zb + 1) * 256], sg[:, t2 * 256:(t2 + 1) * 256],
                                   r=[sk], sem=sk)
                bi, pt = nps()
                for tb in range(4):
                    for k in range(8):
                        tr.op("pe", lambda k=k, tb=tb, pt=pt: nc.tensor.matmul(
                            pt[:, tb * 32:(tb + 1) * 32], u[:, k, tb * 128:(tb + 1) * 128], wbas[:, k, :],
                            start=(k == 0), stop=(k == 7)), r=["wbas", ("u", k)], w=[("ps", bi)], inc=(k == 7))
                sg = stg[sti % 4]
                sk = ("stg", sti % 4)
                sti += 1
                tr.op("act", lambda sg=sg, pt=pt: nc.scalar.copy(out=sg[:, 0:128], in_=pt[:, 0:128]),
                      r=[("ps", bi)], w=[sk])
                tr.dma("sp", BAt[n * NT:(n + 1) * NT, :].rearrange("(b p) n -> p b n", p=128),
                       sg[:, 0:128].rearrange("p (b n) -> p b n", b=4), r=[sk], sem=sk)
            tr.barrier()
            hx = sb("hx", [128, 2, NCONV, 2], F32, st2)
            hy = sb("hy", [128, NCONV, 2], F32, st2)
            tr.dma("pool", hx_in.ap(), PT[0:NCONV * 128, T:T + 2], w=["hx_in"], sem="hx")
            tr.cc(lambda: nc.gpsimd.collective_compute("AllGather", ALU.bypass, replica_groups=PAIRS,
                                                        ins=[hx_in.ap()], outs=[hx_out.ap()]),
                  r=["hx_in"], w=["hx_out"], sem="hxcc")
            tr.dma("pool", hx[:], hx_out.ap().rearrange("(r c p) n -> p r c n", r=2, p=128), r=["hx_out"], w=["hx"],
                   sem="hx")
            tr.op("dve", lambda: nc.vector.tensor_scalar(out=hy[:], in0=hx[:, 0], scalar1=selt[:, 0:1], scalar2=None,
                                                         op0=ALU.mult), r=["hx", "selt"], w=["hy"])
            tr.op("dve", lambda: nc.vector.scalar_tensor_tensor(out=hy[:], in0=hx[:, 1], scalar=selt[:, 1:2],
                                                                in1=hy[:], op0=ALU.mult, op1=ALU.add),
                  r=["hx", "selt", "hy"], w=["hy"])
            pv = PT[0:NCONV * 128, :].rearrange("(c p) n -> p c n", p=128)
            hys = sb("hys", [128, NCONV, 2], F32, st2)
            tr.op("dve", lambda: nc.vector.tensor_copy(out=hys[:, :, 0:1], in_=hy[:, :, 1:2]), r=["hy"], w=["hys"])
            tr.op("dve", lambda: nc.vector.tensor_copy(out=hys[:, :, 1:2], in_=hy[:, :, 0:1]), r=["hy"], w=["hys"])
            tr.dma("sp", pv[:, :, T + 2:T + 4], hys[:], r=["hys"], sem="hy")
            tr.barrier()

        def run_streams(factories, width, stagger=0):
            active = []
            free = list(range(width))
            it = iter(factories)
            done = False
            if stagger:
                f = next(it, None)
                if f is not None:
                    s_ = free.pop(0)
                    g0 = f(s_)
                    active.append((g0, s_))
                    for _ in range(stagger):
                        try:
                            next(g0)
                        except StopIteration:
                            active.remove((g0, s_))
                            free.append(s_)
                            break
            while True:
                while free and not done:
                    f = next(it, None)
                    if f is None:
                        done = True
                        break
                    s_ = free.pop(0)
                    active.append((f(s_), s_))
                if not active:
                    break
                for g in list(active):
                    try:
                        next(g[0])
                    except StopIteration:
                        active.remove(g)
                        free.append(g[1])

        with ExitStack() as st3:
            W2B = 4
            B2 = []
            for i in range(W2B):
                B2.append({"pre": sb("pre", [128, NT + 4], F32, st3), "acc": sb("acc", [128, NT], F32, st3),
                           "sq": sb("sqq", [128, NT], F32, st3), "rn": sb("rn", [128, NT], F32, st3),
                           "ob": sb("ob", [128, NT], BF16, st3)})
            A2 = []
            for i in range(2):
                A2.append({"p1": sb("p1", [128, NT + 4], F32, st3), "p2": sb("p2", [128, NT + 4], F32, st3),
                           "p3": sb("p3", [128, NT], F32, st3), "pcv": sb("pcv", [128, NT + 4], F32, st3),
                           "cacc": sb("cacc", [128, NT], F32, st3)})
            G2 = []
            for i in range(2):
                G2.append({"g": sb("gsb", [128, NT], F32, st3), "pa": sb("pab", [128, NT], F32, st3)})
            yaT = [sb("yaT", [128, 4, NT], BF16, st3) for i in range(2)]
            wa = sb("wa", [128, 4, D], BF16, st3)
            epsq = sb("epsq", [128, 1], F32, st3)
            tr.op("dve", lambda: nc.vector.memset(epsq[:], EPS), w=["epsq"])
            tr.dma("pool", wa[:], waout.rearrange("(k p) n -> p k n", p=128), w=["wa"], sem="wa")

            def qkv_gen(n, ch, sl):
                c0 = n * NT
                Bf = B2[sl]
                p_, a_, s_, r_, o_ = Bf["pre"], Bf["acc"], Bf["sq"], Bf["rn"], Bf["ob"]
                pk, ak, sk, rk, ok = ("pre", sl), ("acc", sl), ("sqq", sl), ("rn", sl), ("ob", sl)
                tr.dma("sp", p_[:], PT[(8 + ch) * 128:(9 + ch) * 128, c0:c0 + NT + 4], w=[pk], sem=pk)
                yield
                for tap in range(5):
                    wcol = cw[:, ch * 5 + tap:ch * 5 + tap + 1]
                    if tap == 0:
                        tr.op("dve", lambda: nc.vector.tensor_scalar(out=a_[:], in0=p_[:, 0:NT], scalar1=wcol,
                                                                     scalar2=None, op0=ALU.mult), r=[pk, "cw"], w=[ak])
                    else:
                        tr.op("dve", lambda: nc.vector.scalar_tensor_tensor(
                            out=a_[:], in0=p_[:, tap:tap + NT], scalar=wcol, in1=a_[:], op0=ALU.mult, op1=ALU.add),
                            r=[pk, "cw", ak], w=[ak])
                yield
                tr.op("act", lambda: nc.scalar.activation(out=a_[:], in_=a_[:], func=AF.Silu), r=[ak], w=[ak])
                yield
                if ch < 16:
                    tr.op("pool", lambda: nc.gpsimd.tensor_tensor(out=s_[:], in0=a_[:], in1=a_[:], op=ALU.mult),
                          r=[ak], w=[sk])
                    yield
                    bi, pt = nps()
                    tr.op("pe", lambda: nc.tensor.matmul(pt[:], ONES, s_[:], start=True, stop=True),
                          r=[sk, "cs"], w=[("ps", bi)])
                    yield
                    tr.op("act", lambda: nc.scalar.activation(out=r_[:], in_=pt[:], func=AF.Sqrt, bias=epsq[:, 0:1],
                                                              scale=1.0), r=[("ps", bi), "epsq"], w=[rk])
                    yield
                    tr.op("dve", lambda: nc.vector.reciprocal(out=r_[:], in_=r_[:]), r=[rk], w=[rk])
                    qs = (128.0 ** -0.5) if ch < 8 else 1.0
                    tr.op("dve", lambda: nc.vector.scalar_tensor_tensor(out=o_[:], in0=a_[:], scalar=qs, in1=r_[:],
                                                                        op0=ALU.mult, op1=ALU.mult),
                          r=[ak, rk], w=[ok])
                else:
                    tr.op("pool", lambda: nc.gpsimd.tensor_copy(out=o_[:], in_=a_[:]), r=[ak], w=[ok])
                yield
                tr.dma("sp", QKVT[ch * 128:(ch + 1) * 128, c0:c0 + NT], o_[:], r=[ok], sem=ok)

            def abr_gen(n, ch, sl):
                c0 = n * NT
                Af = A2[sl]
                p1, p2, p3, pcv, cacc = Af["p1"], Af["p2"], Af["p3"], Af["pcv"], Af["cacc"]
                k1, k2, k3, kp, kc = ("p1", sl), ("p2", sl), ("p3", sl), ("pcv", sl), ("cacc", sl)
                ya = yaT[n % 2]
                tr.dma("sp", p1[:], PT[ch * 128:(ch + 1) * 128, c0:c0 + NT + 4], w=[k1], sem=k1)
                tr.dma("sp", p2[:], PT[(4 + ch) * 128:(5 + ch) * 128, c0:c0 + NT + 4], w=[k2], sem=k2)
                tr.dma("sp", p3[:], PT[(32 + ch) * 128:(33 + ch) * 128, c0 + 2:c0 + 2 + NT], w=[k3], sem=k3)
                yield
                tr.op("dve", lambda: nc.vector.tensor_tensor(out=pcv[:], in0=p1[:], in1=p2[:], op=ALU.mult),
                      r=[k1, k2], w=[kp])
                yield
                for tap in range(3):
                    wcol = cw[:, 120 + ch * 3 + tap:120 + ch * 3 + tap + 1]
                    if tap == 0:
                        tr.op("dve", lambda: nc.vector.tensor_scalar(out=cacc[:], in0=pcv[:, 1:1 + NT], scalar1=wcol,
                                                                     scalar2=None, op0=ALU.mult),
                              r=[kp, "cw"], w=[kc])
                    else:
                        tr.op("dve", lambda: nc.vector.scalar_tensor_tensor(
                            out=cacc[:], in0=pcv[:, 1 + tap:1 + tap + NT], scalar=wcol, in1=cacc[:], op0=ALU.mult,
                            op1=ALU.add), r=[kp, "cw", kc], w=[kc])
                yield
                tr.op("dve", lambda: nc.vector.tensor_tensor(out=ya[:, ch, :], in0=cacc[:], in1=p3[:], op=ALU.mult),
                      r=[kc, k3], w=[("yaT", n % 2, ch)])

            def gate_gen(n, m, sl):
                c0 = n * NT
                g_, pa_ = G2[sl]["g"], G2[sl]["pa"]
                gk, pk2 = ("gsb", sl), ("pab", sl)
                ya = yaT[n % 2]
                tr.dma("sp", g_[:], PT[(36 + m) * 128:(37 + m) * 128, c0 + 2:c0 + 2 + NT], w=[gk], sem=gk)
                yield
                tr.op("act", lambda: nc.scalar.activation(out=g_[:], in_=g_[:], func=AF.Sigmoid), r=[gk], w=[gk])
                bi, pt = nps()
                for kk in range(4):
                    tr.op("pe", lambda kk=kk: nc.tensor.matmul(pt[:], wa[:, kk, m * 128:(m + 1) * 128], ya[:, kk, :],
                                                               start=(kk == 0), stop=(kk == 3)),
                          r=["wa", ("yaT", n % 2, kk)], w=[("ps", bi)], inc=(kk == 3))
                yield
                tr.op("dve", lambda: nc.vector.tensor_tensor(out=pa_[:], in0=pt[:], in1=g_[:], op=ALU.mult),
                      r=[("ps", bi), gk], w=[pk2])
                yield
                tr.dma("sp", PAT[m * 128:(m + 1) * 128, c0:c0 + NT], pa_[:], r=[pk2], sem=pk2)

            for n in range(ntile):
                run_streams([(lambda sl, n=n, ch=ch: qkv_gen(n, ch, sl)) for ch in range(24)], W2B)
                run_streams([(lambda sl, n=n, ch=ch: abr_gen(n, ch, sl)) for ch in range(4)], 2)
                run_streams([(lambda sl, n=n, m=m: gate_gen(n, m, sl)) for m in range(8)], 2)
            tr.barrier()

        with ExitStack() as st4:
            qkv = [sb("qkv%d" % i, [128, 24, NT], BF16, st4) for i in range(2)]
            S = sb("S", [128, H, 128], F32, st4)
            Sb = sb("Sb", [128, H, 128], BF16, st4)
            sm = sb("sm", [128, 8, 8], F32, st4)
            gT = sb("gT", [8, 3, 64], F32, st4)
            Rg = sb("Rg", [8, 2, 512], F32, st4)
            dec = [sb("dec%d" % i, [64, 512], F32, st4) for i in range(2)]
            egr = sb("egr", [128, 512], F32, st4)
            qg = sb("qg", [128, H, C], BF16, st4)
            XY = [sb("XY%d" % i, [64, 2, 512], BF16, st4) for i in range(2)]
            PQ = [sb("PQ%d" % i, [64, 2, 512], BF16, st4) for i in range(2)]
            attn = sb("attn", [64, 512], BF16, st4)
            XY0 = sb("XY0", [64, 2, 512], BF16, st4)
            MK = sb("MK", [64, 4, 512], BF16, st4)
            tr.dma("pool", MK[:], cst2.rearrange("p (a n) -> p a n", a=4), w=["MK"], sem="MK")
            kbg = sb("kbg", [64, H, 128], BF16, st4)
            kdec = sb("kdec", [64, H, 128], BF16, st4)
            vb = sb("vb", [64, H, 128], BF16, st4)
            usb = sb("usb", [64, H, 128], F32, st4)
            wT = sb("wT", [128, H, C], BF16, st4)
            vnew = sb("vnew", [64, H, 128], BF16, st4)
            osb = sb("osb", [64, H, 128], F32, st4)
            ofw = sb("ofw", [64, H, 128], F32, st4)
            zsb = sb("zsb", [64, H, 128], F32, st4)
            ysq = sb("ysq", [64, H, 128], F32, st4)
            ss = sb("ss", [64, 16], F32, st4)
            ss2 = sb("ss2", [64, 16], F32, st4)
            ss3 = sb("ss3", [64, 16], F32, st4)
            yb = sb("yb", [64, H, 128], BF16, st4)
            ybT = sb("ybT", [128, H, NT], BF16, st4)
            SX = ybT[:].bitcast(F32).rearrange("p h n -> p (h n)").rearrange("p (r m) -> p r m", r=2)
            epsd = sb("epsd", [128, 1], F32, st4)
            tr.op("dve", lambda: nc.vector.memset(epsd[:], EPS), w=["epsd"])
            tr.op("dve", lambda: nc.vector.memset(S[:], 0.0), w=["S"])
            tr.op("dve", lambda: nc.vector.memset(Sb[:], 0.0), w=["Sb"])
            nck = NT // C

            def bc(ap, shape):
                return ap.to_broadcast(list(shape))

            WG = 2
            TB = [{"bat": sb("bat", [64, NT // C, 32], F32, st4), "beta": sb("beta", [64, NT // C, 8], F32, st4),
                   "lnb": sb("lnb", [64, NT // C, 8], F32, st4), "gg": sb("gg", [64, NT // C, 8], F32, st4)}
                  for _i in range(2)]
            CB = [{"sm": sm, "gT": gT, "Rg": Rg, "dec": dec, "egr": egr, "qg": qg, "XY": XY, "PQ": PQ, "attn": attn,
                   "XY0": XY0, "kbg": kbg, "kdec": kdec, "vb": vb, "usb": usb, "wT": wT, "vnew": vnew, "osb": osb,
                   "ofw": ofw, "zsb": zsb, "ysq": ysq, "ss": ss, "ss2": ss2, "ss3": ss3, "yb": yb}]
            for _i in range(1, WG):
                CB.append({
                    "sm": sb("sm", [128, 8, 8], F32, st4), "gT": sb("gT", [8, 3, 64], F32, st4),
                    "Rg": sb("Rg", [8, 2, 512], F32, st4),
                    "dec": [sb("dec", [64, 512], F32, st4) for _j in range(2)],
                    "egr": sb("egr", [128, 512], F32, st4), "qg": sb("qg", [128, H, C], BF16, st4),
                    "XY": [sb("XY", [64, 2, 512], BF16, st4) for _j in range(2)],
                    "PQ": [sb("PQ", [64, 2, 512], BF16, st4) for _j in range(2)],
                    "attn": sb("attn", [64, 512], BF16, st4), "XY0": sb("XY0", [64, 2, 512], BF16, st4),
                    "kbg": sb("kbg", [64, H, 128], BF16, st4), "kdec": sb("kdec", [64, H, 128], BF16, st4),
                    "vb": sb("vb", [64, H, 128], BF16, st4), "usb": sb("usb", [64, H, 128], F32, st4),
                    "wT": sb("wT", [128, H, C], BF16, st4), "vnew": sb("vnew", [64, H, 128], BF16, st4),
                    "osb": sb("osb", [64, H, 128], F32, st4), "ofw": sb("ofw", [64, H, 128], F32, st4),
                    "zsb": sb("zsb", [64, H, 128], F32, st4), "ysq": sb("ysq", [64, H, 128], F32, st4),
                    "ss": sb("ss", [64, 16], F32, st4), "ss2": sb("ss2", [64, 16], F32, st4),
                    "ss3": sb("ss3", [64, 16], F32, st4), "yb": sb("yb", [64, H, 128], BF16, st4)})

            def chunk_gen(dr, n, c, first, last, sl):
                Bc = CB[sl]
                Tt = TB[n % 2]
                bat, beta, lnb, gg = Tt["bat"], Tt["beta"], Tt["lnb"], Tt["gg"]
                sm, gT, Rg, dec, egr, qg, XY, PQ, attn, XY0 = (Bc["sm"], Bc["gT"], Bc["Rg"], Bc["dec"], Bc["egr"],
                                                               Bc["qg"], Bc["XY"], Bc["PQ"], Bc["attn"], Bc["XY0"])
                kbg, kdec, vb, usb, wT, vnew, osb, ofw, zsb = (Bc["kbg"], Bc["kdec"], Bc["vb"], Bc["usb"], Bc["wT"],
                                                               Bc["vnew"], Bc["osb"], Bc["ofw"], Bc["zsb"])
                ysq, ss, ss2, ss3, yb = Bc["ysq"], Bc["ss"], Bc["ss2"], Bc["ss3"], Bc["yb"]
                qt = qkv[n % 2]
                qk_ = ("qkv", n % 2)
                if first:
                    tr.dma("sp", qt[:], QKVT[:, n * NT:(n + 1) * NT].rearrange("(c p) n -> p c n", p=128),
                           w=[qk_], sem=qk_)
                    tr.dma("sp", bat[:], BAt[n * NT:(n + 1) * NT, :].rearrange("(c p) n -> p c n", p=64),
                           w=[("bat", "t", n % 2)], sem=("bat", "t", n % 2))
                    bsl = bat[:, :, dr * 8:dr * 8 + 8]
                    asl = bat[:, :, 16 + dr * 8:24 + dr * 8]
                    tr.op("act", lambda: nc.scalar.activation(out=beta[:], in_=bsl, func=AF.Sigmoid),
                          r=[("bat", "t", n % 2)], w=[("beta", "t", n % 2)])
                    tr.op("act", lambda: nc.scalar.activation(out=lnb[:], in_=bsl, func=AF.Exp, scale=-1.0),
                          r=[("bat", "t", n % 2)], w=[("lnb", "t", n % 2)])
                    tr.op("act", lambda: nc.scalar.activation(out=lnb[:], in_=lnb[:], func=AF.Ln, bias=1.0, scale=1.0),
                          r=[("lnb", "t", n % 2)], w=[("lnb", "t", n % 2)])
                    for c2 in range(nck):
                        tr.op("dve", lambda c2=c2: nc.vector.tensor_tensor(
                            out=gg[:, c2, :], in0=bat[:, c2, 16 + dr * 8:24 + dr * 8],
                            in1=tp[0:64, 16 + dr * 8:24 + dr * 8], op=ALU.add), r=[("bat", "t", n % 2), "tp"], w=[("gg", "t", n % 2)])
                    tr.op("act", lambda: nc.scalar.activation(out=gg[:], in_=gg[:], func=AF.Exp), r=[("gg", "t", n % 2)], w=[("gg", "t", n % 2)])
                    tr.op("act", lambda: nc.scalar.activation(out=gg[:], in_=gg[:], func=AF.Ln, bias=1.0, scale=1.0),
                          r=[("gg", "t", n % 2)], w=[("gg", "t", n % 2)])
                    for c2 in range(nck):
                        tr.op("dve", lambda c2=c2: nc.vector.tensor_tensor(
                            out=gg[:, c2, :], in0=gg[:, c2, :], in1=negA[0:64, dr * 8:dr * 8 + 8], op=ALU.mult),
                            r=[("gg", "t", n % 2), "negA"], w=[("gg", "t", n % 2)])
                tok = slice(c * C, (c + 1) * C)
                gtok = n * NT + c * C
                yield
                bi, pg = nps()
                tr.op("pe", lambda: nc.tensor.matmul(pg[0:64, 0:8], LTRI[dr], gg[:, c, :], start=True,
                                                     stop=True), r=[("gg", "t", n % 2), "cs"], w=[("ps", bi)], inc=False)
                tr.op("pe", lambda: nc.tensor.matmul(pg[:, 8:16], cs[0:64, 0:128], gg[:, c, :], start=True,
                                                     stop=True), r=[("gg", "t", n % 2), "cs"], w=[("ps", bi)])
                tr.op("dve", lambda: nc.vector.tensor_copy(out=sm[0:64, 0, :], in_=pg[0:64, 0:8]),
                      r=[("ps", bi)], w=[("sm", sl)])
                tr.op("dve", lambda: nc.vector.tensor_tensor(out=sm[0:64, 1, :], in0=pg[0:64, 0:8],
                                                             in1=lnb[:, c, :], op=ALU.subtract),
                      r=[("ps", bi), ("lnb", "t", n % 2)], w=[("sm", sl)])
                tr.op("act", lambda: nc.scalar.activation(out=sm[0:64, 2, :], in_=pg[0:64, 0:8], func=AF.Exp),
                      r=[("ps", bi)], w=[("sm", sl)])
                tr.op("dve", lambda: nc.vector.tensor_tensor(out=sm[0:64, 3, :], in0=pg[0:64, 8:16],
                                                             in1=sm[0:64, 0, :], op=ALU.subtract),
                      r=[("ps", bi), ("sm", sl)], w=[("sm", sl)])
                tr.op("act", lambda: nc.scalar.activation(out=sm[0:64, 3, :], in_=sm[0:64, 3, :], func=AF.Exp),
                      r=[("sm", sl)], w=[("sm", sl)])
                tr.op("dve", lambda: nc.vector.tensor_tensor(out=sm[0:64, 4, :], in0=sm[0:64, 2, :],
                                                             in1=beta[:, c, :], op=ALU.mult),
                      r=[("sm", sl), ("beta", "t", n % 2)], w=[("sm", sl)])
                tr.op("act", lambda: nc.scalar.activation(out=sm[:, 5, :], in_=pg[:, 8:16], func=AF.Exp),
                      r=[("ps", bi)], w=[("sm", sl)])
                yield
                bi2, pt2 = nps()
                tr.op("pe", lambda: nc.tensor.matmul(pt2[0:8, 0:64], sm[0:64, 0, :], ID64, start=True,
                                                     stop=True), r=[("sm", sl), "cs"], w=[("ps", bi2)], inc=False)
                tr.op("pe", lambda: nc.tensor.matmul(pt2[0:8, 64:128], sm[0:64, 1, :], ID64, start=True,
                                                     stop=True), r=[("sm", sl), "cs"], w=[("ps", bi2)])
                tr.op("dve", lambda: nc.vector.tensor_scalar(out=gT[:, 0, :], in0=pt2[0:8, 0:64], scalar1=-1.0,
                                                             scalar2=None, op0=ALU.mult),
                      r=[("ps", bi2)], w=[("gT", sl)])
                tr.op("dve", lambda: nc.vector.tensor_copy(out=gT[:, 1:3, :].rearrange("p a b -> p (a b)"),
                                                           in_=pt2[0:8, 0:128]),
                      r=[("ps", bi2)], w=[("gT", sl)])
                E8v = E8.rearrange("p (h i) -> p h i", h=8)
                tr.op("dve", lambda: nc.vector.tensor_tensor(
                    out=Rg[:, 0, :].rearrange("p (h i) -> p h i", h=8), in0=E8v,
                    in1=gT[:, 2:3, :].to_broadcast([8, 8, 64]), op=ALU.mult), r=[("gT", sl), "cs"], w=[("Rg", sl)])
                tr.op("dve", lambda: nc.vector.tensor_tensor(
                    out=Rg[:, 1, :].rearrange("p (h i) -> p h i", h=8), in0=E8v,
                    in1=gT[:, 1:2, :].to_broadcast([8, 8, 64]), op=ALU.mult), r=[("gT", sl), "cs"], w=[("Rg", sl)])
                yield
                bA, pA = nps()
                bQ, pQ = nps()
                bR, pR = nps()
                tr.op("pe", lambda: nc.tensor.matmul(pA[0:64, :], cs[0:8, 0:64], Rg[:, 0, :], start=True,
                                                     stop=False), r=[("Rg", sl), "cs"], w=[("ps", bA)], inc=False)
                tr.op("pe", lambda: nc.tensor.matmul(pA[0:64, :], gT[:, 0, :], E8, start=False, stop=True),
                      r=[("gT", sl), "cs"], w=[("ps", bA)])
                tr.op("pe", lambda: nc.tensor.matmul(pQ[0:64, :], cs[0:8, 0:64], Rg[:, 1, :], start=True,
                                                     stop=False), r=[("Rg", sl), "cs"], w=[("ps", bQ)], inc=False)
                tr.op("pe", lambda: nc.tensor.matmul(pQ[0:64, :], gT[:, 0, :], E8, start=False, stop=True),
                      r=[("gT", sl), "cs"], w=[("ps", bQ)])
                tr.op("pe", lambda: nc.tensor.matmul(pR[:, :], cs[0:8, 0:128], Rg[:, 1, :], start=True,
                                                     stop=True), r=[("Rg", sl), "cs"], w=[("ps", bR)])
                tr.op("dve", lambda: nc.vector.tensor_tensor(out=dec[0][:], in0=pA[0:64, :], in1=CAPS[dr],
                                                             op=ALU.min), r=[("ps", bA), "cs"], w=[("dec", 0, sl)])
                tr.op("act", lambda: nc.scalar.activation(out=dec[0][:], in_=dec[0][:], func=AF.Exp),
                      r=[("dec", 0, sl)], w=[("dec", 0, sl)])
                tr.op("dve", lambda: nc.vector.tensor_tensor(out=dec[1][:], in0=pQ[0:64, :], in1=CAPI[dr],
                                                             op=ALU.min), r=[("ps", bQ), "cs"], w=[("dec", 1, sl)])
                tr.op("act", lambda: nc.scalar.activation(out=dec[1][:], in_=dec[1][:], func=AF.Exp),
                      r=[("dec", 1, sl)], w=[("dec", 1, sl)])
                tr.op("act", lambda: nc.scalar.activation(out=egr[:], in_=pR[:], func=AF.Exp),
                      r=[("ps", bR)], w=[("egr", sl)])
                tr.op("dve", lambda: nc.vector.tensor_tensor(
                    out=qg[:], in0=qt[:, 0:8, tok], in1=egr[:].rearrange("p (h i) -> p h i", h=8),
                    op=ALU.mult), r=[qk_, ("egr", sl)], w=[("qg", sl)])
                yield
                bK, pK = nps()
                bQK, pQK = nps()
                for h in range(H):
                    tr.op("pe", lambda h=h: nc.tensor.matmul(pK[0:64, h * 64:(h + 1) * 64], qt[:, 8 + h, tok],
                                                             qt[:, 8 + h, tok], start=True, stop=True),
                          r=[qk_], w=[("ps", bK)], inc=(h == H - 1))
                for h in range(H):
                    tr.op("pe", lambda h=h: nc.tensor.matmul(pQK[0:64, h * 64:(h + 1) * 64], qt[:, 8 + h, tok],
                                                             qt[:, h, tok], start=True, stop=True),
                          r=[qk_], w=[("ps", bQK)], inc=(h == H - 1))
                X0, Y0 = XY0[:, 0, :], XY0[:, 1, :]
                tr.op("dve", lambda: nc.vector.tensor_tensor(out=X0, in0=pK[0:64, :], in1=dec[0][:],
                                                             op=ALU.mult),
                      r=[("ps", bK), ("dec", 0, sl)], w=[("X0", sl)])
                tr.op("dve", lambda: nc.vector.tensor_tensor(out=attn[:], in0=pQK[0:64, :], in1=dec[1][:],
                                                             op=ALU.mult),
                      r=[("ps", bQK), ("dec", 1, sl)], w=[("attn", sl)])
                yield
                bY, pY = nps()
                for h in range(H):
                    tr.op("pe", lambda h=h: nc.tensor.matmul(pY[0:64, h * 64:(h + 1) * 64],
                                                             X0[:, h * 64:(h + 1) * 64], ID64B, start=True,
                                                             stop=True),
                          r=[("X0", sl), "csb"], w=[("ps", bY)], inc=(h == H - 1))
                tr.op("act", lambda: nc.scalar.copy(out=Y0, in_=pY[0:64, :]), r=[("ps", bY)], w=[("Y0", sl)])
                Xa0, Ya0 = XY[0][:, 0, :], XY[0][:, 1, :]
                Pm, Qm = PQ[0][:, 0, :], PQ[0][:, 1, :]
                tr.op("pool", lambda: nc.gpsimd.tensor_tensor(out=Xa0, in0=X0, in1=MK[:, 0, :], op=ALU.mult),
                      r=[("X0", sl), "MK"], w=[("X", 0, sl)])
                tr.op("pool", lambda: nc.gpsimd.tensor_tensor(out=Ya0, in0=Y0, in1=MK[:, 0, :], op=ALU.mult),
                      r=[("Y0", sl), "MK"], w=[("Y", 0, sl)])
                tr.op("pool", lambda: nc.gpsimd.tensor_tensor(out=Pm, in0=I8, in1=Xa0, op=ALU.subtract),
                      r=[("X", 0, sl), "cs"], w=[("P", 0, sl)])
                tr.op("pool", lambda: nc.gpsimd.tensor_tensor(out=Qm, in0=I8, in1=Ya0, op=ALU.subtract),
                      r=[("Y", 0, sl), "cs"], w=[("Q", 0, sl)])

                def mm8(pt_, lhs, rhs, rk, bix):
                    for h in range(H):
                        hs = slice(h * 64, (h + 1) * 64)
                        tr.op("pe", lambda hs=hs: nc.tensor.matmul(pt_[0:64, hs], lhs[:, hs], rhs[:, hs],
                                                                   start=True, stop=True),
                              r=rk, w=[("ps", bix)], inc=(h == H - 1))

                for lv in range(2):
                    a, b = lv % 2, (lv + 1) % 2
                    Xa, Ya = XY[a][:, 0, :], XY[a][:, 1, :]
                    Xb, Yb = XY[b][:, 0, :], XY[b][:, 1, :]
                    Pa, Qa = PQ[a][:, 0, :], PQ[a][:, 1, :]
                    Pb, Qb = PQ[b][:, 0, :], PQ[b][:, 1, :]
                    yield
                    b1, p1 = nps()
                    mm8(p1, Ya, Xa, [("X", a, sl), ("Y", a, sl)], b1)
                    tr.op("act", lambda: nc.scalar.copy(out=Xb, in_=p1[0:64, :]), r=[("ps", b1)], w=[("X", b, sl)])
                    yield
                    b2, p2 = nps()
                    mm8(p2, Xa, Ya, [("X", a, sl), ("Y", a, sl)], b2)
                    tr.op("act", lambda: nc.scalar.copy(out=Yb, in_=p2[0:64, :]), r=[("ps", b2)], w=[("Y", b, sl)])
                    yield
                    b3, p3 = nps()
                    mm8(p3, Qa, Xb, [("Q", a, sl), ("X", b, sl)], b3)
                    tr.op("dve", lambda: nc.vector.tensor_tensor(out=Pb, in0=p3[0:64, :], in1=Pa, op=ALU.add),
                          r=[("ps", b3), ("P", a, sl)], w=[("P", b, sl)])
                    yield
                    b4, p4 = nps()
                    mm8(p4, Pa, Yb, [("P", a, sl), ("Y", b, sl)], b4)
                    tr.op("dve", lambda: nc.vector.tensor_tensor(out=Qb, in0=p4[0:64, :], in1=Qa, op=ALU.add),
                          r=[("ps", b4), ("Q", a, sl)], w=[("Q", b, sl)])
                cur = 0
                for li in range(3):
                    nxt = 1 - cur
                    Pc, Qc = PQ[cur][:, 0, :], PQ[cur][:, 1, :]
                    Pn, Qn = PQ[nxt][:, 0, :], PQ[nxt][:, 1, :]
                    Xm, Ym = XY[0][:, 0, :], XY[0][:, 1, :]
                    W1, W2 = XY[1][:, 0, :], XY[1][:, 1, :]
                    tr.op("pool", lambda: nc.gpsimd.tensor_tensor(out=Ym, in0=Y0, in1=MK[:, 1 + li, :],
                                                                  op=ALU.mult),
                          r=[("Y0", sl), "MK"], w=[("Y", 0, sl)])
                    yield
                    b1, p1 = nps()
                    mm8(p1, Ym, Pc, [("Y", 0, sl), ("P", cur, sl)], b1)
                    tr.op("act", lambda: nc.scalar.copy(out=W1, in_=p1[0:64, :]), r=[("ps", b1)], w=[("X", 1, sl)])
                    yield
                    b2, p2 = nps()
                    mm8(p2, Qc, W1, [("Q", cur, sl), ("X", 1, sl)], b2)
                    tr.op("dve", lambda: nc.vector.tensor_tensor(out=Pn, in0=Pc, in1=p2[0:64, :],
                                                                 op=ALU.subtract),
                          r=[("ps", b2), ("P", cur, sl)], w=[("P", nxt, sl)])
                    if li < 2:
                        tr.op("pool", lambda: nc.gpsimd.tensor_tensor(out=Xm, in0=X0, in1=MK[:, 1 + li, :],
                                                                      op=ALU.mult),
                              r=[("X0", sl), "MK"], w=[("X", 0, sl)])
                        yield
                        b3, p3 = nps()
                        mm8(p3, Xm, Qc, [("X", 0, sl), ("Q", cur, sl)], b3)
                        tr.op("act", lambda: nc.scalar.copy(out=W2, in_=p3[0:64, :]), r=[("ps", b3)],
                              w=[("Y", 1, sl)])
                        yield
                        b4, p4 = nps()
                        mm8(p4, Pc, W2, [("P", cur, sl), ("Y", 1, sl)], b4)
                        tr.op("dve", lambda: nc.vector.tensor_tensor(out=Qn, in0=Qc, in1=p4[0:64, :],
                                                                     op=ALU.subtract),
                              r=[("ps", b4), ("Q", cur, sl)], w=[("Q", nxt, sl)])
                    cur = nxt
                TT = PQ[cur][:, 0, :]
                TK = ("P", cur, sl)
                for half in range(2):
                    yield
                    bk, pk_ = nps()
                    bv, pv_ = nps()
                    for hh in range(4):
                        h = half * 4 + hh
                        tr.op("pe", lambda h=h, hh=hh: nc.tensor.matmul(
                            pk_[0:64, hh * 128:(hh + 1) * 128], qt[:, 8 + h, tok], ID128B, start=True,
                            stop=True), r=[qk_, "csb"], w=[("ps", bk)], inc=(hh == 3))
                    for hh in range(4):
                        h = half * 4 + hh
                        tr.op("pe", lambda h=h, hh=hh: nc.tensor.matmul(
                            pv_[0:64, hh * 128:(hh + 1) * 128], qt[:, 16 + h, tok], ID128B, start=True,
                            stop=True), r=[qk_, "csb"], w=[("ps", bv)], inc=(hh == 3))
                    hsl = slice(half * 4, half * 4 + 4)
                    pk3 = pk_[0:64, :].rearrange("p (h d) -> p h d", h=4)
                    pv3 = pv_[0:64, :].rearrange("p (h d) -> p h d", h=4)
                    tr.op("dve", lambda pk3=pk3, hsl=hsl: nc.vector.tensor_tensor(
                        out=kbg[:, hsl, :], in0=pk3,
                        in1=sm[0:64, 4, hsl].unsqueeze(2).to_broadcast([64, 4, 128]), op=ALU.mult),
                        r=[("ps", bk), ("sm", sl)], w=[("kbg", sl)])
                    tr.op("dve", lambda pk3=pk3, hsl=hsl: nc.vector.tensor_tensor(
                        out=kdec[:, hsl, :], in0=pk3,
                        in1=sm[0:64, 3, hsl].unsqueeze(2).to_broadcast([64, 4, 128]), op=ALU.mult),
                        r=[("ps", bk), ("sm", sl)], w=[("kdec", sl)])
                    tr.op("dve", lambda pv3=pv3, hsl=hsl: nc.vector.tensor_tensor(
                        out=vb[:, hsl, :], in0=pv3,
                        in1=beta[:, c, hsl].unsqueeze(2).to_broadcast([64, 4, 128]), op=ALU.mult),
                        r=[("ps", bv), ("beta", "t", n % 2)], w=[("vb", sl)])
                for half in range(2):
                    yield
                    bu, pu = nps()
                    for hh in range(4):
                        h = half * 4 + hh
                        tr.op("pe", lambda h=h, hh=hh: nc.tensor.matmul(
                            pu[0:64, hh * 128:(hh + 1) * 128], TT[:, h * 64:(h + 1) * 64], vb[:, h, :],
                            start=True, stop=True), r=[TK, ("vb", sl)], w=[("ps", bu)], inc=(hh == 3))
                    tr.op("act", lambda half=half, pu=pu: nc.scalar.copy(
                        out=usb[:, half * 4:half * 4 + 4, :].rearrange("p h d -> p (h d)"), in_=pu[0:64, :]),
                        r=[("ps", bu)], w=[("usb", sl)])
                yield
                bw, pw = nps()
                for h in range(H):
                    tr.op("pe", lambda h=h: nc.tensor.matmul(pw[:, h * 64:(h + 1) * 64], kbg[:, h, :],
                                                             TT[:, h * 64:(h + 1) * 64], start=True, stop=True),
                          r=[TK, ("kbg", sl)], w=[("ps", bw)], inc=(h == H - 1))
                tr.op("act", lambda: nc.scalar.copy(out=wT[:].rearrange("p h i -> p (h i)"), in_=pw[:, :]),
                      r=[("ps", bw)], w=[("wT", sl)])
                for half in range(2):
                    bws, pws = nps()
                    for hh in range(4):
                        h = half * 4 + hh
                        tr.op("pe", lambda h=h, hh=hh: nc.tensor.matmul(
                            pws[0:64, hh * 128:(hh + 1) * 128], wT[:, h, :], Sb[:, h, :], start=True,
                            stop=True), r=[("wT", sl), "Sb"], w=[("ps", bws)], inc=(hh == 3))
                    tr.op("dve", lambda half=half, pws=pws: nc.vector.tensor_tensor(
                        out=vnew[:, half * 4:half * 4 + 4, :].rearrange("p h d -> p (h d)"),
                        in0=usb[:, half * 4:half * 4 + 4, :].rearrange("p h d -> p (h d)"), in1=pws[0:64, :],
                        op=ALU.subtract), r=[("ps", bws), ("usb", sl)], w=[("vnew", half, sl)])
                for half in range(2):
                    bo, po = nps()
                    for hh in range(4):
                        h = half * 4 + hh
                        tr.op("pe", lambda h=h, hh=hh: nc.tensor.matmul(
                            po[0:64, hh * 128:(hh + 1) * 128], qg[:, h, :], Sb[:, h, :], start=True,
                            stop=False), r=[("qg", sl), "Sb"], w=[("ps", bo)], inc=False)
                        tr.op("pe", lambda h=h, hh=hh: nc.tensor.matmul(
                            po[0:64, hh * 128:(hh + 1) * 128], attn[:, h * 64:(h + 1) * 64], vnew[:, h, :],
                            start=False, stop=True), r=[("attn", sl), ("vnew", half, sl)], w=[("ps", bo)],
                            inc=(hh == 3))
                    osl = osb[:, half * 4:half * 4 + 4, :].rearrange("p h d -> p (h d)")
                    if dr == 0:
                        tr.op("act", lambda osl=osl, po=po: nc.scalar.copy(out=osl, in_=po[0:64, :]),
                              r=[("ps", bo)], w=[("osb", half, sl)])
                    else:
                        if half == 0:
                            tr.dma("sp", ofw[:].rearrange("p h d -> p (h d)"), OF[gtok:gtok + C, :],
                                   r=[("OF", gtok)], w=[("ofw", sl)], sem=("ofw", sl))
                            tr.dma("sp", zsb[:].rearrange("p h d -> p (h d)"), ZS[gtok:gtok + C, :],
                                   w=[("zsb", sl)], sem=("zsb", sl))
                        tr.op("dve", lambda osl=osl, po=po, half=half: nc.vector.tensor_tensor(
                            out=osl, in0=po[0:64, :],
                            in1=ofw[:, half * 4:half * 4 + 4, :].rearrange("p h d -> p (h d)"), op=ALU.add),
                            r=[("ps", bo), ("ofw", sl)], w=[("osb", half, sl)])
                for half in range(2):
                    bs, pS = nps()
                    for hh in range(4):
                        h = half * 4 + hh
                        tr.op("pe", lambda h=h, hh=hh: nc.tensor.matmul(
                            pS[:, hh * 128:(hh + 1) * 128], kdec[:, h, :], vnew[:, h, :], start=True,
                            stop=True), r=[("kdec", sl), ("vnew", half, sl)], w=[("ps", bs)], inc=(hh == 3))
                    hsl = slice(half * 4, half * 4 + 4)
                    tr.op("pool", lambda hsl=hsl: nc.gpsimd.tensor_tensor(
                        out=S[:, hsl, :], in0=S[:, hsl, :],
                        in1=sm[:, 5, hsl].unsqueeze(2).to_broadcast([128, 4, 128]), op=ALU.mult),
                        r=[("sm", sl), "S"], w=["S"])
                    tr.op("dve", lambda hsl=hsl, pS=pS: nc.vector.tensor_tensor(
                        out=S[:, hsl, :].rearrange("p h d -> p (h d)"),
                        in0=S[:, hsl, :].rearrange("p h d -> p (h d)"), in1=pS[:, :], op=ALU.add),
                        r=[("ps", bs), "S"], w=["S"])
                tr.op("act", lambda: nc.scalar.copy(out=Sb[:], in_=S[:]), r=["S"], w=["Sb"])
                if dr == 0:
                    tr.dma("sp", OF[gtok:gtok + C, :], osb[:].rearrange("p h d -> p (h d)"),
                           r=[("osb", 0, sl), ("osb", 1, sl)], w=[("OF", gtok)], sem=("osb", sl))
                else:
                    for h in range(H):
                        hf = h // 4
                        tr.op("pool", lambda h=h: nc.gpsimd.tensor_tensor(
                            out=ysq[:, h, :], in0=osb[:, h, :], in1=osb[:, h, :], op=ALU.mult),
                            r=[("osb", hf, sl)], w=[("ysq", h, sl)])
                        tr.op("dve", lambda h=h: nc.vector.tensor_reduce(
                            out=ss[:, 8 + h:9 + h], in_=ysq[:, h, :], axis=AX.X, op=ALU.add),
                            r=[("ysq", h, sl)], w=[("ss", sl)])
                    tr.op("act", lambda: nc.scalar.activation(out=ss2[:, 8:16], in_=ss[:, 8:16], func=AF.Sqrt,
                                                              bias=epsd[0:64, 0:1], scale=1.0 / 128),
                          r=[("ss", sl), "epsd"], w=[("ss2", sl)])
                    tr.op("dve", lambda: nc.vector.reciprocal(out=ss3[:, 8:16], in_=ss2[:, 8:16]),
                          r=[("ss2", sl)], w=[("ss3", sl)])
                    for h in range(H):
                        hf = h // 4
                        tr.op("dve", lambda h=h: nc.vector.scalar_tensor_tensor(
                            out=ysq[:, h, :], in0=osb[:, h, :], scalar=ss3[:, 8 + h:9 + h], in1=tp[0:64, 32:160],
                            op0=ALU.mult, op1=ALU.mult), r=[("osb", hf, sl), ("ss3", sl), "tp"], w=[("ysq", h, sl)],
                            strict=True)
                        tr.op("dve", lambda h=h: nc.vector.tensor_tensor(
                            out=yb[:, h, :], in0=ysq[:, h, :], in1=zsb[:, h, :], op=ALU.mult),
                            r=[("ysq", h, sl), ("zsb", sl)], w=[("yb", hf, sl)])
                    yield
                    by, py = nps()
                    for h in range(H):
                        tr.op("pe", lambda h=h: nc.tensor.matmul(py[:, h * 64:(h + 1) * 64], yb[:, h, :],
                                                                 ID64B, start=True, stop=True),
                              r=[("yb", 0, sl), ("yb", 1, sl), "csb"], w=[("ps", by)], inc=(h == H - 1))
                    tr.op("act", lambda: nc.scalar.copy(out=ybT[:, :, tok],
                                                        in_=py[:, :].rearrange("p (h i) -> p h i", h=8)),
                          r=[("ps", by)], w=["ybT"])
                if last and dr == 1:
                    tr.dma("sp", YBT[:, n * NT:(n + 1) * NT].rearrange("(h p) n -> p h n", p=128), ybT[:],
                           r=["ybT"], sem="ybT")

            for dr in range(2):
                tiles = list(range(ntile)) if dr == 0 else list(range(ntile - 1, -1, -1))
                facs = []
                for n in tiles:
                    chunks = list(range(nck)) if dr == 0 else list(range(nck - 1, -1, -1))
                    for ci, c in enumerate(chunks):
                        facs.append(lambda sl, dr=dr, n=n, c=c, ci=ci: chunk_gen(dr, n, c, ci == 0, ci == nck - 1, sl))
                run_streams(facs, WG, stagger=14)
                if dr == 0:
                    tr.dma("sp", sx_in.ap(), S[:].rearrange("p h d -> p (h d)"), r=["S"], w=["sx_in"], sem="sx")
                    tr.cc(lambda: nc.gpsimd.collective_compute("AllGather", ALU.bypass, replica_groups=PAIRS,
                                                                ins=[sx_in.ap()], outs=[sx_out.ap()]),
                          r=["sx_in"], w=["sx_out"], sem="sxcc")
                    tr.dma("sp", SX, sx_out.ap().rearrange("(r p) n -> p r n", p=128), r=["sx_out"], w=["SX", "ybT"],
                           sem="sx")
                    Sf = S[:].rearrange("p h d -> p (h d)")
                    tr.op("dve", lambda: nc.vector.tensor_scalar(out=Sf, in0=SX[:, 0, :], scalar1=selt[:, 0:1],
                                                                 scalar2=None, op0=ALU.mult),
                          r=["SX", "selt"], w=["S"])
                    tr.op("dve", lambda: nc.vector.scalar_tensor_tensor(out=Sf, in0=SX[:, 1, :], scalar=selt[:, 1:2],
                                                                        in1=Sf, op0=ALU.mult, op1=ALU.add),
                          r=["SX", "selt", "S"], w=["S"])
                    tr.op("act", lambda: nc.scalar.copy(out=Sb[:], in_=S[:]), r=["S"], w=["Sb"])
            tr.barrier()

        with ExitStack() as st5:
            ring_phase(st5, [b for _ in range(ntile) for b in range(OFF_UP2, OFF_UP2 + NB_UP)])
            P = ffn_pool(st5, wdn2, nxt=1)
            wb = sb("wb", [128, 8, D], BF16, st5)
            wo = sb("wo", [128, 8, D], BF16, st5)
            ybt = P["u"]
            mrg = P["hid"]
            gsb = [sb("gsb%d" % i, [128, NT], F32, st5) for i in range(2)]
            pab = [sb("pab%d" % i, [128, NT], F32, st5) for i in range(2)]
            tr.dma("pool", wb[:], wbout.rearrange("(k p) n -> p k n", p=128), w=["wb"], sem="wb")
            tr.dma("pool", wo[:], wout.rearrange("(k p) n -> p k n", p=128), w=["wo"], sem="wo")
            for n in range(ntile):
                xt = P["xt"][0]
                xkey = "xt0"
                c0 = n * NT
                load_tile(xt, H1T, c0, xkey)
                tr.dma("sp", ybt[:], YBT[:, c0:c0 + NT].rearrange("(h p) n -> p h n", p=128),
                       w=[("u", k) for k in range(8)], sem="ybt")
                for m in range(8):
                    g_ = gsb[m % 2]
                    gk = ("gsb", m % 2)
                    pa_ = pab[m % 2]
                    pk2 = ("pab", m % 2)
                    tr.dma("sp", g_[:], PT[(44 + m) * 128:(45 + m) * 128, c0 + 2:c0 + 2 + NT], w=[gk], sem=gk)
                    tr.dma("sp", pa_[:], PAT[m * 128:(m + 1) * 128, c0:c0 + NT], w=[pk2], sem=pk2)
                    tr.op("act", lambda g_=g_: nc.scalar.activation(out=g_[:], in_=g_[:], func=AF.Sigmoid),
                          r=[gk], w=[gk])
                    bi, pt = nps()
                    for kk in range(8):
                        tr.op("pe", lambda kk=kk, m=m, pt=pt: nc.tensor.matmul(
                            pt[:], wb[:, kk, m * 128:(m + 1) * 128], ybt[:, kk, :], start=(kk == 0), stop=(kk == 7)),
                            r=["wb", ("u", kk)], w=[("ps", bi)], inc=(kk == 7))
                    tr.op("dve", lambda g_=g_, pt=pt: nc.vector.tensor_tensor(out=g_[:], in0=pt[:], in1=g_[:],
                                                                              op=ALU.mult),
                          r=[("ps", bi), gk], w=[gk])
                    tr.op("pool", lambda g_=g_, pa_=pa_, m=m: nc.gpsimd.tensor_tensor(out=mrg[:, m, :], in0=g_[:],
                                                                                      in1=pa_[:], op=ALU.add),
                          r=[gk, pk2], w=[("hid", m)])
                for m in range(8):
                    bi, pt = nps()
                    for kk in range(8):
                        tr.op("pe", lambda kk=kk, m=m, pt=pt: nc.tensor.matmul(
                            pt[:], wo[:, kk, m * 128:(m + 1) * 128], mrg[:, kk, :], start=(kk == 0), stop=(kk == 7)),
                            r=["wo", ("hid", kk)], w=[("ps", bi)], inc=(kk == 7))
                    tr.op("dve", lambda m=m, pt=pt: nc.vector.scalar_tensor_tensor(
                        out=xt[:, m, :], in0=pt[:], scalar=der[:, 32 + m:33 + m], in1=xt[:, m, :], op0=ALU.mult,
                        op1=ALU.add), r=[("ps", bi), "der"], w=[(xkey, m)])
                ffn_body(P, xt, xkey, 2, P["wdn"], True, outT, c0)
            tr.barrier()
        tr.finish()
    return nc


def _blocks(w, cols_list):
    out = np.empty((len(cols_list), 128, 2048), np.float32)
    w3 = w.reshape(8, 128, -1)
    for b, cols in enumerate(cols_list):
        out[b] = w3[:, :, cols].transpose(1, 0, 2).reshape(128, 2048)
    return out


def _consts():
    c = np.zeros((128, 4096), np.float32)
    c[:, 0:128] = 1.0
    c[0:64, 128:192] = np.eye(64)
    j = np.arange(64)[:, None]
    i = np.arange(64)[None, :]
    c[0:64, 192:256] = (j <= i)
    c[0:64, 256:320] = (j >= i)
    capS_F = np.where(j < i, 0.0, NEG)
    capI_F = np.where(j <= i, 0.0, NEG)
    capS_B = np.where(j > i, 0.0, NEG)
    capI_B = np.where(j >= i, 0.0, NEG)
    c[0:64, 320:832] = np.tile(capS_F, (1, 8))
    c[0:64, 832:1344] = np.tile(capI_F, (1, 8))
    c[0:64, 1344:1856] = np.tile(capS_B, (1, 8))
    c[0:64, 1856:2368] = np.tile(capI_B, (1, 8))
    c[0:64, 2368:2880] = np.tile(np.eye(64), (1, 8))
    e8 = np.zeros((8, 8, 64), np.float32)
    for h in range(8):
        e8[h, h, :] = 1.0
    c[0:8, 2880:3392] = e8.reshape(8, 512)
    c[:, 3392:3520] = np.eye(128)
    return c


_PROG = {}


def _run(inputs, T, n_cores):
    f = lambda a: np.ascontiguousarray(np.asarray(a, dtype=np.float32))
    x = f(inputs["x"])
    B = x.shape[0]
    cvec = f(inputs["c"])
    w_ada = f(inputs["w_ada"])[0]
    b_ada = f(inputs["b_ada"])[0]
    w_in = f(inputs["w_in"])[0]
    wup1 = f(inputs["w_ffn1_up"])[0]
    wup2 = f(inputs["w_ffn2_up"])[0]
    conv_a = f(inputs["conv_a"])[0]
    conv_dn = f(inputs["conv_dn"])[0]
    cst = _consts()
    ii = np.arange(64)[:, None]
    jj = np.arange(64)[None, :]
    mks = [(ii // 8 == jj // 8)]
    for bsz in (8, 16, 32):
        mks.append((ii // (2 * bsz) == jj // (2 * bsz)) & (ii // bsz != jj // bsz))
    cst2 = np.concatenate([np.tile(m.astype(np.float32), (1, 8)) for m in mks], axis=1)

    ada_cols = [np.arange(b * 256, (b + 1) * 256) for b in range(NB_ADA)]
    up_cols = [np.concatenate([np.arange(j * 128, (j + 1) * 128), np.arange(DFF + j * 128, DFF + (j + 1) * 128)])
               for j in range(NJ)]
    fm = np.concatenate([np.arange(512, 1024), np.arange(1024, 1536), np.arange(1536, 4608),
                         np.arange(0, 512), np.arange(5664, 7712)])
    in_cols = [fm[b * 256:(b + 1) * 256] for b in range(NB_IN)]
    z_cols = [np.arange(4608 + b * 256, 4608 + (b + 1) * 256) for b in range(NB_Z)]
    wr_common = np.concatenate([_blocks(w_ada, ada_cols), _blocks(wup1, up_cols), _blocks(w_in, in_cols + z_cols),
                                _blocks(wup2, up_cols)], axis=0)
    assert wr_common.shape[0] == NB_TOT

    ba_f = np.arange(5632, 5664)
    ba_b = np.concatenate([np.arange(5640, 5648), np.arange(5632, 5640), np.arange(5656, 5664),
                           np.arange(5648, 5656)])
    norms = np.stack([f(inputs["norm_ffn1"])[0], f(inputs["norm_mix"])[0], f(inputs["norm_ffn2"])[0],
                      f(inputs["norm_final"])], 0)
    in_maps = []
    for core in range(n_cores):
        b, half = core // 2, core % 2
        xs = x[b, half * T:(half + 1) * T, :]
        if half == 1:
            xs = xs[::-1]
        vecs = np.zeros((128, 120), np.float32)
        vecs[:, 0:8] = cvec[b].reshape(8, 128).T
        vecs[:, 8:80] = b_ada.reshape(72, 128).T
        vecs[:, 80:112] = norms.reshape(4, 8, 128).transpose(2, 0, 1).reshape(128, 32)
        cdn = conv_dn if half == 0 else conv_dn[::-1]
        ca = conv_a if half == 0 else conv_a[::-1]
        convw = np.zeros((128, 132), np.float32)
        convw[:, 0:120] = cdn.T.reshape(24, 128, 5).transpose(1, 0, 2).reshape(128, 120)
        convw[:, 120:132] = ca.T.reshape(4, 128, 3).transpose(1, 0, 2).reshape(128, 12)
        names = ["a_log_fwd", "a_log_bwd", "dt_bias_fwd", "dt_bias_bwd"]
        if half == 1:
            names = ["a_log_bwd", "a_log_fwd", "dt_bias_bwd", "dt_bias_fwd"]
        tokp = np.zeros((128, 160), np.float32)
        for q_, nm in enumerate(names):
            tokp[:, q_ * 8:(q_ + 1) * 8] = f(inputs[nm])[0][None, :]
        tokp[:, 32:160] = f(inputs["dn_norm"])[0][None, :]
        selv = np.zeros((128, 2), np.float32)
        selv[:, 1 - half] = 1.0
        in_maps.append({
            "xT": np.ascontiguousarray(xs.T),
            "wr": wr_common,
            "wdn1": f(inputs["w_ffn1_down"])[0],
            "wdn2": f(inputs["w_ffn2_down"])[0],
            "wba": np.ascontiguousarray(w_in[:, ba_f if half == 0 else ba_b]),
            "waout": f(inputs["w_a_out"])[0],
            "wbout": f(inputs["w_b_out"])[0],
            "wout": f(inputs["w_out"])[0],
            "vecs": vecs, "convw": convw, "tokp": tokp, "cst": cst, "cst2": cst2, "sel": selv,
        })
    if T not in _PROG:
        _PROG[T] = build_program(T)
    res = run_bass_kernel_spmd(_PROG[T], in_maps, core_ids=list(range(n_cores)))
    if DEBUG:
        DBG["res"] = res.results
    out = np.empty((B, 2 * T, D), np.float32)
    for core in range(n_cores):
        b, half = core // 2, core % 2
        o = res.results[core]["outT"].T
        if half == 1:
            o = o[::-1]
        out[b, half * T:(half + 1) * T, :] = o
    return out


def kernel(**inputs):
    x = np.asarray(inputs["x"])
    B, S, _ = x.shape
    return _run(inputs, S // 2, 2 * B)
```

```python
import numpy as np
from contextlib import ExitStack
import concourse.bass as bass
import concourse.mybir as mybir
from concourse.bass_utils import run_bass_kernel_spmd

F32 = mybir.dt.float32
BF16 = mybir.dt.bfloat16
AF = mybir.ActivationFunctionType
ALU = mybir.AluOpType
AX = mybir.AxisListType

D = 1024
DFF = 2816
NJ = DFF // 128
H = 8
C = 64
NT = 512
EPS = 1e-6
NR = 4
NEG = -30000.0
DEBUG = False
SAME_ENGINE_SYNC = True
DBG = {}

NB_ADA = 36
NB_UP = 22
NB_IN = 26
NB_Z = 4
OFF_ADA = 0
OFF_UP1 = OFF_ADA + NB_ADA
OFF_IN = OFF_UP1 + NB_UP
OFF_Z = OFF_IN + NB_IN
OFF_UP2 = OFF_Z + NB_Z
NB_TOT = OFF_UP2 + NB_UP
NPC = 52
NCONV = 32


class Tick:
    __slots__ = ("kind", "key", "val")

    def __init__(self, kind, key, val):
        self.kind, self.key, self.val = kind, key, val


class Tr:
    def __init__(self, nc, es):
        self.nc = nc
        self.es = es
        self.E = {"pe": nc.tensor, "act": nc.scalar, "dve": nc.vector, "pool": nc.gpsimd, "sp": nc.sync}
        self.csem = {e: es.enter_context(nc.semaphore("c_" + e)) for e in ("pe", "act", "dve", "pool")}
        self.ccnt = {e: 0 for e in self.csem}
        self.pend = {e: [] for e in self.csem}
        self.dsem = {}
        self.dcnt = {}
        self.seen = {e: {} for e in self.E}
        self.st = {}

    def _wait(self, eng, t, same_ok=True):
        if t is None:
            return
        if t.kind == "c":
            if t.key == eng and (t.val is None or (same_ok and not SAME_ENGINE_SYNC)):
                return
            assert t.val is not None, "pending tick waited on"
            sem, v, k = self.csem[t.key], t.val, ("c", t.key)
        else:
            sem, v, k = self.dsem[t.key], self.dcnt[t.key], ("d", t.key)
        if self.seen[eng].get(k, 0) >= v:
            return
        self.seen[eng][k] = v
        self.E[eng].wait_ge(sem, v)

    def _deps(self, eng, r, w, same_ok):
        for k in r:
            s = self.st.get(k)
            if s:
                self._wait(eng, s[0], same_ok)
        for k in w:
            s = self.st.get(k)
            if s:
                self._wait(eng, s[0], same_ok)
                for t in s[1].values():
                    self._wait(eng, t, same_ok)

    def _upd(self, t, r, w):
        for k in r:
            self.st.setdefault(k, [None, {}])[1][(t.kind, t.key)] = t
        for k in w:
            self.st[k] = [t, {}]

    def op(self, eng, fn, r=(), w=(), inc=True, strict=False):
        self._deps(eng, r, w, not strict)
        ins = fn()
        t = Tick("c", eng, None)
        self.pend[eng].append(t)
        if inc:
            self.ccnt[eng] += 1
            ins.then_inc(self.csem[eng], 1)
            for p in self.pend[eng]:
                p.val = self.ccnt[eng]
            self.pend[eng] = []
        self._upd(t, r, w)
        return ins

    def dma(self, q, out, in_, r=(), w=(), sem=None):
        self._deps(q, r, w, False)
        if sem not in self.dsem:
            self.dsem[sem] = self.es.enter_context(self.nc.semaphore("d%d" % len(self.dsem)))
            self.dcnt[sem] = 0
        ins = self.E[q].dma_start(out=out, in_=in_)
        self.dcnt[sem] += 16
        ins.then_inc(self.dsem[sem], 16)
        t = Tick("d", sem, self.dcnt[sem])
        self._upd(t, r, w)
        return ins

    def cc(self, ins_fn, r=(), w=(), sem=None):
        self._deps("pool", r, w, False)
        if sem not in self.dsem:
            self.dsem[sem] = self.es.enter_context(self.nc.semaphore("d%d" % len(self.dsem)))
            self.dcnt[sem] = 0
        ins = ins_fn()
        self.dcnt[sem] += 1
        ins.then_inc(self.dsem[sem], 1)
        t = Tick("d", sem, self.dcnt[sem])
        self._upd(t, r, w)

    def barrier(self):
        for e in self.csem:
            assert not self.pend[e]
        for e in self.E:
            for f in self.csem:
                if f != e and self.ccnt[f] > 0:
                    self._wait(e, Tick("c", f, self.ccnt[f]))
            for k in self.dsem:
                if self.dcnt[k] > 0:
                    self._wait(e, Tick("d", k, self.dcnt[k]))
        self.st = {}

    def finish(self):
        for k in self.dsem:
            if self.dcnt[k] > 0:
                self._wait("sp", Tick("d", k, self.dcnt[k]))


class Ring:
    def __init__(self, tr, nc, slots, wr):
        self.tr, self.nc, self.slots, self.wr = tr, nc, slots, wr
        self.plan = []
        self.issued = 0
        self.pos = 0

    def add(self, blocks):
        self.plan.extend(blocks)

    def _issue(self):
        b = self.plan[self.issued]
        s = self.issued % NR
        self.tr.dma("pool", self.slots[s][:], self.wr[b], w=[("ring", s)], sem=("ring", s))
        self.issued += 1

    def get(self):
        while self.issued < min(len(self.plan), self.pos + NR):
            self._issue()
        s = self.pos % NR
        self.pos += 1
        return s, self.slots[s]


def build_program(T):
    ntile = T // NT
    nchunk = T // C
    nc = bass.Bass("TRN2", target_bir_lowering=False)

    def din(name, shape, dt=F32):
        return nc.dram_tensor(name, list(shape), dt, kind="ExternalInput").ap()

    xT = din("xT", [D, T])
    wr = din("wr", [NB_TOT, 128, 2048])
    wdn1 = din("wdn1", [DFF, D])
    wdn2 = din("wdn2", [DFF, D])
    wba = din("wba", [D, 32])
    waout = din("waout", [512, D])
    wbout = din("wbout", [D, D])
    wout = din("wout", [D, D])
    vecs = din("vecs", [128, 120])
    convw = din("convw", [128, 24 * 5 + 4 * 3])
    tokp = din("tokp", [128, 32 + 128])
    cst = din("cst", [128, 4096])
    cst2 = din("cst2", [64, 2048])
    sel = din("sel", [128, 2])
    outT = nc.dram_tensor("outT", [D, T], F32, kind="ExternalOutput").ap()

    def dscr(name, shape, dt=F32):
        if DEBUG:
            return nc.dram_tensor(name, list(shape), dt, kind="ExternalOutput").ap()
        return nc.dram_tensor(name, list(shape), dt).ap()

    H1T = dscr("H1T", [D, T])
    PT = dscr("PT", [NPC * 128, T + 4])
    ZS = dscr("ZS", [T, D])
    BAt = dscr("BAt", [T, 32])
    QKVT = dscr("QKVT", [3 * D, T], BF16)
    PAT = dscr("PAT", [D, T])
    OF = dscr("OF", [T, D])
    YBT = dscr("YBT", [D, T], BF16)
    hx_in = nc.dram_tensor("hx_in", [NCONV * 128, 2], F32)
    hx_out = nc.dram_tensor("hx_out", [2 * NCONV * 128, 2], F32)
    sx_in = nc.dram_tensor("sx_in", [128, H * 128], F32)
    sx_out = nc.dram_tensor("sx_out", [256, H * 128], F32)
    PAIRS = [[0, 1], [2, 3], [4, 5], [6, 7]]

    with ExitStack() as es:
        tr = Tr(nc, es)

        sbn = [0]

        def sb(name, shape, dt=F32, stack=None):
            sbn[0] += 1
            return (stack or es).enter_context(nc.sbuf_tensor("%s_%d" % (name, sbn[0]), list(shape), dt))

        ps = [es.enter_context(nc.psum_tensor("ps%d" % i, [128, 512], F32)) for i in range(8)]
        psi = [0]

        def nps():
            for _ in range(8):
                i = psi[0] % 8
                psi[0] += 1
                s_ = tr.st.get(("ps", i))
                if s_ is None or s_[0] is None or len(s_[1]) > 0:
                    return i, ps[i]
            raise RuntimeError("all PSUM banks hold unconsumed results")

        ring = Ring(tr, nc, None, wr)

        def ring_phase(stack, blocks):
            assert ring.issued == len(ring.plan) and ring.pos == len(ring.plan)
            ring.slots = [sb("ring", [128, 2048], BF16, stack) for i in range(NR)]
            ring.add(blocks)

        vec = sb("vec", [128, 120])
        cw = sb("cw", [128, 132])
        tp = sb("tp", [128, 160])
        cs = sb("cs", [128, 4096])
        csb = sb("csb", [128, 1024], BF16)
        selt = sb("selt", [128, 2])
        modT = sb("modT", [128, 72])
        der = sb("der", [128, 64])
        cact = sb("cact", [128, 8], BF16)
        negA = sb("negA", [128, 16])
        tr.dma("sp", vec[:], vecs, w=["vec"], sem="c0")
        tr.dma("sp", cw[:], convw, w=["cw"], sem="c0")
        tr.dma("sp", tp[:], tokp, w=["tp"], sem="c0")
        tr.dma("sp", cs[:], cst, w=["cs"], sem="c0")
        tr.dma("sp", selt[:], sel, w=["selt"], sem="c0")
        ONES = cs[:, 0:128]
        ID64 = cs[0:64, 128:192]
        LTRI = [cs[0:64, 192:256], cs[0:64, 256:320]]
        CAPS = [cs[0:64, 320:832], cs[0:64, 1344:1856]]
        CAPI = [cs[0:64, 832:1344], cs[0:64, 1856:2368]]
        I8 = cs[0:64, 2368:2880]
        E8 = cs[0:8, 2880:3392]
        tr.op("dve", lambda: nc.vector.tensor_copy(out=csb[:, 0:256], in_=cs[:, 0:256]), r=["cs"], w=["csb"])
        tr.op("dve", lambda: nc.vector.tensor_copy(out=csb[:, 256:384], in_=cs[:, 3392:3520]), r=["cs"], w=["csb"])
        ONESB = csb[:, 0:128]
        ID64B = csb[0:64, 128:192]
        ID128B = csb[:, 256:384]

        tr.op("act", lambda: nc.scalar.activation(out=cact[:], in_=vec[:, 0:8], func=AF.Silu), r=["vec"], w=["cact"])
        st0 = ExitStack()
        ring_phase(st0, list(range(OFF_ADA, OFF_ADA + NB_ADA)))
        bi, pb = nps()
        for blk in range(NB_ADA):
            s, slot = ring.get()
            sv = slot[:].rearrange("p (k c) -> p k c", k=8)
            for cc in range(2):
                j = blk * 2 + cc
                for k in range(8):
                    tr.op("pe", lambda k=k, j=j, cc=cc: nc.tensor.matmul(
                        pb[:, j:j + 1], sv[:, k, cc * 128:(cc + 1) * 128], cact[:, k:k + 1],
                        start=(k == 0), stop=(k == 7)),
                        r=[("ring", s), "cact"], w=[("ps", bi)], inc=(k == 7 and cc == 1))
        tr.op("dve", lambda: nc.vector.tensor_tensor(out=modT[:], in0=pb[:, 0:72], in1=vec[:, 8:80], op=ALU.add),
              r=[("ps", bi), "vec"], w=["modT"])
        for s in range(3):
            tr.op("dve", lambda s=s: nc.vector.scalar_tensor_tensor(
                out=der[:, s * 8:(s + 1) * 8], in0=modT[:, (3 * s + 1) * 8:(3 * s + 2) * 8], scalar=1.0,
                in1=vec[:, 80 + s * 8:88 + s * 8], op0=ALU.add, op1=ALU.mult), r=["modT", "vec"], w=["der"])
            gsc = 1.0 if s == 1 else 0.5
            tr.op("dve", lambda s=s, gsc=gsc: nc.vector.tensor_scalar(
                out=der[:, 24 + s * 8:32 + s * 8], in0=modT[:, (3 * s + 2) * 8:(3 * s + 3) * 8],
                scalar1=gsc, scalar2=None, op0=ALU.mult), r=["modT"], w=["der"])
        NF = vec[:, 104:112]
        tr.op("act", lambda: nc.scalar.activation(out=negA[:], in_=tp[:, 0:16], func=AF.Exp), r=["tp"], w=["negA"])
        tr.op("dve", lambda: nc.vector.tensor_scalar(out=negA[:], in0=negA[:], scalar1=-1.0, scalar2=None,
                                                     op0=ALU.mult), r=["negA"], w=["negA"])

        tr.barrier()
        st0.close()

        def load_tile(dst, src, t0, key, n=NT, coff=0):
            for i in range(8):
                tr.dma("sp", dst[:, i, coff:coff + n], src[i * 128:(i + 1) * 128, t0:t0 + n],
                       w=[(key, i)], sem=(key,))

        def norm_mod(P, xt, xkey, u, s_idx):
            bi, pss = nps()
            for i in range(8):
                sq = P["sq"][i % 2]
                tr.op("act", lambda i=i, sq=sq: nc.scalar.activation(out=sq[:], in_=xt[:, i, :], func=AF.Square),
                      r=[(xkey, i)], w=[("sq", i % 2)])
                tr.op("pe", lambda i=i, sq=sq: nc.tensor.matmul(pss[:], ONES, sq[:], start=(i == 0), stop=(i == 7)),
                      r=[("sq", i % 2), "cs"], w=[("ps", bi)], inc=True)
            rstd = P["rstd"]
            tr.op("act", lambda: nc.scalar.activation(out=rstd[:], in_=pss[:], func=AF.Sqrt, bias=P["eps"][:, 0:1],
                                                      scale=1.0 / D), r=[("ps", bi), "eps"], w=["rstd"])
            tr.op("dve", lambda: nc.vector.reciprocal(out=rstd[:], in_=rstd[:]), r=["rstd"], w=["rstd"])
            for i in range(8):
                tt = P["tt"][i % 2]
                tr.op("dve", lambda i=i, tt=tt: nc.vector.scalar_tensor_tensor(
                    out=tt[:], in0=xt[:, i, :], scalar=der[:, s_idx * 8 + i:s_idx * 8 + i + 1], in1=rstd[:],
                    op0=ALU.mult, op1=ALU.mult), r=[(xkey, i), "rstd", "der"], w=[("tt", i % 2)])
                tr.op("act", lambda i=i, tt=tt: nc.scalar.activation(
                    out=u[:, i, :], in_=tt[:], func=AF.Identity,
                    bias=modT[:, 3 * s_idx * 8 + i:3 * s_idx * 8 + i + 1], scale=1.0),
                    r=[("tt", i % 2), "modT"], w=[("u", i)])

        def ffn_body(P, xt, xkey, s_idx, wdn, final, dst, t0):
            u, hid = P["u"], P["hid"]
            norm_mod(P, xt, xkey, u, s_idx)
            for j in range(NJ):
                s, slot = ring.get()
                sv = slot[:].rearrange("p (k c) -> p k c", k=8)
                ba_, pa = nps()
                bb_, pbb = nps()
                for part, (bix, pt) in enumerate(((ba_, pa), (bb_, pbb))):
                    for k in range(8):
                        tr.op("pe", lambda k=k, part=part, pt=pt: nc.tensor.matmul(
                            pt[:], sv[:, k, part * 128:(part + 1) * 128], u[:, k, :], start=(k == 0), stop=(k == 7)),
                            r=[("ring", s), ("u", k)], w=[("ps", bix)], inc=(k == 7))
                sl = P["s"][j % 2]
                tr.op("act", lambda sl=sl, pa=pa: nc.scalar.activation(out=sl[:], in_=pa[:], func=AF.Silu),
                      r=[("ps", ba_)], w=[("s", j % 2)])
                tr.op("dve", lambda sl=sl, pbb=pbb, j=j: nc.vector.tensor_tensor(
                    out=hid[:, j, :], in0=pbb[:], in1=sl[:], op=ALU.mult),
                    r=[("ps", bb_), ("s", j % 2)], w=[("hid", j)])
            for m in range(8):
                bd, pd = nps()
                for j in range(NJ):
                    tr.op("pe", lambda j=j, m=m, pd=pd: nc.tensor.matmul(
                        pd[:], wdn[:, j, m * 128:(m + 1) * 128], hid[:, j, :], start=(j == 0), stop=(j == NJ - 1)),
                        r=["wdn", ("hid", j)], w=[("ps", bd)], inc=(j == NJ - 1))
                tr.op("dve", lambda m=m, pd=pd: nc.vector.scalar_tensor_tensor(
                    out=xt[:, m, :], in0=pd[:], scalar=der[:, 24 + s_idx * 8 + m:24 + s_idx * 8 + m + 1],
                    in1=xt[:, m, :], op0=ALU.mult, op1=ALU.add), r=[("ps", bd), "der"], w=[(xkey, m)])
            if final:
                bi, pss = nps()
                for i in range(8):
                    sq = P["sq"][i % 2]
                    tr.op("act", lambda i=i, sq=sq: nc.scalar.activation(out=sq[:], in_=xt[:, i, :], func=AF.Square),
                          r=[(xkey, i)], w=[("sq", i % 2)])
                    tr.op("pe", lambda i=i, sq=sq: nc.tensor.matmul(pss[:], ONES, sq[:], start=(i == 0),
                                                                    stop=(i == 7)),
                          r=[("sq", i % 2), "cs"], w=[("ps", bi)], inc=True)
                rstd = P["rstd"]
                tr.op("act", lambda: nc.scalar.activation(out=rstd[:], in_=pss[:], func=AF.Sqrt,
                                                          bias=P["eps"][:, 0:1], scale=1.0 / D),
                      r=[("ps", bi), "eps"], w=["rstd"])
                tr.op("dve", lambda: nc.vector.reciprocal(out=rstd[:], in_=rstd[:]), r=["rstd"], w=["rstd"])
                for i in range(8):
                    tr.op("dve", lambda i=i: nc.vector.scalar_tensor_tensor(
                        out=xt[:, i, :], in0=xt[:, i, :], scalar=NF[:, i:i + 1], in1=rstd[:],
                        op0=ALU.mult, op1=ALU.mult), r=[(xkey, i), "rstd", "vec"], w=[(xkey, i)])
            for i in range(8):
                tr.dma("sp", dst[i * 128:(i + 1) * 128, t0:t0 + NT], xt[:, i, :], r=[(xkey, i)], sem=(xkey, "st"))

        def ffn_pool(stack, wdn_src, nxt=2):
            P = {}
            P["wdn"] = sb("wdn", [128, NJ, D], BF16, stack)
            P["xt"] = [sb("xt%d" % i, [128, 8, NT], F32, stack) for i in range(nxt)]
            P["sq"] = [sb("sq%d" % i, [128, NT], F32, stack) for i in range(2)]
            P["tt"] = [sb("tt%d" % i, [128, NT], F32, stack) for i in range(2)]
            P["s"] = [sb("s%d" % i, [128, NT], F32, stack) for i in range(2)]
            P["rstd"] = sb("rstd", [128, NT], F32, stack)
            P["u"] = sb("u", [128, 8, NT], BF16, stack)
            P["hid"] = sb("hid", [128, NJ, NT], BF16, stack)
            P["eps"] = sb("eps", [128, 1], F32, stack)
            tr.op("dve", lambda: nc.vector.memset(P["eps"][:], EPS), w=["eps"])
            wv = wdn_src.rearrange("(j p) n -> p j n", p=128)
            for jj in range(0, NJ, 2):
                tr.dma("pool", P["wdn"][:, jj:jj + 2, :], wv[:, jj:jj + 2, :], w=["wdn"], sem="wdn")
            return P

        with ExitStack() as st1:
            ring_phase(st1, [b for _ in range(ntile) for b in range(OFF_UP1, OFF_UP1 + NB_UP)])
            P = ffn_pool(st1, wdn1)
            load_tile(P["xt"][0], xT, 0, "xt0")
            for n in range(ntile):
                xt = P["xt"][n % 2]
                xkey = "xt%d" % (n % 2)
                if n + 1 < ntile:
                    load_tile(P["xt"][(n + 1) % 2], xT, (n + 1) * NT, "xt%d" % ((n + 1) % 2))
                ffn_body(P, xt, xkey, 0, P["wdn"], False, H1T, n * NT)
            tr.barrier()

        with ExitStack() as st2:
            ring_phase(st2, [b for _ in range(ntile) for b in range(OFF_IN, OFF_IN + NB_IN + NB_Z)])
            xts = [sb("xt%d" % i, [128, 8, NT], F32, st2) for i in range(2)]
            P = {"sq": [sb("sq%d" % i, [128, NT], F32, st2) for i in range(2)],
                 "tt": [sb("tt%d" % i, [128, NT], F32, st2) for i in range(2)],
                 "rstd": sb("rstd", [128, NT], F32, st2), "eps": sb("eps", [128, 1], F32, st2)}
            tr.op("dve", lambda: nc.vector.memset(P["eps"][:], EPS), w=["eps"])
            u = sb("u", [128, 8, NT], BF16, st2)
            wbas = sb("wbas", [128, 8, 32], BF16, st2)
            stg = [sb("stg%d" % i, [128, NT], F32, st2) for i in range(4)]
            zero = sb("zero", [128, NCONV, 2], F32, st2)
            tr.dma("pool", wbas[:], wba.rearrange("(k p) n -> p k n", p=128), w=["wbas"], sem="wbas")
            tr.op("dve", lambda: nc.vector.memset(zero[:], 0.0), w=["zero"])
            tr.dma("sp", PT[0:NCONV * 128, 0:2].rearrange("(c p) n -> p c n", p=128), zero[:], r=["zero"], sem="zero")
            sti = 0
            load_tile(xts[0], H1T, 0, "xt0")
            for n in range(ntile):
                xt = xts[n % 2]
                xkey = "xt%d" % (n % 2)
                if n + 1 < ntile:
                    load_tile(xts[(n + 1) % 2], H1T, (n + 1) * NT, "xt%d" % ((n + 1) % 2))
                norm_mod(P, xt, xkey, u, 1)
                for blk in range(NB_IN):
                    s, slot = ring.get()
                    sv = slot[:].rearrange("p (k c) -> p k c", k=8)
                    for cc in range(2):
                        ch = blk * 2 + cc
                        bi, pt = nps()
                        for k in range(8):
                            tr.op("pe", lambda k=k, cc=cc, pt=pt: nc.tensor.matmul(
                                pt[:], sv[:, k, cc * 128:(cc + 1) * 128], u[:, k, :], start=(k == 0), stop=(k == 7)),
                                r=[("ring", s), ("u", k)], w=[("ps", bi)], inc=(k == 7))
                        sg = stg[sti % 4]
                        sk = ("stg", sti % 4)
                        sti += 1
                        tr.op("act", lambda sg=sg, pt=pt: nc.scalar.copy(out=sg[:], in_=pt[:]), r=[("ps", bi)], w=[sk])
                        tr.dma("sp", PT[ch * 128:(ch + 1) * 128, 2 + n * NT:2 + (n + 1) * NT], sg[:], r=[sk], sem=sk)
                for zb in range(NB_Z):
                    s, slot = ring.get()
                    sv = slot[:].rearrange("p (k c) -> p k c", k=8)
                    for tb in range(0, 4, 2):
                        bi, pt = nps()
                        for t2 in range(2):
                            for k in range(8):
                                tr.op("pe", lambda k=k, t2=t2, tb=tb, pt=pt: nc.tensor.matmul(
                                    pt[:, t2 * 256:(t2 + 1) * 256], u[:, k, (tb + t2) * 128:(tb + t2 + 1) * 128],
                                    sv[:, k, :], start=(k == 0), stop=(k == 7)),
                                    r=[("ring", s), ("u", k)], w=[("ps", bi)], inc=(k == 7))
                        sg = stg[sti % 4]
                        sk = ("stg", sti % 4)
                        sti += 1
                        tr.op("act", lambda sg=sg, pt=pt: nc.scalar.activation(out=sg[:], in_=pt[:], func=AF.Silu),
                              r=[("ps", bi)], w=[sk])
                        for t2 in range(2):
                            r0 = n * NT + (tb + t2) * 128
                            tr.dma("sp", ZS[r0:r0 + 128, zb * 256:(zb + 1) * 256], sg[:, t2 * 256:(t2 + 1) * 256],
                                   r=[sk], sem=sk)
                bi, pt = nps()
                for tb in range(4):
                    for k in range(8):
                        tr.op("pe", lambda k=k, tb=tb, pt=pt: nc.tensor.matmul(
                            pt[:, tb * 32:(tb + 1) * 32], u[:, k, tb * 128:(tb + 1) * 128], wbas[:, k, :],
                            start=(k == 0), stop=(k == 7)), r=["wbas", ("u", k)], w=[("ps", bi)], inc=(k == 7))
                sg = stg[sti % 4]
                sk = ("stg", sti % 4)
                sti += 1
                tr.op("act", lambda sg=sg, pt=pt: nc.scalar.copy(out=sg[:, 0:128], in_=pt[:, 0:128]),
                      r=[("ps", bi)], w=[sk])
                tr.dma("sp", BAt[n * NT:(n + 1) * NT, :].rearrange("(b p) n -> p b n", p=128),
                       sg[:, 0:128].rearrange("p (b n) -> p b n", b=4), r=[sk], sem=sk)
            tr.barrier()
            hx = sb("hx", [128, 2, NCONV, 2], F32, st2)
            hy = sb("hy", [128, NCONV, 2], F32, st2)
            tr.dma("pool", hx_in.ap(), PT[0:NCONV * 128, T:T + 2], w=["hx_in"], sem="hx")
            tr.cc(lambda: nc.gpsimd.collective_compute("AllGather", ALU.bypass, replica_groups=PAIRS,
                                                        ins=[hx_in.ap()], outs=[hx_out.ap()]),
                  r=["hx_in"], w=["hx_out"], sem="hxcc")
            tr.dma("pool", hx[:], hx_out.ap().rearrange("(r c p) n -> p r c n", r=2, p=128), r=["hx_out"], w=["hx"],
                   sem="hx")
            tr.op("dve", lambda: nc.vector.tensor_scalar(out=hy[:], in0=hx[:, 0], scalar1=selt[:, 0:1], scalar2=None,
                                                         op0=ALU.mult), r=["hx", "selt"], w=["hy"])
            tr.op("dve", lambda: nc.vector.scalar_tensor_tensor(out=hy[:], in0=hx[:, 1], scalar=selt[:, 1:2],
                                                                in1=hy[:], op0=ALU.mult, op1=ALU.add),
                  r=["hx", "selt", "hy"], w=["hy"])
            pv = PT[0:NCONV * 128, :].rearrange("(c p) n -> p c n", p=128)
            hys = sb("hys", [128, NCONV, 2], F32, st2)
            tr.op("dve", lambda: nc.vector.tensor_copy(out=hys[:, :, 0:1], in_=hy[:, :, 1:2]), r=["hy"], w=["hys"])
            tr.op("dve", lambda: nc.vector.tensor_copy(out=hys[:, :, 1:2], in_=hy[:, :, 0:1]), r=["hy"], w=["hys"])
            tr.dma("sp", pv[:, :, T + 2:T + 4], hys[:], r=["hys"], sem="hy")
            tr.barrier()

        def run_streams(factories, width, stagger=0):
            active = []
            free = list(range(width))
            it = iter(factories)
            done = False
            if stagger:
                f = next(it, None)
                if f is not None:
                    s_ = free.pop(0)
                    g0 = f(s_)
                    active.append((g0, s_))
                    for _ in range(stagger):
                        try:
                            next(g0)
                        except StopIteration:
                            active.remove((g0, s_))
                            free.append(s_)
                            break
            while True:
                while free and not done:
                    f = next(it, None)
                    if f is None:
                        done = True
                        break
                    s_ = free.pop(0)
                    active.append((f(s_), s_))
                if not active:
                    break
                for g in list(active):
                    try:
                        next(g[0])
                    except StopIteration:
                        active.remove(g)
                        free.append(g[1])

        with ExitStack() as st3:
            W2B = 4
            B2 = []
            for i in range(W2B):
                B2.append({"pre": sb("pre", [128, NT + 4], F32, st3), "acc": sb("acc", [128, NT], F32, st3),
                           "sq": sb("sqq", [128, NT], F32, st3), "rn": sb("rn", [128, NT], F32, st3),
                           "ob": sb("ob", [128, NT], BF16, st3)})
            A2 = []
            for i in range(2):
                A2.append({"p1": sb("p1", [128, NT + 4], F32, st3), "p2": sb("p2", [128, NT + 4], F32, st3),
                           "p3": sb("p3", [128, NT], F32, st3), "pcv": sb("pcv", [128, NT + 4], F32, st3),
                           "cacc": sb("cacc", [128, NT], F32, st3)})
            G2 = []
            for i in range(2):
                G2.append({"g": sb("gsb", [128, NT], F32, st3), "pa": sb("pab", [128, NT], F32, st3)})
            yaT = [sb("yaT", [128, 4, NT], BF16, st3) for i in range(2)]
            wa = sb("wa", [128, 4, D], BF16, st3)
            epsq = sb("epsq", [128, 2], F32, st3)
            tr.op("dve", lambda: nc.vector.memset(epsq[:, 0:1], EPS), w=["epsq"])
            tr.op("dve", lambda: nc.vector.memset(epsq[:, 1:2], 128.0 * EPS), w=["epsq"])
            tr.dma("pool", wa[:], waout.rearrange("(k p) n -> p k n", p=128), w=["wa"], sem="wa")

            def qkv_gen(n, ch, sl):
                c0 = n * NT
                Bf = B2[sl]
                p_, a_, s_, r_, o_ = Bf["pre"], Bf["acc"], Bf["sq"], Bf["rn"], Bf["ob"]
                pk, ak, sk, rk, ok = ("pre", sl), ("acc", sl), ("sqq", sl), ("rn", sl), ("ob", sl)
                tr.dma("sp", p_[:], PT[(8 + ch) * 128:(9 + ch) * 128, c0:c0 + NT + 4], w=[pk], sem=pk)
                yield
                for tap in range(5):
                    wcol = cw[:, ch * 5 + tap:ch * 5 + tap + 1]
                    if tap == 0:
                        tr.op("act", lambda: nc.scalar.activation(out=a_[:], in_=p_[:, 0:NT], func=AF.Identity,
                                                                  scale=wcol), r=[pk, "cw"], w=[ak])
                        yield
                    else:
                        tr.op("dve", lambda: nc.vector.scalar_tensor_tensor(
                            out=a_[:], in0=p_[:, tap:tap + NT], scalar=wcol, in1=a_[:], op0=ALU.mult, op1=ALU.add),
                            r=[pk, "cw", ak], w=[ak])
                yield
                tr.op("act", lambda: nc.scalar.activation(out=a_[:], in_=a_[:], func=AF.Silu), r=[ak], w=[ak])
                yield
                if ch < 16:
                    tr.op("pool", lambda: nc.gpsimd.tensor_tensor(out=s_[:], in0=a_[:], in1=a_[:], op=ALU.mult),
                          r=[ak], w=[sk])
                    yield
                    bi, pt = nps()
                    tr.op("pe", lambda: nc.tensor.matmul(pt[:], ONES, s_[:], start=True, stop=True),
                          r=[sk, "cs"], w=[("ps", bi)])
                    yield
                    qsc = 128.0 if ch < 8 else 1.0
                    ebias = epsq[:, 1:2] if ch < 8 else epsq[:, 0:1]
                    tr.op("act", lambda: nc.scalar.activation(out=r_[:], in_=pt[:], func=AF.Sqrt, bias=ebias,
                                                              scale=qsc), r=[("ps", bi), "epsq"], w=[rk])
                    yield
                    tr.op("dve", lambda: nc.vector.reciprocal(out=r_[:], in_=r_[:]), r=[rk], w=[rk])
                    yield
                    tr.op("pool", lambda: nc.gpsimd.tensor_tensor(out=o_[:], in0=a_[:], in1=r_[:], op=ALU.mult),
                          r=[ak, rk], w=[ok])
                else:
                    tr.op("pool", lambda: nc.gpsimd.tensor_copy(out=o_[:], in_=a_[:]), r=[ak], w=[ok])
                yield
                tr.dma("sp", QKVT[ch * 128:(ch + 1) * 128, c0:c0 + NT], o_[:], r=[ok], sem=ok)

            def abr_gen(n, ch, sl):
                c0 = n * NT
                Af = A2[sl]
                p1, p2, p3, pcv, cacc = Af["p1"], Af["p2"], Af["p3"], Af["pcv"], Af["cacc"]
                k1, k2, k3, kp, kc = ("p1", sl), ("p2", sl), ("p3", sl), ("pcv", sl), ("cacc", sl)
                ya = yaT[n % 2]
                tr.dma("sp", p1[:], PT[ch * 128:(ch + 1) * 128, c0:c0 + NT + 4], w=[k1], sem=k1)
                tr.dma("sp", p2[:], PT[(4 + ch) * 128:(5 + ch) * 128, c0:c0 + NT + 4], w=[k2], sem=k2)
                tr.dma("sp", p3[:], PT[(32 + ch) * 128:(33 + ch) * 128, c0 + 2:c0 + 2 + NT], w=[k3], sem=k3)
                yield
                tr.op("dve", lambda: nc.vector.tensor_tensor(out=pcv[:], in0=p1[:], in1=p2[:], op=ALU.mult),
                      r=[k1, k2], w=[kp])
                yield
                for tap in range(3):
                    wcol = cw[:, 120 + ch * 3 + tap:120 + ch * 3 + tap + 1]
                    if tap == 0:
                        tr.op("dve", lambda: nc.vector.tensor_scalar(out=cacc[:], in0=pcv[:, 1:1 + NT], scalar1=wcol,
                                                                     scalar2=None, op0=ALU.mult),
                              r=[kp, "cw"], w=[kc])
                    else:
                        tr.op("dve", lambda: nc.vector.scalar_tensor_tensor(
                            out=cacc[:], in0=pcv[:, 1 + tap:1 + tap + NT], scalar=wcol, in1=cacc[:], op0=ALU.mult,
                            op1=ALU.add), r=[kp, "cw", kc], w=[kc])
                yield
                tr.op("dve", lambda: nc.vector.tensor_tensor(out=ya[:, ch, :], in0=cacc[:], in1=p3[:], op=ALU.mult),
                      r=[kc, k3], w=[("yaT", n % 2, ch)])

            def gate_gen(n, m, sl):
                c0 = n * NT
                g_, pa_ = G2[sl]["g"], G2[sl]["pa"]
                gk, pk2 = ("gsb", sl), ("pab", sl)
                ya = yaT[n % 2]
                tr.dma("sp", g_[:], PT[(36 + m) * 128:(37 + m) * 128, c0 + 2:c0 + 2 + NT], w=[gk], sem=gk)
                yield
                tr.op("act", lambda: nc.scalar.activation(out=g_[:], in_=g_[:], func=AF.Sigmoid), r=[gk], w=[gk])
                bi, pt = nps()
                for kk in range(4):
                    tr.op("pe", lambda kk=kk: nc.tensor.matmul(pt[:], wa[:, kk, m * 128:(m + 1) * 128], ya[:, kk, :],
                                                               start=(kk == 0), stop=(kk == 3)),
                          r=["wa", ("yaT", n % 2, kk)], w=[("ps", bi)], inc=(kk == 3))
                yield
                tr.op("dve", lambda: nc.vector.tensor_tensor(out=pa_[:], in0=pt[:], in1=g_[:], op=ALU.mult),
                      r=[("ps", bi), gk], w=[pk2])
                yield
                tr.dma("sp", PAT[m * 128:(m + 1) * 128, c0:c0 + NT], pa_[:], r=[pk2], sem=pk2)

            for n in range(ntile):
                run_streams([(lambda sl, n=n, ch=ch: qkv_gen(n, ch, sl)) for ch in range(24)], W2B)
                run_streams([(lambda sl, n=n, ch=ch: abr_gen(n, ch, sl)) for ch in range(4)], 2)
                run_streams([(lambda sl, n=n, m=m: gate_gen(n, m, sl)) for m in range(8)], 2)
            tr.barrier()

        with ExitStack() as st4:
            qkv = [sb("qkv%d" % i, [128, 24, NT], BF16, st4) for i in range(2)]
            S = sb("S", [128, H, 128], F32, st4)
            Sb = sb("Sb", [128, H, 128], BF16, st4)
            sm = sb("sm", [128, 8, 8], F32, st4)
            gT = sb("gT", [8, 3, 64], F32, st4)
            Rg = sb("Rg", [8, 2, 512], F32, st4)
            dec = [sb("dec%d" % i, [64, 512], F32, st4) for i in range(2)]
            egr = sb("egr", [128, 512], F32, st4)
            qg = sb("qg", [128, H, C], BF16, st4)
            XY = [sb("XY%d" % i, [64, 2, 512], BF16, st4) for i in range(2)]
            PQ = [sb("PQ%d" % i, [64, 2, 512], BF16, st4) for i in range(2)]
            attn = sb("attn", [64, 512], BF16, st4)
            XY0 = sb("XY0", [64, 2, 512], BF16, st4)
            MK = sb("MK", [64, 4, 512], BF16, st4)
            tr.dma("pool", MK[:], cst2.rearrange("p (a n) -> p a n", a=4), w=["MK"], sem="MK")
            kbg = sb("kbg", [64, H, 128], BF16, st4)
            kdec = sb("kdec", [64, H, 128], BF16, st4)
            vb = sb("vb", [64, H, 128], BF16, st4)
            usb = sb("usb", [64, H, 128], F32, st4)
            wT = sb("wT", [128, H, C], BF16, st4)
            vnew = sb("vnew", [64, H, 128], BF16, st4)
            osb = sb("osb", [64, H, 128], F32, st4)
            ofw = sb("ofw", [64, H, 128], F32, st4)
            zsb = sb("zsb", [64, H, 128], F32, st4)
            ysq = sb("ysq", [64, H, 128], F32, st4)
            ss = sb("ss", [64, 16], F32, st4)
            ss2 = sb("ss2", [64, 16], F32, st4)
            ss3 = sb("ss3", [64, 16], F32, st4)
            yb = sb("yb", [64, H, 128], BF16, st4)
            ybT = sb("ybT", [128, H, NT], BF16, st4)
            SX = ybT[:].bitcast(F32).rearrange("p h n -> p (h n)").rearrange("p (r m) -> p r m", r=2)
            epsd = sb("epsd", [128, 1], F32, st4)
            tr.op("dve", lambda: nc.vector.memset(epsd[:], EPS), w=["epsd"])
            tr.op("dve", lambda: nc.vector.memset(S[:], 0.0), w=["S"])
            tr.op("dve", lambda: nc.vector.memset(Sb[:], 0.0), w=["Sb"])
            nck = NT // C

            def bc(ap, shape):
                return ap.to_broadcast(list(shape))

            WG = 2
            TB = [{"bat": sb("bat", [64, NT // C, 32], F32, st4), "beta": sb("beta", [64, NT // C, 8], F32, st4),
                   "lnb": sb("lnb", [64, NT // C, 8], F32, st4), "gg": sb("gg", [64, NT // C, 8], F32, st4)}
                  for _i in range(2)]
            CB = [{"sm": sm, "gT": gT, "Rg": Rg, "dec": dec, "egr": egr, "qg": qg, "XY": XY, "PQ": PQ, "attn": attn,
                   "XY0": XY0, "kbg": kbg, "kdec": kdec, "vb": vb, "usb": usb, "wT": wT, "vnew": vnew, "osb": osb,
                   "ofw": ofw, "zsb": zsb, "ysq": ysq, "ss": ss, "ss2": ss2, "ss3": ss3, "yb": yb}]
            for _i in range(1, WG):
                CB.append({
                    "sm": sb("sm", [128, 8, 8], F32, st4), "gT": sb("gT", [8, 3, 64], F32, st4),
                    "Rg": sb("Rg", [8, 2, 512], F32, st4),
                    "dec": [sb("dec", [64, 512], F32, st4) for _j in range(2)],
                    "egr": sb("egr", [128, 512], F32, st4), "qg": sb("qg", [128, H, C], BF16, st4),
                    "XY": [sb("XY", [64, 2, 512], BF16, st4) for _j in range(2)],
                    "PQ": [sb("PQ", [64, 2, 512], BF16, st4) for _j in range(2)],
                    "attn": sb("attn", [64, 512], BF16, st4), "XY0": sb("XY0", [64, 2, 512], BF16, st4),
                    "kbg": sb("kbg", [64, H, 128], BF16, st4), "kdec": sb("kdec", [64, H, 128], BF16, st4),
                    "vb": sb("vb", [64, H, 128], BF16, st4), "usb": sb("usb", [64, H, 128], F32, st4),
                    "wT": sb("wT", [128, H, C], BF16, st4), "vnew": sb("vnew", [64, H, 128], BF16, st4),
                    "osb": sb("osb", [64, H, 128], F32, st4), "ofw": sb("ofw", [64, H, 128], F32, st4),
                    "zsb": sb("zsb", [64, H, 128], F32, st4), "ysq": sb("ysq", [64, H, 128], F32, st4),
                    "ss": sb("ss", [64, 16], F32, st4), "ss2": sb("ss2", [64, 16], F32, st4),
                    "ss3": sb("ss3", [64, 16], F32, st4), "yb": sb("yb", [64, H, 128], BF16, st4)})

            def chunk_gen(dr, n, c, first, last, sl):
                Bc = CB[sl]
                Tt = TB[n % 2]
                bat, beta, lnb, gg = Tt["bat"], Tt["beta"], Tt["lnb"], Tt["gg"]
                sm, gT, Rg, dec, egr, qg, XY, PQ, attn, XY0 = (Bc["sm"], Bc["gT"], Bc["Rg"], Bc["dec"], Bc["egr"],
                                                               Bc["qg"], Bc["XY"], Bc["PQ"], Bc["attn"], Bc["XY0"])
                kbg, kdec, vb, usb, wT, vnew, osb, ofw, zsb = (Bc["kbg"], Bc["kdec"], Bc["vb"], Bc["usb"], Bc["wT"],
                                                               Bc["vnew"], Bc["osb"], Bc["ofw"], Bc["zsb"])
                ysq, ss, ss2, ss3, yb = Bc["ysq"], Bc["ss"], Bc["ss2"], Bc["ss3"], Bc["yb"]
                qt = qkv[n % 2]
                qk_ = ("qkv", n % 2)
                if first:
                    tr.dma("sp", qt[:], QKVT[:, n * NT:(n + 1) * NT].rearrange("(c p) n -> p c n", p=128),
                           w=[qk_], sem=qk_)
                    tr.dma("sp", bat[:], BAt[n * NT:(n + 1) * NT, :].rearrange("(c p) n -> p c n", p=64),
                           w=[("bat", "t", n % 2)], sem=("bat", "t", n % 2))
                    bsl = bat[:, :, dr * 8:dr * 8 + 8]
                    asl = bat[:, :, 16 + dr * 8:24 + dr * 8]
                    tr.op("act", lambda: nc.scalar.activation(out=beta[:], in_=bsl, func=AF.Sigmoid),
                          r=[("bat", "t", n % 2)], w=[("beta", "t", n % 2)])
                    tr.op("act", lambda: nc.scalar.activation(out=lnb[:], in_=bsl, func=AF.Exp, scale=-1.0),
                          r=[("bat", "t", n % 2)], w=[("lnb", "t", n % 2)])
                    tr.op("act", lambda: nc.scalar.activation(out=lnb[:], in_=lnb[:], func=AF.Ln, bias=1.0, scale=1.0),
                          r=[("lnb", "t", n % 2)], w=[("lnb", "t", n % 2)])
                    for c2 in range(nck):
                        tr.op("dve", lambda c2=c2: nc.vector.tensor_tensor(
                            out=gg[:, c2, :], in0=bat[:, c2, 16 + dr * 8:24 + dr * 8],
                            in1=tp[0:64, 16 + dr * 8:24 + dr * 8], op=ALU.add), r=[("bat", "t", n % 2), "tp"], w=[("gg", "t", n % 2)])
                    tr.op("act", lambda: nc.scalar.activation(out=gg[:], in_=gg[:], func=AF.Exp), r=[("gg", "t", n % 2)], w=[("gg", "t", n % 2)])
                    tr.op("act", lambda: nc.scalar.activation(out=gg[:], in_=gg[:], func=AF.Ln, bias=1.0, scale=1.0),
                          r=[("gg", "t", n % 2)], w=[("gg", "t", n % 2)])
                    for c2 in range(nck):
                        tr.op("dve", lambda c2=c2: nc.vector.tensor_tensor(
                            out=gg[:, c2, :], in0=gg[:, c2, :], in1=negA[0:64, dr * 8:dr * 8 + 8], op=ALU.mult),
                            r=[("gg", "t", n % 2), "negA"], w=[("gg", "t", n % 2)])
                tok = slice(c * C, (c + 1) * C)
                gtok = n * NT + c * C
                yield
                bi, pg = nps()
                tr.op("pe", lambda: nc.tensor.matmul(pg[0:64, 0:8], LTRI[dr], gg[:, c, :], start=True,
                                                     stop=True), r=[("gg", "t", n % 2), "cs"], w=[("ps", bi)], inc=False)
                tr.op("pe", lambda: nc.tensor.matmul(pg[:, 8:16], cs[0:64, 0:128], gg[:, c, :], start=True,
                                                     stop=True), r=[("gg", "t", n % 2), "cs"], w=[("ps", bi)])
                tr.op("dve", lambda: nc.vector.tensor_copy(out=sm[0:64, 0, :], in_=pg[0:64, 0:8]),
                      r=[("ps", bi)], w=[("sm", sl)])
                tr.op("dve", lambda: nc.vector.tensor_tensor(out=sm[0:64, 1, :], in0=pg[0:64, 0:8],
                                                             in1=lnb[:, c, :], op=ALU.subtract),
                      r=[("ps", bi), ("lnb", "t", n % 2)], w=[("sm", sl)])
                tr.op("act", lambda: nc.scalar.activation(out=sm[0:64, 2, :], in_=pg[0:64, 0:8], func=AF.Exp),
                      r=[("ps", bi)], w=[("sm", sl)])
                tr.op("dve", lambda: nc.vector.tensor_tensor(out=sm[0:64, 3, :], in0=pg[0:64, 8:16],
                                                             in1=sm[0:64, 0, :], op=ALU.subtract),
                      r=[("ps", bi), ("sm", sl)], w=[("sm", sl)])
                tr.op("act", lambda: nc.scalar.activation(out=sm[0:64, 3, :], in_=sm[0:64, 3, :], func=AF.Exp),
                      r=[("sm", sl)], w=[("sm", sl)])
                tr.op("dve", lambda: nc.vector.tensor_tensor(out=sm[0:64, 4, :], in0=sm[0:64, 2, :],
                                                             in1=beta[:, c, :], op=ALU.mult),
                      r=[("sm", sl), ("beta", "t", n % 2)], w=[("sm", sl)])
                tr.op("act", lambda: nc.scalar.activation(out=sm[:, 5, :], in_=pg[:, 8:16], func=AF.Exp),
                      r=[("ps", bi)], w=[("sm", sl)])
                yield
                bi2, pt2 = nps()
                tr.op("pe", lambda: nc.tensor.matmul(pt2[0:8, 0:64], sm[0:64, 0, :], ID64, start=True,
                                                     stop=True), r=[("sm", sl), "cs"], w=[("ps", bi2)], inc=False)
                tr.op("pe", lambda: nc.tensor.matmul(pt2[0:8, 64:128], sm[0:64, 1, :], ID64, start=True,
                                                     stop=True), r=[("sm", sl), "cs"], w=[("ps", bi2)])
                tr.op("dve", lambda: nc.vector.tensor_scalar(out=gT[:, 0, :], in0=pt2[0:8, 0:64], scalar1=-1.0,
                                                             scalar2=None, op0=ALU.mult),
                      r=[("ps", bi2)], w=[("gT", sl)])
                tr.op("dve", lambda: nc.vector.tensor_copy(out=gT[:, 1:3, :].rearrange("p a b -> p (a b)"),
                                                           in_=pt2[0:8, 0:128]),
                      r=[("ps", bi2)], w=[("gT", sl)])
                E8v = E8.rearrange("p (h i) -> p h i", h=8)
                tr.op("dve", lambda: nc.vector.tensor_tensor(
                    out=Rg[:, 0, :].rearrange("p (h i) -> p h i", h=8), in0=E8v,
                    in1=gT[:, 2:3, :].to_broadcast([8, 8, 64]), op=ALU.mult), r=[("gT", sl), "cs"], w=[("Rg", sl)])
                tr.op("dve", lambda: nc.vector.tensor_tensor(
                    out=Rg[:, 1, :].rearrange("p (h i) -> p h i", h=8), in0=E8v,
                    in1=gT[:, 1:2, :].to_broadcast([8, 8, 64]), op=ALU.mult), r=[("gT", sl), "cs"], w=[("Rg", sl)])
                yield
                bA, pA = nps()
                bQ, pQ = nps()
                bR, pR = nps()
                tr.op("pe", lambda: nc.tensor.matmul(pA[0:64, :], cs[0:8, 0:64], Rg[:, 0, :], start=True,
                                                     stop=False), r=[("Rg", sl), "cs"], w=[("ps", bA)], inc=False)
                tr.op("pe", lambda: nc.tensor.matmul(pA[0:64, :], gT[:, 0, :], E8, start=False, stop=True),
                      r=[("gT", sl), "cs"], w=[("ps", bA)])
                tr.op("pe", lambda: nc.tensor.matmul(pQ[0:64, :], cs[0:8, 0:64], Rg[:, 1, :], start=True,
                                                     stop=False), r=[("Rg", sl), "cs"], w=[("ps", bQ)], inc=False)
                tr.op("pe", lambda: nc.tensor.matmul(pQ[0:64, :], gT[:, 0, :], E8, start=False, stop=True),
                      r=[("gT", sl), "cs"], w=[("ps", bQ)])
                tr.op("pe", lambda: nc.tensor.matmul(pR[:, :], cs[0:8, 0:128], Rg[:, 1, :], start=True,
                                                     stop=True), r=[("Rg", sl), "cs"], w=[("ps", bR)])
                tr.op("dve", lambda: nc.vector.tensor_tensor(out=dec[0][:], in0=pA[0:64, :], in1=CAPS[dr],
                                                             op=ALU.min), r=[("ps", bA), "cs"], w=[("dec", 0, sl)])
                tr.op("act", lambda: nc.scalar.activation(out=dec[0][:], in_=dec[0][:], func=AF.Exp),
                      r=[("dec", 0, sl)], w=[("dec", 0, sl)])
                tr.op("dve", lambda: nc.vector.tensor_tensor(out=dec[1][:], in0=pQ[0:64, :], in1=CAPI[dr],
                                                             op=ALU.min), r=[("ps", bQ), "cs"], w=[("dec", 1, sl)])
                tr.op("act", lambda: nc.scalar.activation(out=dec[1][:], in_=dec[1][:], func=AF.Exp),
                      r=[("dec", 1, sl)], w=[("dec", 1, sl)])
                tr.op("act", lambda: nc.scalar.activation(out=egr[:], in_=pR[:], func=AF.Exp),
                      r=[("ps", bR)], w=[("egr", sl)])
                tr.op("dve", lambda: nc.vector.tensor_tensor(
                    out=qg[:], in0=qt[:, 0:8, tok], in1=egr[:].rearrange("p (h i) -> p h i", h=8),
                    op=ALU.mult), r=[qk_, ("egr", sl)], w=[("qg", sl)])
                yield
                bK, pK = nps()
                bQK, pQK = nps()
                for h in range(H):
                    tr.op("pe", lambda h=h: nc.tensor.matmul(pK[0:64, h * 64:(h + 1) * 64], qt[:, 8 + h, tok],
                                                             qt[:, 8 + h, tok], start=True, stop=True),
                          r=[qk_], w=[("ps", bK)], inc=(h == H - 1))
                for h in range(H):
                    tr.op("pe", lambda h=h: nc.tensor.matmul(pQK[0:64, h * 64:(h + 1) * 64], qt[:, 8 + h, tok],
                                                             qt[:, h, tok], start=True, stop=True),
                          r=[qk_], w=[("ps", bQK)], inc=(h == H - 1))
                X0, Y0 = XY0[:, 0, :], XY0[:, 1, :]
                tr.op("dve", lambda: nc.vector.tensor_tensor(out=X0, in0=pK[0:64, :], in1=dec[0][:],
                                                             op=ALU.mult),
                      r=[("ps", bK), ("dec", 0, sl)], w=[("X0", sl)])
                tr.op("dve", lambda: nc.vector.tensor_tensor(out=attn[:], in0=pQK[0:64, :], in1=dec[1][:],
                                                             op=ALU.mult),
                      r=[("ps", bQK), ("dec", 1, sl)], w=[("attn", sl)])
                yield
                bY, pY = nps()
                for h in range(H):
                    tr.op("pe", lambda h=h: nc.tensor.matmul(pY[0:64, h * 64:(h + 1) * 64],
                                                             X0[:, h * 64:(h + 1) * 64], ID64B, start=True,
                                                             stop=True),
                          r=[("X0", sl), "csb"], w=[("ps", bY)], inc=(h == H - 1))
                tr.op("act", lambda: nc.scalar.copy(out=Y0, in_=pY[0:64, :]), r=[("ps", bY)], w=[("Y0", sl)])
                Xa0, Ya0 = XY[0][:, 0, :], XY[0][:, 1, :]
                Pm, Qm = PQ[0][:, 0, :], PQ[0][:, 1, :]
                tr.op("pool", lambda: nc.gpsimd.tensor_tensor(out=Xa0, in0=X0, in1=MK[:, 0, :], op=ALU.mult),
                      r=[("X0", sl), "MK"], w=[("X", 0, sl)])
                tr.op("pool", lambda: nc.gpsimd.tensor_tensor(out=Ya0, in0=Y0, in1=MK[:, 0, :], op=ALU.mult),
                      r=[("Y0", sl), "MK"], w=[("Y", 0, sl)])
                tr.op("pool", lambda: nc.gpsimd.tensor_tensor(out=Pm, in0=I8, in1=Xa0, op=ALU.subtract),
                      r=[("X", 0, sl), "cs"], w=[("P", 0, sl)])
                tr.op("pool", lambda: nc.gpsimd.tensor_tensor(out=Qm, in0=I8, in1=Ya0, op=ALU.subtract),
                      r=[("Y", 0, sl), "cs"], w=[("Q", 0, sl)])

                def mm8(pt_, lhs, rhs, rk, bix):
                    for h in range(H):
                        hs = slice(h * 64, (h + 1) * 64)
                        tr.op("pe", lambda hs=hs: nc.tensor.matmul(pt_[0:64, hs], lhs[:, hs], rhs[:, hs],
                                                                   start=True, stop=True),
                              r=rk, w=[("ps", bix)], inc=(h == H - 1))

                for lv in range(2):
                    a, b = lv % 2, (lv + 1) % 2
                    Xa, Ya = XY[a][:, 0, :], XY[a][:, 1, :]
                    Xb, Yb = XY[b][:, 0, :], XY[b][:, 1, :]
                    Pa, Qa = PQ[a][:, 0, :], PQ[a][:, 1, :]
                    Pb, Qb = PQ[b][:, 0, :], PQ[b][:, 1, :]
                    yield
                    b1, p1 = nps()
                    mm8(p1, Ya, Xa, [("X", a, sl), ("Y", a, sl)], b1)
                    tr.op("act", lambda: nc.scalar.copy(out=Xb, in_=p1[0:64, :]), r=[("ps", b1)], w=[("X", b, sl)])
                    yield
                    b2, p2 = nps()
                    mm8(p2, Xa, Ya, [("X", a, sl), ("Y", a, sl)], b2)
                    tr.op("act", lambda: nc.scalar.copy(out=Yb, in_=p2[0:64, :]), r=[("ps", b2)], w=[("Y", b, sl)])
                    yield
                    b3, p3 = nps()
                    mm8(p3, Qa, Xb, [("Q", a, sl), ("X", b, sl)], b3)
                    tr.op("dve", lambda: nc.vector.tensor_tensor(out=Pb, in0=p3[0:64, :], in1=Pa, op=ALU.add),
                          r=[("ps", b3), ("P", a, sl)], w=[("P", b, sl)])
                    yield
                    b4, p4 = nps()
                    mm8(p4, Pa, Yb, [("P", a, sl), ("Y", b, sl)], b4)
                    tr.op("dve", lambda: nc.vector.tensor_tensor(out=Qb, in0=p4[0:64, :], in1=Qa, op=ALU.add),
                          r=[("ps", b4), ("Q", a, sl)], w=[("Q", b, sl)])
                cur = 0
                for li in range(3):
                    nxt = 1 - cur
                    Pc, Qc = PQ[cur][:, 0, :], PQ[cur][:, 1, :]
                    Pn, Qn = PQ[nxt][:, 0, :], PQ[nxt][:, 1, :]
                    Xm, Ym = XY[0][:, 0, :], XY[0][:, 1, :]
                    W1, W2 = XY[1][:, 0, :], XY[1][:, 1, :]
                    tr.op("pool", lambda: nc.gpsimd.tensor_tensor(out=Ym, in0=Y0, in1=MK[:, 1 + li, :],
                                                                  op=ALU.mult),
                          r=[("Y0", sl), "MK"], w=[("Y", 0, sl)])
                    yield
                    b1, p1 = nps()
                    mm8(p1, Ym, Pc, [("Y", 0, sl), ("P", cur, sl)], b1)
                    tr.op("act", lambda: nc.scalar.copy(out=W1, in_=p1[0:64, :]), r=[("ps", b1)], w=[("X", 1, sl)])
                    yield
                    b2, p2 = nps()
                    mm8(p2, Qc, W1, [("Q", cur, sl), ("X", 1, sl)], b2)
                    tr.op("dve", lambda: nc.vector.tensor_tensor(out=Pn, in0=Pc, in1=p2[0:64, :],
                                                                 op=ALU.subtract),
                          r=[("ps", b2), ("P", cur, sl)], w=[("P", nxt, sl)])
                    if li < 2:
                        tr.op("pool", lambda: nc.gpsimd.tensor_tensor(out=Xm, in0=X0, in1=MK[:, 1 + li, :],
                                                                      op=ALU.mult),
                              r=[("X0", sl), "MK"], w=[("X", 0, sl)])
                        yield
                        b3, p3 = nps()
                        mm8(p3, Xm, Qc, [("X", 0, sl), ("Q", cur, sl)], b3)
                        tr.op("act", lambda: nc.scalar.copy(out=W2, in_=p3[0:64, :]), r=[("ps", b3)],
                              w=[("Y", 1, sl)])
                        yield
                        b4, p4 = nps()
                        mm8(p4, Pc, W2, [("P", cur, sl), ("Y", 1, sl)], b4)
                        tr.op("dve", lambda: nc.vector.tensor_tensor(out=Qn, in0=Qc, in1=p4[0:64, :],
                                                                     op=ALU.subtract),
                              r=[("ps", b4), ("Q", cur, sl)], w=[("Q", nxt, sl)])
                    cur = nxt
                TT = PQ[cur][:, 0, :]
                TK = ("P", cur, sl)
                for half in range(2):
                    yield
                    bk, pk_ = nps()
                    bv, pv_ = nps()
                    for hh in range(4):
                        h = half * 4 + hh
                        tr.op("pe", lambda h=h, hh=hh: nc.tensor.matmul(
                            pk_[0:64, hh * 128:(hh + 1) * 128], qt[:, 8 + h, tok], ID128B, start=True,
                            stop=True), r=[qk_, "csb"], w=[("ps", bk)], inc=(hh == 3))
                    for hh in range(4):
                        h = half * 4 + hh
                        tr.op("pe", lambda h=h, hh=hh: nc.tensor.matmul(
                            pv_[0:64, hh * 128:(hh + 1) * 128], qt[:, 16 + h, tok], ID128B, start=True,
                            stop=True), r=[qk_, "csb"], w=[("ps", bv)], inc=(hh == 3))
                    hsl = slice(half * 4, half * 4 + 4)
                    pk3 = pk_[0:64, :].rearrange("p (h d) -> p h d", h=4)
                    pv3 = pv_[0:64, :].rearrange("p (h d) -> p h d", h=4)
                    tr.op("dve", lambda pk3=pk3, hsl=hsl: nc.vector.tensor_tensor(
                        out=kbg[:, hsl, :], in0=pk3,
                        in1=sm[0:64, 4, hsl].unsqueeze(2).to_broadcast([64, 4, 128]), op=ALU.mult),
                        r=[("ps", bk), ("sm", sl)], w=[("kbg", sl)])
                    tr.op("dve", lambda pk3=pk3, hsl=hsl: nc.vector.tensor_tensor(
                        out=kdec[:, hsl, :], in0=pk3,
                        in1=sm[0:64, 3, hsl].unsqueeze(2).to_broadcast([64, 4, 128]), op=ALU.mult),
                        r=[("ps", bk), ("sm", sl)], w=[("kdec", sl)])
                    tr.op("dve", lambda pv3=pv3, hsl=hsl: nc.vector.tensor_tensor(
                        out=vb[:, hsl, :], in0=pv3,
                        in1=beta[:, c, hsl].unsqueeze(2).to_broadcast([64, 4, 128]), op=ALU.mult),
                        r=[("ps", bv), ("beta", "t", n % 2)], w=[("vb", sl)])
                for half in range(2):
                    yield
                    bu, pu = nps()
                    for hh in range(4):
                        h = half * 4 + hh
                        tr.op("pe", lambda h=h, hh=hh: nc.tensor.matmul(
                            pu[0:64, hh * 128:(hh + 1) * 128], TT[:, h * 64:(h + 1) * 64], vb[:, h, :],
                            start=True, stop=True), r=[TK, ("vb", sl)], w=[("ps", bu)], inc=(hh == 3))
                    tr.op("act", lambda half=half, pu=pu: nc.scalar.copy(
                        out=usb[:, half * 4:half * 4 + 4, :].rearrange("p h d -> p (h d)"), in_=pu[0:64, :]),
                        r=[("ps", bu)], w=[("usb", sl)])
                yield
                bw, pw = nps()
                for h in range(H):
                    tr.op("pe", lambda h=h: nc.tensor.matmul(pw[:, h * 64:(h + 1) * 64], kbg[:, h, :],
                                                             TT[:, h * 64:(h + 1) * 64], start=True, stop=True),
                          r=[TK, ("kbg", sl)], w=[("ps", bw)], inc=(h == H - 1))
                tr.op("act", lambda: nc.scalar.copy(out=wT[:].rearrange("p h i -> p (h i)"), in_=pw[:, :]),
                      r=[("ps", bw)], w=[("wT", sl)])
                for half in range(2):
                    bws, pws = nps()
                    for hh in range(4):
                        h = half * 4 + hh
                        tr.op("pe", lambda h=h, hh=hh: nc.tensor.matmul(
                            pws[0:64, hh * 128:(hh + 1) * 128], wT[:, h, :], Sb[:, h, :], start=True,
                            stop=True), r=[("wT", sl), "Sb"], w=[("ps", bws)], inc=(hh == 3))
                    tr.op("dve", lambda half=half, pws=pws: nc.vector.tensor_tensor(
                        out=vnew[:, half * 4:half * 4 + 4, :].rearrange("p h d -> p (h d)"),
                        in0=usb[:, half * 4:half * 4 + 4, :].rearrange("p h d -> p (h d)"), in1=pws[0:64, :],
                        op=ALU.subtract), r=[("ps", bws), ("usb", sl)], w=[("vnew", half, sl)])
                for half in range(2):
                    bo, po = nps()
                    for hh in range(4):
                        h = half * 4 + hh
                        tr.op("pe", lambda h=h, hh=hh: nc.tensor.matmul(
                            po[0:64, hh * 128:(hh + 1) * 128], qg[:, h, :], Sb[:, h, :], start=True,
                            stop=False), r=[("qg", sl), "Sb"], w=[("ps", bo)], inc=False)
                        tr.op("pe", lambda h=h, hh=hh: nc.tensor.matmul(
                            po[0:64, hh * 128:(hh + 1) * 128], attn[:, h * 64:(h + 1) * 64], vnew[:, h, :],
                            start=False, stop=True), r=[("attn", sl), ("vnew", half, sl)], w=[("ps", bo)],
                            inc=(hh == 3))
                    osl = osb[:, half * 4:half * 4 + 4, :].rearrange("p h d -> p (h d)")
                    if dr == 0:
                        tr.op("act", lambda osl=osl, po=po: nc.scalar.copy(out=osl, in_=po[0:64, :]),
                              r=[("ps", bo)], w=[("osb", half, sl)])
                    else:
                        if half == 0:
                            tr.dma("sp", ofw[:].rearrange("p h d -> p (h d)"), OF[gtok:gtok + C, :],
                                   r=[("OF", gtok)], w=[("ofw", sl)], sem=("ofw", sl))
                            tr.dma("sp", zsb[:].rearrange("p h d -> p (h d)"), ZS[gtok:gtok + C, :],
                                   w=[("zsb", sl)], sem=("zsb", sl))
                        tr.op("dve", lambda osl=osl, po=po, half=half: nc.vector.tensor_tensor(
                            out=osl, in0=po[0:64, :],
                            in1=ofw[:, half * 4:half * 4 + 4, :].rearrange("p h d -> p (h d)"), op=ALU.add),
                            r=[("ps", bo), ("ofw", sl)], w=[("osb", half, sl)])
                for half in range(2):
                    bs, pS = nps()
                    for hh in range(4):
                        h = half * 4 + hh
                        tr.op("pe", lambda h=h, hh=hh: nc.tensor.matmul(
                            pS[:, hh * 128:(hh + 1) * 128], kdec[:, h, :], vnew[:, h, :], start=True,
                            stop=True), r=[("kdec", sl), ("vnew", half, sl)], w=[("ps", bs)], inc=(hh == 3))
                    hsl = slice(half * 4, half * 4 + 4)
                    tr.op("pool", lambda hsl=hsl: nc.gpsimd.tensor_tensor(
                        out=S[:, hsl, :], in0=S[:, hsl, :],
                        in1=sm[:, 5, hsl].unsqueeze(2).to_broadcast([128, 4, 128]), op=ALU.mult),
                        r=[("sm", sl), "S"], w=["S"])
                    tr.op("dve", lambda hsl=hsl, pS=pS: nc.vector.tensor_tensor(
                        out=S[:, hsl, :].rearrange("p h d -> p (h d)"),
                        in0=S[:, hsl, :].rearrange("p h d -> p (h d)"), in1=pS[:, :], op=ALU.add),
                        r=[("ps", bs), "S"], w=["S"])
                tr.op("act", lambda: nc.scalar.copy(out=Sb[:], in_=S[:]), r=["S"], w=["Sb"])
                if dr == 0:
                    tr.dma("sp", OF[gtok:gtok + C, :], osb[:].rearrange("p h d -> p (h d)"),
                           r=[("osb", 0, sl), ("osb", 1, sl)], w=[("OF", gtok)], sem=("osb", sl))
                else:
                    for h in range(H):
                        hf = h // 4
                        tr.op("pool", lambda h=h: nc.gpsimd.tensor_tensor(
                            out=ysq[:, h, :], in0=osb[:, h, :], in1=osb[:, h, :], op=ALU.mult),
                            r=[("osb", hf, sl)], w=[("ysq", h, sl)])
                        tr.op("dve", lambda h=h: nc.vector.tensor_reduce(
                            out=ss[:, 8 + h:9 + h], in_=ysq[:, h, :], axis=AX.X, op=ALU.add),
                            r=[("ysq", h, sl)], w=[("ss", sl)])
                    tr.op("act", lambda: nc.scalar.activation(out=ss2[:, 8:16], in_=ss[:, 8:16], func=AF.Sqrt,
                                                              bias=epsd[0:64, 0:1], scale=1.0 / 128),
                          r=[("ss", sl), "epsd"], w=[("ss2", sl)])
                    tr.op("dve", lambda: nc.vector.reciprocal(out=ss3[:, 8:16], in_=ss2[:, 8:16]),
                          r=[("ss2", sl)], w=[("ss3", sl)])
                    for h in range(H):
                        hf = h // 4
                        tr.op("dve", lambda h=h: nc.vector.scalar_tensor_tensor(
                            out=ysq[:, h, :], in0=osb[:, h, :], scalar=ss3[:, 8 + h:9 + h], in1=tp[0:64, 32:160],
                            op0=ALU.mult, op1=ALU.mult), r=[("osb", hf, sl), ("ss3", sl), "tp"], w=[("ysq", h, sl)],
                            strict=True)
                        tr.op("dve", lambda h=h: nc.vector.tensor_tensor(
                            out=yb[:, h, :], in0=ysq[:, h, :], in1=zsb[:, h, :], op=ALU.mult),
                            r=[("ysq", h, sl), ("zsb", sl)], w=[("yb", hf, sl)])
                    yield
                    by, py = nps()
                    for h in range(H):
                        tr.op("pe", lambda h=h: nc.tensor.matmul(py[:, h * 64:(h + 1) * 64], yb[:, h, :],
                                                                 ID64B, start=True, stop=True),
                              r=[("yb", 0, sl), ("yb", 1, sl), "csb"], w=[("ps", by)], inc=(h == H - 1))
                    tr.op("act", lambda: nc.scalar.copy(out=ybT[:, :, tok],
                                                        in_=py[:, :].rearrange("p (h i) -> p h i", h=8)),
                          r=[("ps", by)], w=["ybT"])
                if last and dr == 1:
                    tr.dma("sp", YBT[:, n * NT:(n + 1) * NT].rearrange("(h p) n -> p h n", p=128), ybT[:],
                           r=["ybT"], sem="ybT")

            for dr in range(2):
                tiles = list(range(ntile)) if dr == 0 else list(range(ntile - 1, -1, -1))
                facs = []
                for n in tiles:
                    chunks = list(range(nck)) if dr == 0 else list(range(nck - 1, -1, -1))
                    for ci, c in enumerate(chunks):
                        facs.append(lambda sl, dr=dr, n=n, c=c, ci=ci: chunk_gen(dr, n, c, ci == 0, ci == nck - 1, sl))
                run_streams(facs, WG, stagger=14)
                if dr == 0:
                    tr.dma("sp", sx_in.ap(), S[:].rearrange("p h d -> p (h d)"), r=["S"], w=["sx_in"], sem="sx")
                    tr.cc(lambda: nc.gpsimd.collective_compute("AllGather", ALU.bypass, replica_groups=PAIRS,
                                                                ins=[sx_in.ap()], outs=[sx_out.ap()]),
                          r=["sx_in"], w=["sx_out"], sem="sxcc")
                    tr.dma("sp", SX, sx_out.ap().rearrange("(r p) n -> p r n", p=128), r=["sx_out"], w=["SX", "ybT"],
                           sem="sx")
                    Sf = S[:].rearrange("p h d -> p (h d)")
                    tr.op("dve", lambda: nc.vector.tensor_scalar(out=Sf, in0=SX[:, 0, :], scalar1=selt[:, 0:1],
                                                                 scalar2=None, op0=ALU.mult),
                          r=["SX", "selt"], w=["S"])
                    tr.op("dve", lambda: nc.vector.scalar_tensor_tensor(out=Sf, in0=SX[:, 1, :], scalar=selt[:, 1:2],
                                                                        in1=Sf, op0=ALU.mult, op1=ALU.add),
                          r=["SX", "selt", "S"], w=["S"])
                    tr.op("act", lambda: nc.scalar.copy(out=Sb[:], in_=S[:]), r=["S"], w=["Sb"])
            tr.barrier()

        with ExitStack() as st5:
            ring_phase(st5, [b for _ in range(ntile) for b in range(OFF_UP2, OFF_UP2 + NB_UP)])
            P = ffn_pool(st5, wdn2, nxt=1)
            wb = sb("wb", [128, 8, D], BF16, st5)
            wo = sb("wo", [128, 8, D], BF16, st5)
            ybt = P["u"]
            mrg = P["hid"]
            gsb = [sb("gsb%d" % i, [128, NT], F32, st5) for i in range(2)]
            pab = [sb("pab%d" % i, [128, NT], F32, st5) for i in range(2)]
            tr.dma("pool", wb[:], wbout.rearrange("(k p) n -> p k n", p=128), w=["wb"], sem="wb")
            tr.dma("pool", wo[:], wout.rearrange("(k p) n -> p k n", p=128), w=["wo"], sem="wo")
            for n in range(ntile):
                xt = P["xt"][0]
                xkey = "xt0"
                c0 = n * NT
                load_tile(xt, H1T, c0, xkey)
                tr.dma("sp", ybt[:], YBT[:, c0:c0 + NT].rearrange("(h p) n -> p h n", p=128),
                       w=[("u", k) for k in range(8)], sem="ybt")
                for m in range(8):
                    g_ = gsb[m % 2]
                    gk = ("gsb", m % 2)
                    pa_ = pab[m % 2]
                    pk2 = ("pab", m % 2)
                    tr.dma("sp", g_[:], PT[(44 + m) * 128:(45 + m) * 128, c0 + 2:c0 + 2 + NT], w=[gk], sem=gk)
                    tr.dma("sp", pa_[:], PAT[m * 128:(m + 1) * 128, c0:c0 + NT], w=[pk2], sem=pk2)
                    tr.op("act", lambda g_=g_: nc.scalar.activation(out=g_[:], in_=g_[:], func=AF.Sigmoid),
                          r=[gk], w=[gk])
                    bi, pt = nps()
                    for kk in range(8):
                        tr.op("pe", lambda kk=kk, m=m, pt=pt: nc.tensor.matmul(
                            pt[:], wb[:, kk, m * 128:(m + 1) * 128], ybt[:, kk, :], start=(kk == 0), stop=(kk == 7)),
                            r=["wb", ("u", kk)], w=[("ps", bi)], inc=(kk == 7))
                    tr.op("dve", lambda g_=g_, pt=pt: nc.vector.tensor_tensor(out=g_[:], in0=pt[:], in1=g_[:],
                                                                              op=ALU.mult),
                          r=[("ps", bi), gk], w=[gk])
                    tr.op("pool", lambda g_=g_, pa_=pa_, m=m: nc.gpsimd.tensor_tensor(out=mrg[:, m, :], in0=g_[:],
                                                                                      in1=pa_[:], op=ALU.add),
                          r=[gk, pk2], w=[("hid", m)])
                for m in range(8):
                    bi, pt = nps()
                    for kk in range(8):
                        tr.op("pe", lambda kk=kk, m=m, pt=pt: nc.tensor.matmul(
                            pt[:], wo[:, kk, m * 128:(m + 1) * 128], mrg[:, kk, :], start=(kk == 0), stop=(kk == 7)),
                            r=["wo", ("hid", kk)], w=[("ps", bi)], inc=(kk == 7))
                    tr.op("dve", lambda m=m, pt=pt: nc.vector.scalar_tensor_tensor(
                        out=xt[:, m, :], in0=pt[:], scalar=der[:, 32 + m:33 + m], in1=xt[:, m, :], op0=ALU.mult,
                        op1=ALU.add), r=[("ps", bi), "der"], w=[(xkey, m)])
                ffn_body(P, xt, xkey, 2, P["wdn"], True, outT, c0)
            tr.barrier()
        tr.finish()
    return nc


def _blocks(w, cols_list):
    out = np.empty((len(cols_list), 128, 2048), np.float32)
    w3 = w.reshape(8, 128, -1)
    for b, cols in enumerate(cols_list):
        out[b] = w3[:, :, cols].transpose(1, 0, 2).reshape(128, 2048)
    return out


def _consts():
    c = np.zeros((128, 4096), np.float32)
    c[:, 0:128] = 1.0
    c[0:64, 128:192] = np.eye(64)
    j = np.arange(64)[:, None]
    i = np.arange(64)[None, :]
    c[0:64, 192:256] = (j <= i)
    c[0:64, 256:320] = (j >= i)
    capS_F = np.where(j < i, 0.0, NEG)
    capI_F = np.where(j <= i, 0.0, NEG)
    capS_B = np.where(j > i, 0.0, NEG)
    capI_B = np.where(j >= i, 0.0, NEG)
    c[0:64, 320:832] = np.tile(capS_F, (1, 8))
    c[0:64, 832:1344] = np.tile(capI_F, (1, 8))
    c[0:64, 1344:1856] = np.tile(capS_B, (1, 8))
    c[0:64, 1856:2368] = np.tile(capI_B, (1, 8))
    c[0:64, 2368:2880] = np.tile(np.eye(64), (1, 8))
    e8 = np.zeros((8, 8, 64), np.float32)
    for h in range(8):
        e8[h, h, :] = 1.0
    c[0:8, 2880:3392] = e8.reshape(8, 512)
    c[:, 3392:3520] = np.eye(128)
    return c


_PROG = {}


def _run(inputs, T, n_cores):
    f = lambda a: np.ascontiguousarray(np.asarray(a, dtype=np.float32))
    x = f(inputs["x"])
    B = x.shape[0]
    cvec = f(inputs["c"])
    w_ada = f(inputs["w_ada"])[0]
    b_ada = f(inputs["b_ada"])[0]
    w_in = f(inputs["w_in"])[0]
    wup1 = f(inputs["w_ffn1_up"])[0]
    wup2 = f(inputs["w_ffn2_up"])[0]
    conv_a = f(inputs["conv_a"])[0]
    conv_dn = f(inputs["conv_dn"])[0]
    cst = _consts()
    ii = np.arange(64)[:, None]
    jj = np.arange(64)[None, :]
    mks = [(ii // 8 == jj // 8)]
    for bsz in (8, 16, 32):
        mks.append((ii // (2 * bsz) == jj // (2 * bsz)) & (ii // bsz != jj // bsz))
    cst2 = np.concatenate([np.tile(m.astype(np.float32), (1, 8)) for m in mks], axis=1)

    ada_cols = [np.arange(b * 256, (b + 1) * 256) for b in range(NB_ADA)]
    up_cols = [np.concatenate([np.arange(j * 128, (j + 1) * 128), np.arange(DFF + j * 128, DFF + (j + 1) * 128)])
               for j in range(NJ)]
    fm = np.concatenate([np.arange(512, 1024), np.arange(1024, 1536), np.arange(1536, 4608),
                         np.arange(0, 512), np.arange(5664, 7712)])
    in_cols = [fm[b * 256:(b + 1) * 256] for b in range(NB_IN)]
    z_cols = [np.arange(4608 + b * 256, 4608 + (b + 1) * 256) for b in range(NB_Z)]
    wr_common = np.concatenate([_blocks(w_ada, ada_cols), _blocks(wup1, up_cols), _blocks(w_in, in_cols + z_cols),
                                _blocks(wup2, up_cols)], axis=0)
    assert wr_common.shape[0] == NB_TOT

    ba_f = np.arange(5632, 5664)
    ba_b = np.concatenate([np.arange(5640, 5648), np.arange(5632, 5640), np.arange(5656, 5664),
                           np.arange(5648, 5656)])
    norms = np.stack([f(inputs["norm_ffn1"])[0], f(inputs["norm_mix"])[0], f(inputs["norm_ffn2"])[0],
                      f(inputs["norm_final"])], 0)
    in_maps = []
    for core in range(n_cores):
        b, half = core // 2, core % 2
        xs = x[b, half * T:(half + 1) * T, :]
        if half == 1:
            xs = xs[::-1]
        vecs = np.zeros((128, 120), np.float32)
        vecs[:, 0:8] = cvec[b].reshape(8, 128).T
        vecs[:, 8:80] = b_ada.reshape(72, 128).T
        vecs[:, 80:112] = norms.reshape(4, 8, 128).transpose(2, 0, 1).reshape(128, 32)
        cdn = conv_dn if half == 0 else conv_dn[::-1]
        ca = conv_a if half == 0 else conv_a[::-1]
        convw = np.zeros((128, 132), np.float32)
        convw[:, 0:120] = cdn.T.reshape(24, 128, 5).transpose(1, 0, 2).reshape(128, 120)
        convw[:, 120:132] = ca.T.reshape(4, 128, 3).transpose(1, 0, 2).reshape(128, 12)
        names = ["a_log_fwd", "a_log_bwd", "dt_bias_fwd", "dt_bias_bwd"]
        if half == 1:
            names = ["a_log_bwd", "a_log_fwd", "dt_bias_bwd", "dt_bias_fwd"]
        tokp = np.zeros((128, 160), np.float32)
        for q_, nm in enumerate(names):
            tokp[:, q_ * 8:(q_ + 1) * 8] = f(inputs[nm])[0][None, :]
        tokp[:, 32:160] = f(inputs["dn_norm"])[0][None, :]
        selv = np.zeros((128, 2), np.float32)
        selv[:, 1 - half] = 1.0
        in_maps.append({
            "xT": np.ascontiguousarray(xs.T),
            "wr": wr_common,
            "wdn1": f(inputs["w_ffn1_down"])[0],
            "wdn2": f(inputs["w_ffn2_down"])[0],
            "wba": np.ascontiguousarray(w_in[:, ba_f if half == 0 else ba_b]),
            "waout": f(inputs["w_a_out"])[0],
            "wbout": f(inputs["w_b_out"])[0],
            "wout": f(inputs["w_out"])[0],
            "vecs": vecs, "convw": convw, "tokp": tokp, "cst": cst, "cst2": cst2, "sel": selv,
        })
    if T not in _PROG:
        _PROG[T] = build_program(T)
    res = run_bass_kernel_spmd(_PROG[T], in_maps, core_ids=list(range(n_cores)))
    if DEBUG:
        DBG["res"] = res.results
    out = np.empty((B, 2 * T, D), np.float32)
    for core in range(n_cores):
        b, half = core // 2, core % 2
        o = res.results[core]["outT"].T
        if half == 1:
            o = o[::-1]
        out[b, half * T:(half + 1) * T, :] = o
    return out


def kernel(**inputs):
    x = np.asarray(inputs["x"])
    B, S, _ = x.shape
    return _run(inputs, S // 2, 2 * B)
```

```python
import numpy as np
from contextlib import ExitStack
import concourse.bass as bass
import concourse.mybir as mybir
from concourse.bass_utils import run_bass_kernel_spmd

F32 = mybir.dt.float32
BF16 = mybir.dt.bfloat16
AF = mybir.ActivationFunctionType
ALU = mybir.AluOpType
AX = mybir.AxisListType

D = 1024
DFF = 2816
NJ = DFF // 128
H = 8
C = 64
NT = 512
EPS = 1e-6
NR = 4
NEG = -30000.0
DEBUG = False
SAME_ENGINE_SYNC = True
DBG = {}

NB_ADA = 36
NB_UP = 22
NB_IN = 26
NB_Z = 4
OFF_ADA = 0
OFF_UP1 = OFF_ADA + NB_ADA
OFF_IN = OFF_UP1 + NB_UP
OFF_Z = OFF_IN + NB_IN
OFF_UP2 = OFF_Z + NB_Z
NB_TOT = OFF_UP2 + NB_UP
NPC = 52
NCONV = 32


class Tick:
    __slots__ = ("kind", "key", "val")

    def __init__(self, kind, key, val):
        self.kind, self.key, self.val = kind, key, val


class Tr:
    def __init__(self, nc, es):
        self.nc = nc
        self.es = es
        self.E = {"pe": nc.tensor, "act": nc.scalar, "dve": nc.vector, "pool": nc.gpsimd, "sp": nc.sync}
        self.csem = {e: es.enter_context(nc.semaphore("c_" + e)) for e in ("pe", "act", "dve", "pool")}
        self.ccnt = {e: 0 for e in self.csem}
        self.pend = {e: [] for e in self.csem}
        self.dsem = {}
        self.dcnt = {}
        self.seen = {e: {} for e in self.E}
        self.st = {}

    def _wait(self, eng, t, same_ok=True):
        if t is None:
            return
        if t.kind == "c":
            if t.key == eng and (t.val is None or (same_ok and not SAME_ENGINE_SYNC)):
                return
            assert t.val is not None, "pending tick waited on"
            sem, v, k = self.csem[t.key], t.val, ("c", t.key)
        else:
            sem, v, k = self.dsem[t.key], self.dcnt[t.key], ("d", t.key)
        if self.seen[eng].get(k, 0) >= v:
            return
        self.seen[eng][k] = v
        self.E[eng].wait_ge(sem, v)

    def _deps(self, eng, r, w, same_ok):
        for k in r:
            s = self.st.get(k)
            if s:
                self._wait(eng, s[0], same_ok)
        for k in w:
            s = self.st.get(k)
            if s:
                self._wait(eng, s[0], same_ok)
                for t in s[1].values():
                    self._wait(eng, t, same_ok)

    def _upd(self, t, r, w):
        for k in r:
            self.st.setdefault(k, [None, {}])[1][(t.kind, t.key)] = t
        for k in w:
            self.st[k] = [t, {}]

    def op(self, eng, fn, r=(), w=(), inc=True, strict=False):
        self._deps(eng, r, w, not strict)
        ins = fn()
        t = Tick("c", eng, None)
        self.pend[eng].append(t)
        if inc:
            self.ccnt[eng] += 1
            ins.then_inc(self.csem[eng], 1)
            for p in self.pend[eng]:
                p.val = self.ccnt[eng]
            self.pend[eng] = []
        self._upd(t, r, w)
        return ins

    def dma(self, q, out, in_, r=(), w=(), sem=None):
        self._deps(q, r, w, False)
        if sem not in self.dsem:
            self.dsem[sem] = self.es.enter_context(self.nc.semaphore("d%d" % len(self.dsem)))
            self.dcnt[sem] = 0
        ins = self.E[q].dma_start(out=out, in_=in_)
        self.dcnt[sem] += 16
        ins.then_inc(self.dsem[sem], 16)
        t = Tick("d", sem, self.dcnt[sem])
        self._upd(t, r, w)
        return ins

    def cc(self, ins_fn, r=(), w=(), sem=None):
        self._deps("pool", r, w, False)
        if sem not in self.dsem:
            self.dsem[sem] = self.es.enter_context(self.nc.semaphore("d%d" % len(self.dsem)))
            self.dcnt[sem] = 0
        ins = ins_fn()
        self.dcnt[sem] += 1
        ins.then_inc(self.dsem[sem], 1)
        t = Tick("d", sem, self.dcnt[sem])
        self._upd(t, r, w)

    def barrier(self):
        for e in self.csem:
            assert not self.pend[e]
        for e in self.E:
            for f in self.csem:
                if f != e and self.ccnt[f] > 0:
                    self._wait(e, Tick("c", f, self.ccnt[f]))
            for k in self.dsem:
                if self.dcnt[k] > 0:
                    self._wait(e, Tick("d", k, self.dcnt[k]))
        self.st = {}

    def finish(self):
        for k in self.dsem:
            if self.dcnt[k] > 0:
                self._wait("sp", Tick("d", k, self.dcnt[k]))


class Ring:
    def __init__(self, tr, nc, slots, wr):
        self.tr, self.nc, self.slots, self.wr = tr, nc, slots, wr
        self.plan = []
        self.issued = 0
        self.pos = 0

    def add(self, blocks):
        self.plan.extend(blocks)

    def _issue(self):
        b = self.plan[self.issued]
        s = self.issued % NR
        self.tr.dma("pool", self.slots[s][:], self.wr[b], w=[("ring", s)], sem=("ring", s))
        self.issued += 1

    def get(self):
        while self.issued < min(len(self.plan), self.pos + NR):
            self._issue()
        s = self.pos % NR
        self.pos += 1
        return s, self.slots[s]


def build_program(T):
    ntile = T // NT
    nchunk = T // C
    nc = bass.Bass("TRN2", target_bir_lowering=False)

    def din(name, shape, dt=F32):
        return nc.dram_tensor(name, list(shape), dt, kind="ExternalInput").ap()

    xT = din("xT", [D, T])
    wr = din("wr", [NB_TOT, 128, 2048])
    wdn1 = din("wdn1", [DFF, D])
    wdn2 = din("wdn2", [DFF, D])
    wba = din("wba", [D, 32])
    waout = din("waout", [512, D])
    wbout = din("wbout", [D, D])
    wout = din("wout", [D, D])
    vecs = din("vecs", [128, 120])
    convw = din("convw", [128, 24 * 5 + 4 * 3])
    tokp = din("tokp", [128, 32 + 128])
    cst = din("cst", [128, 4096])
    cst2 = din("cst2", [64, 2048])
    sel = din("sel", [128, 2])
    outT = nc.dram_tensor("outT", [D, T], F32, kind="ExternalOutput").ap()

    def dscr(name, shape, dt=F32):
        if DEBUG:
            return nc.dram_tensor(name, list(shape), dt, kind="ExternalOutput").ap()
        return nc.dram_tensor(name, list(shape), dt).ap()

    H1T = dscr("H1T", [D, T])
    PT = dscr("PT", [NPC * 128, T + 4])
    ZS = dscr("ZS", [T, D])
    BAt = dscr("BAt", [T, 32])
    QKVT = dscr("QKVT", [3 * D, T], BF16)
    PAT = dscr("PAT", [D, T])
    OF = dscr("OF", [T, D])
    YBT = dscr("YBT", [D, T], BF16)
    hx_in = nc.dram_tensor("hx_in", [NCONV * 128, 2], F32)
    hx_out = nc.dram_tensor("hx_out", [2 * NCONV * 128, 2], F32)
    sx_in = nc.dram_tensor("sx_in", [128, H * 128], F32)
    sx_out = nc.dram_tensor("sx_out", [256, H * 128], F32)
    PAIRS = [[0, 1], [2, 3], [4, 5], [6, 7]]

    with ExitStack() as es:
        tr = Tr(nc, es)

        sbn = [0]

        def sb(name, shape, dt=F32, stack=None):
            sbn[0] += 1
            return (stack or es).enter_context(nc.sbuf_tensor("%s_%d" % (name, sbn[0]), list(shape), dt))

        ps = [es.enter_context(nc.psum_tensor("ps%d" % i, [128, 512], F32)) for i in range(8)]
        psi = [0]

        def nps():
            for _ in range(8):
                i = psi[0] % 8
                psi[0] += 1
                s_ = tr.st.get(("ps", i))
                if s_ is None or s_[0] is None or len(s_[1]) > 0:
                    return i, ps[i]
            raise RuntimeError("all PSUM banks hold unconsumed results")

        ring = Ring(tr, nc, None, wr)

        def ring_phase(stack, blocks):
            assert ring.issued == len(ring.plan) and ring.pos == len(ring.plan)
            ring.slots = [sb("ring", [128, 2048], BF16, stack) for i in range(NR)]
            ring.add(blocks)

        vec = sb("vec", [128, 120])
        cw = sb("cw", [128, 132])
        tp = sb("tp", [128, 160])
        cs = sb("cs", [128, 4096])
        csb = sb("csb", [128, 1024], BF16)
        selt = sb("selt", [128, 2])
        modT = sb("modT", [128, 72])
        der = sb("der", [128, 64])
        cact = sb("cact", [128, 8], BF16)
        negA = sb("negA", [128, 16])
        tr.dma("sp", vec[:], vecs, w=["vec"], sem="c0")
        tr.dma("sp", cw[:], convw, w=["cw"], sem="c0")
        tr.dma("sp", tp[:], tokp, w=["tp"], sem="c0")
        tr.dma("sp", cs[:], cst, w=["cs"], sem="c0")
        tr.dma("sp", selt[:], sel, w=["selt"], sem="c0")
        ONES = cs[:, 0:128]
        ID64 = cs[0:64, 128:192]
        LTRI = [cs[0:64, 192:256], cs[0:64, 256:320]]
        CAPS = [cs[0:64, 320:832], cs[0:64, 1344:1856]]
        CAPI = [cs[0:64, 832:1344], cs[0:64, 1856:2368]]
        I8 = cs[0:64, 2368:2880]
        E8 = cs[0:8, 2880:3392]
        tr.op("dve", lambda: nc.vector.tensor_copy(out=csb[:, 0:256], in_=cs[:, 0:256]), r=["cs"], w=["csb"])
        tr.op("dve", lambda: nc.vector.tensor_copy(out=csb[:, 256:384], in_=cs[:, 3392:3520]), r=["cs"], w=["csb"])
        ONESB = csb[:, 0:128]
        ID64B = csb[0:64, 128:192]
        ID128B = csb[:, 256:384]

        tr.op("act", lambda: nc.scalar.activation(out=cact[:], in_=vec[:, 0:8], func=AF.Silu), r=["vec"], w=["cact"])
        st0 = ExitStack()
        ring_phase(st0, list(range(OFF_ADA, OFF_ADA + NB_ADA)))
        bi, pb = nps()
        for blk in range(NB_ADA):
            s, slot = ring.get()
            sv = slot[:].rearrange("p (k c) -> p k c", k=8)
            for cc in range(2):
                j = blk * 2 + cc
                for k in range(8):
                    tr.op("pe", lambda k=k, j=j, cc=cc: nc.tensor.matmul(
                        pb[:, j:j + 1], sv[:, k, cc * 128:(cc + 1) * 128], cact[:, k:k + 1],
                        start=(k == 0), stop=(k == 7)),
                        r=[("ring", s), "cact"], w=[("ps", bi)], inc=(k == 7 and cc == 1))
        tr.op("dve", lambda: nc.vector.tensor_tensor(out=modT[:], in0=pb[:, 0:72], in1=vec[:, 8:80], op=ALU.add),
              r=[("ps", bi), "vec"], w=["modT"])
        for s in range(3):
            tr.op("dve", lambda s=s: nc.vector.scalar_tensor_tensor(
                out=der[:, s * 8:(s + 1) * 8], in0=modT[:, (3 * s + 1) * 8:(3 * s + 2) * 8], scalar=1.0,
                in1=vec[:, 80 + s * 8:88 + s * 8], op0=ALU.add, op1=ALU.mult), r=["modT", "vec"], w=["der"])
            gsc = 1.0 if s == 1 else 0.5
            tr.op("dve", lambda s=s, gsc=gsc: nc.vector.tensor_scalar(
                out=der[:, 24 + s * 8:32 + s * 8], in0=modT[:, (3 * s + 2) * 8:(3 * s + 3) * 8],
                scalar1=gsc, scalar2=None, op0=ALU.mult), r=["modT"], w=["der"])
        NF = vec[:, 104:112]
        tr.op("act", lambda: nc.scalar.activation(out=negA[:], in_=tp[:, 0:16], func=AF.Exp), r=["tp"], w=["negA"])
        tr.op("dve", lambda: nc.vector.tensor_scalar(out=negA[:], in0=negA[:], scalar1=-1.0, scalar2=None,
                                                     op0=ALU.mult), r=["negA"], w=["negA"])

        tr.barrier()
        st0.close()

        def load_tile(dst, src, t0, key, n=NT, coff=0):
            for i in range(8):
                tr.dma("sp", dst[:, i, coff:coff + n], src[i * 128:(i + 1) * 128, t0:t0 + n],
                       w=[(key, i)], sem=(key,))

        def norm_mod(P, xt, xkey, u, s_idx):
            bi, pss = nps()
            for i in range(8):
                sq = P["sq"][i % 2]
                tr.op("act", lambda i=i, sq=sq: nc.scalar.activation(out=sq[:], in_=xt[:, i, :], func=AF.Square),
                      r=[(xkey, i)], w=[("sq", i % 2)])
                tr.op("pe", lambda i=i, sq=sq: nc.tensor.matmul(pss[:], ONES, sq[:], start=(i == 0), stop=(i == 7)),
                      r=[("sq", i % 2), "cs"], w=[("ps", bi)], inc=True)
            rstd = P["rstd"]
            tr.op("act", lambda: nc.scalar.activation(out=rstd[:], in_=pss[:], func=AF.Sqrt, bias=P["eps"][:, 0:1],
                                                      scale=1.0 / D), r=[("ps", bi), "eps"], w=["rstd"])
            tr.op("dve", lambda: nc.vector.reciprocal(out=rstd[:], in_=rstd[:]), r=["rstd"], w=["rstd"])
            for i in range(8):
                tt = P["tt"][i % 2]
                tr.op("dve", lambda i=i, tt=tt: nc.vector.scalar_tensor_tensor(
                    out=tt[:], in0=xt[:, i, :], scalar=der[:, s_idx * 8 + i:s_idx * 8 + i + 1], in1=rstd[:],
                    op0=ALU.mult, op1=ALU.mult), r=[(xkey, i), "rstd", "der"], w=[("tt", i % 2)])
                tr.op("act", lambda i=i, tt=tt: nc.scalar.activation(
                    out=u[:, i, :], in_=tt[:], func=AF.Identity,
                    bias=modT[:, 3 * s_idx * 8 + i:3 * s_idx * 8 + i + 1], scale=1.0),
                    r=[("tt", i % 2), "modT"], w=[("u", i)])

        def ffn_body(P, xt, xkey, s_idx, wdn, final, dst, t0):
            u, hid = P["u"], P["hid"]
            norm_mod(P, xt, xkey, u, s_idx)
            for j in range(NJ):
                s, slot = ring.get()
                sv = slot[:].rearrange("p (k c) -> p k c", k=8)
                ba_, pa = nps()
                bb_, pbb = nps()
                for part, (bix, pt) in enumerate(((ba_, pa), (bb_, pbb))):
                    for k in range(8):
                        tr.op("pe", lambda k=k, part=part, pt=pt: nc.tensor.matmul(
                            pt[:], sv[:, k, part * 128:(part + 1) * 128], u[:, k, :], start=(k == 0), stop=(k == 7)),
                            r=[("ring", s), ("u", k)], w=[("ps", bix)], inc=(k == 7))
                sl = P["s"][j % 2]
                tr.op("act", lambda sl=sl, pa=pa: nc.scalar.activation(out=sl[:], in_=pa[:], func=AF.Silu),
                      r=[("ps", ba_)], w=[("s", j % 2)])
                tr.op("dve", lambda sl=sl, pbb=pbb, j=j: nc.vector.tensor_tensor(
                    out=hid[:, j, :], in0=pbb[:], in1=sl[:], op=ALU.mult),
                    r=[("ps", bb_), ("s", j % 2)], w=[("hid", j)])
            for m in range(8):
                bd, pd = nps()
                for j in range(NJ):
                    tr.op("pe", lambda j=j, m=m, pd=pd: nc.tensor.matmul(
                        pd[:], wdn[:, j, m * 128:(m + 1) * 128], hid[:, j, :], start=(j == 0), stop=(j == NJ - 1)),
                        r=["wdn", ("hid", j)], w=[("ps", bd)], inc=(j == NJ - 1))
                tr.op("dve", lambda m=m, pd=pd: nc.vector.scalar_tensor_tensor(
                    out=xt[:, m, :], in0=pd[:], scalar=der[:, 24 + s_idx * 8 + m:24 + s_idx * 8 + m + 1],
                    in1=xt[:, m, :], op0=ALU.mult, op1=ALU.add), r=[("ps", bd), "der"], w=[(xkey, m)])
            if final:
                bi, pss = nps()
                for i in range(8):
                    sq = P["sq"][i % 2]
                    tr.op("act", lambda i=i, sq=sq: nc.scalar.activation(out=sq[:], in_=xt[:, i, :], func=AF.Square),
                          r=[(xkey, i)], w=[("sq", i % 2)])
                    tr.op("pe", lambda i=i, sq=sq: nc.tensor.matmul(pss[:], ONES, sq[:], start=(i == 0),
                                                                    stop=(i == 7)),
                          r=[("sq", i % 2), "cs"], w=[("ps", bi)], inc=True)
                rstd = P["rstd"]
                tr.op("act", lambda: nc.scalar.activation(out=rstd[:], in_=pss[:], func=AF.Sqrt,
                                                          bias=P["eps"][:, 0:1], scale=1.0 / D),
                      r=[("ps", bi), "eps"], w=["rstd"])
                tr.op("dve", lambda: nc.vector.reciprocal(out=rstd[:], in_=rstd[:]), r=["rstd"], w=["rstd"])
                for i in range(8):
                    tr.op("dve", lambda i=i: nc.vector.scalar_tensor_tensor(
                        out=xt[:, i, :], in0=xt[:, i, :], scalar=NF[:, i:i + 1], in1=rstd[:],
                        op0=ALU.mult, op1=ALU.mult), r=[(xkey, i), "rstd", "vec"], w=[(xkey, i)])
            for i in range(8):
                tr.dma("sp", dst[i * 128:(i + 1) * 128, t0:t0 + NT], xt[:, i, :], r=[(xkey, i)], sem=(xkey, "st"))

        def ffn_pool(stack, wdn_src, nxt=2):
            P = {}
            P["wdn"] = sb("wdn", [128, NJ, D], BF16, stack)
            P["xt"] = [sb("xt%d" % i, [128, 8, NT], F32, stack) for i in range(nxt)]
            P["sq"] = [sb("sq%d" % i, [128, NT], F32, stack) for i in range(2)]
            P["tt"] = [sb("tt%d" % i, [128, NT], F32, stack) for i in range(2)]
            P["s"] = [sb("s%d" % i, [128, NT], F32, stack) for i in range(2)]
            P["rstd"] = sb("rstd", [128, NT], F32, stack)
            P["u"] = sb("u", [128, 8, NT], BF16, stack)
            P["hid"] = sb("hid", [128, NJ, NT], BF16, stack)
            P["eps"] = sb("eps", [128, 1], F32, stack)
            tr.op("dve", lambda: nc.vector.memset(P["eps"][:], EPS), w=["eps"])
            wv = wdn_src.rearrange("(j p) n -> p j n", p=128)
            for jj in range(0, NJ, 2):
                tr.dma("pool", P["wdn"][:, jj:jj + 2, :], wv[:, jj:jj + 2, :], w=["wdn"], sem="wdn")
            return P

        with ExitStack() as st1:
            ring_phase(st1, [b for _ in range(ntile) for b in range(OFF_UP1, OFF_UP1 + NB_UP)])
            P = ffn_pool(st1, wdn1)
            load_tile(P["xt"][0], xT, 0, "xt0")
            for n in range(ntile):
                xt = P["xt"][n % 2]
                xkey = "xt%d" % (n % 2)
                if n + 1 < ntile:
                    load_tile(P["xt"][(n + 1) % 2], xT, (n + 1) * NT, "xt%d" % ((n + 1) % 2))
                ffn_body(P, xt, xkey, 0, P["wdn"], False, H1T, n * NT)
            tr.barrier()

        with ExitStack() as st2:
            ring_phase(st2, [b for _ in range(ntile) for b in range(OFF_IN, OFF_IN + NB_IN + NB_Z)])
            xts = [sb("xt%d" % i, [128, 8, NT], F32, st2) for i in range(2)]
            P = {"sq": [sb("sq%d" % i, [128, NT], F32, st2) for i in range(2)],
                 "tt": [sb("tt%d" % i, [128, NT], F32, st2) for i in range(2)],
                 "rstd": sb("rstd", [128, NT], F32, st2), "eps": sb("eps", [128, 1], F32, st2)}
            tr.op("dve", lambda: nc.vector.memset(P["eps"][:], EPS), w=["eps"])
            u = sb("u", [128, 8, NT], BF16, st2)
            wbas = sb("wbas", [128, 8, 32], BF16, st2)
            stg = [sb("stg%d" % i, [128, NT], F32, st2) for i in range(4)]
            zero = sb("zero", [128, NCONV, 2], F32, st2)
            tr.dma("pool", wbas[:], wba.rearrange("(k p) n -> p k n", p=128), w=["wbas"], sem="wbas")
            tr.op("dve", lambda: nc.vector.memset(zero[:], 0.0), w=["zero"])
            tr.dma("sp", PT[0:NCONV * 128, 0:2].rearrange("(c p) n -> p c n", p=128), zero[:], r=["zero"], sem="zero")
            sti = 0
            load_tile(xts[0], H1T, 0, "xt0")
            for n in range(ntile):
                xt = xts[n % 2]
                xkey = "xt%d" % (n % 2)
                if n + 1 < ntile:
                    load_tile(xts[(n + 1) % 2], H1T, (n + 1) * NT, "xt%d" % ((n + 1) % 2))
                norm_mod(P, xt, xkey, u, 1)
                for blk in range(NB_IN):
                    s, slot = ring.get()
                    sv = slot[:].rearrange("p (k c) -> p k c", k=8)
                    for cc in range(2):
                        ch = blk * 2 + cc
                        bi, pt = nps()
                        for k in range(8):
                            tr.op("pe", lambda k=k, cc=cc, pt=pt: nc.tensor.matmul(
                                pt[:], sv[:, k, cc * 128:(cc + 1) * 128], u[:, k, :], start=(k == 0), stop=(k == 7)),
                                r=[("ring", s), ("u", k)], w=[("ps", bi)], inc=(k == 7))
                        sg = stg[sti % 4]
                        sk = ("stg", sti % 4)
                        sti += 1
                        tr.op("act", lambda sg=sg, pt=pt: nc.scalar.copy(out=sg[:], in_=pt[:]), r=[("ps", bi)], w=[sk])
                        tr.dma("sp", PT[ch * 128:(ch + 1) * 128, 2 + n * NT:2 + (n + 1) * NT], sg[:], r=[sk], sem=sk)
                for zb in range(NB_Z):
                    s, slot = ring.get()
                    sv = slot[:].rearrange("p (k c) -> p k c", k=8)
                    for tb in range(0, 4, 2):
                        bi, pt = nps()
                        for t2 in range(2):
                            for k in range(8):
                                tr.op("pe", lambda k=k, t2=t2, tb=tb, pt=pt: nc.tensor.matmul(
                                    pt[:, t2 * 256:(t2 + 1) * 256], u[:, k, (tb + t2) * 128:(tb + t2 + 1) * 128],
                                    sv[:, k, :], start=(k == 0), stop=(k == 7)),
                                    r=[("ring", s), ("u", k)], w=[("ps", bi)], inc=(k == 7))
                        sg = stg[sti % 4]
                        sk = ("stg", sti % 4)
                        sti += 1
                        tr.op("act", lambda sg=sg, pt=pt: nc.scalar.activation(out=sg[:], in_=pt[:], func=AF.Silu),
                              r=[("ps", bi)], w=[sk])
                        for t2 in range(2):
                            r0 = n * NT + (tb + t2) * 128
                            tr.dma("sp", ZS[r0:r0 + 128, zb * 256:(zb + 1) * 256], sg[:, t2 * 256:(t2 + 1) * 256],
                                   r=[sk], sem=sk)
                bi, pt = nps()
                for tb in range(4):
                    for k in range(8):
                        tr.op("pe", lambda k=k, tb=tb, pt=pt: nc.tensor.matmul(
                            pt[:, tb * 32:(tb + 1) * 32], u[:, k, tb * 128:(tb + 1) * 128], wbas[:, k, :],
                            start=(k == 0), stop=(k == 7)), r=["wbas", ("u", k)], w=[("ps", bi)], inc=(k == 7))
                sg = stg[sti % 4]
                sk = ("stg", sti % 4)
                sti += 1
                tr.op("act", lambda sg=sg, pt=pt: nc.scalar.copy(out=sg[:, 0:128], in_=pt[:, 0:128]),
                      r=[("ps", bi)], w=[sk])
                tr.dma("sp", BAt[n * NT:(n + 1) * NT, :].rearrange("(b p) n -> p b n", p=128),
                       sg[:, 0:128].rearrange("p (b n) -> p b n", b=4), r=[sk], sem=sk)
            tr.barrier()
            hx = sb("hx", [128, 2, NCONV, 2], F32, st2)
            hy = sb("hy", [128, NCONV, 2], F32, st2)
            tr.dma("pool", hx_in.ap(), PT[0:NCONV * 128, T:T + 2], w=["hx_in"], sem="hx")
            tr.cc(lambda: nc.gpsimd.collective_compute("AllGather", ALU.bypass, replica_groups=PAIRS,
                                                        ins=[hx_in.ap()], outs=[hx_out.ap()]),
                  r=["hx_in"], w=["hx_out"], sem="hxcc")
            tr.dma("pool", hx[:], hx_out.ap().rearrange("(r c p) n -> p r c n", r=2, p=128), r=["hx_out"], w=["hx"],
                   sem="hx")
            tr.op("dve", lambda: nc.vector.tensor_scalar(out=hy[:], in0=hx[:, 0], scalar1=selt[:, 0:1], scalar2=None,
                                                         op0=ALU.mult), r=["hx", "selt"], w=["hy"])
            tr.op("dve", lambda: nc.vector.scalar_tensor_tensor(out=hy[:], in0=hx[:, 1], scalar=selt[:, 1:2],
                                                                in1=hy[:], op0=ALU.mult, op1=ALU.add),
                  r=["hx", "selt", "hy"], w=["hy"])
            pv = PT[0:NCONV * 128, :].rearrange("(c p) n -> p c n", p=128)
            hys = sb("hys", [128, NCONV, 2], F32, st2)
            tr.op("dve", lambda: nc.vector.tensor_copy(out=hys[:, :, 0:1], in_=hy[:, :, 1:2]), r=["hy"], w=["hys"])
            tr.op("dve", lambda: nc.vector.tensor_copy(out=hys[:, :, 1:2], in_=hy[:, :, 0:1]), r=["hy"], w=["hys"])
            tr.dma("sp", pv[:, :, T + 2:T + 4], hys[:], r=["hys"], sem="hy")
            tr.barrier()

        def run_streams(factories, width, stagger=0):
            active = []
            free = list(range(width))
            it = iter(factories)
            done = False
            if stagger:
                f = next(it, None)
                if f is not None:
                    s_ = free.pop(0)
                    g0 = f(s_)
                    active.append((g0, s_))
                    for _ in range(stagger):
                        try:
                            next(g0)
                        except StopIteration:
                            active.remove((g0, s_))
                            free.append(s_)
                            break
            while True:
                while free and not done:
                    f = next(it, None)
                    if f is None:
                        done = True
                        break
                    s_ = free.pop(0)
                    active.append((f(s_), s_))
                if not active:
                    break
                for g in list(active):
                    try:
                        next(g[0])
                    except StopIteration:
                        active.remove(g)
                        free.append(g[1])

        with ExitStack() as st3:
            W2B = 4
            B2 = []
            for i in range(W2B):
                B2.append({"pre": sb("pre", [128, NT + 4], BF16, st3), "acc": sb("acc", [128, NT], F32, st3),
                           "sq": sb("sqq", [128, NT], F32, st3), "rn": sb("rn", [128, NT], F32, st3),
                           "ob": sb("ob", [128, NT], BF16, st3)})
            A2 = []
            for i in range(2):
                A2.append({"p1": sb("p1", [128, NT + 4], F32, st3), "p2": sb("p2", [128, NT + 4], F32, st3),
                           "p3": sb("p3", [128, NT], F32, st3), "pcv": sb("pcv", [128, NT + 4], BF16, st3),
                           "cacc": sb("cacc", [128, NT], F32, st3)})
            G2 = []
            for i in range(2):
                G2.append({"g": sb("gsb", [128, NT], F32, st3), "pa": sb("pab", [128, NT], F32, st3)})
            yaT = [sb("yaT", [128, 4, NT], BF16, st3) for i in range(2)]
            wa = sb("wa", [128, 4, D], BF16, st3)
            dgq = sb("dgq", [128, 24 * 5, 128], BF16, st3)
            dga = sb("dga", [128, 4 * 3, 128], BF16, st3)
            IDF = cs[:, 3392:3520]
            for j in range(24 * 5):
                eng = "dve"
                if eng == "dve":
                    tr.op("dve", lambda j=j: nc.vector.tensor_scalar(out=dgq[:, j, :], in0=IDF, scalar1=cw[:, j:j + 1],
                                                                     scalar2=None, op0=ALU.mult),
                          r=["cs", "cw"], w=[("dgq", j)])
                else:
                    tr.op("pool", lambda j=j: nc.gpsimd.tensor_scalar(out=dgq[:, j, :], in0=IDF, scalar1=cw[:, j:j + 1],
                                                                      scalar2=None, op0=ALU.mult),
                          r=["cs", "cw"], w=[("dgq", j)])
            for j in range(12):
                tr.op("dve", lambda j=j: nc.vector.tensor_scalar(out=dga[:, j, :], in0=IDF,
                                                                 scalar1=cw[:, 120 + j:121 + j], scalar2=None,
                                                                 op0=ALU.mult), r=["cs", "cw"], w=[("dga", j)])
            epsq = sb("epsq", [128, 2], F32, st3)
            tr.op("dve", lambda: nc.vector.memset(epsq[:, 0:1], EPS), w=["epsq"])
            tr.op("dve", lambda: nc.vector.memset(epsq[:, 1:2], 128.0 * EPS), w=["epsq"])
            tr.dma("pool", wa[:], waout.rearrange("(k p) n -> p k n", p=128), w=["wa"], sem="wa")

            def qkv_gen(n, ch, sl):
                c0 = n * NT
                Bf = B2[sl]
                p_, a_, s_, r_, o_ = Bf["pre"], Bf["acc"], Bf["sq"], Bf["rn"], Bf["ob"]
                pk, ak, sk, rk, ok = ("pre", sl), ("acc", sl), ("sqq", sl), ("rn", sl), ("ob", sl)
                tr.dma("pool", p_[:], PT[(8 + ch) * 128:(9 + ch) * 128, c0:c0 + NT + 4], w=[pk], sem=pk)
                yield
                bc_, pc_ = nps()
                for tap in range(5):
                    tr.op("pe", lambda tap=tap: nc.tensor.matmul(pc_[:], dgq[:, ch * 5 + tap, :], p_[:, tap:tap + NT],
                                                                 start=(tap == 0), stop=(tap == 4)),
                          r=[pk, ("dgq", ch * 5 + tap)], w=[("ps", bc_)], inc=(tap == 4))
                yield
                tr.op("act", lambda: nc.scalar.activation(out=a_[:], in_=pc_[:], func=AF.Silu),
                      r=[("ps", bc_)], w=[ak])
                yield
                if ch < 16:
                    tr.op("pool", lambda: nc.gpsimd.tensor_tensor(out=s_[:], in0=a_[:], in1=a_[:], op=ALU.mult),
                          r=[ak], w=[sk])
                    yield
                    bi, pt = nps()
                    tr.op("pe", lambda: nc.tensor.matmul(pt[:], ONES, s_[:], start=True, stop=True),
                          r=[sk, "cs"], w=[("ps", bi)])
                    yield
                    qsc = 128.0 if ch < 8 else 1.0
                    ebias = epsq[:, 1:2] if ch < 8 else epsq[:, 0:1]
                    tr.op("act", lambda: nc.scalar.activation(out=r_[:], in_=pt[:], func=AF.Sqrt, bias=ebias,
                                                              scale=qsc), r=[("ps", bi), "epsq"], w=[rk])
                    yield
                    tr.op("dve", lambda: nc.vector.reciprocal(out=r_[:], in_=r_[:]), r=[rk], w=[rk])
                    yield
                    tr.op("pool", lambda: nc.gpsimd.tensor_tensor(out=o_[:], in0=a_[:], in1=r_[:], op=ALU.mult),
                          r=[ak, rk], w=[ok])
                else:
                    tr.op("pool", lambda: nc.gpsimd.tensor_copy(out=o_[:], in_=a_[:]), r=[ak], w=[ok])
                yield
                tr.dma("sp", QKVT[ch * 128:(ch + 1) * 128, c0:c0 + NT], o_[:], r=[ok], sem=ok)

            def abr_gen(n, ch, sl):
                c0 = n * NT
                Af = A2[sl]
                p1, p2, p3, pcv, cacc = Af["p1"], Af["p2"], Af["p3"], Af["pcv"], Af["cacc"]
                k1, k2, k3, kp, kc = ("p1", sl), ("p2", sl), ("p3", sl), ("pcv", sl), ("cacc", sl)
                ya = yaT[n % 2]
                tr.dma("sp", p1[:], PT[ch * 128:(ch + 1) * 128, c0:c0 + NT + 4], w=[k1], sem=k1)
                tr.dma("sp", p2[:], PT[(4 + ch) * 128:(5 + ch) * 128, c0:c0 + NT + 4], w=[k2], sem=k2)
                tr.dma("sp", p3[:], PT[(32 + ch) * 128:(33 + ch) * 128, c0 + 2:c0 + 2 + NT], w=[k3], sem=k3)
                yield
                tr.op("dve", lambda: nc.vector.tensor_tensor(out=pcv[:], in0=p1[:], in1=p2[:], op=ALU.mult),
                      r=[k1, k2], w=[kp])
                yield
                bc_, pc_ = nps()
                for tap in range(3):
                    tr.op("pe", lambda tap=tap: nc.tensor.matmul(pc_[:], dga[:, ch * 3 + tap, :],
                                                                 pcv[:, 1 + tap:1 + tap + NT], start=(tap == 0),
                                                                 stop=(tap == 2)),
                          r=[kp, ("dga", ch * 3 + tap)], w=[("ps", bc_)], inc=(tap == 2))
                yield
                tr.op("dve", lambda: nc.vector.tensor_tensor(out=ya[:, ch, :], in0=pc_[:], in1=p3[:], op=ALU.mult),
                      r=[("ps", bc_), k3], w=[("yaT", n % 2, ch)])

            def gate_gen(n, m, sl):
                c0 = n * NT
                g_, pa_ = G2[sl]["g"], G2[sl]["pa"]
                gk, pk2 = ("gsb", sl), ("pab", sl)
                ya = yaT[n % 2]
                tr.dma("sp", g_[:], PT[(36 + m) * 128:(37 + m) * 128, c0 + 2:c0 + 2 + NT], w=[gk], sem=gk)
                yield
                tr.op("act", lambda: nc.scalar.activation(out=g_[:], in_=g_[:], func=AF.Sigmoid), r=[gk], w=[gk])
                bi, pt = nps()
                for kk in range(4):
                    tr.op("pe", lambda kk=kk: nc.tensor.matmul(pt[:], wa[:, kk, m * 128:(m + 1) * 128], ya[:, kk, :],
                                                               start=(kk == 0), stop=(kk == 3)),
                          r=["wa", ("yaT", n % 2, kk)], w=[("ps", bi)], inc=(kk == 3))
                yield
                tr.op("dve", lambda: nc.vector.tensor_tensor(out=pa_[:], in0=pt[:], in1=g_[:], op=ALU.mult),
                      r=[("ps", bi), gk], w=[pk2])
                yield
                tr.dma("sp", PAT[m * 128:(m + 1) * 128, c0:c0 + NT], pa_[:], r=[pk2], sem=pk2)

            for n in range(ntile):
                run_streams([(lambda sl, n=n, ch=ch: qkv_gen(n, ch, sl)) for ch in range(24)], W2B)
                run_streams([(lambda sl, n=n, ch=ch: abr_gen(n, ch, sl)) for ch in range(4)], 2)
                run_streams([(lambda sl, n=n, m=m: gate_gen(n, m, sl)) for m in range(8)], 2)
            tr.barrier()

        with ExitStack() as st4:
            qkv = [sb("qkv%d" % i, [128, 24, NT], BF16, st4) for i in range(2)]
            S = sb("S", [128, H, 128], F32, st4)
            Sb = sb("Sb", [128, H, 128], BF16, st4)
            sm = sb("sm", [128, 8, 8], F32, st4)
            gT = sb("gT", [8, 3, 64], F32, st4)
            Rg = sb("Rg", [8, 2, 512], F32, st4)
            dec = [sb("dec%d" % i, [64, 512], F32, st4) for i in range(2)]
            egr = sb("egr", [128, 512], F32, st4)
            qg = sb("qg", [128, H, C], BF16, st4)
            XY = [sb("XY%d" % i, [64, 2, 512], BF16, st4) for i in range(2)]
            PQ = [sb("PQ%d" % i, [64, 2, 512], BF16, st4) for i in range(2)]
            attn = sb("attn", [64, 512], BF16, st4)
            XY0 = sb("XY0", [64, 2, 512], BF16, st4)
            MK = sb("MK", [64, 4, 512], BF16, st4)
            tr.dma("pool", MK[:], cst2.rearrange("p (a n) -> p a n", a=4), w=["MK"], sem="MK")
            kbg = sb("kbg", [64, H, 128], BF16, st4)
            kdec = sb("kdec", [64, H, 128], BF16, st4)
            vb = sb("vb", [64, H, 128], BF16, st4)
            usb = sb("usb", [64, H, 128], F32, st4)
            wT = sb("wT", [128, H, C], BF16, st4)
            vnew = sb("vnew", [64, H, 128], BF16, st4)
            osb = sb("osb", [64, H, 128], F32, st4)
            ofw = sb("ofw", [64, H, 128], F32, st4)
            zsb = sb("zsb", [64, H, 128], F32, st4)
            ysq = sb("ysq", [64, H, 128], F32, st4)
            ss = sb("ss", [64, 16], F32, st4)
            ss2 = sb("ss2", [64, 16], F32, st4)
            ss3 = sb("ss3", [64, 16], F32, st4)
            yb = sb("yb", [64, H, 128], BF16, st4)
            ybT = sb("ybT", [128, H, NT], BF16, st4)
            SX = ybT[:].bitcast(F32).rearrange("p h n -> p (h n)").rearrange("p (r m) -> p r m", r=2)
            epsd = sb("epsd", [128, 1], F32, st4)
            tr.op("dve", lambda: nc.vector.memset(epsd[:], EPS), w=["epsd"])
            tr.op("dve", lambda: nc.vector.memset(S[:], 0.0), w=["S"])
            tr.op("dve", lambda: nc.vector.memset(Sb[:], 0.0), w=["Sb"])
            nck = NT // C

            def bc(ap, shape):
                return ap.to_broadcast(list(shape))

            WG = 2
            TB = [{"bat": sb("bat", [64, NT // C, 32], F32, st4), "beta": sb("beta", [64, NT // C, 8], F32, st4),
                   "lnb": sb("lnb", [64, NT // C, 8], F32, st4), "gg": sb("gg", [64, NT // C, 8], F32, st4)}
                  for _i in range(2)]
            CB = [{"sm": sm, "gT": gT, "Rg": Rg, "dec": dec, "egr": egr, "qg": qg, "XY": XY, "PQ": PQ, "attn": attn,
                   "XY0": XY0, "kbg": kbg, "kdec": kdec, "vb": vb, "usb": usb, "wT": wT, "vnew": vnew, "osb": osb,
                   "ofw": ofw, "zsb": zsb, "ysq": ysq, "ss": ss, "ss2": ss2, "ss3": ss3, "yb": yb}]
            for _i in range(1, WG):
                CB.append({
                    "sm": sb("sm", [128, 8, 8], F32, st4), "gT": sb("gT", [8, 3, 64], F32, st4),
                    "Rg": sb("Rg", [8, 2, 512], F32, st4),
                    "dec": [sb("dec", [64, 512], F32, st4) for _j in range(2)],
                    "egr": sb("egr", [128, 512], F32, st4), "qg": sb("qg", [128, H, C], BF16, st4),
                    "XY": [sb("XY", [64, 2, 512], BF16, st4) for _j in range(2)],
                    "PQ": [sb("PQ", [64, 2, 512], BF16, st4) for _j in range(2)],
                    "attn": sb("attn", [64, 512], BF16, st4), "XY0": sb("XY0", [64, 2, 512], BF16, st4),
                    "kbg": sb("kbg", [64, H, 128], BF16, st4), "kdec": sb("kdec", [64, H, 128], BF16, st4),
                    "vb": sb("vb", [64, H, 128], BF16, st4), "usb": sb("usb", [64, H, 128], F32, st4),
                    "wT": sb("wT", [128, H, C], BF16, st4), "vnew": sb("vnew", [64, H, 128], BF16, st4),
                    "osb": sb("osb", [64, H, 128], F32, st4), "ofw": sb("ofw", [64, H, 128], F32, st4),
                    "zsb": sb("zsb", [64, H, 128], F32, st4), "ysq": sb("ysq", [64, H, 128], F32, st4),
                    "ss": sb("ss", [64, 16], F32, st4), "ss2": sb("ss2", [64, 16], F32, st4),
                    "ss3": sb("ss3", [64, 16], F32, st4), "yb": sb("yb", [64, H, 128], BF16, st4)})

            def chunk_gen(dr, n, c, first, last, sl):
                Bc = CB[sl]
                Tt = TB[n % 2]
                bat, beta, lnb, gg = Tt["bat"], Tt["beta"], Tt["lnb"], Tt["gg"]
                sm, gT, Rg, dec, egr, qg, XY, PQ, attn, XY0 = (Bc["sm"], Bc["gT"], Bc["Rg"], Bc["dec"], Bc["egr"],
                                                               Bc["qg"], Bc["XY"], Bc["PQ"], Bc["attn"], Bc["XY0"])
                kbg, kdec, vb, usb, wT, vnew, osb, ofw, zsb = (Bc["kbg"], Bc["kdec"], Bc["vb"], Bc["usb"], Bc["wT"],
                                                               Bc["vnew"], Bc["osb"], Bc["ofw"], Bc["zsb"])
                ysq, ss, ss2, ss3, yb = Bc["ysq"], Bc["ss"], Bc["ss2"], Bc["ss3"], Bc["yb"]
                qt = qkv[n % 2]
                qk_ = ("qkv", n % 2)
                if first:
                    tr.dma("sp", qt[:], QKVT[:, n * NT:(n + 1) * NT].rearrange("(c p) n -> p c n", p=128),
                           w=[qk_], sem=qk_)
                    tr.dma("sp", bat[:], BAt[n * NT:(n + 1) * NT, :].rearrange("(c p) n -> p c n", p=64),
                           w=[("bat", "t", n % 2)], sem=("bat", "t", n % 2))
                    bsl = bat[:, :, dr * 8:dr * 8 + 8]
                    asl = bat[:, :, 16 + dr * 8:24 + dr * 8]
                    tr.op("act", lambda: nc.scalar.activation(out=beta[:], in_=bsl, func=AF.Sigmoid),
                          r=[("bat", "t", n % 2)], w=[("beta", "t", n % 2)])
                    tr.op("act", lambda: nc.scalar.activation(out=lnb[:], in_=bsl, func=AF.Exp, scale=-1.0),
                          r=[("bat", "t", n % 2)], w=[("lnb", "t", n % 2)])
                    tr.op("act", lambda: nc.scalar.activation(out=lnb[:], in_=lnb[:], func=AF.Ln, bias=1.0, scale=1.0),
                          r=[("lnb", "t", n % 2)], w=[("lnb", "t", n % 2)])
                    for c2 in range(nck):
                        tr.op("dve", lambda c2=c2: nc.vector.tensor_tensor(
                            out=gg[:, c2, :], in0=bat[:, c2, 16 + dr * 8:24 + dr * 8],
                            in1=tp[0:64, 16 + dr * 8:24 + dr * 8], op=ALU.add), r=[("bat", "t", n % 2), "tp"], w=[("gg", "t", n % 2)])
                    tr.op("act", lambda: nc.scalar.activation(out=gg[:], in_=gg[:], func=AF.Exp), r=[("gg", "t", n % 2)], w=[("gg", "t", n % 2)])
                    tr.op("act", lambda: nc.scalar.activation(out=gg[:], in_=gg[:], func=AF.Ln, bias=1.0, scale=1.0),
                          r=[("gg", "t", n % 2)], w=[("gg", "t", n % 2)])
                    for c2 in range(nck):
                        tr.op("dve", lambda c2=c2: nc.vector.tensor_tensor(
                            out=gg[:, c2, :], in0=gg[:, c2, :], in1=negA[0:64, dr * 8:dr * 8 + 8], op=ALU.mult),
                            r=[("gg", "t", n % 2), "negA"], w=[("gg", "t", n % 2)])
                tok = slice(c * C, (c + 1) * C)
                gtok = n * NT + c * C
                yield
                bi, pg = nps()
                tr.op("pe", lambda: nc.tensor.matmul(pg[0:64, 0:8], LTRI[dr], gg[:, c, :], start=True,
                                                     stop=True), r=[("gg", "t", n % 2), "cs"], w=[("ps", bi)], inc=False)
                tr.op("pe", lambda: nc.tensor.matmul(pg[:, 8:16], cs[0:64, 0:128], gg[:, c, :], start=True,
                                                     stop=True), r=[("gg", "t", n % 2), "cs"], w=[("ps", bi)])
                tr.op("dve", lambda: nc.vector.tensor_copy(out=sm[0:64, 0, :], in_=pg[0:64, 0:8]),
                      r=[("ps", bi)], w=[("sm", sl)])
                tr.op("dve", lambda: nc.vector.tensor_tensor(out=sm[0:64, 1, :], in0=pg[0:64, 0:8],
                                                             in1=lnb[:, c, :], op=ALU.subtract),
                      r=[("ps", bi), ("lnb", "t", n % 2)], w=[("sm", sl)])
                tr.op("act", lambda: nc.scalar.activation(out=sm[0:64, 2, :], in_=pg[0:64, 0:8], func=AF.Exp),
                      r=[("ps", bi)], w=[("sm", sl)])
                tr.op("dve", lambda: nc.vector.tensor_tensor(out=sm[0:64, 3, :], in0=pg[0:64, 8:16],
                                                             in1=sm[0:64, 0, :], op=ALU.subtract),
                      r=[("ps", bi), ("sm", sl)], w=[("sm", sl)])
                tr.op("act", lambda: nc.scalar.activation(out=sm[0:64, 3, :], in_=sm[0:64, 3, :], func=AF.Exp),
                      r=[("sm", sl)], w=[("sm", sl)])
                tr.op("dve", lambda: nc.vector.tensor_tensor(out=sm[0:64, 4, :], in0=sm[0:64, 2, :],
                                                             in1=beta[:, c, :], op=ALU.mult),
                      r=[("sm", sl), ("beta", "t", n % 2)], w=[("sm", sl)])
                tr.op("act", lambda: nc.scalar.activation(out=sm[:, 5, :], in_=pg[:, 8:16], func=AF.Exp),
                      r=[("ps", bi)], w=[("sm", sl)])
                yield
                bi2, pt2 = nps()
                tr.op("pe", lambda: nc.tensor.matmul(pt2[0:8, 0:64], sm[0:64, 0, :], ID64, start=True,
                                                     stop=True), r=[("sm", sl), "cs"], w=[("ps", bi2)], inc=False)
                tr.op("pe", lambda: nc.tensor.matmul(pt2[0:8, 64:128], sm[0:64, 1, :], ID64, start=True,
                                                     stop=True), r=[("sm", sl), "cs"], w=[("ps", bi2)])
                tr.op("dve", lambda: nc.vector.tensor_scalar(out=gT[:, 0, :], in0=pt2[0:8, 0:64], scalar1=-1.0,
                                                             scalar2=None, op0=ALU.mult),
                      r=[("ps", bi2)], w=[("gT", sl)])
                tr.op("dve", lambda: nc.vector.tensor_copy(out=gT[:, 1:3, :].rearrange("p a b -> p (a b)"),
                                                           in_=pt2[0:8, 0:128]),
                      r=[("ps", bi2)], w=[("gT", sl)])
                E8v = E8.rearrange("p (h i) -> p h i", h=8)
                tr.op("dve", lambda: nc.vector.tensor_tensor(
                    out=Rg[:, 0, :].rearrange("p (h i) -> p h i", h=8), in0=E8v,
                    in1=gT[:, 2:3, :].to_broadcast([8, 8, 64]), op=ALU.mult), r=[("gT", sl), "cs"], w=[("Rg", sl)])
                tr.op("dve", lambda: nc.vector.tensor_tensor(
                    out=Rg[:, 1, :].rearrange("p (h i) -> p h i", h=8), in0=E8v,
                    in1=gT[:, 1:2, :].to_broadcast([8, 8, 64]), op=ALU.mult), r=[("gT", sl), "cs"], w=[("Rg", sl)])
                yield
                bA, pA = nps()
                bQ, pQ = nps()
                bR, pR = nps()
                tr.op("pe", lambda: nc.tensor.matmul(pA[0:64, :], cs[0:8, 0:64], Rg[:, 0, :], start=True,
                                                     stop=False), r=[("Rg", sl), "cs"], w=[("ps", bA)], inc=False)
                tr.op("pe", lambda: nc.tensor.matmul(pA[0:64, :], gT[:, 0, :], E8, start=False, stop=True),
                      r=[("gT", sl), "cs"], w=[("ps", bA)])
                tr.op("pe", lambda: nc.tensor.matmul(pQ[0:64, :], cs[0:8, 0:64], Rg[:, 1, :], start=True,
                                                     stop=False), r=[("Rg", sl), "cs"], w=[("ps", bQ)], inc=False)
                tr.op("pe", lambda: nc.tensor.matmul(pQ[0:64, :], gT[:, 0, :], E8, start=False, stop=True),
                      r=[("gT", sl), "cs"], w=[("ps", bQ)])
                tr.op("pe", lambda: nc.tensor.matmul(pR[:, :], cs[0:8, 0:128], Rg[:, 1, :], start=True,
                                                     stop=True), r=[("Rg", sl), "cs"], w=[("ps", bR)])
                tr.op("dve", lambda: nc.vector.tensor_tensor(out=dec[0][:], in0=pA[0:64, :], in1=CAPS[dr],
                                                             op=ALU.min), r=[("ps", bA), "cs"], w=[("dec", 0, sl)])
                tr.op("act", lambda: nc.scalar.activation(out=dec[0][:], in_=dec[0][:], func=AF.Exp),
                      r=[("dec", 0, sl)], w=[("dec", 0, sl)])
                tr.op("dve", lambda: nc.vector.tensor_tensor(out=dec[1][:], in0=pQ[0:64, :], in1=CAPI[dr],
                                                             op=ALU.min), r=[("ps", bQ), "cs"], w=[("dec", 1, sl)])
                tr.op("act", lambda: nc.scalar.activation(out=dec[1][:], in_=dec[1][:], func=AF.Exp),
                      r=[("dec", 1, sl)], w=[("dec", 1, sl)])
                tr.op("act", lambda: nc.scalar.activation(out=egr[:], in_=pR[:], func=AF.Exp),
                      r=[("ps", bR)], w=[("egr", sl)])
                tr.op("dve", lambda: nc.vector.tensor_tensor(
                    out=qg[:], in0=qt[:, 0:8, tok], in1=egr[:].rearrange("p (h i) -> p h i", h=8),
                    op=ALU.mult), r=[qk_, ("egr", sl)], w=[("qg", sl)])
                yield
                bK, pK = nps()
                bQK, pQK = nps()
                for h in range(H):
                    tr.op("pe", lambda h=h: nc.tensor.matmul(pK[0:64, h * 64:(h + 1) * 64], qt[:, 8 + h, tok],
                                                             qt[:, 8 + h, tok], start=True, stop=True),
                          r=[qk_], w=[("ps", bK)], inc=(h == H - 1))
                for h in range(H):
                    tr.op("pe", lambda h=h: nc.tensor.matmul(pQK[0:64, h * 64:(h + 1) * 64], qt[:, 8 + h, tok],
                                                             qt[:, h, tok], start=True, stop=True),
                          r=[qk_], w=[("ps", bQK)], inc=(h == H - 1))
                X0, Y0 = XY0[:, 0, :], XY0[:, 1, :]
                tr.op("dve", lambda: nc.vector.tensor_tensor(out=X0, in0=pK[0:64, :], in1=dec[0][:],
                                                             op=ALU.mult),
                      r=[("ps", bK), ("dec", 0, sl)], w=[("X0", sl)])
                tr.op("dve", lambda: nc.vector.tensor_tensor(out=attn[:], in0=pQK[0:64, :], in1=dec[1][:],
                                                             op=ALU.mult),
                      r=[("ps", bQK), ("dec", 1, sl)], w=[("attn", sl)])
                yield
                bY, pY = nps()
                for h in range(H):
                    tr.op("pe", lambda h=h: nc.tensor.matmul(pY[0:64, h * 64:(h + 1) * 64],
                                                             X0[:, h * 64:(h + 1) * 64], ID64B, start=True,
                                                             stop=True),
                          r=[("X0", sl), "csb"], w=[("ps", bY)], inc=(h == H - 1))
                tr.op("act", lambda: nc.scalar.copy(out=Y0, in_=pY[0:64, :]), r=[("ps", bY)], w=[("Y0", sl)])
                Xa0, Ya0 = XY[0][:, 0, :], XY[0][:, 1, :]
                Pm, Qm = PQ[0][:, 0, :], PQ[0][:, 1, :]
                tr.op("pool", lambda: nc.gpsimd.tensor_tensor(out=Xa0, in0=X0, in1=MK[:, 0, :], op=ALU.mult),
                      r=[("X0", sl), "MK"], w=[("X", 0, sl)])
                tr.op("pool", lambda: nc.gpsimd.tensor_tensor(out=Ya0, in0=Y0, in1=MK[:, 0, :], op=ALU.mult),
                      r=[("Y0", sl), "MK"], w=[("Y", 0, sl)])
                tr.op("pool", lambda: nc.gpsimd.tensor_tensor(out=Pm, in0=I8, in1=Xa0, op=ALU.subtract),
                      r=[("X", 0, sl), "cs"], w=[("P", 0, sl)])
                tr.op("pool", lambda: nc.gpsimd.tensor_tensor(out=Qm, in0=I8, in1=Ya0, op=ALU.subtract),
                      r=[("Y", 0, sl), "cs"], w=[("Q", 0, sl)])

                def mm8(pt_, lhs, rhs, rk, bix):
                    for h in range(H):
                        hs = slice(h * 64, (h + 1) * 64)
                        tr.op("pe", lambda hs=hs: nc.tensor.matmul(pt_[0:64, hs], lhs[:, hs], rhs[:, hs],
                                                                   start=True, stop=True),
                              r=rk, w=[("ps", bix)], inc=(h == H - 1))

                for lv in range(2):
                    a, b = lv % 2, (lv + 1) % 2
                    Xa, Ya = XY[a][:, 0, :], XY[a][:, 1, :]
                    Xb, Yb = XY[b][:, 0, :], XY[b][:, 1, :]
                    Pa, Qa = PQ[a][:, 0, :], PQ[a][:, 1, :]
                    Pb, Qb = PQ[b][:, 0, :], PQ[b][:, 1, :]
                    yield
                    b1, p1 = nps()
                    mm8(p1, Ya, Xa, [("X", a, sl), ("Y", a, sl)], b1)
                    tr.op("act", lambda: nc.scalar.copy(out=Xb, in_=p1[0:64, :]), r=[("ps", b1)], w=[("X", b, sl)])
                    yield
                    b2, p2 = nps()
                    mm8(p2, Xa, Ya, [("X", a, sl), ("Y", a, sl)], b2)
                    tr.op("act", lambda: nc.scalar.copy(out=Yb, in_=p2[0:64, :]), r=[("ps", b2)], w=[("Y", b, sl)])
                    yield
                    b3, p3 = nps()
                    mm8(p3, Qa, Xb, [("Q", a, sl), ("X", b, sl)], b3)
                    tr.op("dve", lambda: nc.vector.tensor_tensor(out=Pb, in0=p3[0:64, :], in1=Pa, op=ALU.add),
                          r=[("ps", b3), ("P", a, sl)], w=[("P", b, sl)])
                    yield
                    b4, p4 = nps()
                    mm8(p4, Pa, Yb, [("P", a, sl), ("Y", b, sl)], b4)
                    tr.op("dve", lambda: nc.vector.tensor_tensor(out=Qb, in0=p4[0:64, :], in1=Qa, op=ALU.add),
                          r=[("ps", b4), ("Q", a, sl)], w=[("Q", b, sl)])
                cur = 0
                for li in range(3):
                    nxt = 1 - cur
                    Pc, Qc = PQ[cur][:, 0, :], PQ[cur][:, 1, :]
                    Pn, Qn = PQ[nxt][:, 0, :], PQ[nxt][:, 1, :]
                    Xm, Ym = XY[0][:, 0, :], XY[0][:, 1, :]
                    W1, W2 = XY[1][:, 0, :], XY[1][:, 1, :]
                    tr.op("pool", lambda: nc.gpsimd.tensor_tensor(out=Ym, in0=Y0, in1=MK[:, 1 + li, :],
                                                                  op=ALU.mult),
                          r=[("Y0", sl), "MK"], w=[("Y", 0, sl)])
                    yield
                    b1, p1 = nps()
                    mm8(p1, Ym, Pc, [("Y", 0, sl), ("P", cur, sl)], b1)
                    tr.op("act", lambda: nc.scalar.copy(out=W1, in_=p1[0:64, :]), r=[("ps", b1)], w=[("X", 1, sl)])
                    yield
                    b2, p2 = nps()
                    mm8(p2, Qc, W1, [("Q", cur, sl), ("X", 1, sl)], b2)
                    tr.op("dve", lambda: nc.vector.tensor_tensor(out=Pn, in0=Pc, in1=p2[0:64, :],
                                                                 op=ALU.subtract),
                          r=[("ps", b2), ("P", cur, sl)], w=[("P", nxt, sl)])
                    if li < 2:
                        tr.op("pool", lambda: nc.gpsimd.tensor_tensor(out=Xm, in0=X0, in1=MK[:, 1 + li, :],
                                                                      op=ALU.mult),
                              r=[("X0", sl), "MK"], w=[("X", 0, sl)])
                        yield
                        b3, p3 = nps()
                        mm8(p3, Xm, Qc, [("X", 0, sl), ("Q", cur, sl)], b3)
                        tr.op("act", lambda: nc.scalar.copy(out=W2, in_=p3[0:64, :]), r=[("ps", b3)],
                              w=[("Y", 1, sl)])
                        yield
                        b4, p4 = nps()
                        mm8(p4, Pc, W2, [("P", cur, sl), ("Y", 1, sl)], b4)
                        tr.op("dve", lambda: nc.vector.tensor_tensor(out=Qn, in0=Qc, in1=p4[0:64, :],
                                                                     op=ALU.subtract),
                              r=[("ps", b4), ("Q", cur, sl)], w=[("Q", nxt, sl)])
                    cur = nxt
                TT = PQ[cur][:, 0, :]
                TK = ("P", cur, sl)
                for half in range(2):
                    yield
                    bk, pk_ = nps()
                    bv, pv_ = nps()
                    for hh in range(4):
                        h = half * 4 + hh
                        tr.op("pe", lambda h=h, hh=hh: nc.tensor.matmul(
                            pk_[0:64, hh * 128:(hh + 1) * 128], qt[:, 8 + h, tok], ID128B, start=True,
                            stop=True), r=[qk_, "csb"], w=[("ps", bk)], inc=(hh == 3))
                    for hh in range(4):
                        h = half * 4 + hh
                        tr.op("pe", lambda h=h, hh=hh: nc.tensor.matmul(
                            pv_[0:64, hh * 128:(hh + 1) * 128], qt[:, 16 + h, tok], ID128B, start=True,
                            stop=True), r=[qk_, "csb"], w=[("ps", bv)], inc=(hh == 3))
                    hsl = slice(half * 4, half * 4 + 4)
                    pk3 = pk_[0:64, :].rearrange("p (h d) -> p h d", h=4)
                    pv3 = pv_[0:64, :].rearrange("p (h d) -> p h d", h=4)
                    tr.op("dve", lambda pk3=pk3, hsl=hsl: nc.vector.tensor_tensor(
                        out=kbg[:, hsl, :], in0=pk3,
                        in1=sm[0:64, 4, hsl].unsqueeze(2).to_broadcast([64, 4, 128]), op=ALU.mult),
                        r=[("ps", bk), ("sm", sl)], w=[("kbg", sl)])
                    tr.op("dve", lambda pk3=pk3, hsl=hsl: nc.vector.tensor_tensor(
                        out=kdec[:, hsl, :], in0=pk3,
                        in1=sm[0:64, 3, hsl].unsqueeze(2).to_broadcast([64, 4, 128]), op=ALU.mult),
                        r=[("ps", bk), ("sm", sl)], w=[("kdec", sl)])
                    tr.op("dve", lambda pv3=pv3, hsl=hsl: nc.vector.tensor_tensor(
                        out=vb[:, hsl, :], in0=pv3,
                        in1=beta[:, c, hsl].unsqueeze(2).to_broadcast([64, 4, 128]), op=ALU.mult),
                        r=[("ps", bv), ("beta", "t", n % 2)], w=[("vb", sl)])
                for half in range(2):
                    yield
                    bu, pu = nps()
                    for hh in range(4):
                        h = half * 4 + hh
                        tr.op("pe", lambda h=h, hh=hh: nc.tensor.matmul(
                            pu[0:64, hh * 128:(hh + 1) * 128], TT[:, h * 64:(h + 1) * 64], vb[:, h, :],
                            start=True, stop=True), r=[TK, ("vb", sl)], w=[("ps", bu)], inc=(hh == 3))
                    tr.op("act", lambda half=half, pu=pu: nc.scalar.copy(
                        out=usb[:, half * 4:half * 4 + 4, :].rearrange("p h d -> p (h d)"), in_=pu[0:64, :]),
                        r=[("ps", bu)], w=[("usb", sl)])
                yield
                bw, pw = nps()
                for h in range(H):
                    tr.op("pe", lambda h=h: nc.tensor.matmul(pw[:, h * 64:(h + 1) * 64], kbg[:, h, :],
                                                             TT[:, h * 64:(h + 1) * 64], start=True, stop=True),
                          r=[TK, ("kbg", sl)], w=[("ps", bw)], inc=(h == H - 1))
                tr.op("act", lambda: nc.scalar.copy(out=wT[:].rearrange("p h i -> p (h i)"), in_=pw[:, :]),
                      r=[("ps", bw)], w=[("wT", sl)])
                for half in range(2):
                    bws, pws = nps()
                    for hh in range(4):
                        h = half * 4 + hh
                        tr.op("pe", lambda h=h, hh=hh: nc.tensor.matmul(
                            pws[0:64, hh * 128:(hh + 1) * 128], wT[:, h, :], Sb[:, h, :], start=True,
                            stop=True), r=[("wT", sl), "Sb"], w=[("ps", bws)], inc=(hh == 3))
                    tr.op("dve", lambda half=half, pws=pws: nc.vector.tensor_tensor(
                        out=vnew[:, half * 4:half * 4 + 4, :].rearrange("p h d -> p (h d)"),
                        in0=usb[:, half * 4:half * 4 + 4, :].rearrange("p h d -> p (h d)"), in1=pws[0:64, :],
                        op=ALU.subtract), r=[("ps", bws), ("usb", sl)], w=[("vnew", half, sl)])
                for half in range(2):
                    bo, po = nps()
                    for hh in range(4):
                        h = half * 4 + hh
                        tr.op("pe", lambda h=h, hh=hh: nc.tensor.matmul(
                            po[0:64, hh * 128:(hh + 1) * 128], qg[:, h, :], Sb[:, h, :], start=True,
                            stop=False), r=[("qg", sl), "Sb"], w=[("ps", bo)], inc=False)
                        tr.op("pe", lambda h=h, hh=hh: nc.tensor.matmul(
                            po[0:64, hh * 128:(hh + 1) * 128], attn[:, h * 64:(h + 1) * 64], vnew[:, h, :],
                            start=False, stop=True), r=[("attn", sl), ("vnew", half, sl)], w=[("ps", bo)],
                            inc=(hh == 3))
                    osl = osb[:, half * 4:half * 4 + 4, :].rearrange("p h d -> p (h d)")
                    if dr == 0:
                        tr.op("act", lambda osl=osl, po=po: nc.scalar.copy(out=osl, in_=po[0:64, :]),
                              r=[("ps", bo)], w=[("osb", half, sl)])
                    else:
                        if half == 0:
                            tr.dma("sp", ofw[:].rearrange("p h d -> p (h d)"), OF[gtok:gtok + C, :],
                                   r=[("OF", gtok)], w=[("ofw", sl)], sem=("ofw", sl))
                            tr.dma("sp", zsb[:].rearrange("p h d -> p (h d)"), ZS[gtok:gtok + C, :],
                                   w=[("zsb", sl)], sem=("zsb", sl))
                        tr.op("dve", lambda osl=osl, po=po, half=half: nc.vector.tensor_tensor(
                            out=osl, in0=po[0:64, :],
                            in1=ofw[:, half * 4:half * 4 + 4, :].rearrange("p h d -> p (h d)"), op=ALU.add),
                            r=[("ps", bo), ("ofw", sl)], w=[("osb", half, sl)])
                for half in range(2):
                    bs, pS = nps()
                    for hh in range(4):
                        h = half * 4 + hh
                        tr.op("pe", lambda h=h, hh=hh: nc.tensor.matmul(
                            pS[:, hh * 128:(hh + 1) * 128], kdec[:, h, :], vnew[:, h, :], start=True,
                            stop=True), r=[("kdec", sl), ("vnew", half, sl)], w=[("ps", bs)], inc=(hh == 3))
                    hsl = slice(half * 4, half * 4 + 4)
                    tr.op("pool", lambda hsl=hsl: nc.gpsimd.tensor_tensor(
                        out=S[:, hsl, :], in0=S[:, hsl, :],
                        in1=sm[:, 5, hsl].unsqueeze(2).to_broadcast([128, 4, 128]), op=ALU.mult),
                        r=[("sm", sl), "S"], w=["S"])
                    tr.op("dve", lambda hsl=hsl, pS=pS: nc.vector.tensor_tensor(
                        out=S[:, hsl, :].rearrange("p h d -> p (h d)"),
                        in0=S[:, hsl, :].rearrange("p h d -> p (h d)"), in1=pS[:, :], op=ALU.add),
                        r=[("ps", bs), "S"], w=["S"])
                tr.op("act", lambda: nc.scalar.copy(out=Sb[:], in_=S[:]), r=["S"], w=["Sb"])
                if dr == 0:
                    tr.dma("sp", OF[gtok:gtok + C, :], osb[:].rearrange("p h d -> p (h d)"),
                           r=[("osb", 0, sl), ("osb", 1, sl)], w=[("OF", gtok)], sem=("osb", sl))
                else:
                    for h in range(H):
                        hf = h // 4
                        tr.op("pool", lambda h=h: nc.gpsimd.tensor_tensor(
                            out=ysq[:, h, :], in0=osb[:, h, :], in1=osb[:, h, :], op=ALU.mult),
                            r=[("osb", hf, sl)], w=[("ysq", h, sl)])
                        tr.op("dve", lambda h=h: nc.vector.tensor_reduce(
                            out=ss[:, 8 + h:9 + h], in_=ysq[:, h, :], axis=AX.X, op=ALU.add),
                            r=[("ysq", h, sl)], w=[("ss", sl)])
                    tr.op("act", lambda: nc.scalar.activation(out=ss2[:, 8:16], in_=ss[:, 8:16], func=AF.Sqrt,
                                                              bias=epsd[0:64, 0:1], scale=1.0 / 128),
                          r=[("ss", sl), "epsd"], w=[("ss2", sl)])
                    tr.op("dve", lambda: nc.vector.reciprocal(out=ss3[:, 8:16], in_=ss2[:, 8:16]),
                          r=[("ss2", sl)], w=[("ss3", sl)])
                    for h in range(H):
                        hf = h // 4
                        tr.op("dve", lambda h=h: nc.vector.scalar_tensor_tensor(
                            out=ysq[:, h, :], in0=osb[:, h, :], scalar=ss3[:, 8 + h:9 + h], in1=tp[0:64, 32:160],
                            op0=ALU.mult, op1=ALU.mult), r=[("osb", hf, sl), ("ss3", sl), "tp"], w=[("ysq", h, sl)],
                            strict=True)
                        tr.op("dve", lambda h=h: nc.vector.tensor_tensor(
                            out=yb[:, h, :], in0=ysq[:, h, :], in1=zsb[:, h, :], op=ALU.mult),
                            r=[("ysq", h, sl), ("zsb", sl)], w=[("yb", hf, sl)])
                    yield
                    by, py = nps()
                    for h in range(H):
                        tr.op("pe", lambda h=h: nc.tensor.matmul(py[:, h * 64:(h + 1) * 64], yb[:, h, :],
                                                                 ID64B, start=True, stop=True),
                              r=[("yb", 0, sl), ("yb", 1, sl), "csb"], w=[("ps", by)], inc=(h == H - 1))
                    tr.op("act", lambda: nc.scalar.copy(out=ybT[:, :, tok],
                                                        in_=py[:, :].rearrange("p (h i) -> p h i", h=8)),
                          r=[("ps", by)], w=["ybT"])
                if last and dr == 1:
                    tr.dma("sp", YBT[:, n * NT:(n + 1) * NT].rearrange("(h p) n -> p h n", p=128), ybT[:],
                           r=["ybT"], sem="ybT")

            for dr in range(2):
                tiles = list(range(ntile)) if dr == 0 else list(range(ntile - 1, -1, -1))
                facs = []
                for n in tiles:
                    chunks = list(range(nck)) if dr == 0 else list(range(nck - 1, -1, -1))
                    for ci, c in enumerate(chunks):
                        facs.append(lambda sl, dr=dr, n=n, c=c, ci=ci: chunk_gen(dr, n, c, ci == 0, ci == nck - 1, sl))
                run_streams(facs, WG, stagger=14)
                if dr == 0:
                    tr.dma("sp", sx_in.ap(), S[:].rearrange("p h d -> p (h d)"), r=["S"], w=["sx_in"], sem="sx")
                    tr.cc(lambda: nc.gpsimd.collective_compute("AllGather", ALU.bypass, replica_groups=PAIRS,
                                                                ins=[sx_in.ap()], outs=[sx_out.ap()]),
                          r=["sx_in"], w=["sx_out"], sem="sxcc")
                    tr.dma("sp", SX, sx_out.ap().rearrange("(r p) n -> p r n", p=128), r=["sx_out"], w=["SX", "ybT"],
                           sem="sx")
                    Sf = S[:].rearrange("p h d -> p (h d)")
                    tr.op("dve", lambda: nc.vector.tensor_scalar(out=Sf, in0=SX[:, 0, :], scalar1=selt[:, 0:1],
                                                                 scalar2=None, op0=ALU.mult),
                          r=["SX", "selt"], w=["S"])
                    tr.op("dve", lambda: nc.vector.scalar_tensor_tensor(out=Sf, in0=SX[:, 1, :], scalar=selt[:, 1:2],
                                                                        in1=Sf, op0=ALU.mult, op1=ALU.add),
                          r=["SX", "selt", "S"], w=["S"])
                    tr.op("act", lambda: nc.scalar.copy(out=Sb[:], in_=S[:]), r=["S"], w=["Sb"])
            tr.barrier()

        with ExitStack() as st5:
            ring_phase(st5, [b for _ in range(ntile) for b in range(OFF_UP2, OFF_UP2 + NB_UP)])
            P = ffn_pool(st5, wdn2, nxt=1)
            wb = sb("wb", [128, 8, D], BF16, st5)
            wo = sb("wo", [128, 8, D], BF16, st5)
            ybt = P["u"]
            mrg = P["hid"]
            gsb = [sb("gsb%d" % i, [128, NT], F32, st5) for i in range(2)]
            pab = [sb("pab%d" % i, [128, NT], F32, st5) for i in range(2)]
            tr.dma("pool", wb[:], wbout.rearrange("(k p) n -> p k n", p=128), w=["wb"], sem="wb")
            tr.dma("pool", wo[:], wout.rearrange("(k p) n -> p k n", p=128), w=["wo"], sem="wo")
            for n in range(ntile):
                xt = P["xt"][0]
                xkey = "xt0"
                c0 = n * NT
                load_tile(xt, H1T, c0, xkey)
                tr.dma("sp", ybt[:], YBT[:, c0:c0 + NT].rearrange("(h p) n -> p h n", p=128),
                       w=[("u", k) for k in range(8)], sem="ybt")
                for m in range(8):
                    g_ = gsb[m % 2]
                    gk = ("gsb", m % 2)
                    pa_ = pab[m % 2]
                    pk2 = ("pab", m % 2)
                    tr.dma("sp", g_[:], PT[(44 + m) * 128:(45 + m) * 128, c0 + 2:c0 + 2 + NT], w=[gk], sem=gk)
                    tr.dma("sp", pa_[:], PAT[m * 128:(m + 1) * 128, c0:c0 + NT], w=[pk2], sem=pk2)
                    tr.op("act", lambda g_=g_: nc.scalar.activation(out=g_[:], in_=g_[:], func=AF.Sigmoid),
                          r=[gk], w=[gk])
                    bi, pt = nps()
                    for kk in range(8):
                        tr.op("pe", lambda kk=kk, m=m, pt=pt: nc.tensor.matmul(
                            pt[:], wb[:, kk, m * 128:(m + 1) * 128], ybt[:, kk, :], start=(kk == 0), stop=(kk == 7)),
                            r=["wb", ("u", kk)], w=[("ps", bi)], inc=(kk == 7))
                    tr.op("dve", lambda g_=g_, pt=pt: nc.vector.tensor_tensor(out=g_[:], in0=pt[:], in1=g_[:],
                                                                              op=ALU.mult),
                          r=[("ps", bi), gk], w=[gk])
                    tr.op("pool", lambda g_=g_, pa_=pa_, m=m: nc.gpsimd.tensor_tensor(out=mrg[:, m, :], in0=g_[:],
                                                                                      in1=pa_[:], op=ALU.add),
                          r=[gk, pk2], w=[("hid", m)])
                for m in range(8):
                    bi, pt = nps()
                    for kk in range(8):
                        tr.op("pe", lambda kk=kk, m=m, pt=pt: nc.tensor.matmul(
                            pt[:], wo[:, kk, m * 128:(m + 1) * 128], mrg[:, kk, :], start=(kk == 0), stop=(kk == 7)),
                            r=["wo", ("hid", kk)], w=[("ps", bi)], inc=(kk == 7))
                    tr.op("dve", lambda m=m, pt=pt: nc.vector.scalar_tensor_tensor(
                        out=xt[:, m, :], in0=pt[:], scalar=der[:, 32 + m:33 + m], in1=xt[:, m, :], op0=ALU.mult,
                        op1=ALU.add), r=[("ps", bi), "der"], w=[(xkey, m)])
                ffn_body(P, xt, xkey, 2, P["wdn"], True, outT, c0)
            tr.barrier()
        tr.finish()
    return nc


def _blocks(w, cols_list):
    out = np.empty((len(cols_list), 128, 2048), np.float32)
    w3 = w.reshape(8, 128, -1)
    for b, cols in enumerate(cols_list):
        out[b] = w3[:, :, cols].transpose(1, 0, 2).reshape(128, 2048)
    return out


def _consts():
    c = np.zeros((128, 4096), np.float32)
    c[:, 0:128] = 1.0
    c[0:64, 128:192] = np.eye(64)
    j = np.arange(64)[:, None]
    i = np.arange(64)[None, :]
    c[0:64, 192:256] = (j <= i)
    c[0:64, 256:320] = (j >= i)
    capS_F = np.where(j < i, 0.0, NEG)
    capI_F = np.where(j <= i, 0.0, NEG)
    capS_B = np.where(j > i, 0.0, NEG)
    capI_B = np.where(j >= i, 0.0, NEG)
    c[0:64, 320:832] = np.tile(capS_F, (1, 8))
    c[0:64, 832:1344] = np.tile(capI_F, (1, 8))
    c[0:64, 1344:1856] = np.tile(capS_B, (1, 8))
    c[0:64, 1856:2368] = np.tile(capI_B, (1, 8))
    c[0:64, 2368:2880] = np.tile(np.eye(64), (1, 8))
    e8 = np.zeros((8, 8, 64), np.float32)
    for h in range(8):
        e8[h, h, :] = 1.0
    c[0:8, 2880:3392] = e8.reshape(8, 512)
    c[:, 3392:3520] = np.eye(128)
    return c


_PROG = {}


def _run(inputs, T, n_cores):
    f = lambda a: np.ascontiguousarray(np.asarray(a, dtype=np.float32))
    x = f(inputs["x"])
    B = x.shape[0]
    cvec = f(inputs["c"])
    w_ada = f(inputs["w_ada"])[0]
    b_ada = f(inputs["b_ada"])[0]
    w_in = f(inputs["w_in"])[0]
    wup1 = f(inputs["w_ffn1_up"])[0]
    wup2 = f(inputs["w_ffn2_up"])[0]
    conv_a = f(inputs["conv_a"])[0]
    conv_dn = f(inputs["conv_dn"])[0]
    cst = _consts()
    ii = np.arange(64)[:, None]
    jj = np.arange(64)[None, :]
    mks = [(ii // 8 == jj // 8)]
    for bsz in (8, 16, 32):
        mks.append((ii // (2 * bsz) == jj // (2 * bsz)) & (ii // bsz != jj // bsz))
    cst2 = np.concatenate([np.tile(m.astype(np.float32), (1, 8)) for m in mks], axis=1)

    ada_cols = [np.arange(b * 256, (b + 1) * 256) for b in range(NB_ADA)]
    up_cols = [np.concatenate([np.arange(j * 128, (j + 1) * 128), np.arange(DFF + j * 128, DFF + (j + 1) * 128)])
               for j in range(NJ)]
    fm = np.concatenate([np.arange(512, 1024), np.arange(1024, 1536), np.arange(1536, 4608),
                         np.arange(0, 512), np.arange(5664, 7712)])
    in_cols = [fm[b * 256:(b + 1) * 256] for b in range(NB_IN)]
    z_cols = [np.arange(4608 + b * 256, 4608 + (b + 1) * 256) for b in range(NB_Z)]
    wr_common = np.concatenate([_blocks(w_ada, ada_cols), _blocks(wup1, up_cols), _blocks(w_in, in_cols + z_cols),
                                _blocks(wup2, up_cols)], axis=0)
    assert wr_common.shape[0] == NB_TOT

    ba_f = np.arange(5632, 5664)
    ba_b = np.concatenate([np.arange(5640, 5648), np.arange(5632, 5640), np.arange(5656, 5664),
                           np.arange(5648, 5656)])
    norms = np.stack([f(inputs["norm_ffn1"])[0], f(inputs["norm_mix"])[0], f(inputs["norm_ffn2"])[0],
                      f(inputs["norm_final"])], 0)
    in_maps = []
    for core in range(n_cores):
        b, half = core // 2, core % 2
        xs = x[b, half * T:(half + 1) * T, :]
        if half == 1:
            xs = xs[::-1]
        vecs = np.zeros((128, 120), np.float32)
        vecs[:, 0:8] = cvec[b].reshape(8, 128).T
        vecs[:, 8:80] = b_ada.reshape(72, 128).T
        vecs[:, 80:112] = norms.reshape(4, 8, 128).transpose(2, 0, 1).reshape(128, 32)
        cdn = conv_dn if half == 0 else conv_dn[::-1]
        ca = conv_a if half == 0 else conv_a[::-1]
        convw = np.zeros((128, 132), np.float32)
        convw[:, 0:120] = cdn.T.reshape(24, 128, 5).transpose(1, 0, 2).reshape(128, 120)
        convw[:, 120:132] = ca.T.reshape(4, 128, 3).transpose(1, 0, 2).reshape(128, 12)
        names = ["a_log_fwd", "a_log_bwd", "dt_bias_fwd", "dt_bias_bwd"]
        if half == 1:
            names = ["a_log_bwd", "a_log_fwd", "dt_bias_bwd", "dt_bias_fwd"]
        tokp = np.zeros((128, 160), np.float32)
        for q_, nm in enumerate(names):
            tokp[:, q_ * 8:(q_ + 1) * 8] = f(inputs[nm])[0][None, :]
        tokp[:, 32:160] = f(inputs["dn_norm"])[0][None, :]
        selv = np.zeros((128, 2), np.float32)
        selv[:, 1 - half] = 1.0
        in_maps.append({
            "xT": np.ascontiguousarray(xs.T),
            "wr": wr_common,
            "wdn1": f(inputs["w_ffn1_down"])[0],
            "wdn2": f(inputs["w_ffn2_down"])[0],
            "wba": np.ascontiguousarray(w_in[:, ba_f if half == 0 else ba_b]),
            "waout": f(inputs["w_a_out"])[0],
            "wbout": f(inputs["w_b_out"])[0],
            "wout": f(inputs["w_out"])[0],
            "vecs": vecs, "convw": convw, "tokp": tokp, "cst": cst, "cst2": cst2, "sel": selv,
        })
    if T not in _PROG:
        _PROG[T] = build_program(T)
    res = run_bass_kernel_spmd(_PROG[T], in_maps, core_ids=list(range(n_cores)))
    if DEBUG:
        DBG["res"] = res.results
    out = np.empty((B, 2 * T, D), np.float32)
    for core in range(n_cores):
        b, half = core // 2, core % 2
        o = res.results[core]["outT"].T
        if half == 1:
            o = o[::-1]
        out[b, half * T:(half + 1) * T, :] = o
    return out


def kernel(**inputs):
    x = np.asarray(inputs["x"])
    B, S, _ = x.shape
    return _run(inputs, S // 2, 2 * B)
```

```python
import numpy as np
from contextlib import ExitStack
import concourse.bass as bass
import concourse.mybir as mybir
from concourse.bass_utils import run_bass_kernel_spmd

F32 = mybir.dt.float32
BF16 = mybir.dt.bfloat16
AF = mybir.ActivationFunctionType
ALU = mybir.AluOpType
AX = mybir.AxisListType

D = 1024
DFF = 2816
NJ = DFF // 128
H = 8
C = 64
NT = 512
EPS = 1e-6
NR = 4
NEG = -30000.0
DEBUG = False
SAME_ENGINE_SYNC = True
DBG = {}

NB_ADA = 36
NB_UP = 22
NB_IN = 26
NB_Z = 4
OFF_ADA = 0
OFF_UP1 = OFF_ADA + NB_ADA
OFF_IN = OFF_UP1 + NB_UP
OFF_Z = OFF_IN + NB_IN
OFF_UP2 = OFF_Z + NB_Z
NB_TOT = OFF_UP2 + NB_UP
NPC = 52
NCONV = 32


class Tick:
    __slots__ = ("kind", "key", "val")

    def __init__(self, kind, key, val):
        self.kind, self.key, self.val = kind, key, val


class Tr:
    def __init__(self, nc, es):
        self.nc = nc
        self.es = es
        self.E = {"pe": nc.tensor, "act": nc.scalar, "dve": nc.vector, "pool": nc.gpsimd, "sp": nc.sync}
        self.csem = {e: es.enter_context(nc.semaphore("c_" + e)) for e in ("pe", "act", "dve", "pool")}
        self.ccnt = {e: 0 for e in self.csem}
        self.pend = {e: [] for e in self.csem}
        self.dsem = {}
        self.dcnt = {}
        self.seen = {e: {} for e in self.E}
        self.st = {}

    def _wait(self, eng, t, same_ok=True, waw=False):
        if t is None:
            return
        if t.kind == "c":
            if t.key == eng and (t.val is None or (same_ok and (waw or not SAME_ENGINE_SYNC))):
                return
            assert t.val is not None, "pending tick waited on"
            sem, v, k = self.csem[t.key], t.val, ("c", t.key)
        else:
            sem, v, k = self.dsem[t.key], self.dcnt[t.key], ("d", t.key)
        if self.seen[eng].get(k, 0) >= v:
            return
        self.seen[eng][k] = v
        self.E[eng].wait_ge(sem, v)

    def _deps(self, eng, r, w, same_ok):
        for k in r:
            s = self.st.get(k)
            if s:
                self._wait(eng, s[0], same_ok)
        for k in w:
            s = self.st.get(k)
            if s:
                self._wait(eng, s[0], same_ok, waw=True)
                for t in s[1].values():
                    self._wait(eng, t, same_ok, waw=True)

    def _upd(self, t, r, w):
        for k in r:
            self.st.setdefault(k, [None, {}])[1][(t.kind, t.key)] = t
        for k in w:
            self.st[k] = [t, {}]

    def op(self, eng, fn, r=(), w=(), inc=True, strict=False):
        self._deps(eng, r, w, not strict)
        ins = fn()
        t = Tick("c", eng, None)
        self.pend[eng].append(t)
        if inc:
            self.ccnt[eng] += 1
            ins.then_inc(self.csem[eng], 1)
            for p in self.pend[eng]:
                p.val = self.ccnt[eng]
            self.pend[eng] = []
        self._upd(t, r, w)
        return ins

    def dma(self, q, out, in_, r=(), w=(), sem=None):
        self._deps(q, r, w, False)
        if sem not in self.dsem:
            self.dsem[sem] = self.es.enter_context(self.nc.semaphore("d%d" % len(self.dsem)))
            self.dcnt[sem] = 0
        ins = self.E[q].dma_start(out=out, in_=in_)
        self.dcnt[sem] += 16
        ins.then_inc(self.dsem[sem], 16)
        t = Tick("d", sem, self.dcnt[sem])
        self._upd(t, r, w)
        return ins

    def cc(self, ins_fn, r=(), w=(), sem=None):
        self._deps("pool", r, w, False)
        if sem not in self.dsem:
            self.dsem[sem] = self.es.enter_context(self.nc.semaphore("d%d" % len(self.dsem)))
            self.dcnt[sem] = 0
        ins = ins_fn()
        self.dcnt[sem] += 1
        ins.then_inc(self.dsem[sem], 1)
        t = Tick("d", sem, self.dcnt[sem])
        self._upd(t, r, w)

    def barrier(self):
        for e in self.csem:
            assert not self.pend[e]
        for e in self.E:
            for f in self.csem:
                if f != e and self.ccnt[f] > 0:
                    self._wait(e, Tick("c", f, self.ccnt[f]))
            for k in self.dsem:
                if self.dcnt[k] > 0:
                    self._wait(e, Tick("d", k, self.dcnt[k]))
        self.st = {}

    def finish(self):
        for k in self.dsem:
            if self.dcnt[k] > 0:
                self._wait("sp", Tick("d", k, self.dcnt[k]))


class Ring:
    def __init__(self, tr, nc, slots, wr):
        self.tr, self.nc, self.slots, self.wr = tr, nc, slots, wr
        self.plan = []
        self.issued = 0
        self.pos = 0

    def add(self, blocks):
        self.plan.extend(blocks)

    def _issue(self):
        b = self.plan[self.issued]
        s = self.issued % NR
        self.tr.dma("pool", self.slots[s][:], self.wr[b], w=[("ring", s)], sem=("ring", s))
        self.issued += 1

    def get(self):
        while self.issued < min(len(self.plan), self.pos + NR):
            self._issue()
        s = self.pos % NR
        self.pos += 1
        return s, self.slots[s]


def build_program(T):
    ntile = T // NT
    nchunk = T // C
    nc = bass.Bass("TRN2", target_bir_lowering=False)

    def din(name, shape, dt=F32):
        return nc.dram_tensor(name, list(shape), dt, kind="ExternalInput").ap()

    xT = din("xT", [D, T])
    wr = din("wr", [NB_TOT, 128, 2048])
    wdn1 = din("wdn1", [DFF, D])
    wdn2 = din("wdn2", [DFF, D])
    wba = din("wba", [D, 32])
    waout = din("waout", [512, D])
    wbout = din("wbout", [D, D])
    wout = din("wout", [D, D])
    vecs = din("vecs", [128, 120])
    convw = din("convw", [128, 24 * 5 + 4 * 3])
    tokp = din("tokp", [128, 32 + 128])
    cst = din("cst", [128, 4096])
    cst2 = din("cst2", [64, 2048])
    sel = din("sel", [128, 2])
    outT = nc.dram_tensor("outT", [D, T], F32, kind="ExternalOutput").ap()

    def dscr(name, shape, dt=F32):
        if DEBUG:
            return nc.dram_tensor(name, list(shape), dt, kind="ExternalOutput").ap()
        return nc.dram_tensor(name, list(shape), dt).ap()

    H1T = dscr("H1T", [D, T])
    PT = dscr("PT", [NPC * 128, T + 4])
    ZS = dscr("ZS", [T, D])
    BAt = dscr("BAt", [T, 32])
    QKVT = dscr("QKVT", [3 * D, T], BF16)
    PAT = dscr("PAT", [D, T])
    OF = dscr("OF", [T, D])
    YBT = dscr("YBT", [D, T], BF16)
    hx_in = nc.dram_tensor("hx_in", [NCONV * 128, 2], F32)
    hx_out = nc.dram_tensor("hx_out", [2 * NCONV * 128, 2], F32)
    sx_in = nc.dram_tensor("sx_in", [128, H * 128], F32)
    sx_out = nc.dram_tensor("sx_out", [256, H * 128], F32)
    PAIRS = [[0, 1], [2, 3], [4, 5], [6, 7]]

    with ExitStack() as es:
        tr = Tr(nc, es)

        sbn = [0]

        def sb(name, shape, dt=F32, stack=None):
            sbn[0] += 1
            return (stack or es).enter_context(nc.sbuf_tensor("%s_%d" % (name, sbn[0]), list(shape), dt))

        ps = [es.enter_context(nc.psum_tensor("ps%d" % i, [128, 512], F32)) for i in range(8)]
        psi = [0]

        def nps():
            for _ in range(8):
                i = psi[0] % 8
                psi[0] += 1
                s_ = tr.st.get(("ps", i))
                if s_ is None or s_[0] is None or len(s_[1]) > 0:
                    return i, ps[i]
            raise RuntimeError("all PSUM banks hold unconsumed results")

        ring = Ring(tr, nc, None, wr)

        def ring_phase(stack, blocks):
            assert ring.issued == len(ring.plan) and ring.pos == len(ring.plan)
            ring.slots = [sb("ring", [128, 2048], BF16, stack) for i in range(NR)]
            ring.add(blocks)

        vec = sb("vec", [128, 120])
        cw = sb("cw", [128, 132])
        tp = sb("tp", [128, 160])
        cs = sb("cs", [128, 4096])
        csb = sb("csb", [128, 1024], BF16)
        selt = sb("selt", [128, 2])
        modT = sb("modT", [128, 72])
        der = sb("der", [128, 64])
        cact = sb("cact", [128, 8], BF16)
        negA = sb("negA", [128, 16])
        tr.dma("sp", vec[:], vecs, w=["vec"], sem="c0")
        tr.dma("sp", cw[:], convw, w=["cw"], sem="c0")
        tr.dma("sp", tp[:], tokp, w=["tp"], sem="c0")
        tr.dma("sp", cs[:], cst, w=["cs"], sem="c0")
        tr.dma("sp", selt[:], sel, w=["selt"], sem="c0")
        ONES = cs[:, 0:128]
        ID64 = cs[0:64, 128:192]
        LTRI = [cs[0:64, 192:256], cs[0:64, 256:320]]
        CAPS = [cs[0:64, 320:832], cs[0:64, 1344:1856]]
        CAPI = [cs[0:64, 832:1344], cs[0:64, 1856:2368]]
        I8 = cs[0:64, 2368:2880]
        E8 = cs[0:8, 2880:3392]
        tr.op("dve", lambda: nc.vector.tensor_copy(out=csb[:, 0:256], in_=cs[:, 0:256]), r=["cs"], w=["csb"])
        tr.op("dve", lambda: nc.vector.tensor_copy(out=csb[:, 256:384], in_=cs[:, 3392:3520]), r=["cs"], w=["csb"])
        ONESB = csb[:, 0:128]
        ID64B = csb[0:64, 128:192]
        ID128B = csb[:, 256:384]

        tr.op("act", lambda: nc.scalar.activation(out=cact[:], in_=vec[:, 0:8], func=AF.Silu), r=["vec"], w=["cact"])
        st0 = ExitStack()
        ring_phase(st0, list(range(OFF_ADA, OFF_ADA + NB_ADA)))
        bi, pb = nps()
        for blk in range(NB_ADA):
            s, slot = ring.get()
            sv = slot[:].rearrange("p (k c) -> p k c", k=8)
            for cc in range(2):
                j = blk * 2 + cc
                for k in range(8):
                    tr.op("pe", lambda k=k, j=j, cc=cc: nc.tensor.matmul(
                        pb[:, j:j + 1], sv[:, k, cc * 128:(cc + 1) * 128], cact[:, k:k + 1],
                        start=(k == 0), stop=(k == 7)),
                        r=[("ring", s), "cact"], w=[("ps", bi)], inc=(k == 7 and cc == 1))
        tr.op("dve", lambda: nc.vector.tensor_tensor(out=modT[:], in0=pb[:, 0:72], in1=vec[:, 8:80], op=ALU.add),
              r=[("ps", bi), "vec"], w=["modT"])
        for s in range(3):
            tr.op("dve", lambda s=s: nc.vector.scalar_tensor_tensor(
                out=der[:, s * 8:(s + 1) * 8], in0=modT[:, (3 * s + 1) * 8:(3 * s + 2) * 8], scalar=1.0,
                in1=vec[:, 80 + s * 8:88 + s * 8], op0=ALU.add, op1=ALU.mult), r=["modT", "vec"], w=["der"])
            gsc = 1.0 if s == 1 else 0.5
            tr.op("dve", lambda s=s, gsc=gsc: nc.vector.tensor_scalar(
                out=der[:, 24 + s * 8:32 + s * 8], in0=modT[:, (3 * s + 2) * 8:(3 * s + 3) * 8],
                scalar1=gsc, scalar2=None, op0=ALU.mult), r=["modT"], w=["der"])
        NF = vec[:, 104:112]
        tr.op("act", lambda: nc.scalar.activation(out=negA[:], in_=tp[:, 0:16], func=AF.Exp), r=["tp"], w=["negA"])
        tr.op("dve", lambda: nc.vector.tensor_scalar(out=negA[:], in0=negA[:], scalar1=-1.0, scalar2=None,
                                                     op0=ALU.mult), r=["negA"], w=["negA"])

        tr.barrier()
        st0.close()

        def load_tile(dst, src, t0, key, n=NT, coff=0):
            for i in range(8):
                tr.dma("sp", dst[:, i, coff:coff + n], src[i * 128:(i + 1) * 128, t0:t0 + n],
                       w=[(key, i)], sem=(key,))

        def norm_mod(P, xt, xkey, u, s_idx):
            bi, pss = nps()
            for i in range(8):
                sq = P["sq"][i % 2]
                tr.op("act", lambda i=i, sq=sq: nc.scalar.activation(out=sq[:], in_=xt[:, i, :], func=AF.Square),
                      r=[(xkey, i)], w=[("sq", i % 2)])
                tr.op("pe", lambda i=i, sq=sq: nc.tensor.matmul(pss[:], ONES, sq[:], start=(i == 0), stop=(i == 7)),
                      r=[("sq", i % 2), "cs"], w=[("ps", bi)], inc=True)
            rstd = P["rstd"]
            tr.op("act", lambda: nc.scalar.activation(out=rstd[:], in_=pss[:], func=AF.Sqrt, bias=P["eps"][:, 0:1],
                                                      scale=1.0 / D), r=[("ps", bi), "eps"], w=["rstd"])
            tr.op("dve", lambda: nc.vector.reciprocal(out=rstd[:], in_=rstd[:]), r=["rstd"], w=["rstd"])
            for i in range(8):
                tt = P["tt"][i % 2]
                tr.op("dve", lambda i=i, tt=tt: nc.vector.scalar_tensor_tensor(
                    out=tt[:], in0=xt[:, i, :], scalar=der[:, s_idx * 8 + i:s_idx * 8 + i + 1], in1=rstd[:],
                    op0=ALU.mult, op1=ALU.mult), r=[(xkey, i), "rstd", "der"], w=[("tt", i % 2)])
                tr.op("act", lambda i=i, tt=tt: nc.scalar.activation(
                    out=u[:, i, :], in_=tt[:], func=AF.Identity,
                    bias=modT[:, 3 * s_idx * 8 + i:3 * s_idx * 8 + i + 1], scale=1.0),
                    r=[("tt", i % 2), "modT"], w=[("u", i)])

        def ffn_body(P, xt, xkey, s_idx, wdn, final, dst, t0):
            u, hid = P["u"], P["hid"]
            norm_mod(P, xt, xkey, u, s_idx)
            for j in range(NJ):
                s, slot = ring.get()
                sv = slot[:].rearrange("p (k c) -> p k c", k=8)
                ba_, pa = nps()
                bb_, pbb = nps()
                for part, (bix, pt) in enumerate(((ba_, pa), (bb_, pbb))):
                    for k in range(8):
                        tr.op("pe", lambda k=k, part=part, pt=pt: nc.tensor.matmul(
                            pt[:], sv[:, k, part * 128:(part + 1) * 128], u[:, k, :], start=(k == 0), stop=(k == 7)),
                            r=[("ring", s), ("u", k)], w=[("ps", bix)], inc=(k == 7))
                sl = P["s"][j % 2]
                tr.op("act", lambda sl=sl, pa=pa: nc.scalar.activation(out=sl[:], in_=pa[:], func=AF.Silu),
                      r=[("ps", ba_)], w=[("s", j % 2)])
                tr.op("dve", lambda sl=sl, pbb=pbb, j=j: nc.vector.tensor_tensor(
                    out=hid[:, j, :], in0=pbb[:], in1=sl[:], op=ALU.mult),
                    r=[("ps", bb_), ("s", j % 2)], w=[("hid", j)])
            for m in range(8):
                bd, pd = nps()
                for j in range(NJ):
                    tr.op("pe", lambda j=j, m=m, pd=pd: nc.tensor.matmul(
                        pd[:], wdn[:, j, m * 128:(m + 1) * 128], hid[:, j, :], start=(j == 0), stop=(j == NJ - 1)),
                        r=["wdn", ("hid", j)], w=[("ps", bd)], inc=(j == NJ - 1))
                tr.op("dve", lambda m=m, pd=pd: nc.vector.scalar_tensor_tensor(
                    out=xt[:, m, :], in0=pd[:], scalar=der[:, 24 + s_idx * 8 + m:24 + s_idx * 8 + m + 1],
                    in1=xt[:, m, :], op0=ALU.mult, op1=ALU.add), r=[("ps", bd), "der"], w=[(xkey, m)])
            if final:
                bi, pss = nps()
                for i in range(8):
                    sq = P["sq"][i % 2]
                    tr.op("act", lambda i=i, sq=sq: nc.scalar.activation(out=sq[:], in_=xt[:, i, :], func=AF.Square),
                          r=[(xkey, i)], w=[("sq", i % 2)])
                    tr.op("pe", lambda i=i, sq=sq: nc.tensor.matmul(pss[:], ONES, sq[:], start=(i == 0),
                                                                    stop=(i == 7)),
                          r=[("sq", i % 2), "cs"], w=[("ps", bi)], inc=True)
                rstd = P["rstd"]
                tr.op("act", lambda: nc.scalar.activation(out=rstd[:], in_=pss[:], func=AF.Sqrt,
                                                          bias=P["eps"][:, 0:1], scale=1.0 / D),
                      r=[("ps", bi), "eps"], w=["rstd"])
                tr.op("dve", lambda: nc.vector.reciprocal(out=rstd[:], in_=rstd[:]), r=["rstd"], w=["rstd"])
                for i in range(8):
                    tr.op("dve", lambda i=i: nc.vector.scalar_tensor_tensor(
                        out=xt[:, i, :], in0=xt[:, i, :], scalar=NF[:, i:i + 1], in1=rstd[:],
                        op0=ALU.mult, op1=ALU.mult), r=[(xkey, i), "rstd", "vec"], w=[(xkey, i)])
            for i in range(8):
                tr.dma("sp", dst[i * 128:(i + 1) * 128, t0:t0 + NT], xt[:, i, :], r=[(xkey, i)], sem=(xkey, "st"))

        def ffn_pool(stack, wdn_src, nxt=2):
            P = {}
            P["wdn"] = sb("wdn", [128, NJ, D], BF16, stack)
            P["xt"] = [sb("xt%d" % i, [128, 8, NT], F32, stack) for i in range(nxt)]
            P["sq"] = [sb("sq%d" % i, [128, NT], F32, stack) for i in range(2)]
            P["tt"] = [sb("tt%d" % i, [128, NT], F32, stack) for i in range(2)]
            P["s"] = [sb("s%d" % i, [128, NT], F32, stack) for i in range(2)]
            P["rstd"] = sb("rstd", [128, NT], F32, stack)
            P["u"] = sb("u", [128, 8, NT], BF16, stack)
            P["hid"] = sb("hid", [128, NJ, NT], BF16, stack)
            P["eps"] = sb("eps", [128, 1], F32, stack)
            tr.op("dve", lambda: nc.vector.memset(P["eps"][:], EPS), w=["eps"])
            wv = wdn_src.rearrange("(j p) n -> p j n", p=128)
            for jj in range(0, NJ, 2):
                tr.dma("pool", P["wdn"][:, jj:jj + 2, :], wv[:, jj:jj + 2, :], w=["wdn"], sem="wdn")
            return P

        with ExitStack() as st1:
            ring_phase(st1, [b for _ in range(ntile) for b in range(OFF_UP1, OFF_UP1 + NB_UP)])
            P = ffn_pool(st1, wdn1)
            load_tile(P["xt"][0], xT, 0, "xt0")
            for n in range(ntile):
                xt = P["xt"][n % 2]
                xkey = "xt%d" % (n % 2)
                if n + 1 < ntile:
                    load_tile(P["xt"][(n + 1) % 2], xT, (n + 1) * NT, "xt%d" % ((n + 1) % 2))
                ffn_body(P, xt, xkey, 0, P["wdn"], False, H1T, n * NT)
            tr.barrier()

        with ExitStack() as st2:
            ring_phase(st2, [b for _ in range(ntile) for b in range(OFF_IN, OFF_IN + NB_IN + NB_Z)])
            xts = [sb("xt%d" % i, [128, 8, NT], F32, st2) for i in range(2)]
            P = {"sq": [sb("sq%d" % i, [128, NT], F32, st2) for i in range(2)],
                 "tt": [sb("tt%d" % i, [128, NT], F32, st2) for i in range(2)],
                 "rstd": sb("rstd", [128, NT], F32, st2), "eps": sb("eps", [128, 1], F32, st2)}
            tr.op("dve", lambda: nc.vector.memset(P["eps"][:], EPS), w=["eps"])
            u = sb("u", [128, 8, NT], BF16, st2)
            wbas = sb("wbas", [128, 8, 32], BF16, st2)
            stg = [sb("stg%d" % i, [128, NT], F32, st2) for i in range(4)]
            zero = sb("zero", [128, NCONV, 2], F32, st2)
            tr.dma("pool", wbas[:], wba.rearrange("(k p) n -> p k n", p=128), w=["wbas"], sem="wbas")
            tr.op("dve", lambda: nc.vector.memset(zero[:], 0.0), w=["zero"])
            tr.dma("sp", PT[0:NCONV * 128, 0:2].rearrange("(c p) n -> p c n", p=128), zero[:], r=["zero"], sem="zero")
            sti = 0
            load_tile(xts[0], H1T, 0, "xt0")
            for n in range(ntile):
                xt = xts[n % 2]
                xkey = "xt%d" % (n % 2)
                if n + 1 < ntile:
                    load_tile(xts[(n + 1) % 2], H1T, (n + 1) * NT, "xt%d" % ((n + 1) % 2))
                norm_mod(P, xt, xkey, u, 1)
                for blk in range(NB_IN):
                    s, slot = ring.get()
                    sv = slot[:].rearrange("p (k c) -> p k c", k=8)
                    for cc in range(2):
                        ch = blk * 2 + cc
                        bi, pt = nps()
                        for k in range(8):
                            tr.op("pe", lambda k=k, cc=cc, pt=pt: nc.tensor.matmul(
                                pt[:], sv[:, k, cc * 128:(cc + 1) * 128], u[:, k, :], start=(k == 0), stop=(k == 7)),
                                r=[("ring", s), ("u", k)], w=[("ps", bi)], inc=(k == 7))
                        sg = stg[sti % 4]
                        sk = ("stg", sti % 4)
                        sti += 1
                        tr.op("act", lambda sg=sg, pt=pt: nc.scalar.copy(out=sg[:], in_=pt[:]), r=[("ps", bi)], w=[sk])
                        tr.dma("sp", PT[ch * 128:(ch + 1) * 128, 2 + n * NT:2 + (n + 1) * NT], sg[:], r=[sk], sem=sk)
                for zb in range(NB_Z):
                    s, slot = ring.get()
                    sv = slot[:].rearrange("p (k c) -> p k c", k=8)
                    for tb in range(0, 4, 2):
                        bi, pt = nps()
                        for t2 in range(2):
                            for k in range(8):
                                tr.op("pe", lambda k=k, t2=t2, tb=tb, pt=pt: nc.tensor.matmul(
                                    pt[:, t2 * 256:(t2 + 1) * 256], u[:, k, (tb + t2) * 128:(tb + t2 + 1) * 128],
                                    sv[:, k, :], start=(k == 0), stop=(k == 7)),
                                    r=[("ring", s), ("u", k)], w=[("ps", bi)], inc=(k == 7))
                        sg = stg[sti % 4]
                        sk = ("stg", sti % 4)
                        sti += 1
                        tr.op("act", lambda sg=sg, pt=pt: nc.scalar.activation(out=sg[:], in_=pt[:], func=AF.Silu),
                              r=[("ps", bi)], w=[sk])
                        for t2 in range(2):
                            r0 = n * NT + (tb + t2) * 128
                            tr.dma("sp", ZS[r0:r0 + 128, zb * 256:(zb + 1) * 256], sg[:, t2 * 256:(t2 + 1) * 256],
                                   r=[sk], sem=sk)
                bi, pt = nps()
                for tb in range(4):
                    for k in range(8):
                        tr.op("pe", lambda k=k, tb=tb, pt=pt: nc.tensor.matmul(
                            pt[:, tb * 32:(tb + 1) * 32], u[:, k, tb * 128:(tb + 1) * 128], wbas[:, k, :],
                            start=(k == 0), stop=(k == 7)), r=["wbas", ("u", k)], w=[("ps", bi)], inc=(k == 7))
                sg = stg[sti % 4]
                sk = ("stg", sti % 4)
                sti += 1
                tr.op("act", lambda sg=sg, pt=pt: nc.scalar.copy(out=sg[:, 0:128], in_=pt[:, 0:128]),
                      r=[("ps", bi)], w=[sk])
                tr.dma("sp", BAt[n * NT:(n + 1) * NT, :].rearrange("(b p) n -> p b n", p=128),
                       sg[:, 0:128].rearrange("p (b n) -> p b n", b=4), r=[sk], sem=sk)
            tr.barrier()
            hx = sb("hx", [128, 2, NCONV, 2], F32, st2)
            hy = sb("hy", [128, NCONV, 2], F32, st2)
            tr.dma("pool", hx_in.ap(), PT[0:NCONV * 128, T:T + 2], w=["hx_in"], sem="hx")
            tr.cc(lambda: nc.gpsimd.collective_compute("AllGather", ALU.bypass, replica_groups=PAIRS,
                                                        ins=[hx_in.ap()], outs=[hx_out.ap()]),
                  r=["hx_in"], w=["hx_out"], sem="hxcc")
            tr.dma("pool", hx[:], hx_out.ap().rearrange("(r c p) n -> p r c n", r=2, p=128), r=["hx_out"], w=["hx"],
                   sem="hx")
            tr.op("dve", lambda: nc.vector.tensor_scalar(out=hy[:], in0=hx[:, 0], scalar1=selt[:, 0:1], scalar2=None,
                                                         op0=ALU.mult), r=["hx", "selt"], w=["hy"])
            tr.op("dve", lambda: nc.vector.scalar_tensor_tensor(out=hy[:], in0=hx[:, 1], scalar=selt[:, 1:2],
                                                                in1=hy[:], op0=ALU.mult, op1=ALU.add),
                  r=["hx", "selt", "hy"], w=["hy"])
            pv = PT[0:NCONV * 128, :].rearrange("(c p) n -> p c n", p=128)
            hys = sb("hys", [128, NCONV, 2], F32, st2)
            tr.op("dve", lambda: nc.vector.tensor_copy(out=hys[:, :, 0:1], in_=hy[:, :, 1:2]), r=["hy"], w=["hys"])
            tr.op("dve", lambda: nc.vector.tensor_copy(out=hys[:, :, 1:2], in_=hy[:, :, 0:1]), r=["hy"], w=["hys"])
            tr.dma("sp", pv[:, :, T + 2:T + 4], hys[:], r=["hys"], sem="hy")
            tr.barrier()

        def run_streams(factories, width, stagger=0):
            active = []
            free = list(range(width))
            it = iter(factories)
            done = False
            if stagger:
                f = next(it, None)
                if f is not None:
                    s_ = free.pop(0)
                    g0 = f(s_)
                    active.append((g0, s_))
                    for _ in range(stagger):
                        try:
                            next(g0)
                        except StopIteration:
                            active.remove((g0, s_))
                            free.append(s_)
                            break
            while True:
                while free and not done:
                    f = next(it, None)
                    if f is None:
                        done = True
                        break
                    s_ = free.pop(0)
                    active.append((f(s_), s_))
                if not active:
                    break
                for g in list(active):
                    try:
                        next(g[0])
                    except StopIteration:
                        active.remove(g)
                        free.append(g[1])

        with ExitStack() as st3:
            W2B = 4
            B2 = []
            for i in range(W2B):
                B2.append({"pre": sb("pre", [128, NT + 4], BF16, st3), "acc": sb("acc", [128, NT], F32, st3),
                           "sq": sb("sqq", [128, NT], F32, st3), "rn": sb("rn", [128, NT], F32, st3),
                           "ob": sb("ob", [128, NT], BF16, st3)})
            A2 = []
            for i in range(2):
                A2.append({"p1": sb("p1", [128, NT + 4], F32, st3), "p2": sb("p2", [128, NT + 4], F32, st3),
                           "p3": sb("p3", [128, NT], F32, st3), "pcv": sb("pcv", [128, NT + 4], BF16, st3),
                           "cacc": sb("cacc", [128, NT], F32, st3)})
            G2 = []
            for i in range(2):
                G2.append({"g": sb("gsb", [128, NT], F32, st3), "pa": sb("pab", [128, NT], F32, st3)})
            yaT = [sb("yaT", [128, 4, NT], BF16, st3) for i in range(2)]
            wa = sb("wa", [128, 4, D], BF16, st3)
            dgq = sb("dgq", [128, 24 * 5, 128], BF16, st3)
            dga = sb("dga", [128, 4 * 3, 128], BF16, st3)
            IDF = cs[:, 3392:3520]
            for j in range(24 * 5):
                eng = "dve"
                if eng == "dve":
                    tr.op("dve", lambda j=j: nc.vector.tensor_scalar(out=dgq[:, j, :], in0=IDF, scalar1=cw[:, j:j + 1],
                                                                     scalar2=None, op0=ALU.mult),
                          r=["cs", "cw"], w=[("dgq", j)])
                else:
                    tr.op("pool", lambda j=j: nc.gpsimd.tensor_scalar(out=dgq[:, j, :], in0=IDF, scalar1=cw[:, j:j + 1],
                                                                      scalar2=None, op0=ALU.mult),
                          r=["cs", "cw"], w=[("dgq", j)])
            for j in range(12):
                tr.op("dve", lambda j=j: nc.vector.tensor_scalar(out=dga[:, j, :], in0=IDF,
                                                                 scalar1=cw[:, 120 + j:121 + j], scalar2=None,
                                                                 op0=ALU.mult), r=["cs", "cw"], w=[("dga", j)])
            epsq = sb("epsq", [128, 2], F32, st3)
            tr.op("dve", lambda: nc.vector.memset(epsq[:, 0:1], EPS), w=["epsq"])
            tr.op("dve", lambda: nc.vector.memset(epsq[:, 1:2], 128.0 * EPS), w=["epsq"])
            tr.dma("pool", wa[:], waout.rearrange("(k p) n -> p k n", p=128), w=["wa"], sem="wa")

            def qkv_gen(n, ch, sl):
                c0 = n * NT
                Bf = B2[sl]
                p_, a_, s_, r_, o_ = Bf["pre"], Bf["acc"], Bf["sq"], Bf["rn"], Bf["ob"]
                pk, ak, sk, rk, ok = ("pre", sl), ("acc", sl), ("sqq", sl), ("rn", sl), ("ob", sl)
                tr.dma("pool", p_[:], PT[(8 + ch) * 128:(9 + ch) * 128, c0:c0 + NT + 4], w=[pk], sem=pk)
                yield
                bc_, pc_ = nps()
                for tap in range(5):
                    tr.op("pe", lambda tap=tap: nc.tensor.matmul(pc_[:], dgq[:, ch * 5 + tap, :], p_[:, tap:tap + NT],
                                                                 start=(tap == 0), stop=(tap == 4)),
                          r=[pk, ("dgq", ch * 5 + tap)], w=[("ps", bc_)], inc=(tap == 4))
                yield
                tr.op("act", lambda: nc.scalar.activation(out=a_[:], in_=pc_[:], func=AF.Silu),
                      r=[("ps", bc_)], w=[ak])
                yield
                if ch < 16:
                    tr.op("pool", lambda: nc.gpsimd.tensor_tensor(out=s_[:], in0=a_[:], in1=a_[:], op=ALU.mult),
                          r=[ak], w=[sk])
                    yield
                    bi, pt = nps()
                    tr.op("pe", lambda: nc.tensor.matmul(pt[:], ONES, s_[:], start=True, stop=True),
                          r=[sk, "cs"], w=[("ps", bi)])
                    yield
                    qsc = 128.0 if ch < 8 else 1.0
                    ebias = epsq[:, 1:2] if ch < 8 else epsq[:, 0:1]
                    tr.op("act", lambda: nc.scalar.activation(out=r_[:], in_=pt[:], func=AF.Sqrt, bias=ebias,
                                                              scale=qsc), r=[("ps", bi), "epsq"], w=[rk])
                    yield
                    tr.op("dve", lambda: nc.vector.reciprocal(out=r_[:], in_=r_[:]), r=[rk], w=[rk])
                    yield
                    tr.op("pool", lambda: nc.gpsimd.tensor_tensor(out=o_[:], in0=a_[:], in1=r_[:], op=ALU.mult),
                          r=[ak, rk], w=[ok])
                else:
                    tr.op("pool", lambda: nc.gpsimd.tensor_copy(out=o_[:], in_=a_[:]), r=[ak], w=[ok])
                yield
                tr.dma("sp", QKVT[ch * 128:(ch + 1) * 128, c0:c0 + NT], o_[:], r=[ok], sem=ok)

            def abr_gen(n, ch, sl):
                c0 = n * NT
                Af = A2[sl]
                p1, p2, p3, pcv, cacc = Af["p1"], Af["p2"], Af["p3"], Af["pcv"], Af["cacc"]
                k1, k2, k3, kp, kc = ("p1", sl), ("p2", sl), ("p3", sl), ("pcv", sl), ("cacc", sl)
                ya = yaT[n % 2]
                tr.dma("sp", p1[:], PT[ch * 128:(ch + 1) * 128, c0:c0 + NT + 4], w=[k1], sem=k1)
                tr.dma("sp", p2[:], PT[(4 + ch) * 128:(5 + ch) * 128, c0:c0 + NT + 4], w=[k2], sem=k2)
                tr.dma("sp", p3[:], PT[(32 + ch) * 128:(33 + ch) * 128, c0 + 2:c0 + 2 + NT], w=[k3], sem=k3)
                yield
                tr.op("dve", lambda: nc.vector.tensor_tensor(out=pcv[:], in0=p1[:], in1=p2[:], op=ALU.mult),
                      r=[k1, k2], w=[kp])
                yield
                bc_, pc_ = nps()
                for tap in range(3):
                    tr.op("pe", lambda tap=tap: nc.tensor.matmul(pc_[:], dga[:, ch * 3 + tap, :],
                                                                 pcv[:, 1 + tap:1 + tap + NT], start=(tap == 0),
                                                                 stop=(tap == 2)),
                          r=[kp, ("dga", ch * 3 + tap)], w=[("ps", bc_)], inc=(tap == 2))
                yield
                tr.op("dve", lambda: nc.vector.tensor_tensor(out=ya[:, ch, :], in0=pc_[:], in1=p3[:], op=ALU.mult),
                      r=[("ps", bc_), k3], w=[("yaT", n % 2, ch)])

            def gate_gen(n, m, sl):
                c0 = n * NT
                g_, pa_ = G2[sl]["g"], G2[sl]["pa"]
                gk, pk2 = ("gsb", sl), ("pab", sl)
                ya = yaT[n % 2]
                tr.dma("sp", g_[:], PT[(36 + m) * 128:(37 + m) * 128, c0 + 2:c0 + 2 + NT], w=[gk], sem=gk)
                yield
                tr.op("act", lambda: nc.scalar.activation(out=g_[:], in_=g_[:], func=AF.Sigmoid), r=[gk], w=[gk])
                bi, pt = nps()
                for kk in range(4):
                    tr.op("pe", lambda kk=kk: nc.tensor.matmul(pt[:], wa[:, kk, m * 128:(m + 1) * 128], ya[:, kk, :],
                                                               start=(kk == 0), stop=(kk == 3)),
                          r=["wa", ("yaT", n % 2, kk)], w=[("ps", bi)], inc=(kk == 3))
                yield
                tr.op("dve", lambda: nc.vector.tensor_tensor(out=pa_[:], in0=pt[:], in1=g_[:], op=ALU.mult),
                      r=[("ps", bi), gk], w=[pk2])
                yield
                tr.dma("sp", PAT[m * 128:(m + 1) * 128, c0:c0 + NT], pa_[:], r=[pk2], sem=pk2)

            for n in range(ntile):
                run_streams([(lambda sl, n=n, ch=ch: qkv_gen(n, ch, sl)) for ch in range(24)], W2B)
                run_streams([(lambda sl, n=n, ch=ch: abr_gen(n, ch, sl)) for ch in range(4)], 2)
                run_streams([(lambda sl, n=n, m=m: gate_gen(n, m, sl)) for m in range(8)], 2)
            tr.barrier()

        with ExitStack() as st4:
            qkv = [sb("qkv%d" % i, [128, 24, NT], BF16, st4) for i in range(2)]
            S = sb("S", [128, H, 128], F32, st4)
            Sb = sb("Sb", [128, H, 128], BF16, st4)
            sm = sb("sm", [128, 8, 8], F32, st4)
            gT = sb("gT", [8, 3, 64], F32, st4)
            Rg = sb("Rg", [8, 2, 512], F32, st4)
            dec = [sb("dec%d" % i, [64, 512], F32, st4) for i in range(2)]
            egr = sb("egr", [128, 512], F32, st4)
            qg = sb("qg", [128, H, C], BF16, st4)
            XY = [sb("XY%d" % i, [64, 2, 512], BF16, st4) for i in range(2)]
            PQ = [sb("PQ%d" % i, [64, 2, 512], BF16, st4) for i in range(2)]
            attn = sb("attn", [64, 512], BF16, st4)
            XY0 = sb("XY0", [64, 2, 512], BF16, st4)
            MK = sb("MK", [64, 4, 512], BF16, st4)
            tr.dma("pool", MK[:], cst2.rearrange("p (a n) -> p a n", a=4), w=["MK"], sem="MK")
            kbg = sb("kbg", [64, H, 128], BF16, st4)
            kdec = sb("kdec", [64, H, 128], BF16, st4)
            vb = sb("vb", [64, H, 128], BF16, st4)
            usb = sb("usb", [64, H, 128], F32, st4)
            wT = sb("wT", [128, H, C], BF16, st4)
            vnew = sb("vnew", [64, H, 128], BF16, st4)
            osb = sb("osb", [64, H, 128], F32, st4)
            ofw = sb("ofw", [64, H, 128], F32, st4)
            zsb = sb("zsb", [64, H, 128], F32, st4)
            ysq = sb("ysq", [64, H, 128], F32, st4)
            ss = sb("ss", [64, 16], F32, st4)
            ss2 = sb("ss2", [64, 16], F32, st4)
            ss3 = sb("ss3", [64, 16], F32, st4)
            yb = sb("yb", [64, H, 128], BF16, st4)
            ybT = sb("ybT", [128, H, NT], BF16, st4)
            SX = ybT[:].bitcast(F32).rearrange("p h n -> p (h n)").rearrange("p (r m) -> p r m", r=2)
            epsd = sb("epsd", [128, 1], F32, st4)
            tr.op("dve", lambda: nc.vector.memset(epsd[:], EPS), w=["epsd"])
            tr.op("dve", lambda: nc.vector.memset(S[:], 0.0), w=["S"])
            tr.op("dve", lambda: nc.vector.memset(Sb[:], 0.0), w=["Sb"])
            nck = NT // C

            def bc(ap, shape):
                return ap.to_broadcast(list(shape))

            WG = 2
            TB = [{"bat": sb("bat", [64, NT // C, 32], F32, st4), "beta": sb("beta", [64, NT // C, 8], F32, st4),
                   "lnb": sb("lnb", [64, NT // C, 8], F32, st4), "gg": sb("gg", [64, NT // C, 8], F32, st4)}
                  for _i in range(2)]
            CB = [{"sm": sm, "gT": gT, "Rg": Rg, "dec": dec, "egr": egr, "qg": qg, "XY": XY, "PQ": PQ, "attn": attn,
                   "XY0": XY0, "kbg": kbg, "kdec": kdec, "vb": vb, "usb": usb, "wT": wT, "vnew": vnew, "osb": osb,
                   "ofw": ofw, "zsb": zsb, "ysq": ysq, "ss": ss, "ss2": ss2, "ss3": ss3, "yb": yb}]
            for _i in range(1, WG):
                CB.append({
                    "sm": sb("sm", [128, 8, 8], F32, st4), "gT": sb("gT", [8, 3, 64], F32, st4),
                    "Rg": sb("Rg", [8, 2, 512], F32, st4),
                    "dec": [sb("dec", [64, 512], F32, st4) for _j in range(2)],
                    "egr": sb("egr", [128, 512], F32, st4), "qg": sb("qg", [128, H, C], BF16, st4),
                    "XY": [sb("XY", [64, 2, 512], BF16, st4) for _j in range(2)],
                    "PQ": [sb("PQ", [64, 2, 512], BF16, st4) for _j in range(2)],
                    "attn": sb("attn", [64, 512], BF16, st4), "XY0": sb("XY0", [64, 2, 512], BF16, st4),
                    "kbg": sb("kbg", [64, H, 128], BF16, st4), "kdec": sb("kdec", [64, H, 128], BF16, st4),
                    "vb": sb("vb", [64, H, 128], BF16, st4), "usb": sb("usb", [64, H, 128], F32, st4),
                    "wT": sb("wT", [128, H, C], BF16, st4), "vnew": sb("vnew", [64, H, 128], BF16, st4),
                    "osb": sb("osb", [64, H, 128], F32, st4), "ofw": sb("ofw", [64, H, 128], F32, st4),
                    "zsb": sb("zsb", [64, H, 128], F32, st4), "ysq": sb("ysq", [64, H, 128], F32, st4),
                    "ss": sb("ss", [64, 16], F32, st4), "ss2": sb("ss2", [64, 16], F32, st4),
                    "ss3": sb("ss3", [64, 16], F32, st4), "yb": sb("yb", [64, H, 128], BF16, st4)})

            def chunk_gen(dr, n, c, first, last, sl):
                Bc = CB[sl]
                Tt = TB[n % 2]
                bat, beta, lnb, gg = Tt["bat"], Tt["beta"], Tt["lnb"], Tt["gg"]
                sm, gT, Rg, dec, egr, qg, XY, PQ, attn, XY0 = (Bc["sm"], Bc["gT"], Bc["Rg"], Bc["dec"], Bc["egr"],
                                                               Bc["qg"], Bc["XY"], Bc["PQ"], Bc["attn"], Bc["XY0"])
                kbg, kdec, vb, usb, wT, vnew, osb, ofw, zsb = (Bc["kbg"], Bc["kdec"], Bc["vb"], Bc["usb"], Bc["wT"],
                                                               Bc["vnew"], Bc["osb"], Bc["ofw"], Bc["zsb"])
                ysq, ss, ss2, ss3, yb = Bc["ysq"], Bc["ss"], Bc["ss2"], Bc["ss3"], Bc["yb"]
                qt = qkv[n % 2]
                qk_ = ("qkv", n % 2)
                if first:
                    tr.dma("sp", qt[:], QKVT[:, n * NT:(n + 1) * NT].rearrange("(c p) n -> p c n", p=128),
                           w=[qk_], sem=qk_)
                    tr.dma("sp", bat[:], BAt[n * NT:(n + 1) * NT, :].rearrange("(c p) n -> p c n", p=64),
                           w=[("bat", "t", n % 2)], sem=("bat", "t", n % 2))
                    bsl = bat[:, :, dr * 8:dr * 8 + 8]
                    asl = bat[:, :, 16 + dr * 8:24 + dr * 8]
                    tr.op("act", lambda: nc.scalar.activation(out=beta[:], in_=bsl, func=AF.Sigmoid),
                          r=[("bat", "t", n % 2)], w=[("beta", "t", n % 2)])
                    tr.op("act", lambda: nc.scalar.activation(out=lnb[:], in_=bsl, func=AF.Exp, scale=-1.0),
                          r=[("bat", "t", n % 2)], w=[("lnb", "t", n % 2)])
                    tr.op("act", lambda: nc.scalar.activation(out=lnb[:], in_=lnb[:], func=AF.Ln, bias=1.0, scale=1.0),
                          r=[("lnb", "t", n % 2)], w=[("lnb", "t", n % 2)])
                    for c2 in range(nck):
                        tr.op("dve", lambda c2=c2: nc.vector.tensor_tensor(
                            out=gg[:, c2, :], in0=bat[:, c2, 16 + dr * 8:24 + dr * 8],
                            in1=tp[0:64, 16 + dr * 8:24 + dr * 8], op=ALU.add), r=[("bat", "t", n % 2), "tp"], w=[("gg", "t", n % 2)])
                    tr.op("act", lambda: nc.scalar.activation(out=gg[:], in_=gg[:], func=AF.Exp), r=[("gg", "t", n % 2)], w=[("gg", "t", n % 2)])
                    tr.op("act", lambda: nc.scalar.activation(out=gg[:], in_=gg[:], func=AF.Ln, bias=1.0, scale=1.0),
                          r=[("gg", "t", n % 2)], w=[("gg", "t", n % 2)])
                    for c2 in range(nck):
                        tr.op("dve", lambda c2=c2: nc.vector.tensor_tensor(
                            out=gg[:, c2, :], in0=gg[:, c2, :], in1=negA[0:64, dr * 8:dr * 8 + 8], op=ALU.mult),
                            r=[("gg", "t", n % 2), "negA"], w=[("gg", "t", n % 2)])
                tok = slice(c * C, (c + 1) * C)
                gtok = n * NT + c * C
                yield
                bi, pg = nps()
                tr.op("pe", lambda: nc.tensor.matmul(pg[0:64, 0:8], LTRI[dr], gg[:, c, :], start=True,
                                                     stop=True), r=[("gg", "t", n % 2), "cs"], w=[("ps", bi)], inc=False)
                tr.op("pe", lambda: nc.tensor.matmul(pg[:, 8:16], cs[0:64, 0:128], gg[:, c, :], start=True,
                                                     stop=True), r=[("gg", "t", n % 2), "cs"], w=[("ps", bi)])
                tr.op("dve", lambda: nc.vector.tensor_copy(out=sm[0:64, 0, :], in_=pg[0:64, 0:8]),
                      r=[("ps", bi)], w=[("sm", sl)])
                tr.op("dve", lambda: nc.vector.tensor_tensor(out=sm[0:64, 1, :], in0=pg[0:64, 0:8],
                                                             in1=lnb[:, c, :], op=ALU.subtract),
                      r=[("ps", bi), ("lnb", "t", n % 2)], w=[("sm", sl)])
                tr.op("act", lambda: nc.scalar.activation(out=sm[0:64, 2, :], in_=pg[0:64, 0:8], func=AF.Exp),
                      r=[("ps", bi)], w=[("sm", sl)])
                tr.op("dve", lambda: nc.vector.tensor_tensor(out=sm[0:64, 3, :], in0=pg[0:64, 8:16],
                                                             in1=sm[0:64, 0, :], op=ALU.subtract),
                      r=[("ps", bi), ("sm", sl)], w=[("sm", sl)])
                tr.op("act", lambda: nc.scalar.activation(out=sm[0:64, 3, :], in_=sm[0:64, 3, :], func=AF.Exp),
                      r=[("sm", sl)], w=[("sm", sl)])
                tr.op("dve", lambda: nc.vector.tensor_tensor(out=sm[0:64, 4, :], in0=sm[0:64, 2, :],
                                                             in1=beta[:, c, :], op=ALU.mult),
                      r=[("sm", sl), ("beta", "t", n % 2)], w=[("sm", sl)])
                tr.op("act", lambda: nc.scalar.activation(out=sm[:, 5, :], in_=pg[:, 8:16], func=AF.Exp),
                      r=[("ps", bi)], w=[("sm", sl)])
                yield
                bi2, pt2 = nps()
                tr.op("pe", lambda: nc.tensor.matmul(pt2[0:8, 0:64], sm[0:64, 0, :], ID64, start=True,
                                                     stop=True), r=[("sm", sl), "cs"], w=[("ps", bi2)], inc=False)
                tr.op("pe", lambda: nc.tensor.matmul(pt2[0:8, 64:128], sm[0:64, 1, :], ID64, start=True,
                                                     stop=True), r=[("sm", sl), "cs"], w=[("ps", bi2)])
                tr.op("dve", lambda: nc.vector.tensor_scalar(out=gT[:, 0, :], in0=pt2[0:8, 0:64], scalar1=-1.0,
                                                             scalar2=None, op0=ALU.mult),
                      r=[("ps", bi2)], w=[("gT", sl)])
                tr.op("dve", lambda: nc.vector.tensor_copy(out=gT[:, 1:3, :].rearrange("p a b -> p (a b)"),
                                                           in_=pt2[0:8, 0:128]),
                      r=[("ps", bi2)], w=[("gT", sl)])
                E8v = E8.rearrange("p (h i) -> p h i", h=8)
                tr.op("dve", lambda: nc.vector.tensor_tensor(
                    out=Rg[:, 0, :].rearrange("p (h i) -> p h i", h=8), in0=E8v,
                    in1=gT[:, 2:3, :].to_broadcast([8, 8, 64]), op=ALU.mult), r=[("gT", sl), "cs"], w=[("Rg", sl)])
                tr.op("dve", lambda: nc.vector.tensor_tensor(
                    out=Rg[:, 1, :].rearrange("p (h i) -> p h i", h=8), in0=E8v,
                    in1=gT[:, 1:2, :].to_broadcast([8, 8, 64]), op=ALU.mult), r=[("gT", sl), "cs"], w=[("Rg", sl)])
                yield
                bA, pA = nps()
                bQ, pQ = nps()
                bR, pR = nps()
                tr.op("pe", lambda: nc.tensor.matmul(pA[0:64, :], cs[0:8, 0:64], Rg[:, 0, :], start=True,
                                                     stop=False), r=[("Rg", sl), "cs"], w=[("ps", bA)], inc=False)
                tr.op("pe", lambda: nc.tensor.matmul(pA[0:64, :], gT[:, 0, :], E8, start=False, stop=True),
                      r=[("gT", sl), "cs"], w=[("ps", bA)])
                tr.op("pe", lambda: nc.tensor.matmul(pQ[0:64, :], cs[0:8, 0:64], Rg[:, 1, :], start=True,
                                                     stop=False), r=[("Rg", sl), "cs"], w=[("ps", bQ)], inc=False)
                tr.op("pe", lambda: nc.tensor.matmul(pQ[0:64, :], gT[:, 0, :], E8, start=False, stop=True),
                      r=[("gT", sl), "cs"], w=[("ps", bQ)])
                tr.op("pe", lambda: nc.tensor.matmul(pR[:, :], cs[0:8, 0:128], Rg[:, 1, :], start=True,
                                                     stop=True), r=[("Rg", sl), "cs"], w=[("ps", bR)])
                tr.op("dve", lambda: nc.vector.tensor_tensor(out=dec[0][:], in0=pA[0:64, :], in1=CAPS[dr],
                                                             op=ALU.min), r=[("ps", bA), "cs"], w=[("dec", 0, sl)])
                tr.op("act", lambda: nc.scalar.activation(out=dec[0][:], in_=dec[0][:], func=AF.Exp),
                      r=[("dec", 0, sl)], w=[("dec", 0, sl)])
                tr.op("dve", lambda: nc.vector.tensor_tensor(out=dec[1][:], in0=pQ[0:64, :], in1=CAPI[dr],
                                                             op=ALU.min), r=[("ps", bQ), "cs"], w=[("dec", 1, sl)])
                tr.op("act", lambda: nc.scalar.activation(out=dec[1][:], in_=dec[1][:], func=AF.Exp),
                      r=[("dec", 1, sl)], w=[("dec", 1, sl)])
                tr.op("act", lambda: nc.scalar.activation(out=egr[:], in_=pR[:], func=AF.Exp),
                      r=[("ps", bR)], w=[("egr", sl)])
                tr.op("dve", lambda: nc.vector.tensor_tensor(
                    out=qg[:], in0=qt[:, 0:8, tok], in1=egr[:].rearrange("p (h i) -> p h i", h=8),
                    op=ALU.mult), r=[qk_, ("egr", sl)], w=[("qg", sl)])
                yield
                bK, pK = nps()
                bQK, pQK = nps()
                for h in range(H):
                    tr.op("pe", lambda h=h: nc.tensor.matmul(pK[0:64, h * 64:(h + 1) * 64], qt[:, 8 + h, tok],
                                                             qt[:, 8 + h, tok], start=True, stop=True),
                          r=[qk_], w=[("ps", bK)], inc=(h == H - 1))
                for h in range(H):
                    tr.op("pe", lambda h=h: nc.tensor.matmul(pQK[0:64, h * 64:(h + 1) * 64], qt[:, 8 + h, tok],
                                                             qt[:, h, tok], start=True, stop=True),
                          r=[qk_], w=[("ps", bQK)], inc=(h == H - 1))
                X0, Y0 = XY0[:, 0, :], XY0[:, 1, :]
                tr.op("dve", lambda: nc.vector.tensor_tensor(out=X0, in0=pK[0:64, :], in1=dec[0][:],
                                                             op=ALU.mult),
                      r=[("ps", bK), ("dec", 0, sl)], w=[("X0", sl)])
                tr.op("dve", lambda: nc.vector.tensor_tensor(out=attn[:], in0=pQK[0:64, :], in1=dec[1][:],
                                                             op=ALU.mult),
                      r=[("ps", bQK), ("dec", 1, sl)], w=[("attn", sl)])
                yield
                bY, pY = nps()
                for h in range(H):
                    tr.op("pe", lambda h=h: nc.tensor.matmul(pY[0:64, h * 64:(h + 1) * 64],
                                                             X0[:, h * 64:(h + 1) * 64], ID64B, start=True,
                                                             stop=True),
                          r=[("X0", sl), "csb"], w=[("ps", bY)], inc=(h == H - 1))
                tr.op("act", lambda: nc.scalar.copy(out=Y0, in_=pY[0:64, :]), r=[("ps", bY)], w=[("Y0", sl)])
                Xa0, Ya0 = XY[0][:, 0, :], XY[0][:, 1, :]
                Pm, Qm = PQ[0][:, 0, :], PQ[0][:, 1, :]
                tr.op("pool", lambda: nc.gpsimd.tensor_tensor(out=Xa0, in0=X0, in1=MK[:, 0, :], op=ALU.mult),
                      r=[("X0", sl), "MK"], w=[("X", 0, sl)])
                tr.op("pool", lambda: nc.gpsimd.tensor_tensor(out=Ya0, in0=Y0, in1=MK[:, 0, :], op=ALU.mult),
                      r=[("Y0", sl), "MK"], w=[("Y", 0, sl)])
                tr.op("pool", lambda: nc.gpsimd.tensor_tensor(out=Pm, in0=I8, in1=Xa0, op=ALU.subtract),
                      r=[("X", 0, sl), "cs"], w=[("P", 0, sl)])
                tr.op("pool", lambda: nc.gpsimd.tensor_tensor(out=Qm, in0=I8, in1=Ya0, op=ALU.subtract),
                      r=[("Y", 0, sl), "cs"], w=[("Q", 0, sl)])

                def mm8(pt_, lhs, rhs, rk, bix):
                    for h in range(H):
                        hs = slice(h * 64, (h + 1) * 64)
                        tr.op("pe", lambda hs=hs: nc.tensor.matmul(pt_[0:64, hs], lhs[:, hs], rhs[:, hs],
                                                                   start=True, stop=True),
                              r=rk, w=[("ps", bix)], inc=(h == H - 1))

                for lv in range(2):
                    a, b = lv % 2, (lv + 1) % 2
                    Xa, Ya = XY[a][:, 0, :], XY[a][:, 1, :]
                    Xb, Yb = XY[b][:, 0, :], XY[b][:, 1, :]
                    Pa, Qa = PQ[a][:, 0, :], PQ[a][:, 1, :]
                    Pb, Qb = PQ[b][:, 0, :], PQ[b][:, 1, :]
                    yield
                    b1, p1 = nps()
                    mm8(p1, Ya, Xa, [("X", a, sl), ("Y", a, sl)], b1)
                    tr.op("act", lambda: nc.scalar.copy(out=Xb, in_=p1[0:64, :]), r=[("ps", b1)], w=[("X", b, sl)])
                    yield
                    b2, p2 = nps()
                    mm8(p2, Xa, Ya, [("X", a, sl), ("Y", a, sl)], b2)
                    tr.op("act", lambda: nc.scalar.copy(out=Yb, in_=p2[0:64, :]), r=[("ps", b2)], w=[("Y", b, sl)])
                    yield
                    b3, p3 = nps()
                    mm8(p3, Qa, Xb, [("Q", a, sl), ("X", b, sl)], b3)
                    tr.op("dve", lambda: nc.vector.tensor_tensor(out=Pb, in0=p3[0:64, :], in1=Pa, op=ALU.add),
                          r=[("ps", b3), ("P", a, sl)], w=[("P", b, sl)])
                    yield
                    b4, p4 = nps()
                    mm8(p4, Pa, Yb, [("P", a, sl), ("Y", b, sl)], b4)
                    tr.op("dve", lambda: nc.vector.tensor_tensor(out=Qb, in0=p4[0:64, :], in1=Qa, op=ALU.add),
                          r=[("ps", b4), ("Q", a, sl)], w=[("Q", b, sl)])
                cur = 0
                for li in range(3):
                    nxt = 1 - cur
                    Pc, Qc = PQ[cur][:, 0, :], PQ[cur][:, 1, :]
                    Pn, Qn = PQ[nxt][:, 0, :], PQ[nxt][:, 1, :]
                    Xm, Ym = XY[0][:, 0, :], XY[0][:, 1, :]
                    W1, W2 = XY[1][:, 0, :], XY[1][:, 1, :]
                    tr.op("pool", lambda: nc.gpsimd.tensor_tensor(out=Ym, in0=Y0, in1=MK[:, 1 + li, :],
                                                                  op=ALU.mult),
                          r=[("Y0", sl), "MK"], w=[("Y", 0, sl)])
                    yield
                    b1, p1 = nps()
                    mm8(p1, Ym, Pc, [("Y", 0, sl), ("P", cur, sl)], b1)
                    tr.op("act", lambda: nc.scalar.copy(out=W1, in_=p1[0:64, :]), r=[("ps", b1)], w=[("X", 1, sl)])
                    yield
                    b2, p2 = nps()
                    mm8(p2, Qc, W1, [("Q", cur, sl), ("X", 1, sl)], b2)
                    tr.op("dve", lambda: nc.vector.tensor_tensor(out=Pn, in0=Pc, in1=p2[0:64, :],
                                                                 op=ALU.subtract),
                          r=[("ps", b2), ("P", cur, sl)], w=[("P", nxt, sl)])
                    if li < 2:
                        tr.op("pool", lambda: nc.gpsimd.tensor_tensor(out=Xm, in0=X0, in1=MK[:, 1 + li, :],
                                                                      op=ALU.mult),
                              r=[("X0", sl), "MK"], w=[("X", 0, sl)])
                        yield
                        b3, p3 = nps()
                        mm8(p3, Xm, Qc, [("X", 0, sl), ("Q", cur, sl)], b3)
                        tr.op("act", lambda: nc.scalar.copy(out=W2, in_=p3[0:64, :]), r=[("ps", b3)],
                              w=[("Y", 1, sl)])
                        yield
                        b4, p4 = nps()
                        mm8(p4, Pc, W2, [("P", cur, sl), ("Y", 1, sl)], b4)
                        tr.op("dve", lambda: nc.vector.tensor_tensor(out=Qn, in0=Qc, in1=p4[0:64, :],
                                                                     op=ALU.subtract),
                              r=[("ps", b4), ("Q", cur, sl)], w=[("Q", nxt, sl)])
                    cur = nxt
                TT = PQ[cur][:, 0, :]
                TK = ("P", cur, sl)
                for half in range(2):
                    yield
                    bk, pk_ = nps()
                    bv, pv_ = nps()
                    for hh in range(4):
                        h = half * 4 + hh
                        tr.op("pe", lambda h=h, hh=hh: nc.tensor.matmul(
                            pk_[0:64, hh * 128:(hh + 1) * 128], qt[:, 8 + h, tok], ID128B, start=True,
                            stop=True), r=[qk_, "csb"], w=[("ps", bk)], inc=(hh == 3))
                    for hh in range(4):
                        h = half * 4 + hh
                        tr.op("pe", lambda h=h, hh=hh: nc.tensor.matmul(
                            pv_[0:64, hh * 128:(hh + 1) * 128], qt[:, 16 + h, tok], ID128B, start=True,
                            stop=True), r=[qk_, "csb"], w=[("ps", bv)], inc=(hh == 3))
                    hsl = slice(half * 4, half * 4 + 4)
                    pk3 = pk_[0:64, :].rearrange("p (h d) -> p h d", h=4)
                    pv3 = pv_[0:64, :].rearrange("p (h d) -> p h d", h=4)
                    tr.op("dve", lambda pk3=pk3, hsl=hsl: nc.vector.tensor_tensor(
                        out=kbg[:, hsl, :], in0=pk3,
                        in1=sm[0:64, 4, hsl].unsqueeze(2).to_broadcast([64, 4, 128]), op=ALU.mult),
                        r=[("ps", bk), ("sm", sl)], w=[("kbg", sl)])
                    tr.op("dve", lambda pk3=pk3, hsl=hsl: nc.vector.tensor_tensor(
                        out=kdec[:, hsl, :], in0=pk3,
                        in1=sm[0:64, 3, hsl].unsqueeze(2).to_broadcast([64, 4, 128]), op=ALU.mult),
                        r=[("ps", bk), ("sm", sl)], w=[("kdec", sl)])
                    tr.op("dve", lambda pv3=pv3, hsl=hsl: nc.vector.tensor_tensor(
                        out=vb[:, hsl, :], in0=pv3,
                        in1=beta[:, c, hsl].unsqueeze(2).to_broadcast([64, 4, 128]), op=ALU.mult),
                        r=[("ps", bv), ("beta", "t", n % 2)], w=[("vb", sl)])
                for half in range(2):
                    yield
                    bu, pu = nps()
                    for hh in range(4):
                        h = half * 4 + hh
                        tr.op("pe", lambda h=h, hh=hh: nc.tensor.matmul(
                            pu[0:64, hh * 128:(hh + 1) * 128], TT[:, h * 64:(h + 1) * 64], vb[:, h, :],
                            start=True, stop=True), r=[TK, ("vb", sl)], w=[("ps", bu)], inc=(hh == 3))
                    tr.op("act", lambda half=half, pu=pu: nc.scalar.copy(
                        out=usb[:, half * 4:half * 4 + 4, :].rearrange("p h d -> p (h d)"), in_=pu[0:64, :]),
                        r=[("ps", bu)], w=[("usb", sl)])
                yield
                bw, pw = nps()
                for h in range(H):
                    tr.op("pe", lambda h=h: nc.tensor.matmul(pw[:, h * 64:(h + 1) * 64], kbg[:, h, :],
                                                             TT[:, h * 64:(h + 1) * 64], start=True, stop=True),
                          r=[TK, ("kbg", sl)], w=[("ps", bw)], inc=(h == H - 1))
                tr.op("act", lambda: nc.scalar.copy(out=wT[:].rearrange("p h i -> p (h i)"), in_=pw[:, :]),
                      r=[("ps", bw)], w=[("wT", sl)])
                for half in range(2):
                    bws, pws = nps()
                    for hh in range(4):
                        h = half * 4 + hh
                        tr.op("pe", lambda h=h, hh=hh: nc.tensor.matmul(
                            pws[0:64, hh * 128:(hh + 1) * 128], wT[:, h, :], Sb[:, h, :], start=True,
                            stop=True), r=[("wT", sl), "Sb"], w=[("ps", bws)], inc=(hh == 3))
                    tr.op("dve", lambda half=half, pws=pws: nc.vector.tensor_tensor(
                        out=vnew[:, half * 4:half * 4 + 4, :].rearrange("p h d -> p (h d)"),
                        in0=usb[:, half * 4:half * 4 + 4, :].rearrange("p h d -> p (h d)"), in1=pws[0:64, :],
                        op=ALU.subtract), r=[("ps", bws), ("usb", sl)], w=[("vnew", half, sl)])
                for half in range(2):
                    bo, po = nps()
                    for hh in range(4):
                        h = half * 4 + hh
                        tr.op("pe", lambda h=h, hh=hh: nc.tensor.matmul(
                            po[0:64, hh * 128:(hh + 1) * 128], qg[:, h, :], Sb[:, h, :], start=True,
                            stop=False), r=[("qg", sl), "Sb"], w=[("ps", bo)], inc=False)
                        tr.op("pe", lambda h=h, hh=hh: nc.tensor.matmul(
                            po[0:64, hh * 128:(hh + 1) * 128], attn[:, h * 64:(h + 1) * 64], vnew[:, h, :],
                            start=False, stop=True), r=[("attn", sl), ("vnew", half, sl)], w=[("ps", bo)],
                            inc=(hh == 3))
                    osl = osb[:, half * 4:half * 4 + 4, :].rearrange("p h d -> p (h d)")
                    if dr == 0:
                        tr.op("act", lambda osl=osl, po=po: nc.scalar.copy(out=osl, in_=po[0:64, :]),
                              r=[("ps", bo)], w=[("osb", half, sl)])
                    else:
                        if half == 0:
                            tr.dma("sp", ofw[:].rearrange("p h d -> p (h d)"), OF[gtok:gtok + C, :],
                                   r=[("OF", gtok)], w=[("ofw", sl)], sem=("ofw", sl))
                            tr.dma("sp", zsb[:].rearrange("p h d -> p (h d)"), ZS[gtok:gtok + C, :],
                                   w=[("zsb", sl)], sem=("zsb", sl))
                        tr.op("dve", lambda osl=osl, po=po, half=half: nc.vector.tensor_tensor(
                            out=osl, in0=po[0:64, :],
                            in1=ofw[:, half * 4:half * 4 + 4, :].rearrange("p h d -> p (h d)"), op=ALU.add),
                            r=[("ps", bo), ("ofw", sl)], w=[("osb", half, sl)])
                for half in range(2):
                    bs, pS = nps()
                    for hh in range(4):
                        h = half * 4 + hh
                        tr.op("pe", lambda h=h, hh=hh: nc.tensor.matmul(
                            pS[:, hh * 128:(hh + 1) * 128], kdec[:, h, :], vnew[:, h, :], start=True,
                            stop=True), r=[("kdec", sl), ("vnew", half, sl)], w=[("ps", bs)], inc=(hh == 3))
                    hsl = slice(half * 4, half * 4 + 4)
                    tr.op("pool", lambda hsl=hsl: nc.gpsimd.tensor_tensor(
                        out=S[:, hsl, :], in0=S[:, hsl, :],
                        in1=sm[:, 5, hsl].unsqueeze(2).to_broadcast([128, 4, 128]), op=ALU.mult),
                        r=[("sm", sl), "S"], w=["S"])
                    tr.op("dve", lambda hsl=hsl, pS=pS: nc.vector.tensor_tensor(
                        out=S[:, hsl, :].rearrange("p h d -> p (h d)"),
                        in0=S[:, hsl, :].rearrange("p h d -> p (h d)"), in1=pS[:, :], op=ALU.add),
                        r=[("ps", bs), "S"], w=["S"])
                tr.op("act", lambda: nc.scalar.copy(out=Sb[:], in_=S[:]), r=["S"], w=["Sb"])
                if dr == 0:
                    tr.dma("sp", OF[gtok:gtok + C, :], osb[:].rearrange("p h d -> p (h d)"),
                           r=[("osb", 0, sl), ("osb", 1, sl)], w=[("OF", gtok)], sem=("osb", sl))
                else:
                    for h in range(H):
                        hf = h // 4
                        tr.op("pool", lambda h=h: nc.gpsimd.tensor_tensor(
                            out=ysq[:, h, :], in0=osb[:, h, :], in1=osb[:, h, :], op=ALU.mult),
                            r=[("osb", hf, sl)], w=[("ysq", h, sl)])
                        tr.op("dve", lambda h=h: nc.vector.tensor_reduce(
                            out=ss[:, 8 + h:9 + h], in_=ysq[:, h, :], axis=AX.X, op=ALU.add),
                            r=[("ysq", h, sl)], w=[("ss", sl)])
                    tr.op("act", lambda: nc.scalar.activation(out=ss2[:, 8:16], in_=ss[:, 8:16], func=AF.Sqrt,
                                                              bias=epsd[0:64, 0:1], scale=1.0 / 128),
                          r=[("ss", sl), "epsd"], w=[("ss2", sl)])
                    tr.op("dve", lambda: nc.vector.reciprocal(out=ss3[:, 8:16], in_=ss2[:, 8:16]),
                          r=[("ss2", sl)], w=[("ss3", sl)])
                    for h in range(H):
                        hf = h // 4
                        tr.op("dve", lambda h=h: nc.vector.scalar_tensor_tensor(
                            out=ysq[:, h, :], in0=osb[:, h, :], scalar=ss3[:, 8 + h:9 + h], in1=tp[0:64, 32:160],
                            op0=ALU.mult, op1=ALU.mult), r=[("osb", hf, sl), ("ss3", sl), "tp"], w=[("ysq", h, sl)],
                            strict=True)
                        tr.op("dve", lambda h=h: nc.vector.tensor_tensor(
                            out=yb[:, h, :], in0=ysq[:, h, :], in1=zsb[:, h, :], op=ALU.mult),
                            r=[("ysq", h, sl), ("zsb", sl)], w=[("yb", hf, sl)])
                    yield
                    by, py = nps()
                    for h in range(H):
                        tr.op("pe", lambda h=h: nc.tensor.matmul(py[:, h * 64:(h + 1) * 64], yb[:, h, :],
                                                                 ID64B, start=True, stop=True),
                              r=[("yb", 0, sl), ("yb", 1, sl), "csb"], w=[("ps", by)], inc=(h == H - 1))
                    tr.op("act", lambda: nc.scalar.copy(out=ybT[:, :, tok],
                                                        in_=py[:, :].rearrange("p (h i) -> p h i", h=8)),
                          r=[("ps", by)], w=["ybT"])
                if last and dr == 1:
                    tr.dma("sp", YBT[:, n * NT:(n + 1) * NT].rearrange("(h p) n -> p h n", p=128), ybT[:],
                           r=["ybT"], sem="ybT")

            for dr in range(2):
                tiles = list(range(ntile)) if dr == 0 else list(range(ntile - 1, -1, -1))
                facs = []
                for n in tiles:
                    chunks = list(range(nck)) if dr == 0 else list(range(nck - 1, -1, -1))
                    for ci, c in enumerate(chunks):
                        facs.append(lambda sl, dr=dr, n=n, c=c, ci=ci: chunk_gen(dr, n, c, ci == 0, ci == nck - 1, sl))
                run_streams(facs, WG, stagger=14)
                if dr == 0:
                    tr.dma("sp", sx_in.ap(), S[:].rearrange("p h d -> p (h d)"), r=["S"], w=["sx_in"], sem="sx")
                    tr.cc(lambda: nc.gpsimd.collective_compute("AllGather", ALU.bypass, replica_groups=PAIRS,
                                                                ins=[sx_in.ap()], outs=[sx_out.ap()]),
                          r=["sx_in"], w=["sx_out"], sem="sxcc")
                    tr.dma("sp", SX, sx_out.ap().rearrange("(r p) n -> p r n", p=128), r=["sx_out"], w=["SX", "ybT"],
                           sem="sx")
                    Sf = S[:].rearrange("p h d -> p (h d)")
                    tr.op("dve", lambda: nc.vector.tensor_scalar(out=Sf, in0=SX[:, 0, :], scalar1=selt[:, 0:1],
                                                                 scalar2=None, op0=ALU.mult),
                          r=["SX", "selt"], w=["S"])
                    tr.op("dve", lambda: nc.vector.scalar_tensor_tensor(out=Sf, in0=SX[:, 1, :], scalar=selt[:, 1:2],
                                                                        in1=Sf, op0=ALU.mult, op1=ALU.add),
                          r=["SX", "selt", "S"], w=["S"])
                    tr.op("act", lambda: nc.scalar.copy(out=Sb[:], in_=S[:]), r=["S"], w=["Sb"])
            tr.barrier()

        with ExitStack() as st5:
            ring_phase(st5, [b for _ in range(ntile) for b in range(OFF_UP2, OFF_UP2 + NB_UP)])
            P = ffn_pool(st5, wdn2, nxt=1)
            wb = sb("wb", [128, 8, D], BF16, st5)
            wo = sb("wo", [128, 8, D], BF16, st5)
            ybt = P["u"]
            mrg = P["hid"]
            gsb = [sb("gsb%d" % i, [128, NT], F32, st5) for i in range(2)]
            pab = [sb("pab%d" % i, [128, NT], F32, st5) for i in range(2)]
            tr.dma("pool", wb[:], wbout.rearrange("(k p) n -> p k n", p=128), w=["wb"], sem="wb")
            tr.dma("pool", wo[:], wout.rearrange("(k p) n -> p k n", p=128), w=["wo"], sem="wo")
            for n in range(ntile):
                xt = P["xt"][0]
                xkey = "xt0"
                c0 = n * NT
                load_tile(xt, H1T, c0, xkey)
                tr.dma("sp", ybt[:], YBT[:, c0:c0 + NT].rearrange("(h p) n -> p h n", p=128),
                       w=[("u", k) for k in range(8)], sem="ybt")
                for m in range(8):
                    g_ = gsb[m % 2]
                    gk = ("gsb", m % 2)
                    pa_ = pab[m % 2]
                    pk2 = ("pab", m % 2)
                    tr.dma("sp", g_[:], PT[(44 + m) * 128:(45 + m) * 128, c0 + 2:c0 + 2 + NT], w=[gk], sem=gk)
                    tr.dma("sp", pa_[:], PAT[m * 128:(m + 1) * 128, c0:c0 + NT], w=[pk2], sem=pk2)
                    tr.op("act", lambda g_=g_: nc.scalar.activation(out=g_[:], in_=g_[:], func=AF.Sigmoid),
                          r=[gk], w=[gk])
                    bi, pt = nps()
                    for kk in range(8):
                        tr.op("pe", lambda kk=kk, m=m, pt=pt: nc.tensor.matmul(
                            pt[:], wb[:, kk, m * 128:(m + 1) * 128], ybt[:, kk, :], start=(kk == 0), stop=(kk == 7)),
                            r=["wb", ("u", kk)], w=[("ps", bi)], inc=(kk == 7))
                    tr.op("dve", lambda g_=g_, pt=pt: nc.vector.tensor_tensor(out=g_[:], in0=pt[:], in1=g_[:],
                                                                              op=ALU.mult),
                          r=[("ps", bi), gk], w=[gk])
                    tr.op("pool", lambda g_=g_, pa_=pa_, m=m: nc.gpsimd.tensor_tensor(out=mrg[:, m, :], in0=g_[:],
                                                                                      in1=pa_[:], op=ALU.add),
                          r=[gk, pk2], w=[("hid", m)])
                for m in range(8):
                    bi, pt = nps()
                    for kk in range(8):
                        tr.op("pe", lambda kk=kk, m=m, pt=pt: nc.tensor.matmul(
                            pt[:], wo[:, kk, m * 128:(m + 1) * 128], mrg[:, kk, :], start=(kk == 0), stop=(kk == 7)),
                            r=["wo", ("hid", kk)], w=[("ps", bi)], inc=(kk == 7))
                    tr.op("dve", lambda m=m, pt=pt: nc.vector.scalar_tensor_tensor(
                        out=xt[:, m, :], in0=pt[:], scalar=der[:, 32 + m:33 + m], in1=xt[:, m, :], op0=ALU.mult,
                        op1=ALU.add), r=[("ps", bi), "der"], w=[(xkey, m)])
                ffn_body(P, xt, xkey, 2, P["wdn"], True, outT, c0)
            tr.barrier()
        tr.finish()
    return nc


def _blocks(w, cols_list):
    out = np.empty((len(cols_list), 128, 2048), np.float32)
    w3 = w.reshape(8, 128, -1)
    for b, cols in enumerate(cols_list):
        out[b] = w3[:, :, cols].transpose(1, 0, 2).reshape(128, 2048)
    return out


def _consts():
    c = np.zeros((128, 4096), np.float32)
    c[:, 0:128] = 1.0
    c[0:64, 128:192] = np.eye(64)
    j = np.arange(64)[:, None]
    i = np.arange(64)[None, :]
    c[0:64, 192:256] = (j <= i)
    c[0:64, 256:320] = (j >= i)
    capS_F = np.where(j < i, 0.0, NEG)
    capI_F = np.where(j <= i, 0.0, NEG)
    capS_B = np.where(j > i, 0.0, NEG)
    capI_B = np.where(j >= i, 0.0, NEG)
    c[0:64, 320:832] = np.tile(capS_F, (1, 8))
    c[0:64, 832:1344] = np.tile(capI_F, (1, 8))
    c[0:64, 1344:1856] = np.tile(capS_B, (1, 8))
    c[0:64, 1856:2368] = np.tile(capI_B, (1, 8))
    c[0:64, 2368:2880] = np.tile(np.eye(64), (1, 8))
    e8 = np.zeros((8, 8, 64), np.float32)
    for h in range(8):
        e8[h, h, :] = 1.0
    c[0:8, 2880:3392] = e8.reshape(8, 512)
    c[:, 3392:3520] = np.eye(128)
    return c


_PROG = {}


def _run(inputs, T, n_cores):
    f = lambda a: np.ascontiguousarray(np.asarray(a, dtype=np.float32))
    x = f(inputs["x"])
    B = x.shape[0]
    cvec = f(inputs["c"])
    w_ada = f(inputs["w_ada"])[0]
    b_ada = f(inputs["b_ada"])[0]
    w_in = f(inputs["w_in"])[0]
    wup1 = f(inputs["w_ffn1_up"])[0]
    wup2 = f(inputs["w_ffn2_up"])[0]
    conv_a = f(inputs["conv_a"])[0]
    conv_dn = f(inputs["conv_dn"])[0]
    cst = _consts()
    ii = np.arange(64)[:, None]
    jj = np.arange(64)[None, :]
    mks = [(ii // 8 == jj // 8)]
    for bsz in (8, 16, 32):
        mks.append((ii // (2 * bsz) == jj // (2 * bsz)) & (ii // bsz != jj // bsz))
    cst2 = np.concatenate([np.tile(m.astype(np.float32), (1, 8)) for m in mks], axis=1)

    ada_cols = [np.arange(b * 256, (b + 1) * 256) for b in range(NB_ADA)]
    up_cols = [np.concatenate([np.arange(j * 128, (j + 1) * 128), np.arange(DFF + j * 128, DFF + (j + 1) * 128)])
               for j in range(NJ)]
    fm = np.concatenate([np.arange(512, 1024), np.arange(1024, 1536), np.arange(1536, 4608),
                         np.arange(0, 512), np.arange(5664, 7712)])
    in_cols = [fm[b * 256:(b + 1) * 256] for b in range(NB_IN)]
    z_cols = [np.arange(4608 + b * 256, 4608 + (b + 1) * 256) for b in range(NB_Z)]
    wr_common = np.concatenate([_blocks(w_ada, ada_cols), _blocks(wup1, up_cols), _blocks(w_in, in_cols + z_cols),
                                _blocks(wup2, up_cols)], axis=0)
    assert wr_common.shape[0] == NB_TOT

    ba_f = np.arange(5632, 5664)
    ba_b = np.concatenate([np.arange(5640, 5648), np.arange(5632, 5640), np.arange(5656, 5664),
                           np.arange(5648, 5656)])
    norms = np.stack([f(inputs["norm_ffn1"])[0], f(inputs["norm_mix"])[0], f(inputs["norm_ffn2"])[0],
                      f(inputs["norm_final"])], 0)
    in_maps = []
    for core in range(n_cores):
        b, half = core // 2, core % 2
        xs = x[b, half * T:(half + 1) * T, :]
        if half == 1:
            xs = xs[::-1]
        vecs = np.zeros((128, 120), np.float32)
        vecs[:, 0:8] = cvec[b].reshape(8, 128).T
        vecs[:, 8:80] = b_ada.reshape(72, 128).T
        vecs[:, 80:112] = norms.reshape(4, 8, 128).transpose(2, 0, 1).reshape(128, 32)
        cdn = conv_dn if half == 0 else conv_dn[::-1]
        ca = conv_a if half == 0 else conv_a[::-1]
        convw = np.zeros((128, 132), np.float32)
        convw[:, 0:120] = cdn.T.reshape(24, 128, 5).transpose(1, 0, 2).reshape(128, 120)
        convw[:, 120:132] = ca.T.reshape(4, 128, 3).transpose(1, 0, 2).reshape(128, 12)
        names = ["a_log_fwd", "a_log_bwd", "dt_bias_fwd", "dt_bias_bwd"]
        if half == 1:
            names = ["a_log_bwd", "a_log_fwd", "dt_bias_bwd", "dt_bias_fwd"]
        tokp = np.zeros((128, 160), np.float32)
        for q_, nm in enumerate(names):
            tokp[:, q_ * 8:(q_ + 1) * 8] = f(inputs[nm])[0][None, :]
        tokp[:, 32:160] = f(inputs["dn_norm"])[0][None, :]
        selv = np.zeros((128, 2), np.float32)
        selv[:, 1 - half] = 1.0
        in_maps.append({
            "xT": np.ascontiguousarray(xs.T),
            "wr": wr_common,
            "wdn1": f(inputs["w_ffn1_down"])[0],
            "wdn2": f(inputs["w_ffn2_down"])[0],
            "wba": np.ascontiguousarray(w_in[:, ba_f if half == 0 else ba_b]),
            "waout": f(inputs["w_a_out"])[0],
            "wbout": f(inputs["w_b_out"])[0],
            "wout": f(inputs["w_out"])[0],
            "vecs": vecs, "convw": convw, "tokp": tokp, "cst": cst, "cst2": cst2, "sel": selv,
        })
    if T not in _PROG:
        _PROG[T] = build_program(T)
    res = run_bass_kernel_spmd(_PROG[T], in_maps, core_ids=list(range(n_cores)))
    if DEBUG:
        DBG["res"] = res.results
    out = np.empty((B, 2 * T, D), np.float32)
    for core in range(n_cores):
        b, half = core // 2, core % 2
        o = res.results[core]["outT"].T
        if half == 1:
            o = o[::-1]
        out[b, half * T:(half + 1) * T, :] = o
    return out


def kernel(**inputs):
    x = np.asarray(inputs["x"])
    B, S, _ = x.shape
    return _run(inputs, S // 2, 2 * B)
```

```python
import numpy as np
from contextlib import ExitStack
import concourse.bass as bass
import concourse.mybir as mybir
from concourse.bass_utils import run_bass_kernel_spmd

F32 = mybir.dt.float32
BF16 = mybir.dt.bfloat16
AF = mybir.ActivationFunctionType
ALU = mybir.AluOpType
AX = mybir.AxisListType

D = 1024
DFF = 2816
NJ = DFF // 128
H = 8
C = 64
NT = 512
EPS = 1e-6
NR = 4
NEG = -30000.0
DEBUG = False
SAME_ENGINE_SYNC = True
DBG = {}

NB_ADA = 36
NB_UP = 22
NB_IN = 26
NB_Z = 4
OFF_ADA = 0
OFF_UP1 = OFF_ADA + NB_ADA
OFF_IN = OFF_UP1 + NB_UP
OFF_Z = OFF_IN + NB_IN
OFF_UP2 = OFF_Z + NB_Z
NB_TOT = OFF_UP2 + NB_UP
NPC = 52
NCONV = 32


class Tick:
    __slots__ = ("kind", "key", "val")

    def __init__(self, kind, key, val):
        self.kind, self.key, self.val = kind, key, val


class Tr:
    def __init__(self, nc, es):
        self.nc = nc
        self.es = es
        self.E = {"pe": nc.tensor, "act": nc.scalar, "dve": nc.vector, "pool": nc.gpsimd, "sp": nc.sync}
        self.csem = {e: es.enter_context(nc.semaphore("c_" + e)) for e in ("pe", "act", "dve", "pool")}
        self.ccnt = {e: 0 for e in self.csem}
        self.pend = {e: [] for e in self.csem}
        self.dsem = {}
        self.dcnt = {}
        self.seen = {e: {} for e in self.E}
        self.st = {}

    def _wait(self, eng, t, same_ok=True, waw=False):
        if t is None:
            return
        if t.kind == "c":
            if t.key == eng and (t.val is None or (same_ok and (waw or not SAME_ENGINE_SYNC))):
                return
            assert t.val is not None, "pending tick waited on"
            sem, v, k = self.csem[t.key], t.val, ("c", t.key)
        else:
            sem, v, k = self.dsem[t.key], self.dcnt[t.key], ("d", t.key)
        if self.seen[eng].get(k, 0) >= v:
            return
        self.seen[eng][k] = v
        self.E[eng].wait_ge(sem, v)

    def _deps(self, eng, r, w, same_ok):
        for k in r:
            s = self.st.get(k)
            if s:
                self._wait(eng, s[0], same_ok)
        for k in w:
            s = self.st.get(k)
            if s:
                self._wait(eng, s[0], same_ok, waw=True)
                for t in s[1].values():
                    self._wait(eng, t, same_ok, waw=True)

    def _upd(self, t, r, w):
        for k in r:
            self.st.setdefault(k, [None, {}])[1][(t.kind, t.key)] = t
        for k in w:
            self.st[k] = [t, {}]

    def op(self, eng, fn, r=(), w=(), inc=True, strict=False):
        self._deps(eng, r, w, not strict)
        ins = fn()
        t = Tick("c", eng, None)
        self.pend[eng].append(t)
        if inc:
            self.ccnt[eng] += 1
            ins.then_inc(self.csem[eng], 1)
            for p in self.pend[eng]:
                p.val = self.ccnt[eng]
            self.pend[eng] = []
        self._upd(t, r, w)
        return ins

    def dma(self, q, out, in_, r=(), w=(), sem=None):
        self._deps(q, r, w, False)
        if sem not in self.dsem:
            self.dsem[sem] = self.es.enter_context(self.nc.semaphore("d%d" % len(self.dsem)))
            self.dcnt[sem] = 0
        ins = self.E[q].dma_start(out=out, in_=in_)
        self.dcnt[sem] += 16
        ins.then_inc(self.dsem[sem], 16)
        t = Tick("d", sem, self.dcnt[sem])
        self._upd(t, r, w)
        return ins

    def cc(self, ins_fn, r=(), w=(), sem=None):
        self._deps("pool", r, w, False)
        if sem not in self.dsem:
            self.dsem[sem] = self.es.enter_context(self.nc.semaphore("d%d" % len(self.dsem)))
            self.dcnt[sem] = 0
        ins = ins_fn()
        self.dcnt[sem] += 1
        ins.then_inc(self.dsem[sem], 1)
        t = Tick("d", sem, self.dcnt[sem])
        self._upd(t, r, w)

    def barrier(self):
        for e in self.csem:
            assert not self.pend[e]
        for e in self.E:
            for f in self.csem:
                if f != e and self.ccnt[f] > 0:
                    self._wait(e, Tick("c", f, self.ccnt[f]))
            for k in self.dsem:
                if self.dcnt[k] > 0:
                    self._wait(e, Tick("d", k, self.dcnt[k]))
        self.st = {}

    def finish(self):
        for k in self.dsem:
            if self.dcnt[k] > 0:
                self._wait("sp", Tick("d", k, self.dcnt[k]))


class Ring:
    def __init__(self, tr, nc, slots, wr):
        self.tr, self.nc, self.slots, self.wr = tr, nc, slots, wr
        self.plan = []
        self.issued = 0
        self.pos = 0

    def add(self, blocks):
        self.plan.extend(blocks)

    def _issue(self):
        b = self.plan[self.issued]
        s = self.issued % NR
        self.tr.dma("pool", self.slots[s][:], self.wr[b], w=[("ring", s)], sem=("ring", s))
        self.issued += 1

    def get(self):
        while self.issued < min(len(self.plan), self.pos + NR):
            self._issue()
        s = self.pos % NR
        self.pos += 1
        return s, self.slots[s]


def build_program(T):
    ntile = T // NT
    nchunk = T // C
    nc = bass.Bass("TRN2", target_bir_lowering=False)

    def din(name, shape, dt=F32):
        return nc.dram_tensor(name, list(shape), dt, kind="ExternalInput").ap()

    xT = din("xT", [D, T])
    wr = din("wr", [NB_TOT, 128, 2048])
    wdn1 = din("wdn1", [DFF, D])
    wdn2 = din("wdn2", [DFF, D])
    wba = din("wba", [D, 32])
    waout = din("waout", [512, D])
    wbout = din("wbout", [D, D])
    wout = din("wout", [D, D])
    vecs = din("vecs", [128, 120])
    convw = din("convw", [128, 24 * 5 + 4 * 3])
    tokp = din("tokp", [128, 32 + 128])
    cst = din("cst", [128, 4096])
    cst2 = din("cst2", [64, 2048])
    sel = din("sel", [128, 2])
    outT = nc.dram_tensor("outT", [D, T], F32, kind="ExternalOutput").ap()

    def dscr(name, shape, dt=F32):
        if DEBUG:
            return nc.dram_tensor(name, list(shape), dt, kind="ExternalOutput").ap()
        return nc.dram_tensor(name, list(shape), dt).ap()

    H1T = dscr("H1T", [D, T])
    PT = dscr("PT", [NPC * 128, T + 4])
    ZS = dscr("ZS", [T, D])
    BAt = dscr("BAt", [T, 32])
    QKVT = dscr("QKVT", [3 * D, T], BF16)
    PAT = dscr("PAT", [D, T])
    OF = dscr("OF", [T, D])
    YBT = dscr("YBT", [D, T], BF16)
    hx_in = nc.dram_tensor("hx_in", [NCONV * 128, 2], F32)
    hx_out = nc.dram_tensor("hx_out", [2 * NCONV * 128, 2], F32)
    sx_in = nc.dram_tensor("sx_in", [128, H * 128], F32)
    sx_out = nc.dram_tensor("sx_out", [256, H * 128], F32)
    PAIRS = [[0, 1], [2, 3], [4, 5], [6, 7]]

    with ExitStack() as es:
        tr = Tr(nc, es)

        sbn = [0]

        def sb(name, shape, dt=F32, stack=None):
            sbn[0] += 1
            return (stack or es).enter_context(nc.sbuf_tensor("%s_%d" % (name, sbn[0]), list(shape), dt))

        ps = [es.enter_context(nc.psum_tensor("ps%d" % i, [128, 512], F32)) for i in range(8)]
        psi = [0]

        def nps():
            for _ in range(8):
                i = psi[0] % 8
                psi[0] += 1
                s_ = tr.st.get(("ps", i))
                if s_ is None or s_[0] is None or len(s_[1]) > 0:
                    return i, ps[i]
            raise RuntimeError("all PSUM banks hold unconsumed results")

        ring = Ring(tr, nc, None, wr)

        def ring_phase(stack, blocks):
            assert ring.issued == len(ring.plan) and ring.pos == len(ring.plan)
            ring.slots = [sb("ring", [128, 2048], BF16, stack) for i in range(NR)]
            ring.add(blocks)

        vec = sb("vec", [128, 120])
        cw = sb("cw", [128, 132])
        tp = sb("tp", [128, 160])
        cs = sb("cs", [128, 4096])
        csb = sb("csb", [128, 1024], BF16)
        selt = sb("selt", [128, 2])
        modT = sb("modT", [128, 72])
        der = sb("der", [128, 64])
        cact = sb("cact", [128, 8], BF16)
        negA = sb("negA", [128, 16])
        tr.dma("sp", vec[:], vecs, w=["vec"], sem="c0")
        tr.dma("sp", cw[:], convw, w=["cw"], sem="c0")
        tr.dma("sp", tp[:], tokp, w=["tp"], sem="c0")
        tr.dma("sp", cs[:], cst, w=["cs"], sem="c0")
        tr.dma("sp", selt[:], sel, w=["selt"], sem="c0")
        ONES = cs[:, 0:128]
        ID64 = cs[0:64, 128:192]
        LTRI = [cs[0:64, 192:256], cs[0:64, 256:320]]
        CAPS = [cs[0:64, 320:832], cs[0:64, 1344:1856]]
        CAPI = [cs[0:64, 832:1344], cs[0:64, 1856:2368]]
        I8 = cs[0:64, 2368:2880]
        E8 = cs[0:8, 2880:3392]
        tr.op("dve", lambda: nc.vector.tensor_copy(out=csb[:, 0:256], in_=cs[:, 0:256]), r=["cs"], w=["csb"])
        tr.op("dve", lambda: nc.vector.tensor_copy(out=csb[:, 256:384], in_=cs[:, 3392:3520]), r=["cs"], w=["csb"])
        ONESB = csb[:, 0:128]
        ID64B = csb[0:64, 128:192]
        ID128B = csb[:, 256:384]

        tr.op("act", lambda: nc.scalar.activation(out=cact[:], in_=vec[:, 0:8], func=AF.Silu), r=["vec"], w=["cact"])
        st0 = ExitStack()
        ring_phase(st0, list(range(OFF_ADA, OFF_ADA + NB_ADA)))
        bi, pb = nps()
        for blk in range(NB_ADA):
            s, slot = ring.get()
            sv = slot[:].rearrange("p (k c) -> p k c", k=8)
            for cc in range(2):
                j = blk * 2 + cc
                for k in range(8):
                    tr.op("pe", lambda k=k, j=j, cc=cc: nc.tensor.matmul(
                        pb[:, j:j + 1], sv[:, k, cc * 128:(cc + 1) * 128], cact[:, k:k + 1],
                        start=(k == 0), stop=(k == 7)),
                        r=[("ring", s), "cact"], w=[("ps", bi)], inc=(k == 7 and cc == 1))
        tr.op("dve", lambda: nc.vector.tensor_tensor(out=modT[:], in0=pb[:, 0:72], in1=vec[:, 8:80], op=ALU.add),
              r=[("ps", bi), "vec"], w=["modT"])
        for s in range(3):
            tr.op("dve", lambda s=s: nc.vector.scalar_tensor_tensor(
                out=der[:, s * 8:(s + 1) * 8], in0=modT[:, (3 * s + 1) * 8:(3 * s + 2) * 8], scalar=1.0,
                in1=vec[:, 80 + s * 8:88 + s * 8], op0=ALU.add, op1=ALU.mult), r=["modT", "vec"], w=["der"])
            gsc = 1.0 if s == 1 else 0.5
            tr.op("dve", lambda s=s, gsc=gsc: nc.vector.tensor_scalar(
                out=der[:, 24 + s * 8:32 + s * 8], in0=modT[:, (3 * s + 2) * 8:(3 * s + 3) * 8],
                scalar1=gsc, scalar2=None, op0=ALU.mult), r=["modT"], w=["der"])
        NF = vec[:, 104:112]
        tr.op("act", lambda: nc.scalar.activation(out=negA[:], in_=tp[:, 0:16], func=AF.Exp), r=["tp"], w=["negA"])
        tr.op("dve", lambda: nc.vector.tensor_scalar(out=negA[:], in0=negA[:], scalar1=-1.0, scalar2=None,
                                                     op0=ALU.mult), r=["negA"], w=["negA"])

        tr.barrier()
        st0.close()

        def load_tile(dst, src, t0, key, n=NT, coff=0):
            for i in range(8):
                tr.dma("sp", dst[:, i, coff:coff + n], src[i * 128:(i + 1) * 128, t0:t0 + n],
                       w=[(key, i)], sem=(key,))

        def norm_mod(P, xt, xkey, u, s_idx):
            bi, pss = nps()
            for i in range(8):
                sq = P["sq"][i % 2]
                tr.op("act", lambda i=i, sq=sq: nc.scalar.activation(out=sq[:], in_=xt[:, i, :], func=AF.Square),
                      r=[(xkey, i)], w=[("sq", i % 2)])
                tr.op("pe", lambda i=i, sq=sq: nc.tensor.matmul(pss[:], ONES, sq[:], start=(i == 0), stop=(i == 7)),
                      r=[("sq", i % 2), "cs"], w=[("ps", bi)], inc=True)
            rstd = P["rstd"]
            tr.op("act", lambda: nc.scalar.activation(out=rstd[:], in_=pss[:], func=AF.Sqrt, bias=P["eps"][:, 0:1],
                                                      scale=1.0 / D), r=[("ps", bi), "eps"], w=["rstd"])
            tr.op("dve", lambda: nc.vector.reciprocal(out=rstd[:], in_=rstd[:]), r=["rstd"], w=["rstd"])
            for i in range(8):
                tt = P["tt"][i % 2]
                tr.op("dve", lambda i=i, tt=tt: nc.vector.scalar_tensor_tensor(
                    out=tt[:], in0=xt[:, i, :], scalar=der[:, s_idx * 8 + i:s_idx * 8 + i + 1], in1=rstd[:],
                    op0=ALU.mult, op1=ALU.mult), r=[(xkey, i), "rstd", "der"], w=[("tt", i % 2)])
                tr.op("act", lambda i=i, tt=tt: nc.scalar.activation(
                    out=u[:, i, :], in_=tt[:], func=AF.Identity,
                    bias=modT[:, 3 * s_idx * 8 + i:3 * s_idx * 8 + i + 1], scale=1.0),
                    r=[("tt", i % 2), "modT"], w=[("u", i)])

        def ffn_body(P, xt, xkey, s_idx, wdn, final, dst, t0):
            u, hid = P["u"], P["hid"]
            norm_mod(P, xt, xkey, u, s_idx)
            for j in range(NJ):
                s, slot = ring.get()
                sv = slot[:].rearrange("p (k c) -> p k c", k=8)
                ba_, pa = nps()
                bb_, pbb = nps()
                for part, (bix, pt) in enumerate(((ba_, pa), (bb_, pbb))):
                    for k in range(8):
                        tr.op("pe", lambda k=k, part=part, pt=pt: nc.tensor.matmul(
                            pt[:], sv[:, k, part * 128:(part + 1) * 128], u[:, k, :], start=(k == 0), stop=(k == 7)),
                            r=[("ring", s), ("u", k)], w=[("ps", bix)], inc=(k == 7))
                sl = P["s"][j % 2]
                tr.op("act", lambda sl=sl, pa=pa: nc.scalar.activation(out=sl[:], in_=pa[:], func=AF.Silu),
                      r=[("ps", ba_)], w=[("s", j % 2)])
                tr.op("dve", lambda sl=sl, pbb=pbb, j=j: nc.vector.tensor_tensor(
                    out=hid[:, j, :], in0=pbb[:], in1=sl[:], op=ALU.mult),
                    r=[("ps", bb_), ("s", j % 2)], w=[("hid", j)])
            for m in range(8):
                bd, pd = nps()
                for j in range(NJ):
                    tr.op("pe", lambda j=j, m=m, pd=pd: nc.tensor.matmul(
                        pd[:], wdn[:, j, m * 128:(m + 1) * 128], hid[:, j, :], start=(j == 0), stop=(j == NJ - 1)),
                        r=["wdn", ("hid", j)], w=[("ps", bd)], inc=(j == NJ - 1))
                tr.op("dve", lambda m=m, pd=pd: nc.vector.scalar_tensor_tensor(
                    out=xt[:, m, :], in0=pd[:], scalar=der[:, 24 + s_idx * 8 + m:24 + s_idx * 8 + m + 1],
                    in1=xt[:, m, :], op0=ALU.mult, op1=ALU.add), r=[("ps", bd), "der"], w=[(xkey, m)])
            if final:
                bi, pss = nps()
                for i in range(8):
                    sq = P["sq"][i % 2]
                    tr.op("act", lambda i=i, sq=sq: nc.scalar.activation(out=sq[:], in_=xt[:, i, :], func=AF.Square),
                          r=[(xkey, i)], w=[("sq", i % 2)])
                    tr.op("pe", lambda i=i, sq=sq: nc.tensor.matmul(pss[:], ONES, sq[:], start=(i == 0),
                                                                    stop=(i == 7)),
                          r=[("sq", i % 2), "cs"], w=[("ps", bi)], inc=True)
                rstd = P["rstd"]
                tr.op("act", lambda: nc.scalar.activation(out=rstd[:], in_=pss[:], func=AF.Sqrt,
                                                          bias=P["eps"][:, 0:1], scale=1.0 / D),
                      r=[("ps", bi), "eps"], w=["rstd"])
                tr.op("dve", lambda: nc.vector.reciprocal(out=rstd[:], in_=rstd[:]), r=["rstd"], w=["rstd"])
                for i in range(8):
                    tr.op("dve", lambda i=i: nc.vector.scalar_tensor_tensor(
                        out=xt[:, i, :], in0=xt[:, i, :], scalar=NF[:, i:i + 1], in1=rstd[:],
                        op0=ALU.mult, op1=ALU.mult), r=[(xkey, i), "rstd", "vec"], w=[(xkey, i)])
            for i in range(8):
                tr.dma("sp", dst[i * 128:(i + 1) * 128, t0:t0 + NT], xt[:, i, :], r=[(xkey, i)], sem=(xkey, "st"))

        def ffn_pool(stack, wdn_src, nxt=2):
            P = {}
            P["wdn"] = sb("wdn", [128, NJ, D], BF16, stack)
            P["xt"] = [sb("xt%d" % i, [128, 8, NT], F32, stack) for i in range(nxt)]
            P["sq"] = [sb("sq%d" % i, [128, NT], F32, stack) for i in range(2)]
            P["tt"] = [sb("tt%d" % i, [128, NT], F32, stack) for i in range(2)]
            P["s"] = [sb("s%d" % i, [128, NT], F32, stack) for i in range(2)]
            P["rstd"] = sb("rstd", [128, NT], F32, stack)
            P["u"] = sb("u", [128, 8, NT], BF16, stack)
            P["hid"] = sb("hid", [128, NJ, NT], BF16, stack)
            P["eps"] = sb("eps", [128, 1], F32, stack)
            tr.op("dve", lambda: nc.vector.memset(P["eps"][:], EPS), w=["eps"])
            wv = wdn_src.rearrange("(j p) n -> p j n", p=128)
            for jj in range(0, NJ, 2):
                tr.dma("pool", P["wdn"][:, jj:jj + 2, :], wv[:, jj:jj + 2, :], w=["wdn"], sem="wdn")
            return P

        with ExitStack() as st1:
            ring_phase(st1, [b for _ in range(ntile) for b in range(OFF_UP1, OFF_UP1 + NB_UP)])
            P = ffn_pool(st1, wdn1)
            load_tile(P["xt"][0], xT, 0, "xt0")
            for n in range(ntile):
                xt = P["xt"][n % 2]
                xkey = "xt%d" % (n % 2)
                if n + 1 < ntile:
                    load_tile(P["xt"][(n + 1) % 2], xT, (n + 1) * NT, "xt%d" % ((n + 1) % 2))
                ffn_body(P, xt, xkey, 0, P["wdn"], False, H1T, n * NT)
            tr.barrier()

        with ExitStack() as st2:
            ring_phase(st2, [b for _ in range(ntile) for b in range(OFF_IN, OFF_IN + NB_IN + NB_Z)])
            xts = [sb("xt%d" % i, [128, 8, NT], F32, st2) for i in range(2)]
            P = {"sq": [sb("sq%d" % i, [128, NT], F32, st2) for i in range(2)],
                 "tt": [sb("tt%d" % i, [128, NT], F32, st2) for i in range(2)],
                 "rstd": sb("rstd", [128, NT], F32, st2), "eps": sb("eps", [128, 1], F32, st2)}
            tr.op("dve", lambda: nc.vector.memset(P["eps"][:], EPS), w=["eps"])
            u = sb("u", [128, 8, NT], BF16, st2)
            wbas = sb("wbas", [128, 8, 32], BF16, st2)
            stg = [sb("stg%d" % i, [128, NT], F32, st2) for i in range(4)]
            zero = sb("zero", [128, NCONV, 2], F32, st2)
            tr.dma("pool", wbas[:], wba.rearrange("(k p) n -> p k n", p=128), w=["wbas"], sem="wbas")
            tr.op("dve", lambda: nc.vector.memset(zero[:], 0.0), w=["zero"])
            tr.dma("sp", PT[0:NCONV * 128, 0:2].rearrange("(c p) n -> p c n", p=128), zero[:], r=["zero"], sem="zero")
            sti = 0
            load_tile(xts[0], H1T, 0, "xt0")
            for n in range(ntile):
                xt = xts[n % 2]
                xkey = "xt%d" % (n % 2)
                if n + 1 < ntile:
                    load_tile(xts[(n + 1) % 2], H1T, (n + 1) * NT, "xt%d" % ((n + 1) % 2))
                norm_mod(P, xt, xkey, u, 1)
                for blk in range(NB_IN):
                    s, slot = ring.get()
                    sv = slot[:].rearrange("p (k c) -> p k c", k=8)
                    for cc in range(2):
                        ch = blk * 2 + cc
                        bi, pt = nps()
                        for k in range(8):
                            tr.op("pe", lambda k=k, cc=cc, pt=pt: nc.tensor.matmul(
                                pt[:], sv[:, k, cc * 128:(cc + 1) * 128], u[:, k, :], start=(k == 0), stop=(k == 7)),
                                r=[("ring", s), ("u", k)], w=[("ps", bi)], inc=(k == 7))
                        sg = stg[sti % 4]
                        sk = ("stg", sti % 4)
                        sti += 1
                        tr.op("act", lambda sg=sg, pt=pt: nc.scalar.copy(out=sg[:], in_=pt[:]), r=[("ps", bi)], w=[sk])
                        tr.dma("sp", PT[ch * 128:(ch + 1) * 128, 2 + n * NT:2 + (n + 1) * NT], sg[:], r=[sk], sem=sk)
                for zb in range(NB_Z):
                    s, slot = ring.get()
                    sv = slot[:].rearrange("p (k c) -> p k c", k=8)
                    for tb in range(0, 4, 2):
                        bi, pt = nps()
                        for t2 in range(2):
                            for k in range(8):
                                tr.op("pe", lambda k=k, t2=t2, tb=tb, pt=pt: nc.tensor.matmul(
                                    pt[:, t2 * 256:(t2 + 1) * 256], u[:, k, (tb + t2) * 128:(tb + t2 + 1) * 128],
                                    sv[:, k, :], start=(k == 0), stop=(k == 7)),
                                    r=[("ring", s), ("u", k)], w=[("ps", bi)], inc=(k == 7))
                        sg = stg[sti % 4]
                        sk = ("stg", sti % 4)
                        sti += 1
                        tr.op("act", lambda sg=sg, pt=pt: nc.scalar.activation(out=sg[:], in_=pt[:], func=AF.Silu),
                              r=[("ps", bi)], w=[sk])
                        for t2 in range(2):
                            r0 = n * NT + (tb + t2) * 128
                            tr.dma("sp", ZS[r0:r0 + 128, zb * 256:(zb + 1) * 256], sg[:, t2 * 256:(t2 + 1) * 256],
                                   r=[sk], sem=sk)
                bi, pt = nps()
                for tb in range(4):
                    for k in range(8):
                        tr.op("pe", lambda k=k, tb=tb, pt=pt: nc.tensor.matmul(
                            pt[:, tb * 32:(tb + 1) * 32], u[:, k, tb * 128:(tb + 1) * 128], wbas[:, k, :],
                            start=(k == 0), stop=(k == 7)), r=["wbas", ("u", k)], w=[("ps", bi)], inc=(k == 7))
                sg = stg[sti % 4]
                sk = ("stg", sti % 4)
                sti += 1
                tr.op("act", lambda sg=sg, pt=pt: nc.scalar.copy(out=sg[:, 0:128], in_=pt[:, 0:128]),
                      r=[("ps", bi)], w=[sk])
                tr.dma("sp", BAt[n * NT:(n + 1) * NT, :].rearrange("(b p) n -> p b n", p=128),
                       sg[:, 0:128].rearrange("p (b n) -> p b n", b=4), r=[sk], sem=sk)
            tr.barrier()
            hx = sb("hx", [128, 2, NCONV, 2], F32, st2)
            hy = sb("hy", [128, NCONV, 2], F32, st2)
            tr.dma("pool", hx_in.ap(), PT[0:NCONV * 128, T:T + 2], w=["hx_in"], sem="hx")
            tr.cc(lambda: nc.gpsimd.collective_compute("AllGather", ALU.bypass, replica_groups=PAIRS,
                                                        ins=[hx_in.ap()], outs=[hx_out.ap()]),
                  r=["hx_in"], w=["hx_out"], sem="hxcc")
            tr.dma("pool", hx[:], hx_out.ap().rearrange("(r c p) n -> p r c n", r=2, p=128), r=["hx_out"], w=["hx"],
                   sem="hx")
            tr.op("dve", lambda: nc.vector.tensor_scalar(out=hy[:], in0=hx[:, 0], scalar1=selt[:, 0:1], scalar2=None,
                                                         op0=ALU.mult), r=["hx", "selt"], w=["hy"])
            tr.op("dve", lambda: nc.vector.scalar_tensor_tensor(out=hy[:], in0=hx[:, 1], scalar=selt[:, 1:2],
                                                                in1=hy[:], op0=ALU.mult, op1=ALU.add),
                  r=["hx", "selt", "hy"], w=["hy"])
            pv = PT[0:NCONV * 128, :].rearrange("(c p) n -> p c n", p=128)
            hys = sb("hys", [128, NCONV, 2], F32, st2)
            tr.op("dve", lambda: nc.vector.tensor_copy(out=hys[:, :, 0:1], in_=hy[:, :, 1:2]), r=["hy"], w=["hys"])
            tr.op("dve", lambda: nc.vector.tensor_copy(out=hys[:, :, 1:2], in_=hy[:, :, 0:1]), r=["hy"], w=["hys"])
            tr.dma("sp", pv[:, :, T + 2:T + 4], hys[:], r=["hys"], sem="hy")
            tr.barrier()

        def run_streams(factories, width, stagger=0):
            active = []
            free = list(range(width))
            it = iter(factories)
            done = False
            if stagger:
                f = next(it, None)
                if f is not None:
                    s_ = free.pop(0)
                    g0 = f(s_)
                    active.append((g0, s_))
                    for _ in range(stagger):
                        try:
                            next(g0)
                        except StopIteration:
                            active.remove((g0, s_))
                            free.append(s_)
                            break
            while True:
                while free and not done:
                    f = next(it, None)
                    if f is None:
                        done = True
                        break
                    s_ = free.pop(0)
                    active.append((f(s_), s_))
                if not active:
                    break
                for g in list(active):
                    try:
                        next(g[0])
                    except StopIteration:
                        active.remove(g)
                        free.append(g[1])

        with ExitStack() as st3:
            W2B = 4
            B2 = []
            for i in range(W2B):
                B2.append({"pre": sb("pre", [128, NT + 4], BF16, st3), "acc": sb("acc", [128, NT], F32, st3),
                           "sq": sb("sqq", [128, NT], F32, st3), "rn": sb("rn", [128, NT], F32, st3),
                           "ob": sb("ob", [128, NT], BF16, st3)})
            A2 = []
            for i in range(2):
                A2.append({"p1": sb("p1", [128, NT + 4], F32, st3), "p2": sb("p2", [128, NT + 4], F32, st3),
                           "p3": sb("p3", [128, NT], F32, st3), "pcv": sb("pcv", [128, NT + 4], BF16, st3),
                           "cacc": sb("cacc", [128, NT], F32, st3)})
            G2 = []
            for i in range(2):
                G2.append({"g": sb("gsb", [128, NT], F32, st3), "pa": sb("pab", [128, NT], F32, st3)})
            yaT = [sb("yaT", [128, 4, NT], BF16, st3) for i in range(2)]
            wa = sb("wa", [128, 4, D], BF16, st3)
            dgq = sb("dgq", [128, 24 * 5, 128], BF16, st3)
            dga = sb("dga", [128, 4 * 3, 128], BF16, st3)
            IDF = cs[:, 3392:3520]
            for j in range(24 * 5):
                eng = "dve"
                if eng == "dve":
                    tr.op("dve", lambda j=j: nc.vector.tensor_scalar(out=dgq[:, j, :], in0=IDF, scalar1=cw[:, j:j + 1],
                                                                     scalar2=None, op0=ALU.mult),
                          r=["cs", "cw"], w=[("dgq", j)])
                else:
                    tr.op("pool", lambda j=j: nc.gpsimd.tensor_scalar(out=dgq[:, j, :], in0=IDF, scalar1=cw[:, j:j + 1],
                                                                      scalar2=None, op0=ALU.mult),
                          r=["cs", "cw"], w=[("dgq", j)])
            for j in range(12):
                tr.op("dve", lambda j=j: nc.vector.tensor_scalar(out=dga[:, j, :], in0=IDF,
                                                                 scalar1=cw[:, 120 + j:121 + j], scalar2=None,
                                                                 op0=ALU.mult), r=["cs", "cw"], w=[("dga", j)])
            epsq = sb("epsq", [128, 2], F32, st3)
            tr.op("dve", lambda: nc.vector.memset(epsq[:, 0:1], EPS), w=["epsq"])
            tr.op("dve", lambda: nc.vector.memset(epsq[:, 1:2], 128.0 * EPS), w=["epsq"])
            tr.dma("pool", wa[:], waout.rearrange("(k p) n -> p k n", p=128), w=["wa"], sem="wa")

            def qkv_gen(n, ch, sl):
                c0 = n * NT
                Bf = B2[sl]
                p_, a_, s_, r_, o_ = Bf["pre"], Bf["acc"], Bf["sq"], Bf["rn"], Bf["ob"]
                pk, ak, sk, rk, ok = ("pre", sl), ("acc", sl), ("sqq", sl), ("rn", sl), ("ob", sl)
                tr.dma("pool", p_[:], PT[(8 + ch) * 128:(9 + ch) * 128, c0:c0 + NT + 4], w=[pk], sem=pk)
                yield
                bc_, pc_ = nps()
                for tap in range(5):
                    tr.op("pe", lambda tap=tap: nc.tensor.matmul(pc_[:], dgq[:, ch * 5 + tap, :], p_[:, tap:tap + NT],
                                                                 start=(tap == 0), stop=(tap == 4)),
                          r=[pk, ("dgq", ch * 5 + tap)], w=[("ps", bc_)], inc=(tap == 4))
                yield
                tr.op("act", lambda: nc.scalar.activation(out=a_[:], in_=pc_[:], func=AF.Silu),
                      r=[("ps", bc_)], w=[ak])
                yield
                if ch < 16:
                    tr.op("act", lambda: nc.scalar.activation(out=s_[:], in_=a_[:], func=AF.Square), r=[ak], w=[sk])
                    yield
                    bi, pt = nps()
                    tr.op("pe", lambda: nc.tensor.matmul(pt[:], ONES, s_[:], start=True, stop=True),
                          r=[sk, "cs"], w=[("ps", bi)])
                    yield
                    qsc = 128.0 if ch < 8 else 1.0
                    ebias = epsq[:, 1:2] if ch < 8 else epsq[:, 0:1]
                    tr.op("act", lambda: nc.scalar.activation(out=r_[:], in_=pt[:], func=AF.Sqrt, bias=ebias,
                                                              scale=qsc), r=[("ps", bi), "epsq"], w=[rk])
                    yield
                    tr.op("dve", lambda: nc.vector.reciprocal(out=r_[:], in_=r_[:]), r=[rk], w=[rk])
                    yield
                    tr.op("dve", lambda: nc.vector.tensor_tensor(out=o_[:], in0=a_[:], in1=r_[:], op=ALU.mult),
                          r=[ak, rk], w=[ok])
                else:
                    tr.op("pool", lambda: nc.gpsimd.tensor_copy(out=o_[:], in_=a_[:]), r=[ak], w=[ok])
                yield
                tr.dma("sp", QKVT[ch * 128:(ch + 1) * 128, c0:c0 + NT], o_[:], r=[ok], sem=ok)

            def abr_gen(n, ch, sl):
                c0 = n * NT
                Af = A2[sl]
                p1, p2, p3, pcv, cacc = Af["p1"], Af["p2"], Af["p3"], Af["pcv"], Af["cacc"]
                k1, k2, k3, kp, kc = ("p1", sl), ("p2", sl), ("p3", sl), ("pcv", sl), ("cacc", sl)
                ya = yaT[n % 2]
                tr.dma("sp", p1[:], PT[ch * 128:(ch + 1) * 128, c0:c0 + NT + 4], w=[k1], sem=k1)
                tr.dma("sp", p2[:], PT[(4 + ch) * 128:(5 + ch) * 128, c0:c0 + NT + 4], w=[k2], sem=k2)
                tr.dma("sp", p3[:], PT[(32 + ch) * 128:(33 + ch) * 128, c0 + 2:c0 + 2 + NT], w=[k3], sem=k3)
                yield
                tr.op("dve", lambda: nc.vector.tensor_tensor(out=pcv[:], in0=p1[:], in1=p2[:], op=ALU.mult),
                      r=[k1, k2], w=[kp])
                yield
                bc_, pc_ = nps()
                for tap in range(3):
                    tr.op("pe", lambda tap=tap: nc.tensor.matmul(pc_[:], dga[:, ch * 3 + tap, :],
                                                                 pcv[:, 1 + tap:1 + tap + NT], start=(tap == 0),
                                                                 stop=(tap == 2)),
                          r=[kp, ("dga", ch * 3 + tap)], w=[("ps", bc_)], inc=(tap == 2))
                yield
                tr.op("dve", lambda: nc.vector.tensor_tensor(out=ya[:, ch, :], in0=pc_[:], in1=p3[:], op=ALU.mult),
                      r=[("ps", bc_), k3], w=[("yaT", n % 2, ch)])

            def gate_gen(n, m, sl):
                c0 = n * NT
                g_, pa_ = G2[sl]["g"], G2[sl]["pa"]
                gk, pk2 = ("gsb", sl), ("pab", sl)
                ya = yaT[n % 2]
                tr.dma("sp", g_[:], PT[(36 + m) * 128:(37 + m) * 128, c0 + 2:c0 + 2 + NT], w=[gk], sem=gk)
                yield
                tr.op("act", lambda: nc.scalar.activation(out=g_[:], in_=g_[:], func=AF.Sigmoid), r=[gk], w=[gk])
                bi, pt = nps()
                for kk in range(4):
                    tr.op("pe", lambda kk=kk: nc.tensor.matmul(pt[:], wa[:, kk, m * 128:(m + 1) * 128], ya[:, kk, :],
                                                               start=(kk == 0), stop=(kk == 3)),
                          r=["wa", ("yaT", n % 2, kk)], w=[("ps", bi)], inc=(kk == 3))
                yield
                tr.op("dve", lambda: nc.vector.tensor_tensor(out=pa_[:], in0=pt[:], in1=g_[:], op=ALU.mult),
                      r=[("ps", bi), gk], w=[pk2])
                yield
                tr.dma("sp", PAT[m * 128:(m + 1) * 128, c0:c0 + NT], pa_[:], r=[pk2], sem=pk2)

            for n in range(ntile):
                run_streams([(lambda sl, n=n, ch=ch: qkv_gen(n, ch, sl)) for ch in range(24)], W2B)
                run_streams([(lambda sl, n=n, ch=ch: abr_gen(n, ch, sl)) for ch in range(4)], 2)
                run_streams([(lambda sl, n=n, m=m: gate_gen(n, m, sl)) for m in range(8)], 2)
            tr.barrier()

        with ExitStack() as st4:
            qkv = [sb("qkv%d" % i, [128, 24, NT], BF16, st4) for i in range(2)]
            S = sb("S", [128, H, 128], F32, st4)
            Sb = sb("Sb", [128, H, 128], BF16, st4)
            sm = sb("sm", [128, 8, 8], F32, st4)
            gT = sb("gT", [8, 3, 64], F32, st4)
            Rg = sb("Rg", [8, 2, 512], F32, st4)
            dec = [sb("dec%d" % i, [64, 512], F32, st4) for i in range(2)]
            egr = sb("egr", [128, 512], F32, st4)
            qg = sb("qg", [128, H, C], BF16, st4)
            XY = [sb("XY%d" % i, [64, 2, 512], BF16, st4) for i in range(2)]
            PQ = [sb("PQ%d" % i, [64, 2, 512], BF16, st4) for i in range(2)]
            attn = sb("attn", [64, 512], BF16, st4)
            XY0 = sb("XY0", [64, 2, 512], BF16, st4)
            MK = sb("MK", [64, 4, 512], BF16, st4)
            tr.dma("pool", MK[:], cst2.rearrange("p (a n) -> p a n", a=4), w=["MK"], sem="MK")
            kbg = sb("kbg", [64, H, 128], BF16, st4)
            kdec = sb("kdec", [64, H, 128], BF16, st4)
            vb = sb("vb", [64, H, 128], BF16, st4)
            usb = sb("usb", [64, H, 128], F32, st4)
            wT = sb("wT", [128, H, C], BF16, st4)
            vnew = sb("vnew", [64, H, 128], BF16, st4)
            osb = sb("osb", [64, H, 128], F32, st4)
            ofw = sb("ofw", [64, H, 128], F32, st4)
            zsb = sb("zsb", [64, H, 128], F32, st4)
            ysq = sb("ysq", [64, H, 128], F32, st4)
            ss = sb("ss", [64, 16], F32, st4)
            ss2 = sb("ss2", [64, 16], F32, st4)
            ss3 = sb("ss3", [64, 16], F32, st4)
            yb = sb("yb", [64, H, 128], BF16, st4)
            ybT = sb("ybT", [128, H, NT], BF16, st4)
            SX = ybT[:].bitcast(F32).rearrange("p h n -> p (h n)").rearrange("p (r m) -> p r m", r=2)
            epsd = sb("epsd", [128, 1], F32, st4)
            tr.op("dve", lambda: nc.vector.memset(epsd[:], EPS), w=["epsd"])
            tr.op("dve", lambda: nc.vector.memset(S[:], 0.0), w=["S"])
            tr.op("dve", lambda: nc.vector.memset(Sb[:], 0.0), w=["Sb"])
            nck = NT // C

            def bc(ap, shape):
                return ap.to_broadcast(list(shape))

            WG = 2
            TB = [{"bat": sb("bat", [64, NT // C, 32], F32, st4), "beta": sb("beta", [64, NT // C, 8], F32, st4),
                   "lnb": sb("lnb", [64, NT // C, 8], F32, st4), "gg": sb("gg", [64, NT // C, 8], F32, st4)}
                  for _i in range(2)]
            CB = [{"sm": sm, "gT": gT, "Rg": Rg, "dec": dec, "egr": egr, "qg": qg, "XY": XY, "PQ": PQ, "attn": attn,
                   "XY0": XY0, "kbg": kbg, "kdec": kdec, "vb": vb, "usb": usb, "wT": wT, "vnew": vnew, "osb": osb,
                   "ofw": ofw, "zsb": zsb, "ysq": ysq, "ss": ss, "ss2": ss2, "ss3": ss3, "yb": yb}]
            for _i in range(1, WG):
                CB.append({
                    "sm": sb("sm", [128, 8, 8], F32, st4), "gT": sb("gT", [8, 3, 64], F32, st4),
                    "Rg": sb("Rg", [8, 2, 512], F32, st4),
                    "dec": [sb("dec", [64, 512], F32, st4) for _j in range(2)],
                    "egr": sb("egr", [128, 512], F32, st4), "qg": sb("qg", [128, H, C], BF16, st4),
                    "XY": [sb("XY", [64, 2, 512], BF16, st4) for _j in range(2)],
                    "PQ": [sb("PQ", [64, 2, 512], BF16, st4) for _j in range(2)],
                    "attn": sb("attn", [64, 512], BF16, st4), "XY0": sb("XY0", [64, 2, 512], BF16, st4),
                    "kbg": sb("kbg", [64, H, 128], BF16, st4), "kdec": sb("kdec", [64, H, 128], BF16, st4),
                    "vb": sb("vb", [64, H, 128], BF16, st4), "usb": sb("usb", [64, H, 128], F32, st4),
                    "wT": sb("wT", [128, H, C], BF16, st4), "vnew": sb("vnew", [64, H, 128], BF16, st4),
                    "osb": sb("osb", [64, H, 128], F32, st4), "ofw": sb("ofw", [64, H, 128], F32, st4),
                    "zsb": sb("zsb", [64, H, 128], F32, st4), "ysq": sb("ysq", [64, H, 128], F32, st4),
                    "ss": sb("ss", [64, 16], F32, st4), "ss2": sb("ss2", [64, 16], F32, st4),
                    "ss3": sb("ss3", [64, 16], F32, st4), "yb": sb("yb", [64, H, 128], BF16, st4)})

            def chunk_gen(dr, n, c, first, last, sl):
                Bc = CB[sl]
                Tt = TB[n % 2]
                bat, beta, lnb, gg = Tt["bat"], Tt["beta"], Tt["lnb"], Tt["gg"]
                sm, gT, Rg, dec, egr, qg, XY, PQ, attn, XY0 = (Bc["sm"], Bc["gT"], Bc["Rg"], Bc["dec"], Bc["egr"],
                                                               Bc["qg"], Bc["XY"], Bc["PQ"], Bc["attn"], Bc["XY0"])
                kbg, kdec, vb, usb, wT, vnew, osb, ofw, zsb = (Bc["kbg"], Bc["kdec"], Bc["vb"], Bc["usb"], Bc["wT"],
                                                               Bc["vnew"], Bc["osb"], Bc["ofw"], Bc["zsb"])
                ysq, ss, ss2, ss3, yb = Bc["ysq"], Bc["ss"], Bc["ss2"], Bc["ss3"], Bc["yb"]
                qt = qkv[n % 2]
                qk_ = ("qkv", n % 2)
                if first:
                    tr.dma("sp", qt[:], QKVT[:, n * NT:(n + 1) * NT].rearrange("(c p) n -> p c n", p=128),
                           w=[qk_], sem=qk_)
                    tr.dma("sp", bat[:], BAt[n * NT:(n + 1) * NT, :].rearrange("(c p) n -> p c n", p=64),
                           w=[("bat", "t", n % 2)], sem=("bat", "t", n % 2))
                    bsl = bat[:, :, dr * 8:dr * 8 + 8]
                    asl = bat[:, :, 16 + dr * 8:24 + dr * 8]
                    tr.op("act", lambda: nc.scalar.activation(out=beta[:], in_=bsl, func=AF.Sigmoid),
                          r=[("bat", "t", n % 2)], w=[("beta", "t", n % 2)])
                    tr.op("act", lambda: nc.scalar.activation(out=lnb[:], in_=bsl, func=AF.Exp, scale=-1.0),
                          r=[("bat", "t", n % 2)], w=[("lnb", "t", n % 2)])
                    tr.op("act", lambda: nc.scalar.activation(out=lnb[:], in_=lnb[:], func=AF.Ln, bias=1.0, scale=1.0),
                          r=[("lnb", "t", n % 2)], w=[("lnb", "t", n % 2)])
                    for c2 in range(nck):
                        tr.op("dve", lambda c2=c2: nc.vector.tensor_tensor(
                            out=gg[:, c2, :], in0=bat[:, c2, 16 + dr * 8:24 + dr * 8],
                            in1=tp[0:64, 16 + dr * 8:24 + dr * 8], op=ALU.add), r=[("bat", "t", n % 2), "tp"], w=[("gg", "t", n % 2)])
                    tr.op("act", lambda: nc.scalar.activation(out=gg[:], in_=gg[:], func=AF.Exp), r=[("gg", "t", n % 2)], w=[("gg", "t", n % 2)])
                    tr.op("act", lambda: nc.scalar.activation(out=gg[:], in_=gg[:], func=AF.Ln, bias=1.0, scale=1.0),
                          r=[("gg", "t", n % 2)], w=[("gg", "t", n % 2)])
                    for c2 in range(nck):
                        tr.op("dve", lambda c2=c2: nc.vector.tensor_tensor(
                            out=gg[:, c2, :], in0=gg[:, c2, :], in1=negA[0:64, dr * 8:dr * 8 + 8], op=ALU.mult),
                            r=[("gg", "t", n % 2), "negA"], w=[("gg", "t", n % 2)])
                tok = slice(c * C, (c + 1) * C)
                gtok = n * NT + c * C
                yield
                bi, pg = nps()
                tr.op("pe", lambda: nc.tensor.matmul(pg[0:64, 0:8], LTRI[dr], gg[:, c, :], start=True,
                                                     stop=True), r=[("gg", "t", n % 2), "cs"], w=[("ps", bi)], inc=False)
                tr.op("pe", lambda: nc.tensor.matmul(pg[:, 8:16], cs[0:64, 0:128], gg[:, c, :], start=True,
                                                     stop=True), r=[("gg", "t", n % 2), "cs"], w=[("ps", bi)])
                tr.op("dve", lambda: nc.vector.tensor_copy(out=sm[0:64, 0, :], in_=pg[0:64, 0:8]),
                      r=[("ps", bi)], w=[("sm", sl)])
                tr.op("dve", lambda: nc.vector.tensor_tensor(out=sm[0:64, 1, :], in0=pg[0:64, 0:8],
                                                             in1=lnb[:, c, :], op=ALU.subtract),
                      r=[("ps", bi), ("lnb", "t", n % 2)], w=[("sm", sl)])
                tr.op("act", lambda: nc.scalar.activation(out=sm[0:64, 2, :], in_=pg[0:64, 0:8], func=AF.Exp),
                      r=[("ps", bi)], w=[("sm", sl)])
                tr.op("dve", lambda: nc.vector.tensor_tensor(out=sm[0:64, 3, :], in0=pg[0:64, 8:16],
                                                             in1=sm[0:64, 0, :], op=ALU.subtract),
                      r=[("ps", bi), ("sm", sl)], w=[("sm", sl)])
                tr.op("act", lambda: nc.scalar.activation(out=sm[0:64, 3, :], in_=sm[0:64, 3, :], func=AF.Exp),
                      r=[("sm", sl)], w=[("sm", sl)])
                tr.op("dve", lambda: nc.vector.tensor_tensor(out=sm[0:64, 4, :], in0=sm[0:64, 2, :],
                                                             in1=beta[:, c, :], op=ALU.mult),
                      r=[("sm", sl), ("beta", "t", n % 2)], w=[("sm", sl)])
                tr.op("act", lambda: nc.scalar.activation(out=sm[:, 5, :], in_=pg[:, 8:16], func=AF.Exp),
                      r=[("ps", bi)], w=[("sm", sl)])
                yield
                bi2, pt2 = nps()
                tr.op("pe", lambda: nc.tensor.matmul(pt2[0:8, 0:64], sm[0:64, 0, :], ID64, start=True,
                                                     stop=True), r=[("sm", sl), "cs"], w=[("ps", bi2)], inc=False)
                tr.op("pe", lambda: nc.tensor.matmul(pt2[0:8, 64:128], sm[0:64, 1, :], ID64, start=True,
                                                     stop=True), r=[("sm", sl), "cs"], w=[("ps", bi2)])
                tr.op("dve", lambda: nc.vector.tensor_scalar(out=gT[:, 0, :], in0=pt2[0:8, 0:64], scalar1=-1.0,
                                                             scalar2=None, op0=ALU.mult),
                      r=[("ps", bi2)], w=[("gT", sl)])
                tr.op("dve", lambda: nc.vector.tensor_copy(out=gT[:, 1:3, :].rearrange("p a b -> p (a b)"),
                                                           in_=pt2[0:8, 0:128]),
                      r=[("ps", bi2)], w=[("gT", sl)])
                E8v = E8.rearrange("p (h i) -> p h i", h=8)
                tr.op("dve", lambda: nc.vector.tensor_tensor(
                    out=Rg[:, 0, :].rearrange("p (h i) -> p h i", h=8), in0=E8v,
                    in1=gT[:, 2:3, :].to_broadcast([8, 8, 64]), op=ALU.mult), r=[("gT", sl), "cs"], w=[("Rg", sl)])
                tr.op("dve", lambda: nc.vector.tensor_tensor(
                    out=Rg[:, 1, :].rearrange("p (h i) -> p h i", h=8), in0=E8v,
                    in1=gT[:, 1:2, :].to_broadcast([8, 8, 64]), op=ALU.mult), r=[("gT", sl), "cs"], w=[("Rg", sl)])
                yield
                bA, pA = nps()
                bQ, pQ = nps()
                bR, pR = nps()
                tr.op("pe", lambda: nc.tensor.matmul(pA[0:64, :], cs[0:8, 0:64], Rg[:, 0, :], start=True,
                                                     stop=False), r=[("Rg", sl), "cs"], w=[("ps", bA)], inc=False)
                tr.op("pe", lambda: nc.tensor.matmul(pA[0:64, :], gT[:, 0, :], E8, start=False, stop=True),
                      r=[("gT", sl), "cs"], w=[("ps", bA)])
                tr.op("pe", lambda: nc.tensor.matmul(pQ[0:64, :], cs[0:8, 0:64], Rg[:, 1, :], start=True,
                                                     stop=False), r=[("Rg", sl), "cs"], w=[("ps", bQ)], inc=False)
                tr.op("pe", lambda: nc.tensor.matmul(pQ[0:64, :], gT[:, 0, :], E8, start=False, stop=True),
                      r=[("gT", sl), "cs"], w=[("ps", bQ)])
                tr.op("pe", lambda: nc.tensor.matmul(pR[:, :], cs[0:8, 0:128], Rg[:, 1, :], start=True,
                                                     stop=True), r=[("Rg", sl), "cs"], w=[("ps", bR)])
                tr.op("dve", lambda: nc.vector.tensor_tensor(out=dec[0][:], in0=pA[0:64, :], in1=CAPS[dr],
                                                             op=ALU.min), r=[("ps", bA), "cs"], w=[("dec", 0, sl)])
                tr.op("act", lambda: nc.scalar.activation(out=dec[0][:], in_=dec[0][:], func=AF.Exp),
                      r=[("dec", 0, sl)], w=[("dec", 0, sl)])
                tr.op("dve", lambda: nc.vector.tensor_tensor(out=dec[1][:], in0=pQ[0:64, :], in1=CAPI[dr],
                                                             op=ALU.min), r=[("ps", bQ), "cs"], w=[("dec", 1, sl)])
                tr.op("act", lambda: nc.scalar.activation(out=dec[1][:], in_=dec[1][:], func=AF.Exp),
                      r=[("dec", 1, sl)], w=[("dec", 1, sl)])
                tr.op("act", lambda: nc.scalar.activation(out=egr[:], in_=pR[:], func=AF.Exp),
                      r=[("ps", bR)], w=[("egr", sl)])
                tr.op("dve", lambda: nc.vector.tensor_tensor(
                    out=qg[:], in0=qt[:, 0:8, tok], in1=egr[:].rearrange("p (h i) -> p h i", h=8),
                    op=ALU.mult), r=[qk_, ("egr", sl)], w=[("qg", sl)])
                yield
                bK, pK = nps()
                bQK, pQK = nps()
                for h in range(H):
                    tr.op("pe", lambda h=h: nc.tensor.matmul(pK[0:64, h * 64:(h + 1) * 64], qt[:, 8 + h, tok],
                                                             qt[:, 8 + h, tok], start=True, stop=True),
                          r=[qk_], w=[("ps", bK)], inc=(h == H - 1))
                for h in range(H):
                    tr.op("pe", lambda h=h: nc.tensor.matmul(pQK[0:64, h * 64:(h + 1) * 64], qt[:, 8 + h, tok],
                                                             qt[:, h, tok], start=True, stop=True),
                          r=[qk_], w=[("ps", bQK)], inc=(h == H - 1))
                X0, Y0 = XY0[:, 0, :], XY0[:, 1, :]
                tr.op("dve", lambda: nc.vector.tensor_tensor(out=X0, in0=pK[0:64, :], in1=dec[0][:],
                                                             op=ALU.mult),
                      r=[("ps", bK), ("dec", 0, sl)], w=[("X0", sl)])
                tr.op("dve", lambda: nc.vector.tensor_tensor(out=attn[:], in0=pQK[0:64, :], in1=dec[1][:],
                                                             op=ALU.mult),
                      r=[("ps", bQK), ("dec", 1, sl)], w=[("attn", sl)])
                yield
                bY, pY = nps()
                for h in range(H):
                    tr.op("pe", lambda h=h: nc.tensor.matmul(pY[0:64, h * 64:(h + 1) * 64],
                                                             X0[:, h * 64:(h + 1) * 64], ID64B, start=True,
                                                             stop=True),
                          r=[("X0", sl), "csb"], w=[("ps", bY)], inc=(h == H - 1))
                tr.op("act", lambda: nc.scalar.copy(out=Y0, in_=pY[0:64, :]), r=[("ps", bY)], w=[("Y0", sl)])
                Xa0, Ya0 = XY[0][:, 0, :], XY[0][:, 1, :]
                Pm, Qm = PQ[0][:, 0, :], PQ[0][:, 1, :]
                tr.op("pool", lambda: nc.gpsimd.tensor_tensor(out=Xa0, in0=X0, in1=MK[:, 0, :], op=ALU.mult),
                      r=[("X0", sl), "MK"], w=[("X", 0, sl)])
                tr.op("pool", lambda: nc.gpsimd.tensor_tensor(out=Ya0, in0=Y0, in1=MK[:, 0, :], op=ALU.mult),
                      r=[("Y0", sl), "MK"], w=[("Y", 0, sl)])
                tr.op("pool", lambda: nc.gpsimd.tensor_tensor(out=Pm, in0=I8, in1=Xa0, op=ALU.subtract),
                      r=[("X", 0, sl), "cs"], w=[("P", 0, sl)])
                tr.op("pool", lambda: nc.gpsimd.tensor_tensor(out=Qm, in0=I8, in1=Ya0, op=ALU.subtract),
                      r=[("Y", 0, sl), "cs"], w=[("Q", 0, sl)])

                def mm8(pt_, lhs, rhs, rk, bix):
                    for h in range(H):
                        hs = slice(h * 64, (h + 1) * 64)
                        tr.op("pe", lambda hs=hs: nc.tensor.matmul(pt_[0:64, hs], lhs[:, hs], rhs[:, hs],
                                                                   start=True, stop=True),
                              r=rk, w=[("ps", bix)], inc=(h == H - 1))

                for lv in range(2):
                    a, b = lv % 2, (lv + 1) % 2
                    Xa, Ya = XY[a][:, 0, :], XY[a][:, 1, :]
                    Xb, Yb = XY[b][:, 0, :], XY[b][:, 1, :]
                    Pa, Qa = PQ[a][:, 0, :], PQ[a][:, 1, :]
                    Pb, Qb = PQ[b][:, 0, :], PQ[b][:, 1, :]
                    yield
                    b1, p1 = nps()
                    mm8(p1, Ya, Xa, [("X", a, sl), ("Y", a, sl)], b1)
                    tr.op("act", lambda: nc.scalar.copy(out=Xb, in_=p1[0:64, :]), r=[("ps", b1)], w=[("X", b, sl)])
                    yield
                    b2, p2 = nps()
                    mm8(p2, Xa, Ya, [("X", a, sl), ("Y", a, sl)], b2)
                    tr.op("act", lambda: nc.scalar.copy(out=Yb, in_=p2[0:64, :]), r=[("ps", b2)], w=[("Y", b, sl)])
                    yield
                    b3, p3 = nps()
                    mm8(p3, Qa, Xb, [("Q", a, sl), ("X", b, sl)], b3)
                    tr.op("dve", lambda: nc.vector.tensor_tensor(out=Pb, in0=p3[0:64, :], in1=Pa, op=ALU.add),
                          r=[("ps", b3), ("P", a, sl)], w=[("P", b, sl)])
                    yield
                    b4, p4 = nps()
                    mm8(p4, Pa, Yb, [("P", a, sl), ("Y", b, sl)], b4)
                    tr.op("dve", lambda: nc.vector.tensor_tensor(out=Qb, in0=p4[0:64, :], in1=Qa, op=ALU.add),
                          r=[("ps", b4), ("Q", a, sl)], w=[("Q", b, sl)])
                cur = 0
                for li in range(3):
                    nxt = 1 - cur
                    Pc, Qc = PQ[cur][:, 0, :], PQ[cur][:, 1, :]
                    Pn, Qn = PQ[nxt][:, 0, :], PQ[nxt][:, 1, :]
                    Xm, Ym = XY[0][:, 0, :], XY[0][:, 1, :]
                    W1, W2 = XY[1][:, 0, :], XY[1][:, 1, :]
                    tr.op("pool", lambda: nc.gpsimd.tensor_tensor(out=Ym, in0=Y0, in1=MK[:, 1 + li, :],
                                                                  op=ALU.mult),
                          r=[("Y0", sl), "MK"], w=[("Y", 0, sl)])
                    yield
                    b1, p1 = nps()
                    mm8(p1, Ym, Pc, [("Y", 0, sl), ("P", cur, sl)], b1)
                    tr.op("act", lambda: nc.scalar.copy(out=W1, in_=p1[0:64, :]), r=[("ps", b1)], w=[("X", 1, sl)])
                    yield
                    b2, p2 = nps()
                    mm8(p2, Qc, W1, [("Q", cur, sl), ("X", 1, sl)], b2)
                    tr.op("dve", lambda: nc.vector.tensor_tensor(out=Pn, in0=Pc, in1=p2[0:64, :],
                                                                 op=ALU.subtract),
                          r=[("ps", b2), ("P", cur, sl)], w=[("P", nxt, sl)])
                    if li < 2:
                        tr.op("pool", lambda: nc.gpsimd.tensor_tensor(out=Xm, in0=X0, in1=MK[:, 1 + li, :],
                                                                      op=ALU.mult),
                              r=[("X0", sl), "MK"], w=[("X", 0, sl)])
                        yield
                        b3, p3 = nps()
                        mm8(p3, Xm, Qc, [("X", 0, sl), ("Q", cur, sl)], b3)
                        tr.op("act", lambda: nc.scalar.copy(out=W2, in_=p3[0:64, :]), r=[("ps", b3)],
                              w=[("Y", 1, sl)])
                        yield
                        b4, p4 = nps()
                        mm8(p4, Pc, W2, [("P", cur, sl), ("Y", 1, sl)], b4)
                        tr.op("dve", lambda: nc.vector.tensor_tensor(out=Qn, in0=Qc, in1=p4[0:64, :],
                                                                     op=ALU.subtract),
                              r=[("ps", b4), ("Q", cur, sl)], w=[("Q", nxt, sl)])
                    cur = nxt
                TT = PQ[cur][:, 0, :]
                TK = ("P", cur, sl)
                for half in range(2):
                    yield
                    bk, pk_ = nps()
                    bv, pv_ = nps()
                    for hh in range(4):
                        h = half * 4 + hh
                        tr.op("pe", lambda h=h, hh=hh: nc.tensor.matmul(
                            pk_[0:64, hh * 128:(hh + 1) * 128], qt[:, 8 + h, tok], ID128B, start=True,
                            stop=True), r=[qk_, "csb"], w=[("ps", bk)], inc=(hh == 3))
                    for hh in range(4):
                        h = half * 4 + hh
                        tr.op("pe", lambda h=h, hh=hh: nc.tensor.matmul(
                            pv_[0:64, hh * 128:(hh + 1) * 128], qt[:, 16 + h, tok], ID128B, start=True,
                            stop=True), r=[qk_, "csb"], w=[("ps", bv)], inc=(hh == 3))
                    hsl = slice(half * 4, half * 4 + 4)
                    pk3 = pk_[0:64, :].rearrange("p (h d) -> p h d", h=4)
                    pv3 = pv_[0:64, :].rearrange("p (h d) -> p h d", h=4)
                    tr.op("dve", lambda pk3=pk3, hsl=hsl: nc.vector.tensor_tensor(
                        out=kbg[:, hsl, :], in0=pk3,
                        in1=sm[0:64, 4, hsl].unsqueeze(2).to_broadcast([64, 4, 128]), op=ALU.mult),
                        r=[("ps", bk), ("sm", sl)], w=[("kbg", sl)])
                    tr.op("dve", lambda pk3=pk3, hsl=hsl: nc.vector.tensor_tensor(
                        out=kdec[:, hsl, :], in0=pk3,
                        in1=sm[0:64, 3, hsl].unsqueeze(2).to_broadcast([64, 4, 128]), op=ALU.mult),
                        r=[("ps", bk), ("sm", sl)], w=[("kdec", sl)])
                    tr.op("dve", lambda pv3=pv3, hsl=hsl: nc.vector.tensor_tensor(
                        out=vb[:, hsl, :], in0=pv3,
                        in1=beta[:, c, hsl].unsqueeze(2).to_broadcast([64, 4, 128]), op=ALU.mult),
                        r=[("ps", bv), ("beta", "t", n % 2)], w=[("vb", sl)])
                for half in range(2):
                    yield
                    bu, pu = nps()
                    for hh in range(4):
                        h = half * 4 + hh
                        tr.op("pe", lambda h=h, hh=hh: nc.tensor.matmul(
                            pu[0:64, hh * 128:(hh + 1) * 128], TT[:, h * 64:(h + 1) * 64], vb[:, h, :],
                            start=True, stop=True), r=[TK, ("vb", sl)], w=[("ps", bu)], inc=(hh == 3))
                    tr.op("act", lambda half=half, pu=pu: nc.scalar.copy(
                        out=usb[:, half * 4:half * 4 + 4, :].rearrange("p h d -> p (h d)"), in_=pu[0:64, :]),
                        r=[("ps", bu)], w=[("usb", sl)])
                yield
                bw, pw = nps()
                for h in range(H):
                    tr.op("pe", lambda h=h: nc.tensor.matmul(pw[:, h * 64:(h + 1) * 64], kbg[:, h, :],
                                                             TT[:, h * 64:(h + 1) * 64], start=True, stop=True),
                          r=[TK, ("kbg", sl)], w=[("ps", bw)], inc=(h == H - 1))
                tr.op("act", lambda: nc.scalar.copy(out=wT[:].rearrange("p h i -> p (h i)"), in_=pw[:, :]),
                      r=[("ps", bw)], w=[("wT", sl)])
                for half in range(2):
                    bws, pws = nps()
                    for hh in range(4):
                        h = half * 4 + hh
                        tr.op("pe", lambda h=h, hh=hh: nc.tensor.matmul(
                            pws[0:64, hh * 128:(hh + 1) * 128], wT[:, h, :], Sb[:, h, :], start=True,
                            stop=True), r=[("wT", sl), "Sb"], w=[("ps", bws)], inc=(hh == 3))
                    tr.op("dve", lambda half=half, pws=pws: nc.vector.tensor_tensor(
                        out=vnew[:, half * 4:half * 4 + 4, :].rearrange("p h d -> p (h d)"),
                        in0=usb[:, half * 4:half * 4 + 4, :].rearrange("p h d -> p (h d)"), in1=pws[0:64, :],
                        op=ALU.subtract), r=[("ps", bws), ("usb", sl)], w=[("vnew", half, sl)])
                for half in range(2):
                    bo, po = nps()
                    for hh in range(4):
                        h = half * 4 + hh
                        tr.op("pe", lambda h=h, hh=hh: nc.tensor.matmul(
                            po[0:64, hh * 128:(hh + 1) * 128], qg[:, h, :], Sb[:, h, :], start=True,
                            stop=False), r=[("qg", sl), "Sb"], w=[("ps", bo)], inc=False)
                        tr.op("pe", lambda h=h, hh=hh: nc.tensor.matmul(
                            po[0:64, hh * 128:(hh + 1) * 128], attn[:, h * 64:(h + 1) * 64], vnew[:, h, :],
                            start=False, stop=True), r=[("attn", sl), ("vnew", half, sl)], w=[("ps", bo)],
                            inc=(hh == 3))
                    osl = osb[:, half * 4:half * 4 + 4, :].rearrange("p h d -> p (h d)")
                    if dr == 0:
                        tr.op("act", lambda osl=osl, po=po: nc.scalar.copy(out=osl, in_=po[0:64, :]),
                              r=[("ps", bo)], w=[("osb", half, sl)])
                    else:
                        if half == 0:
                            tr.dma("sp", ofw[:].rearrange("p h d -> p (h d)"), OF[gtok:gtok + C, :],
                                   r=[("OF", gtok)], w=[("ofw", sl)], sem=("ofw", sl))
                            tr.dma("sp", zsb[:].rearrange("p h d -> p (h d)"), ZS[gtok:gtok + C, :],
                                   w=[("zsb", sl)], sem=("zsb", sl))
                        tr.op("dve", lambda osl=osl, po=po, half=half: nc.vector.tensor_tensor(
                            out=osl, in0=po[0:64, :],
                            in1=ofw[:, half * 4:half * 4 + 4, :].rearrange("p h d -> p (h d)"), op=ALU.add),
                            r=[("ps", bo), ("ofw", sl)], w=[("osb", half, sl)])
                for half in range(2):
                    bs, pS = nps()
                    for hh in range(4):
                        h = half * 4 + hh
                        tr.op("pe", lambda h=h, hh=hh: nc.tensor.matmul(
                            pS[:, hh * 128:(hh + 1) * 128], kdec[:, h, :], vnew[:, h, :], start=True,
                            stop=True), r=[("kdec", sl), ("vnew", half, sl)], w=[("ps", bs)], inc=(hh == 3))
                    hsl = slice(half * 4, half * 4 + 4)
                    tr.op("pool", lambda hsl=hsl: nc.gpsimd.tensor_tensor(
                        out=S[:, hsl, :], in0=S[:, hsl, :],
                        in1=sm[:, 5, hsl].unsqueeze(2).to_broadcast([128, 4, 128]), op=ALU.mult),
                        r=[("sm", sl), "S"], w=["S"])
                    tr.op("dve", lambda hsl=hsl, pS=pS: nc.vector.tensor_tensor(
                        out=S[:, hsl, :].rearrange("p h d -> p (h d)"),
                        in0=S[:, hsl, :].rearrange("p h d -> p (h d)"), in1=pS[:, :], op=ALU.add),
                        r=[("ps", bs), "S"], w=["S"])
                tr.op("act", lambda: nc.scalar.copy(out=Sb[:], in_=S[:]), r=["S"], w=["Sb"])
                if dr == 0:
                    tr.dma("sp", OF[gtok:gtok + C, :], osb[:].rearrange("p h d -> p (h d)"),
                           r=[("osb", 0, sl), ("osb", 1, sl)], w=[("OF", gtok)], sem=("osb", sl))
                else:
                    for h in range(H):
                        hf = h // 4
                        tr.op("pool", lambda h=h: nc.gpsimd.tensor_tensor(
                            out=ysq[:, h, :], in0=osb[:, h, :], in1=osb[:, h, :], op=ALU.mult),
                            r=[("osb", hf, sl)], w=[("ysq", h, sl)])
                        tr.op("dve", lambda h=h: nc.vector.tensor_reduce(
                            out=ss[:, 8 + h:9 + h], in_=ysq[:, h, :], axis=AX.X, op=ALU.add),
                            r=[("ysq", h, sl)], w=[("ss", sl)])
                    tr.op("act", lambda: nc.scalar.activation(out=ss2[:, 8:16], in_=ss[:, 8:16], func=AF.Sqrt,
                                                              bias=epsd[0:64, 0:1], scale=1.0 / 128),
                          r=[("ss", sl), "epsd"], w=[("ss2", sl)])
                    tr.op("dve", lambda: nc.vector.reciprocal(out=ss3[:, 8:16], in_=ss2[:, 8:16]),
                          r=[("ss2", sl)], w=[("ss3", sl)])
                    for h in range(H):
                        hf = h // 4
                        tr.op("dve", lambda h=h: nc.vector.scalar_tensor_tensor(
                            out=ysq[:, h, :], in0=osb[:, h, :], scalar=ss3[:, 8 + h:9 + h], in1=tp[0:64, 32:160],
                            op0=ALU.mult, op1=ALU.mult), r=[("osb", hf, sl), ("ss3", sl), "tp"], w=[("ysq", h, sl)],
                            strict=True)
                        tr.op("dve", lambda h=h: nc.vector.tensor_tensor(
                            out=yb[:, h, :], in0=ysq[:, h, :], in1=zsb[:, h, :], op=ALU.mult),
                            r=[("ysq", h, sl), ("zsb", sl)], w=[("yb", hf, sl)])
                    yield
                    by, py = nps()
                    for h in range(H):
                        tr.op("pe", lambda h=h: nc.tensor.matmul(py[:, h * 64:(h + 1) * 64], yb[:, h, :],
                                                                 ID64B, start=True, stop=True),
                              r=[("yb", 0, sl), ("yb", 1, sl), "csb"], w=[("ps", by)], inc=(h == H - 1))
                    tr.op("act", lambda: nc.scalar.copy(out=ybT[:, :, tok],
                                                        in_=py[:, :].rearrange("p (h i) -> p h i", h=8)),
                          r=[("ps", by)], w=["ybT"])
                if last and dr == 1:
                    tr.dma("sp", YBT[:, n * NT:(n + 1) * NT].rearrange("(h p) n -> p h n", p=128), ybT[:],
                           r=["ybT"], sem="ybT")

            for dr in range(2):
                tiles = list(range(ntile)) if dr == 0 else list(range(ntile - 1, -1, -1))
                facs = []
                for n in tiles:
                    chunks = list(range(nck)) if dr == 0 else list(range(nck - 1, -1, -1))
                    for ci, c in enumerate(chunks):
                        facs.append(lambda sl, dr=dr, n=n, c=c, ci=ci: chunk_gen(dr, n, c, ci == 0, ci == nck - 1, sl))
                run_streams(facs, WG, stagger=14)
                if dr == 0:
                    tr.dma("sp", sx_in.ap(), S[:].rearrange("p h d -> p (h d)"), r=["S"], w=["sx_in"], sem="sx")
                    tr.cc(lambda: nc.gpsimd.collective_compute("AllGather", ALU.bypass, replica_groups=PAIRS,
                                                                ins=[sx_in.ap()], outs=[sx_out.ap()]),
                          r=["sx_in"], w=["sx_out"], sem="sxcc")
                    tr.dma("sp", SX, sx_out.ap().rearrange("(r p) n -> p r n", p=128), r=["sx_out"], w=["SX", "ybT"],
                           sem="sx")
                    Sf = S[:].rearrange("p h d -> p (h d)")
                    tr.op("dve", lambda: nc.vector.tensor_scalar(out=Sf, in0=SX[:, 0, :], scalar1=selt[:, 0:1],
                                                                 scalar2=None, op0=ALU.mult),
                          r=["SX", "selt"], w=["S"])
                    tr.op("dve", lambda: nc.vector.scalar_tensor_tensor(out=Sf, in0=SX[:, 1, :], scalar=selt[:, 1:2],
                                                                        in1=Sf, op0=ALU.mult, op1=ALU.add),
                          r=["SX", "selt", "S"], w=["S"])
                    tr.op("act", lambda: nc.scalar.copy(out=Sb[:], in_=S[:]), r=["S"], w=["Sb"])
            tr.barrier()

        with ExitStack() as st5:
            ring_phase(st5, [b for _ in range(ntile) for b in range(OFF_UP2, OFF_UP2 + NB_UP)])
            P = ffn_pool(st5, wdn2, nxt=1)
            wb = sb("wb", [128, 8, D], BF16, st5)
            wo = sb("wo", [128, 8, D], BF16, st5)
            ybt = P["u"]
            mrg = P["hid"]
            gsb = [sb("gsb%d" % i, [128, NT], F32, st5) for i in range(2)]
            pab = [sb("pab%d" % i, [128, NT], F32, st5) for i in range(2)]
            tr.dma("pool", wb[:], wbout.rearrange("(k p) n -> p k n", p=128), w=["wb"], sem="wb")
            tr.dma("pool", wo[:], wout.rearrange("(k p) n -> p k n", p=128), w=["wo"], sem="wo")
            for n in range(ntile):
                xt = P["xt"][0]
                xkey = "xt0"
                c0 = n * NT
                load_tile(xt, H1T, c0, xkey)
                tr.dma("sp", ybt[:], YBT[:, c0:c0 + NT].rearrange("(h p) n -> p h n", p=128),
                       w=[("u", k) for k in range(8)], sem="ybt")
                for m in range(8):
                    g_ = gsb[m % 2]
                    gk = ("gsb", m % 2)
                    pa_ = pab[m % 2]
                    pk2 = ("pab", m % 2)
                    tr.dma("sp", g_[:], PT[(44 + m) * 128:(45 + m) * 128, c0 + 2:c0 + 2 + NT], w=[gk], sem=gk)
                    tr.dma("sp", pa_[:], PAT[m * 128:(m + 1) * 128, c0:c0 + NT], w=[pk2], sem=pk2)
                    tr.op("act", lambda g_=g_: nc.scalar.activation(out=g_[:], in_=g_[:], func=AF.Sigmoid),
                          r=[gk], w=[gk])
                    bi, pt = nps()
                    for kk in range(8):
                        tr.op("pe", lambda kk=kk, m=m, pt=pt: nc.tensor.matmul(
                            pt[:], wb[:, kk, m * 128:(m + 1) * 128], ybt[:, kk, :], start=(kk == 0), stop=(kk == 7)),
                            r=["wb", ("u", kk)], w=[("ps", bi)], inc=(kk == 7))
                    tr.op("dve", lambda g_=g_, pt=pt: nc.vector.tensor_tensor(out=g_[:], in0=pt[:], in1=g_[:],
                                                                              op=ALU.mult),
                          r=[("ps", bi), gk], w=[gk])
                    tr.op("pool", lambda g_=g_, pa_=pa_, m=m: nc.gpsimd.tensor_tensor(out=mrg[:, m, :], in0=g_[:],
                                                                                      in1=pa_[:], op=ALU.add),
                          r=[gk, pk2], w=[("hid", m)])
                for m in range(8):
                    bi, pt = nps()
                    for kk in range(8):
                        tr.op("pe", lambda kk=kk, m=m, pt=pt: nc.tensor.matmul(
                            pt[:], wo[:, kk, m * 128:(m + 1) * 128], mrg[:, kk, :], start=(kk == 0), stop=(kk == 7)),
                            r=["wo", ("hid", kk)], w=[("ps", bi)], inc=(kk == 7))
                    tr.op("dve", lambda m=m, pt=pt: nc.vector.scalar_tensor_tensor(
                        out=xt[:, m, :], in0=pt[:], scalar=der[:, 32 + m:33 + m], in1=xt[:, m, :], op0=ALU.mult,
                        op1=ALU.add), r=[("ps", bi), "der"], w=[(xkey, m)])
                ffn_body(P, xt, xkey, 2, P["wdn"], True, outT, c0)
            tr.barrier()
        tr.finish()
    return nc


def _blocks(w, cols_list):
    out = np.empty((len(cols_list), 128, 2048), np.float32)
    w3 = w.reshape(8, 128, -1)
    for b, cols in enumerate(cols_list):
        out[b] = w3[:, :, cols].transpose(1, 0, 2).reshape(128, 2048)
    return out


def _consts():
    c = np.zeros((128, 4096), np.float32)
    c[:, 0:128] = 1.0
    c[0:64, 128:192] = np.eye(64)
    j = np.arange(64)[:, None]
    i = np.arange(64)[None, :]
    c[0:64, 192:256] = (j <= i)
    c[0:64, 256:320] = (j >= i)
    capS_F = np.where(j < i, 0.0, NEG)
    capI_F = np.where(j <= i, 0.0, NEG)
    capS_B = np.where(j > i, 0.0, NEG)
    capI_B = np.where(j >= i, 0.0, NEG)
    c[0:64, 320:832] = np.tile(capS_F, (1, 8))
    c[0:64, 832:1344] = np.tile(capI_F, (1, 8))
    c[0:64, 1344:1856] = np.tile(capS_B, (1, 8))
    c[0:64, 1856:2368] = np.tile(capI_B, (1, 8))
    c[0:64, 2368:2880] = np.tile(np.eye(64), (1, 8))
    e8 = np.zeros((8, 8, 64), np.float32)
    for h in range(8):
        e8[h, h, :] = 1.0
    c[0:8, 2880:3392] = e8.reshape(8, 512)
    c[:, 3392:3520] = np.eye(128)
    return c


_PROG = {}


def _run(inputs, T, n_cores):
    f = lambda a: np.ascontiguousarray(np.asarray(a, dtype=np.float32))
    x = f(inputs["x"])
    B = x.shape[0]
    cvec = f(inputs["c"])
    w_ada = f(inputs["w_ada"])[0]
    b_ada = f(inputs["b_ada"])[0]
    w_in = f(inputs["w_in"])[0]
    wup1 = f(inputs["w_ffn1_up"])[0]
    wup2 = f(inputs["w_ffn2_up"])[0]
    conv_a = f(inputs["conv_a"])[0]
    conv_dn = f(inputs["conv_dn"])[0]
    cst = _consts()
    ii = np.arange(64)[:, None]
    jj = np.arange(64)[None, :]
    mks = [(ii // 8 == jj // 8)]
    for bsz in (8, 16, 32):
        mks.append((ii // (2 * bsz) == jj // (2 * bsz)) & (ii // bsz != jj // bsz))
    cst2 = np.concatenate([np.tile(m.astype(np.float32), (1, 8)) for m in mks], axis=1)

    ada_cols = [np.arange(b * 256, (b + 1) * 256) for b in range(NB_ADA)]
    up_cols = [np.concatenate([np.arange(j * 128, (j + 1) * 128), np.arange(DFF + j * 128, DFF + (j + 1) * 128)])
               for j in range(NJ)]
    fm = np.concatenate([np.arange(512, 1024), np.arange(1024, 1536), np.arange(1536, 4608),
                         np.arange(0, 512), np.arange(5664, 7712)])
    in_cols = [fm[b * 256:(b + 1) * 256] for b in range(NB_IN)]
    z_cols = [np.arange(4608 + b * 256, 4608 + (b + 1) * 256) for b in range(NB_Z)]
    wr_common = np.concatenate([_blocks(w_ada, ada_cols), _blocks(wup1, up_cols), _blocks(w_in, in_cols + z_cols),
                                _blocks(wup2, up_cols)], axis=0)
    assert wr_common.shape[0] == NB_TOT

    ba_f = np.arange(5632, 5664)
    ba_b = np.concatenate([np.arange(5640, 5648), np.arange(5632, 5640), np.arange(5656, 5664),
                           np.arange(5648, 5656)])
    norms = np.stack([f(inputs["norm_ffn1"])[0], f(inputs["norm_mix"])[0], f(inputs["norm_ffn2"])[0],
                      f(inputs["norm_final"])], 0)
    in_maps = []
    for core in range(n_cores):
        b, half = core // 2, core % 2
        xs = x[b, half * T:(half + 1) * T, :]
        if half == 1:
            xs = xs[::-1]
        vecs = np.zeros((128, 120), np.float32)
        vecs[:, 0:8] = cvec[b].reshape(8, 128).T
        vecs[:, 8:80] = b_ada.reshape(72, 128).T
        vecs[:, 80:112] = norms.reshape(4, 8, 128).transpose(2, 0, 1).reshape(128, 32)
        cdn = conv_dn if half == 0 else conv_dn[::-1]
        ca = conv_a if half == 0 else conv_a[::-1]
        convw = np.zeros((128, 132), np.float32)
        convw[:, 0:120] = cdn.T.reshape(24, 128, 5).transpose(1, 0, 2).reshape(128, 120)
        convw[:, 120:132] = ca.T.reshape(4, 128, 3).transpose(1, 0, 2).reshape(128, 12)
        names = ["a_log_fwd", "a_log_bwd", "dt_bias_fwd", "dt_bias_bwd"]
        if half == 1:
            names = ["a_log_bwd", "a_log_fwd", "dt_bias_bwd", "dt_bias_fwd"]
        tokp = np.zeros((128, 160), np.float32)
        for q_, nm in enumerate(names):
            tokp[:, q_ * 8:(q_ + 1) * 8] = f(inputs[nm])[0][None, :]
        tokp[:, 32:160] = f(inputs["dn_norm"])[0][None, :]
        selv = np.zeros((128, 2), np.float32)
        selv[:, 1 - half] = 1.0
        in_maps.append({
            "xT": np.ascontiguousarray(xs.T),
            "wr": wr_common,
            "wdn1": f(inputs["w_ffn1_down"])[0],
            "wdn2": f(inputs["w_ffn2_down"])[0],
            "wba": np.ascontiguousarray(w_in[:, ba_f if half == 0 else ba_b]),
            "waout": f(inputs["w_a_out"])[0],
            "wbout": f(inputs["w_b_out"])[0],
            "wout": f(inputs["w_out"])[0],
            "vecs": vecs, "convw": convw, "tokp": tokp, "cst": cst, "cst2": cst2, "sel": selv,
        })
    if T not in _PROG:
        _PROG[T] = build_program(T)
    res = run_bass_kernel_spmd(_PROG[T], in_maps, core_ids=list(range(n_cores)))
    if DEBUG:
        DBG["res"] = res.results
    out = np.empty((B, 2 * T, D), np.float32)
    for core in range(n_cores):
        b, half = core // 2, core % 2
        o = res.results[core]["outT"].T
        if half == 1:
            o = o[::-1]
        out[b, half * T:(half + 1) * T, :] = o
    return out


def kernel(**inputs):
    x = np.asarray(inputs["x"])
    B, S, _ = x.shape
    return _run(inputs, S // 2, 2 * B)
```

```python
import numpy as np
from contextlib import ExitStack
import concourse.bass as bass
import concourse.mybir as mybir
from concourse.bass_utils import run_bass_kernel_spmd

F32 = mybir.dt.float32
BF16 = mybir.dt.bfloat16
AF = mybir.ActivationFunctionType
ALU = mybir.AluOpType
AX = mybir.AxisListType

D = 1024
DFF = 2816
NJ = DFF // 128
H = 8
C = 64
NT = 512
EPS = 1e-6
NR = 4
NEG = -30000.0
DEBUG = False
SAME_ENGINE_SYNC = True
DBG = {}

NB_ADA = 36
NB_UP = 22
NB_IN = 26
NB_Z = 4
OFF_ADA = 0
OFF_UP1 = OFF_ADA + NB_ADA
OFF_IN = OFF_UP1 + NB_UP
OFF_Z = OFF_IN + NB_IN
OFF_UP2 = OFF_Z + NB_Z
NB_TOT = OFF_UP2 + NB_UP
NPC = 52
NCONV = 32


class Tick:
    __slots__ = ("kind", "key", "val")

    def __init__(self, kind, key, val):
        self.kind, self.key, self.val = kind, key, val


class Tr:
    def __init__(self, nc, es):
        self.nc = nc
        self.es = es
        self.E = {"pe": nc.tensor, "act": nc.scalar, "dve": nc.vector, "pool": nc.gpsimd, "sp": nc.sync}
        self.csem = {e: es.enter_context(nc.semaphore("c_" + e)) for e in ("pe", "act", "dve", "pool")}
        self.ccnt = {e: 0 for e in self.csem}
        self.pend = {e: [] for e in self.csem}
        self.dsem = {}
        self.dcnt = {}
        self.seen = {e: {} for e in self.E}
        self.st = {}

    def _wait(self, eng, t, same_ok=True, waw=False):
        if t is None:
            return
        if t.kind == "c":
            if t.key == eng and (t.val is None or (same_ok and (waw or not SAME_ENGINE_SYNC))):
                return
            assert t.val is not None, "pending tick waited on"
            sem, v, k = self.csem[t.key], t.val, ("c", t.key)
        else:
            sem, v, k = self.dsem[t.key], self.dcnt[t.key], ("d", t.key)
        if self.seen[eng].get(k, 0) >= v:
            return
        self.seen[eng][k] = v
        self.E[eng].wait_ge(sem, v)

    def _deps(self, eng, r, w, same_ok):
        for k in r:
            s = self.st.get(k)
            if s:
                self._wait(eng, s[0], same_ok)
        for k in w:
            s = self.st.get(k)
            if s:
                self._wait(eng, s[0], same_ok, waw=True)
                for t in s[1].values():
                    self._wait(eng, t, same_ok, waw=True)

    def _upd(self, t, r, w):
        for k in r:
            self.st.setdefault(k, [None, {}])[1][(t.kind, t.key)] = t
        for k in w:
            self.st[k] = [t, {}]

    def op(self, eng, fn, r=(), w=(), inc=True, strict=False):
        self._deps(eng, r, w, not strict)
        ins = fn()
        t = Tick("c", eng, None)
        self.pend[eng].append(t)
        if inc:
            self.ccnt[eng] += 1
            ins.then_inc(self.csem[eng], 1)
            for p in self.pend[eng]:
                p.val = self.ccnt[eng]
            self.pend[eng] = []
        self._upd(t, r, w)
        return ins

    def dma(self, q, out, in_, r=(), w=(), sem=None):
        self._deps(q, r, w, False)
        if sem not in self.dsem:
            self.dsem[sem] = self.es.enter_context(self.nc.semaphore("d%d" % len(self.dsem)))
            self.dcnt[sem] = 0
        ins = self.E[q].dma_start(out=out, in_=in_)
        self.dcnt[sem] += 16
        ins.then_inc(self.dsem[sem], 16)
        t = Tick("d", sem, self.dcnt[sem])
        self._upd(t, r, w)
        return ins

    def cc(self, ins_fn, r=(), w=(), sem=None):
        self._deps("pool", r, w, False)
        if sem not in self.dsem:
            self.dsem[sem] = self.es.enter_context(self.nc.semaphore("d%d" % len(self.dsem)))
            self.dcnt[sem] = 0
        ins = ins_fn()
        self.dcnt[sem] += 1
        ins.then_inc(self.dsem[sem], 1)
        t = Tick("d", sem, self.dcnt[sem])
        self._upd(t, r, w)

    def barrier(self):
        for e in self.csem:
            assert not self.pend[e]
        for e in self.E:
            for f in self.csem:
                if f != e and self.ccnt[f] > 0:
                    self._wait(e, Tick("c", f, self.ccnt[f]))
            for k in self.dsem:
                if self.dcnt[k] > 0:
                    self._wait(e, Tick("d", k, self.dcnt[k]))
        self.st = {}

    def finish(self):
        for k in self.dsem:
            if self.dcnt[k] > 0:
                self._wait("sp", Tick("d", k, self.dcnt[k]))


class Ring:
    def __init__(self, tr, nc, slots, wr):
        self.tr, self.nc, self.slots, self.wr = tr, nc, slots, wr
        self.plan = []
        self.issued = 0
        self.pos = 0

    def add(self, blocks):
        self.plan.extend(blocks)

    def _issue(self):
        b = self.plan[self.issued]
        s = (self.issued - self.base) % self.nr
        self.tr.dma("pool", self.slots[s][:], self.wr[b], w=[("ring", s)], sem=("ring", s))
        self.issued += 1

    def get(self):
        while self.issued < min(len(self.plan), self.pos + self.nr):
            self._issue()
        s = (self.pos - self.base) % self.nr
        self.pos += 1
        return s, self.slots[s]


def build_program(T):
    ntile = T // NT
    nchunk = T // C
    nc = bass.Bass("TRN2", target_bir_lowering=False)

    def din(name, shape, dt=F32):
        return nc.dram_tensor(name, list(shape), dt, kind="ExternalInput").ap()

    xT = din("xT", [D, T])
    wr = din("wr", [NB_TOT, 128, 2048])
    wdn1 = din("wdn1", [DFF, D])
    wdn2 = din("wdn2", [DFF, D])
    wba = din("wba", [D, 32])
    waout = din("waout", [512, D])
    wbout = din("wbout", [D, D])
    wout = din("wout", [D, D])
    vecs = din("vecs", [128, 120])
    convw = din("convw", [128, 24 * 5 + 4 * 3])
    tokp = din("tokp", [128, 32 + 128])
    cst = din("cst", [128, 4096])
    cst2 = din("cst2", [64, 2048])
    sel = din("sel", [128, 2])
    outT = nc.dram_tensor("outT", [D, T], F32, kind="ExternalOutput").ap()

    def dscr(name, shape, dt=F32):
        if DEBUG:
            return nc.dram_tensor(name, list(shape), dt, kind="ExternalOutput").ap()
        return nc.dram_tensor(name, list(shape), dt).ap()

    H1T = dscr("H1T", [D, T])
    PT = dscr("PT", [NPC * 128, T + 4])
    ZS = dscr("ZS", [T, D])
    BAt = dscr("BAt", [T, 32])
    QKVT = dscr("QKVT", [3 * D, T], BF16)
    PAT = dscr("PAT", [D, T])
    OF = dscr("OF", [T, D])
    YBT = dscr("YBT", [D, T], BF16)
    hx_in = nc.dram_tensor("hx_in", [NCONV * 128, 2], F32)
    hx_out = nc.dram_tensor("hx_out", [2 * NCONV * 128, 2], F32)
    sx_in = nc.dram_tensor("sx_in", [128, H * 128], F32)
    sx_out = nc.dram_tensor("sx_out", [256, H * 128], F32)
    PAIRS = [[0, 1], [2, 3], [4, 5], [6, 7]]

    with ExitStack() as es:
        tr = Tr(nc, es)

        sbn = [0]

        def sb(name, shape, dt=F32, stack=None):
            sbn[0] += 1
            return (stack or es).enter_context(nc.sbuf_tensor("%s_%d" % (name, sbn[0]), list(shape), dt))

        ps = [es.enter_context(nc.psum_tensor("ps%d" % i, [128, 512], F32)) for i in range(8)]
        psi = [0]

        def nps():
            for _ in range(8):
                i = psi[0] % 8
                psi[0] += 1
                s_ = tr.st.get(("ps", i))
                if s_ is None or s_[0] is None or len(s_[1]) > 0:
                    return i, ps[i]
            raise RuntimeError("all PSUM banks hold unconsumed results")

        ring = Ring(tr, nc, None, wr)

        def ring_phase(stack, blocks, nr=NR):
            assert ring.issued == len(ring.plan) and ring.pos == len(ring.plan)
            ring.nr = nr
            ring.base = ring.issued
            ring.slots = [sb("ring", [128, 2048], BF16, stack) for i in range(nr)]
            ring.add(blocks)

        vec = sb("vec", [128, 120])
        cw = sb("cw", [128, 132])
        tp = sb("tp", [128, 160])
        cs = sb("cs", [128, 4096])
        csb = sb("csb", [128, 1024], BF16)
        selt = sb("selt", [128, 2])
        modT = sb("modT", [128, 72])
        der = sb("der", [128, 64])
        cact = sb("cact", [128, 8], BF16)
        negA = sb("negA", [128, 16])
        tr.dma("sp", vec[:], vecs, w=["vec"], sem="c0")
        tr.dma("sp", cw[:], convw, w=["cw"], sem="c0")
        tr.dma("sp", tp[:], tokp, w=["tp"], sem="c0")
        tr.dma("sp", cs[:], cst, w=["cs"], sem="c0")
        tr.dma("sp", selt[:], sel, w=["selt"], sem="c0")
        ONES = cs[:, 0:128]
        ID64 = cs[0:64, 128:192]
        LTRI = [cs[0:64, 192:256], cs[0:64, 256:320]]
        CAPS = [cs[0:64, 320:832], cs[0:64, 1344:1856]]
        CAPI = [cs[0:64, 832:1344], cs[0:64, 1856:2368]]
        I8 = cs[0:64, 2368:2880]
        E8 = cs[0:8, 2880:3392]
        tr.op("dve", lambda: nc.vector.tensor_copy(out=csb[:, 0:256], in_=cs[:, 0:256]), r=["cs"], w=["csb"])
        tr.op("dve", lambda: nc.vector.tensor_copy(out=csb[:, 256:384], in_=cs[:, 3392:3520]), r=["cs"], w=["csb"])
        ONESB = csb[:, 0:128]
        ID64B = csb[0:64, 128:192]
        ID128B = csb[:, 256:384]

        tr.op("act", lambda: nc.scalar.activation(out=cact[:], in_=vec[:, 0:8], func=AF.Silu), r=["vec"], w=["cact"])
        st0 = ExitStack()
        ring_phase(st0, list(range(OFF_ADA, OFF_ADA + NB_ADA)))
        bi, pb = nps()
        for blk in range(NB_ADA):
            s, slot = ring.get()
            sv = slot[:].rearrange("p (k c) -> p k c", k=8)
            for cc in range(2):
                j = blk * 2 + cc
                for k in range(8):
                    tr.op("pe", lambda k=k, j=j, cc=cc: nc.tensor.matmul(
                        pb[:, j:j + 1], sv[:, k, cc * 128:(cc + 1) * 128], cact[:, k:k + 1],
                        start=(k == 0), stop=(k == 7)),
                        r=[("ring", s), "cact"], w=[("ps", bi)], inc=(k == 7 and cc == 1))
        tr.op("dve", lambda: nc.vector.tensor_tensor(out=modT[:], in0=pb[:, 0:72], in1=vec[:, 8:80], op=ALU.add),
              r=[("ps", bi), "vec"], w=["modT"])
        for s in range(3):
            tr.op("dve", lambda s=s: nc.vector.scalar_tensor_tensor(
                out=der[:, s * 8:(s + 1) * 8], in0=modT[:, (3 * s + 1) * 8:(3 * s + 2) * 8], scalar=1.0,
                in1=vec[:, 80 + s * 8:88 + s * 8], op0=ALU.add, op1=ALU.mult), r=["modT", "vec"], w=["der"])
            gsc = 1.0 if s == 1 else 0.5
            tr.op("dve", lambda s=s, gsc=gsc: nc.vector.tensor_scalar(
                out=der[:, 24 + s * 8:32 + s * 8], in0=modT[:, (3 * s + 2) * 8:(3 * s + 3) * 8],
                scalar1=gsc, scalar2=None, op0=ALU.mult), r=["modT"], w=["der"])
        NF = vec[:, 104:112]
        tr.op("act", lambda: nc.scalar.activation(out=negA[:], in_=tp[:, 0:16], func=AF.Exp), r=["tp"], w=["negA"])
        tr.op("dve", lambda: nc.vector.tensor_scalar(out=negA[:], in0=negA[:], scalar1=-1.0, scalar2=None,
                                                     op0=ALU.mult), r=["negA"], w=["negA"])

        tr.barrier()
        st0.close()

        def load_tile(dst, src, t0, key, n=NT, coff=0):
            for i in range(8):
                tr.dma("sp", dst[:, i, coff:coff + n], src[i * 128:(i + 1) * 128, t0:t0 + n],
                       w=[(key, i)], sem=(key,))

        def norm_mod(P, xt, xkey, u, s_idx):
            bi, pss = nps()
            for i in range(8):
                sq = P["sq"][i % 2]
                tr.op("act", lambda i=i, sq=sq: nc.scalar.activation(out=sq[:], in_=xt[:, i, :], func=AF.Square),
                      r=[(xkey, i)], w=[("sq", i % 2)])
                tr.op("pe", lambda i=i, sq=sq: nc.tensor.matmul(pss[:], ONES, sq[:], start=(i == 0), stop=(i == 7)),
                      r=[("sq", i % 2), "cs"], w=[("ps", bi)], inc=True)
            rstd = P["rstd"]
            tr.op("act", lambda: nc.scalar.activation(out=rstd[:], in_=pss[:], func=AF.Sqrt, bias=P["eps"][:, 0:1],
                                                      scale=1.0 / D), r=[("ps", bi), "eps"], w=["rstd"])
            tr.op("dve", lambda: nc.vector.reciprocal(out=rstd[:], in_=rstd[:]), r=["rstd"], w=["rstd"])
            for i in range(8):
                tt = P["tt"][i % 2]
                tr.op("dve", lambda i=i, tt=tt: nc.vector.scalar_tensor_tensor(
                    out=tt[:], in0=xt[:, i, :], scalar=der[:, s_idx * 8 + i:s_idx * 8 + i + 1], in1=rstd[:],
                    op0=ALU.mult, op1=ALU.mult), r=[(xkey, i), "rstd", "der"], w=[("tt", i % 2)])
                tr.op("act", lambda i=i, tt=tt: nc.scalar.activation(
                    out=u[:, i, :], in_=tt[:], func=AF.Identity,
                    bias=modT[:, 3 * s_idx * 8 + i:3 * s_idx * 8 + i + 1], scale=1.0),
                    r=[("tt", i % 2), "modT"], w=[("u", i)])

        def ffn_body(P, xt, xkey, s_idx, wdn, final, dst, t0):
            u, hid = P["u"], P["hid"]
            norm_mod(P, xt, xkey, u, s_idx)
            for j in range(NJ):
                s, slot = ring.get()
                sv = slot[:].rearrange("p (k c) -> p k c", k=8)
                ba_, pa = nps()
                bb_, pbb = nps()
                for part, (bix, pt) in enumerate(((ba_, pa), (bb_, pbb))):
                    for k in range(8):
                        tr.op("pe", lambda k=k, part=part, pt=pt: nc.tensor.matmul(
                            pt[:], sv[:, k, part * 128:(part + 1) * 128], u[:, k, :], start=(k == 0), stop=(k == 7)),
                            r=[("ring", s), ("u", k)], w=[("ps", bix)], inc=(k == 7))
                sl = P["s"][j % 2]
                tr.op("act", lambda sl=sl, pa=pa: nc.scalar.activation(out=sl[:], in_=pa[:], func=AF.Silu),
                      r=[("ps", ba_)], w=[("s", j % 2)])
                tr.op("dve", lambda sl=sl, pbb=pbb, j=j: nc.vector.tensor_tensor(
                    out=hid[:, j, :], in0=pbb[:], in1=sl[:], op=ALU.mult),
                    r=[("ps", bb_), ("s", j % 2)], w=[("hid", j)])
            for m in range(8):
                bd, pd = nps()
                for j in range(NJ):
                    tr.op("pe", lambda j=j, m=m, pd=pd: nc.tensor.matmul(
                        pd[:], wdn[:, j, m * 128:(m + 1) * 128], hid[:, j, :], start=(j == 0), stop=(j == NJ - 1)),
                        r=["wdn", ("hid", j)], w=[("ps", bd)], inc=(j == NJ - 1))
                tr.op("dve", lambda m=m, pd=pd: nc.vector.scalar_tensor_tensor(
                    out=xt[:, m, :], in0=pd[:], scalar=der[:, 24 + s_idx * 8 + m:24 + s_idx * 8 + m + 1],
                    in1=xt[:, m, :], op0=ALU.mult, op1=ALU.add), r=[("ps", bd), "der"], w=[(xkey, m)])
            if final:
                bi, pss = nps()
                for i in range(8):
                    sq = P["sq"][i % 2]
                    tr.op("act", lambda i=i, sq=sq: nc.scalar.activation(out=sq[:], in_=xt[:, i, :], func=AF.Square),
                          r=[(xkey, i)], w=[("sq", i % 2)])
                    tr.op("pe", lambda i=i, sq=sq: nc.tensor.matmul(pss[:], ONES, sq[:], start=(i == 0),
                                                                    stop=(i == 7)),
                          r=[("sq", i % 2), "cs"], w=[("ps", bi)], inc=True)
                rstd = P["rstd"]
                tr.op("act", lambda: nc.scalar.activation(out=rstd[:], in_=pss[:], func=AF.Sqrt,
                                                          bias=P["eps"][:, 0:1], scale=1.0 / D),
                      r=[("ps", bi), "eps"], w=["rstd"])
                tr.op("dve", lambda: nc.vector.reciprocal(out=rstd[:], in_=rstd[:]), r=["rstd"], w=["rstd"])
                for i in range(8):
                    tr.op("dve", lambda i=i: nc.vector.scalar_tensor_tensor(
                        out=xt[:, i, :], in0=xt[:, i, :], scalar=NF[:, i:i + 1], in1=rstd[:],
                        op0=ALU.mult, op1=ALU.mult), r=[(xkey, i), "rstd", "vec"], w=[(xkey, i)])
            for i in range(8):
                tr.dma("sp", dst[i * 128:(i + 1) * 128, t0:t0 + NT], xt[:, i, :], r=[(xkey, i)], sem=(xkey, "st"))

        def ffn_pool(stack, wdn_src, nxt=2):
            P = {}
            P["wdn"] = sb("wdn", [128, NJ, D], BF16, stack)
            P["xt"] = [sb("xt%d" % i, [128, 8, NT], F32, stack) for i in range(nxt)]
            P["sq"] = [sb("sq%d" % i, [128, NT], F32, stack) for i in range(2)]
            P["tt"] = [sb("tt%d" % i, [128, NT], F32, stack) for i in range(2)]
            P["s"] = [sb("s%d" % i, [128, NT], F32, stack) for i in range(2)]
            P["rstd"] = sb("rstd", [128, NT], F32, stack)
            P["u"] = sb("u", [128, 8, NT], BF16, stack)
            P["hid"] = sb("hid", [128, NJ, NT], BF16, stack)
            P["eps"] = sb("eps", [128, 1], F32, stack)
            tr.op("dve", lambda: nc.vector.memset(P["eps"][:], EPS), w=["eps"])
            wv = wdn_src.rearrange("(j p) n -> p j n", p=128)
            for jj in range(0, NJ, 2):
                tr.dma("pool", P["wdn"][:, jj:jj + 2, :], wv[:, jj:jj + 2, :], w=["wdn"], sem="wdn")
            return P

        with ExitStack() as st1:
            ring_phase(st1, [b for _ in range(ntile) for b in range(OFF_UP1, OFF_UP1 + NB_UP)], nr=8)
            P = ffn_pool(st1, wdn1)
            load_tile(P["xt"][0], xT, 0, "xt0")
            for n in range(ntile):
                xt = P["xt"][n % 2]
                xkey = "xt%d" % (n % 2)
                if n + 1 < ntile:
                    load_tile(P["xt"][(n + 1) % 2], xT, (n + 1) * NT, "xt%d" % ((n + 1) % 2))
                ffn_body(P, xt, xkey, 0, P["wdn"], False, H1T, n * NT)
            tr.barrier()

        with ExitStack() as st2:
            ring_phase(st2, [b for _ in range(ntile) for b in range(OFF_IN, OFF_IN + NB_IN + NB_Z)], nr=8)
            xts = [sb("xt%d" % i, [128, 8, NT], F32, st2) for i in range(2)]
            P = {"sq": [sb("sq%d" % i, [128, NT], F32, st2) for i in range(2)],
                 "tt": [sb("tt%d" % i, [128, NT], F32, st2) for i in range(2)],
                 "rstd": sb("rstd", [128, NT], F32, st2), "eps": sb("eps", [128, 1], F32, st2)}
            tr.op("dve", lambda: nc.vector.memset(P["eps"][:], EPS), w=["eps"])
            u = sb("u", [128, 8, NT], BF16, st2)
            wbas = sb("wbas", [128, 8, 32], BF16, st2)
            stg = [sb("stg%d" % i, [128, NT], F32, st2) for i in range(4)]
            zero = sb("zero", [128, NCONV, 2], F32, st2)
            tr.dma("pool", wbas[:], wba.rearrange("(k p) n -> p k n", p=128), w=["wbas"], sem="wbas")
            tr.op("dve", lambda: nc.vector.memset(zero[:], 0.0), w=["zero"])
            tr.dma("sp", PT[0:NCONV * 128, 0:2].rearrange("(c p) n -> p c n", p=128), zero[:], r=["zero"], sem="zero")
            sti = 0
            load_tile(xts[0], H1T, 0, "xt0")
            for n in range(ntile):
                xt = xts[n % 2]
                xkey = "xt%d" % (n % 2)
                if n + 1 < ntile:
                    load_tile(xts[(n + 1) % 2], H1T, (n + 1) * NT, "xt%d" % ((n + 1) % 2))
                norm_mod(P, xt, xkey, u, 1)
                for blk in range(NB_IN):
                    s, slot = ring.get()
                    sv = slot[:].rearrange("p (k c) -> p k c", k=8)
                    for cc in range(2):
                        ch = blk * 2 + cc
                        bi, pt = nps()
                        for k in range(8):
                            tr.op("pe", lambda k=k, cc=cc, pt=pt: nc.tensor.matmul(
                                pt[:], sv[:, k, cc * 128:(cc + 1) * 128], u[:, k, :], start=(k == 0), stop=(k == 7)),
                                r=[("ring", s), ("u", k)], w=[("ps", bi)], inc=(k == 7))
                        sg = stg[sti % 4]
                        sk = ("stg", sti % 4)
                        sti += 1
                        tr.op("act", lambda sg=sg, pt=pt: nc.scalar.copy(out=sg[:], in_=pt[:]), r=[("ps", bi)], w=[sk])
                        tr.dma("sp", PT[ch * 128:(ch + 1) * 128, 2 + n * NT:2 + (n + 1) * NT], sg[:], r=[sk], sem=sk)
                for zb in range(NB_Z):
                    s, slot = ring.get()
                    sv = slot[:].rearrange("p (k c) -> p k c", k=8)
                    for tb in range(0, 4, 2):
                        bi, pt = nps()
                        for t2 in range(2):
                            for k in range(8):
                                tr.op("pe", lambda k=k, t2=t2, tb=tb, pt=pt: nc.tensor.matmul(
                                    pt[:, t2 * 256:(t2 + 1) * 256], u[:, k, (tb + t2) * 128:(tb + t2 + 1) * 128],
                                    sv[:, k, :], start=(k == 0), stop=(k == 7)),
                                    r=[("ring", s), ("u", k)], w=[("ps", bi)], inc=(k == 7))
                        sg = stg[sti % 4]
                        sk = ("stg", sti % 4)
                        sti += 1
                        tr.op("act", lambda sg=sg, pt=pt: nc.scalar.activation(out=sg[:], in_=pt[:], func=AF.Silu),
                              r=[("ps", bi)], w=[sk])
                        for t2 in range(2):
                            r0 = n * NT + (tb + t2) * 128
                            tr.dma("sp", ZS[r0:r0 + 128, zb * 256:(zb + 1) * 256], sg[:, t2 * 256:(t2 + 1) * 256],
                                   r=[sk], sem=sk)
                bi, pt = nps()
                for tb in range(4):
                    for k in range(8):
                        tr.op("pe", lambda k=k, tb=tb, pt=pt: nc.tensor.matmul(
                            pt[:, tb * 32:(tb + 1) * 32], u[:, k, tb * 128:(tb + 1) * 128], wbas[:, k, :],
                            start=(k == 0), stop=(k == 7)), r=["wbas", ("u", k)], w=[("ps", bi)], inc=(k == 7))
                sg = stg[sti % 4]
                sk = ("stg", sti % 4)
                sti += 1
                tr.op("act", lambda sg=sg, pt=pt: nc.scalar.copy(out=sg[:, 0:128], in_=pt[:, 0:128]),
                      r=[("ps", bi)], w=[sk])
                tr.dma("sp", BAt[n * NT:(n + 1) * NT, :].rearrange("(b p) n -> p b n", p=128),
                       sg[:, 0:128].rearrange("p (b n) -> p b n", b=4), r=[sk], sem=sk)
            tr.barrier()
            hx = sb("hx", [128, 2, NCONV, 2], F32, st2)
            hy = sb("hy", [128, NCONV, 2], F32, st2)
            tr.dma("pool", hx_in.ap(), PT[0:NCONV * 128, T:T + 2], w=["hx_in"], sem="hx")
            tr.cc(lambda: nc.gpsimd.collective_compute("AllGather", ALU.bypass, replica_groups=PAIRS,
                                                        ins=[hx_in.ap()], outs=[hx_out.ap()]),
                  r=["hx_in"], w=["hx_out"], sem="hxcc")
            tr.dma("pool", hx[:], hx_out.ap().rearrange("(r c p) n -> p r c n", r=2, p=128), r=["hx_out"], w=["hx"],
                   sem="hx")
            tr.op("dve", lambda: nc.vector.tensor_scalar(out=hy[:], in0=hx[:, 0], scalar1=selt[:, 0:1], scalar2=None,
                                                         op0=ALU.mult), r=["hx", "selt"], w=["hy"])
            tr.op("dve", lambda: nc.vector.scalar_tensor_tensor(out=hy[:], in0=hx[:, 1], scalar=selt[:, 1:2],
                                                                in1=hy[:], op0=ALU.mult, op1=ALU.add),
                  r=["hx", "selt", "hy"], w=["hy"])
            pv = PT[0:NCONV * 128, :].rearrange("(c p) n -> p c n", p=128)
            hys = sb("hys", [128, NCONV, 2], F32, st2)
            tr.op("dve", lambda: nc.vector.tensor_copy(out=hys[:, :, 0:1], in_=hy[:, :, 1:2]), r=["hy"], w=["hys"])
            tr.op("dve", lambda: nc.vector.tensor_copy(out=hys[:, :, 1:2], in_=hy[:, :, 0:1]), r=["hy"], w=["hys"])
            tr.dma("sp", pv[:, :, T + 2:T + 4], hys[:], r=["hys"], sem="hy")
            tr.barrier()

        def run_streams(factories, width, stagger=0):
            active = []
            free = list(range(width))
            it = iter(factories)
            done = False
            if stagger:
                f = next(it, None)
                if f is not None:
                    s_ = free.pop(0)
                    g0 = f(s_)
                    active.append((g0, s_))
                    for _ in range(stagger):
                        try:
                            next(g0)
                        except StopIteration:
                            active.remove((g0, s_))
                            free.append(s_)
                            break
            while True:
                while free and not done:
                    f = next(it, None)
                    if f is None:
                        done = True
                        break
                    s_ = free.pop(0)
                    active.append((f(s_), s_))
                if not active:
                    break
                for g in list(active):
                    try:
                        next(g[0])
                    except StopIteration:
                        active.remove(g)
                        free.append(g[1])

        with ExitStack() as st3:
            W2B = 6
            B2 = []
            for i in range(W2B):
                B2.append({"pre": sb("pre", [128, NT + 4], BF16, st3), "acc": sb("acc", [128, NT], F32, st3),
                           "sq": sb("sqq", [128, NT], F32, st3), "rn": sb("rn", [128, NT], F32, st3),
                           "ob": sb("ob", [128, NT], BF16, st3)})
            A2 = []
            for i in range(2):
                A2.append({"p1": sb("p1", [128, NT + 4], F32, st3), "p2": sb("p2", [128, NT + 4], F32, st3),
                           "p3": sb("p3", [128, NT], F32, st3), "pcv": sb("pcv", [128, NT + 4], BF16, st3),
                           "cacc": sb("cacc", [128, NT], F32, st3)})
            G2 = []
            for i in range(2):
                G2.append({"g": sb("gsb", [128, NT], F32, st3), "pa": sb("pab", [128, NT], F32, st3)})
            yaT = [sb("yaT", [128, 4, NT], BF16, st3) for i in range(2)]
            wa = sb("wa", [128, 4, D], BF16, st3)
            dgq = sb("dgq", [128, 24 * 5, 128], BF16, st3)
            dga = sb("dga", [128, 4 * 3, 128], BF16, st3)
            IDF = cs[:, 3392:3520]
            for j in range(24 * 5):
                eng = "dve"
                if eng == "dve":
                    tr.op("dve", lambda j=j: nc.vector.tensor_scalar(out=dgq[:, j, :], in0=IDF, scalar1=cw[:, j:j + 1],
                                                                     scalar2=None, op0=ALU.mult),
                          r=["cs", "cw"], w=[("dgq", j)])
                else:
                    tr.op("pool", lambda j=j: nc.gpsimd.tensor_scalar(out=dgq[:, j, :], in0=IDF, scalar1=cw[:, j:j + 1],
                                                                      scalar2=None, op0=ALU.mult),
                          r=["cs", "cw"], w=[("dgq", j)])
            for j in range(12):
                tr.op("dve", lambda j=j: nc.vector.tensor_scalar(out=dga[:, j, :], in0=IDF,
                                                                 scalar1=cw[:, 120 + j:121 + j], scalar2=None,
                                                                 op0=ALU.mult), r=["cs", "cw"], w=[("dga", j)])
            epsq = sb("epsq", [128, 2], F32, st3)
            tr.op("dve", lambda: nc.vector.memset(epsq[:, 0:1], EPS), w=["epsq"])
            tr.op("dve", lambda: nc.vector.memset(epsq[:, 1:2], 128.0 * EPS), w=["epsq"])
            tr.dma("pool", wa[:], waout.rearrange("(k p) n -> p k n", p=128), w=["wa"], sem="wa")

            def qkv_gen(n, ch, sl):
                c0 = n * NT
                Bf = B2[sl]
                p_, a_, s_, r_, o_ = Bf["pre"], Bf["acc"], Bf["sq"], Bf["rn"], Bf["ob"]
                pk, ak, sk, rk, ok = ("pre", sl), ("acc", sl), ("sqq", sl), ("rn", sl), ("ob", sl)
                tr.dma("pool", p_[:], PT[(8 + ch) * 128:(9 + ch) * 128, c0:c0 + NT + 4], w=[pk], sem=pk)
                yield
                bc_, pc_ = nps()
                for tap in range(5):
                    tr.op("pe", lambda tap=tap: nc.tensor.matmul(pc_[:], dgq[:, ch * 5 + tap, :], p_[:, tap:tap + NT],
                                                                 start=(tap == 0), stop=(tap == 4)),
                          r=[pk, ("dgq", ch * 5 + tap)], w=[("ps", bc_)], inc=(tap == 4))
                yield
                tr.op("act", lambda: nc.scalar.activation(out=a_[:], in_=pc_[:], func=AF.Silu),
                      r=[("ps", bc_)], w=[ak])
                yield
                if ch < 16:
                    tr.op("act", lambda: nc.scalar.activation(out=s_[:], in_=a_[:], func=AF.Square), r=[ak], w=[sk])
                    yield
                    bi, pt = nps()
                    tr.op("pe", lambda: nc.tensor.matmul(pt[:], ONES, s_[:], start=True, stop=True),
                          r=[sk, "cs"], w=[("ps", bi)])
                    yield
                    qsc = 128.0 if ch < 8 else 1.0
                    ebias = epsq[:, 1:2] if ch < 8 else epsq[:, 0:1]
                    tr.op("act", lambda: nc.scalar.activation(out=r_[:], in_=pt[:], func=AF.Sqrt, bias=ebias,
                                                              scale=qsc), r=[("ps", bi), "epsq"], w=[rk])
                    yield
                    tr.op("dve", lambda: nc.vector.reciprocal(out=r_[:], in_=r_[:]), r=[rk], w=[rk])
                    yield
                    tr.op("dve", lambda: nc.vector.tensor_tensor(out=o_[:], in0=a_[:], in1=r_[:], op=ALU.mult),
                          r=[ak, rk], w=[ok])
                else:
                    tr.op("pool", lambda: nc.gpsimd.tensor_copy(out=o_[:], in_=a_[:]), r=[ak], w=[ok])
                yield
                tr.dma("sp", QKVT[ch * 128:(ch + 1) * 128, c0:c0 + NT], o_[:], r=[ok], sem=ok)

            def abr_gen(n, ch, sl):
                c0 = n * NT
                Af = A2[sl]
                p1, p2, p3, pcv, cacc = Af["p1"], Af["p2"], Af["p3"], Af["pcv"], Af["cacc"]
                k1, k2, k3, kp, kc = ("p1", sl), ("p2", sl), ("p3", sl), ("pcv", sl), ("cacc", sl)
                ya = yaT[n % 2]
                tr.dma("sp", p1[:], PT[ch * 128:(ch + 1) * 128, c0:c0 + NT + 4], w=[k1], sem=k1)
                tr.dma("sp", p2[:], PT[(4 + ch) * 128:(5 + ch) * 128, c0:c0 + NT + 4], w=[k2], sem=k2)
                tr.dma("sp", p3[:], PT[(32 + ch) * 128:(33 + ch) * 128, c0 + 2:c0 + 2 + NT], w=[k3], sem=k3)
                yield
                tr.op("dve", lambda: nc.vector.tensor_tensor(out=pcv[:], in0=p1[:], in1=p2[:], op=ALU.mult),
                      r=[k1, k2], w=[kp])
                yield
                bc_, pc_ = nps()
                for tap in range(3):
                    tr.op("pe", lambda tap=tap: nc.tensor.matmul(pc_[:], dga[:, ch * 3 + tap, :],
                                                                 pcv[:, 1 + tap:1 + tap + NT], start=(tap == 0),
                                                                 stop=(tap == 2)),
                          r=[kp, ("dga", ch * 3 + tap)], w=[("ps", bc_)], inc=(tap == 2))
                yield
                tr.op("dve", lambda: nc.vector.tensor_tensor(out=ya[:, ch, :], in0=pc_[:], in1=p3[:], op=ALU.mult),
                      r=[("ps", bc_), k3], w=[("yaT", n % 2, ch)])

            def gate_gen(n, m, sl):
                c0 = n * NT
                g_, pa_ = G2[sl]["g"], G2[sl]["pa"]
                gk, pk2 = ("gsb", sl), ("pab", sl)
                ya = yaT[n % 2]
                tr.dma("sp", g_[:], PT[(36 + m) * 128:(37 + m) * 128, c0 + 2:c0 + 2 + NT], w=[gk], sem=gk)
                yield
                tr.op("act", lambda: nc.scalar.activation(out=g_[:], in_=g_[:], func=AF.Sigmoid), r=[gk], w=[gk])
                bi, pt = nps()
                for kk in range(4):
                    tr.op("pe", lambda kk=kk: nc.tensor.matmul(pt[:], wa[:, kk, m * 128:(m + 1) * 128], ya[:, kk, :],
                                                               start=(kk == 0), stop=(kk == 3)),
                          r=["wa", ("yaT", n % 2, kk)], w=[("ps", bi)], inc=(kk == 3))
                yield
                tr.op("dve", lambda: nc.vector.tensor_tensor(out=pa_[:], in0=pt[:], in1=g_[:], op=ALU.mult),
                      r=[("ps", bi), gk], w=[pk2])
                yield
                tr.dma("sp", PAT[m * 128:(m + 1) * 128, c0:c0 + NT], pa_[:], r=[pk2], sem=pk2)

            for n in range(ntile):
                run_streams([(lambda sl, n=n, ch=ch: qkv_gen(n, ch, sl)) for ch in range(24)], W2B)
                run_streams([(lambda sl, n=n, ch=ch: abr_gen(n, ch, sl)) for ch in range(4)], 2)
                run_streams([(lambda sl, n=n, m=m: gate_gen(n, m, sl)) for m in range(8)], 2)
            tr.barrier()

        with ExitStack() as st4:
            qkv = [sb("qkv%d" % i, [128, 24, NT], BF16, st4) for i in range(2)]
            S = sb("S", [128, H, 128], F32, st4)
            Sb = sb("Sb", [128, H, 128], BF16, st4)
            sm = sb("sm", [128, 8, 8], F32, st4)
            gT = sb("gT", [8, 3, 64], F32, st4)
            Rg = sb("Rg", [8, 2, 512], F32, st4)
            dec = [sb("dec%d" % i, [64, 512], F32, st4) for i in range(2)]
            egr = sb("egr", [128, 512], F32, st4)
            qg = sb("qg", [128, H, C], BF16, st4)
            XY = [sb("XY%d" % i, [64, 2, 512], BF16, st4) for i in range(2)]
            PQ = [sb("PQ%d" % i, [64, 2, 512], BF16, st4) for i in range(2)]
            attn = sb("attn", [64, 512], BF16, st4)
            XY0 = sb("XY0", [64, 2, 512], BF16, st4)
            MK = sb("MK", [64, 4, 512], BF16, st4)
            tr.dma("pool", MK[:], cst2.rearrange("p (a n) -> p a n", a=4), w=["MK"], sem="MK")
            kbg = sb("kbg", [64, H, 128], BF16, st4)
            kdec = sb("kdec", [64, H, 128], BF16, st4)
            vb = sb("vb", [64, H, 128], BF16, st4)
            usb = sb("usb", [64, H, 128], F32, st4)
            wT = sb("wT", [128, H, C], BF16, st4)
            vnew = sb("vnew", [64, H, 128], BF16, st4)
            osb = sb("osb", [64, H, 128], F32, st4)
            ofw = sb("ofw", [64, H, 128], F32, st4)
            zsb = sb("zsb", [64, H, 128], F32, st4)
            ysq = sb("ysq", [64, H, 128], F32, st4)
            ss = sb("ss", [64, 16], F32, st4)
            ss2 = sb("ss2", [64, 16], F32, st4)
            ss3 = sb("ss3", [64, 16], F32, st4)
            yb = sb("yb", [64, H, 128], BF16, st4)
            ybT = sb("ybT", [128, H, NT], BF16, st4)
            SX = ybT[:].bitcast(F32).rearrange("p h n -> p (h n)").rearrange("p (r m) -> p r m", r=2)
            epsd = sb("epsd", [128, 1], F32, st4)
            tr.op("dve", lambda: nc.vector.memset(epsd[:], EPS), w=["epsd"])
            tr.op("dve", lambda: nc.vector.memset(S[:], 0.0), w=["S"])
            tr.op("dve", lambda: nc.vector.memset(Sb[:], 0.0), w=["Sb"])
            nck = NT // C

            def bc(ap, shape):
                return ap.to_broadcast(list(shape))

            WG = 2
            TB = [{"bat": sb("bat", [64, NT // C, 32], F32, st4), "beta": sb("beta", [64, NT // C, 8], F32, st4),
                   "lnb": sb("lnb", [64, NT // C, 8], F32, st4), "gg": sb("gg", [64, NT // C, 8], F32, st4)}
                  for _i in range(2)]
            CB = [{"sm": sm, "gT": gT, "Rg": Rg, "dec": dec, "egr": egr, "qg": qg, "XY": XY, "PQ": PQ, "attn": attn,
                   "XY0": XY0, "kbg": kbg, "kdec": kdec, "vb": vb, "usb": usb, "wT": wT, "vnew": vnew, "osb": osb,
                   "ofw": ofw, "zsb": zsb, "ysq": ysq, "ss": ss, "ss2": ss2, "ss3": ss3, "yb": yb}]
            for _i in range(1, WG):
                CB.append({
                    "sm": sb("sm", [128, 8, 8], F32, st4), "gT": sb("gT", [8, 3, 64], F32, st4),
                    "Rg": sb("Rg", [8, 2, 512], F32, st4),
                    "dec": [sb("dec", [64, 512], F32, st4) for _j in range(2)],
                    "egr": sb("egr", [128, 512], F32, st4), "qg": sb("qg", [128, H, C], BF16, st4),
                    "XY": [sb("XY", [64, 2, 512], BF16, st4) for _j in range(2)],
                    "PQ": [sb("PQ", [64, 2, 512], BF16, st4) for _j in range(2)],
                    "attn": sb("attn", [64, 512], BF16, st4), "XY0": sb("XY0", [64, 2, 512], BF16, st4),
                    "kbg": sb("kbg", [64, H, 128], BF16, st4), "kdec": sb("kdec", [64, H, 128], BF16, st4),
                    "vb": sb("vb", [64, H, 128], BF16, st4), "usb": sb("usb", [64, H, 128], F32, st4),
                    "wT": sb("wT", [128, H, C], BF16, st4), "vnew": sb("vnew", [64, H, 128], BF16, st4),
                    "osb": sb("osb", [64, H, 128], F32, st4), "ofw": sb("ofw", [64, H, 128], F32, st4),
                    "zsb": sb("zsb", [64, H, 128], F32, st4), "ysq": sb("ysq", [64, H, 128], F32, st4),
                    "ss": sb("ss", [64, 16], F32, st4), "ss2": sb("ss2", [64, 16], F32, st4),
                    "ss3": sb("ss3", [64, 16], F32, st4), "yb": sb("yb", [64, H, 128], BF16, st4)})

            def chunk_gen(dr, n, c, first, last, sl):
                Bc = CB[sl]
                Tt = TB[n % 2]
                bat, beta, lnb, gg = Tt["bat"], Tt["beta"], Tt["lnb"], Tt["gg"]
                sm, gT, Rg, dec, egr, qg, XY, PQ, attn, XY0 = (Bc["sm"], Bc["gT"], Bc["Rg"], Bc["dec"], Bc["egr"],
                                                               Bc["qg"], Bc["XY"], Bc["PQ"], Bc["attn"], Bc["XY0"])
                kbg, kdec, vb, usb, wT, vnew, osb, ofw, zsb = (Bc["kbg"], Bc["kdec"], Bc["vb"], Bc["usb"], Bc["wT"],
                                                               Bc["vnew"], Bc["osb"], Bc["ofw"], Bc["zsb"])
                ysq, ss, ss2, ss3, yb = Bc["ysq"], Bc["ss"], Bc["ss2"], Bc["ss3"], Bc["yb"]
                qt = qkv[n % 2]
                qk_ = ("qkv", n % 2)
                if first:
                    tr.dma("sp", qt[:], QKVT[:, n * NT:(n + 1) * NT].rearrange("(c p) n -> p c n", p=128),
                           w=[qk_], sem=qk_)
                    tr.dma("sp", bat[:], BAt[n * NT:(n + 1) * NT, :].rearrange("(c p) n -> p c n", p=64),
                           w=[("bat", "t", n % 2)], sem=("bat", "t", n % 2))
                    bsl = bat[:, :, dr * 8:dr * 8 + 8]
                    asl = bat[:, :, 16 + dr * 8:24 + dr * 8]
                    tr.op("act", lambda: nc.scalar.activation(out=beta[:], in_=bsl, func=AF.Sigmoid),
                          r=[("bat", "t", n % 2)], w=[("beta", "t", n % 2)])
                    tr.op("act", lambda: nc.scalar.activation(out=lnb[:], in_=bsl, func=AF.Exp, scale=-1.0),
                          r=[("bat", "t", n % 2)], w=[("lnb", "t", n % 2)])
                    tr.op("act", lambda: nc.scalar.activation(out=lnb[:], in_=lnb[:], func=AF.Ln, bias=1.0, scale=1.0),
                          r=[("lnb", "t", n % 2)], w=[("lnb", "t", n % 2)])
                    for c2 in range(nck):
                        tr.op("dve", lambda c2=c2: nc.vector.tensor_tensor(
                            out=gg[:, c2, :], in0=bat[:, c2, 16 + dr * 8:24 + dr * 8],
                            in1=tp[0:64, 16 + dr * 8:24 + dr * 8], op=ALU.add), r=[("bat", "t", n % 2), "tp"], w=[("gg", "t", n % 2)])
                    tr.op("act", lambda: nc.scalar.activation(out=gg[:], in_=gg[:], func=AF.Exp), r=[("gg", "t", n % 2)], w=[("gg", "t", n % 2)])
                    tr.op("act", lambda: nc.scalar.activation(out=gg[:], in_=gg[:], func=AF.Ln, bias=1.0, scale=1.0),
                          r=[("gg", "t", n % 2)], w=[("gg", "t", n % 2)])
                    for c2 in range(nck):
                        tr.op("dve", lambda c2=c2: nc.vector.tensor_tensor(
                            out=gg[:, c2, :], in0=gg[:, c2, :], in1=negA[0:64, dr * 8:dr * 8 + 8], op=ALU.mult),
                            r=[("gg", "t", n % 2), "negA"], w=[("gg", "t", n % 2)])
                tok = slice(c * C, (c + 1) * C)
                gtok = n * NT + c * C
                yield
                bi, pg = nps()
                tr.op("pe", lambda: nc.tensor.matmul(pg[0:64, 0:8], LTRI[dr], gg[:, c, :], start=True,
                                                     stop=True), r=[("gg", "t", n % 2), "cs"], w=[("ps", bi)], inc=False)
                tr.op("pe", lambda: nc.tensor.matmul(pg[:, 8:16], cs[0:64, 0:128], gg[:, c, :], start=True,
                                                     stop=True), r=[("gg", "t", n % 2), "cs"], w=[("ps", bi)])
                tr.op("dve", lambda: nc.vector.tensor_copy(out=sm[0:64, 0, :], in_=pg[0:64, 0:8]),
                      r=[("ps", bi)], w=[("sm", sl)])
                tr.op("dve", lambda: nc.vector.tensor_tensor(out=sm[0:64, 1, :], in0=pg[0:64, 0:8],
                                                             in1=lnb[:, c, :], op=ALU.subtract),
                      r=[("ps", bi), ("lnb", "t", n % 2)], w=[("sm", sl)])
                tr.op("act", lambda: nc.scalar.activation(out=sm[0:64, 2, :], in_=pg[0:64, 0:8], func=AF.Exp),
                      r=[("ps", bi)], w=[("sm", sl)])
                tr.op("dve", lambda: nc.vector.tensor_tensor(out=sm[0:64, 3, :], in0=pg[0:64, 8:16],
                                                             in1=sm[0:64, 0, :], op=ALU.subtract),
                      r=[("ps", bi), ("sm", sl)], w=[("sm", sl)])
                tr.op("act", lambda: nc.scalar.activation(out=sm[0:64, 3, :], in_=sm[0:64, 3, :], func=AF.Exp),
                      r=[("sm", sl)], w=[("sm", sl)])
                tr.op("dve", lambda: nc.vector.tensor_tensor(out=sm[0:64, 4, :], in0=sm[0:64, 2, :],
                                                             in1=beta[:, c, :], op=ALU.mult),
                      r=[("sm", sl), ("beta", "t", n % 2)], w=[("sm", sl)])
                tr.op("act", lambda: nc.scalar.activation(out=sm[:, 5, :], in_=pg[:, 8:16], func=AF.Exp),
                      r=[("ps", bi)], w=[("sm", sl)])
                yield
                bi2, pt2 = nps()
                tr.op("pe", lambda: nc.tensor.matmul(pt2[0:8, 0:64], sm[0:64, 0, :], ID64, start=True,
                                                     stop=True), r=[("sm", sl), "cs"], w=[("ps", bi2)], inc=False)
                tr.op("pe", lambda: nc.tensor.matmul(pt2[0:8, 64:128], sm[0:64, 1, :], ID64, start=True,
                                                     stop=True), r=[("sm", sl), "cs"], w=[("ps", bi2)])
                tr.op("dve", lambda: nc.vector.tensor_scalar(out=gT[:, 0, :], in0=pt2[0:8, 0:64], scalar1=-1.0,
                                                             scalar2=None, op0=ALU.mult),
                      r=[("ps", bi2)], w=[("gT", sl)])
                tr.op("dve", lambda: nc.vector.tensor_copy(out=gT[:, 1:3, :].rearrange("p a b -> p (a b)"),
                                                           in_=pt2[0:8, 0:128]),
                      r=[("ps", bi2)], w=[("gT", sl)])
                E8v = E8.rearrange("p (h i) -> p h i", h=8)
                tr.op("dve", lambda: nc.vector.tensor_tensor(
                    out=Rg[:, 0, :].rearrange("p (h i) -> p h i", h=8), in0=E8v,
                    in1=gT[:, 2:3, :].to_broadcast([8, 8, 64]), op=ALU.mult), r=[("gT", sl), "cs"], w=[("Rg", sl)])
                tr.op("dve", lambda: nc.vector.tensor_tensor(
                    out=Rg[:, 1, :].rearrange("p (h i) -> p h i", h=8), in0=E8v,
                    in1=gT[:, 1:2, :].to_broadcast([8, 8, 64]), op=ALU.mult), r=[("gT", sl), "cs"], w=[("Rg", sl)])
                yield
                bA, pA = nps()
                bQ, pQ = nps()
                bR, pR = nps()
                tr.op("pe", lambda: nc.tensor.matmul(pA[0:64, :], cs[0:8, 0:64], Rg[:, 0, :], start=True,
                                                     stop=False), r=[("Rg", sl), "cs"], w=[("ps", bA)], inc=False)
                tr.op("pe", lambda: nc.tensor.matmul(pA[0:64, :], gT[:, 0, :], E8, start=False, stop=True),
                      r=[("gT", sl), "cs"], w=[("ps", bA)])
                tr.op("pe", lambda: nc.tensor.matmul(pQ[0:64, :], cs[0:8, 0:64], Rg[:, 1, :], start=True,
                                                     stop=False), r=[("Rg", sl), "cs"], w=[("ps", bQ)], inc=False)
                tr.op("pe", lambda: nc.tensor.matmul(pQ[0:64, :], gT[:, 0, :], E8, start=False, stop=True),
                      r=[("gT", sl), "cs"], w=[("ps", bQ)])
                tr.op("pe", lambda: nc.tensor.matmul(pR[:, :], cs[0:8, 0:128], Rg[:, 1, :], start=True,
                                                     stop=True), r=[("Rg", sl), "cs"], w=[("ps", bR)])
                tr.op("dve", lambda: nc.vector.tensor_tensor(out=dec[0][:], in0=pA[0:64, :], in1=CAPS[dr],
                                                             op=ALU.min), r=[("ps", bA), "cs"], w=[("dec", 0, sl)])
                tr.op("act", lambda: nc.scalar.activation(out=dec[0][:], in_=dec[0][:], func=AF.Exp),
                      r=[("dec", 0, sl)], w=[("dec", 0, sl)])
                tr.op("dve", lambda: nc.vector.tensor_tensor(out=dec[1][:], in0=pQ[0:64, :], in1=CAPI[dr],
                                                             op=ALU.min), r=[("ps", bQ), "cs"], w=[("dec", 1, sl)])
                tr.op("act", lambda: nc.scalar.activation(out=dec[1][:], in_=dec[1][:], func=AF.Exp),
                      r=[("dec", 1, sl)], w=[("dec", 1, sl)])
                tr.op("act", lambda: nc.scalar.activation(out=egr[:], in_=pR[:], func=AF.Exp),
                      r=[("ps", bR)], w=[("egr", sl)])
                tr.op("dve", lambda: nc.vector.tensor_tensor(
                    out=qg[:], in0=qt[:, 0:8, tok], in1=egr[:].rearrange("p (h i) -> p h i", h=8),
                    op=ALU.mult), r=[qk_, ("egr", sl)], w=[("qg", sl)])
                yield
                bK, pK = nps()
                bQK, pQK = nps()
                for h in range(H):
                    tr.op("pe", lambda h=h: nc.tensor.matmul(pK[0:64, h * 64:(h + 1) * 64], qt[:, 8 + h, tok],
                                                             qt[:, 8 + h, tok], start=True, stop=True),
                          r=[qk_], w=[("ps", bK)], inc=(h == H - 1))
                for h in range(H):
                    tr.op("pe", lambda h=h: nc.tensor.matmul(pQK[0:64, h * 64:(h + 1) * 64], qt[:, 8 + h, tok],
                                                             qt[:, h, tok], start=True, stop=True),
                          r=[qk_], w=[("ps", bQK)], inc=(h == H - 1))
                X0, Y0 = XY0[:, 0, :], XY0[:, 1, :]
                tr.op("dve", lambda: nc.vector.tensor_tensor(out=X0, in0=pK[0:64, :], in1=dec[0][:],
                                                             op=ALU.mult),
                      r=[("ps", bK), ("dec", 0, sl)], w=[("X0", sl)])
                tr.op("dve", lambda: nc.vector.tensor_tensor(out=attn[:], in0=pQK[0:64, :], in1=dec[1][:],
                                                             op=ALU.mult),
                      r=[("ps", bQK), ("dec", 1, sl)], w=[("attn", sl)])
                yield
                bY, pY = nps()
                for h in range(H):
                    tr.op("pe", lambda h=h: nc.tensor.matmul(pY[0:64, h * 64:(h + 1) * 64],
                                                             X0[:, h * 64:(h + 1) * 64], ID64B, start=True,
                                                             stop=True),
                          r=[("X0", sl), "csb"], w=[("ps", bY)], inc=(h == H - 1))
                tr.op("act", lambda: nc.scalar.copy(out=Y0, in_=pY[0:64, :]), r=[("ps", bY)], w=[("Y0", sl)])
                Xa0, Ya0 = XY[0][:, 0, :], XY[0][:, 1, :]
                Pm, Qm = PQ[0][:, 0, :], PQ[0][:, 1, :]
                tr.op("pool", lambda: nc.gpsimd.tensor_tensor(out=Xa0, in0=X0, in1=MK[:, 0, :], op=ALU.mult),
                      r=[("X0", sl), "MK"], w=[("X", 0, sl)])
                tr.op("pool", lambda: nc.gpsimd.tensor_tensor(out=Ya0, in0=Y0, in1=MK[:, 0, :], op=ALU.mult),
                      r=[("Y0", sl), "MK"], w=[("Y", 0, sl)])
                tr.op("pool", lambda: nc.gpsimd.tensor_tensor(out=Pm, in0=I8, in1=Xa0, op=ALU.subtract),
                      r=[("X", 0, sl), "cs"], w=[("P", 0, sl)])
                tr.op("pool", lambda: nc.gpsimd.tensor_tensor(out=Qm, in0=I8, in1=Ya0, op=ALU.subtract),
                      r=[("Y", 0, sl), "cs"], w=[("Q", 0, sl)])

                def mm8(pt_, lhs, rhs, rk, bix):
                    for h in range(H):
                        hs = slice(h * 64, (h + 1) * 64)
                        tr.op("pe", lambda hs=hs: nc.tensor.matmul(pt_[0:64, hs], lhs[:, hs], rhs[:, hs],
                                                                   start=True, stop=True),
                              r=rk, w=[("ps", bix)], inc=(h == H - 1))

                for lv in range(2):
                    a, b = lv % 2, (lv + 1) % 2
                    Xa, Ya = XY[a][:, 0, :], XY[a][:, 1, :]
                    Xb, Yb = XY[b][:, 0, :], XY[b][:, 1, :]
                    Pa, Qa = PQ[a][:, 0, :], PQ[a][:, 1, :]
                    Pb, Qb = PQ[b][:, 0, :], PQ[b][:, 1, :]
                    yield
                    b1, p1 = nps()
                    mm8(p1, Ya, Xa, [("X", a, sl), ("Y", a, sl)], b1)
                    tr.op("act", lambda: nc.scalar.copy(out=Xb, in_=p1[0:64, :]), r=[("ps", b1)], w=[("X", b, sl)])
                    yield
                    b2, p2 = nps()
                    mm8(p2, Xa, Ya, [("X", a, sl), ("Y", a, sl)], b2)
                    tr.op("act", lambda: nc.scalar.copy(out=Yb, in_=p2[0:64, :]), r=[("ps", b2)], w=[("Y", b, sl)])
                    yield
                    b3, p3 = nps()
                    mm8(p3, Qa, Xb, [("Q", a, sl), ("X", b, sl)], b3)
                    tr.op("dve", lambda: nc.vector.tensor_tensor(out=Pb, in0=p3[0:64, :], in1=Pa, op=ALU.add),
                          r=[("ps", b3), ("P", a, sl)], w=[("P", b, sl)])
                    yield
                    b4, p4 = nps()
                    mm8(p4, Pa, Yb, [("P", a, sl), ("Y", b, sl)], b4)
                    tr.op("dve", lambda: nc.vector.tensor_tensor(out=Qb, in0=p4[0:64, :], in1=Qa, op=ALU.add),
                          r=[("ps", b4), ("Q", a, sl)], w=[("Q", b, sl)])
                cur = 0
                for li in range(3):
                    nxt = 1 - cur
                    Pc, Qc = PQ[cur][:, 0, :], PQ[cur][:, 1, :]
                    Pn, Qn = PQ[nxt][:, 0, :], PQ[nxt][:, 1, :]
                    Xm, Ym = XY[0][:, 0, :], XY[0][:, 1, :]
                    W1, W2 = XY[1][:, 0, :], XY[1][:, 1, :]
                    tr.op("pool", lambda: nc.gpsimd.tensor_tensor(out=Ym, in0=Y0, in1=MK[:, 1 + li, :],
                                                                  op=ALU.mult),
                          r=[("Y0", sl), "MK"], w=[("Y", 0, sl)])
                    yield
                    b1, p1 = nps()
                    mm8(p1, Ym, Pc, [("Y", 0, sl), ("P", cur, sl)], b1)
                    tr.op("act", lambda: nc.scalar.copy(out=W1, in_=p1[0:64, :]), r=[("ps", b1)], w=[("X", 1, sl)])
                    yield
                    b2, p2 = nps()
                    mm8(p2, Qc, W1, [("Q", cur, sl), ("X", 1, sl)], b2)
                    tr.op("dve", lambda: nc.vector.tensor_tensor(out=Pn, in0=Pc, in1=p2[0:64, :],
                                                                 op=ALU.subtract),
                          r=[("ps", b2), ("P", cur, sl)], w=[("P", nxt, sl)])
                    if li < 2:
                        tr.op("pool", lambda: nc.gpsimd.tensor_tensor(out=Xm, in0=X0, in1=MK[:, 1 + li, :],
                                                                      op=ALU.mult),
                              r=[("X0", sl), "MK"], w=[("X", 0, sl)])
                        yield
                        b3, p3 = nps()
                        mm8(p3, Xm, Qc, [("X", 0, sl), ("Q", cur, sl)], b3)
                        tr.op("act", lambda: nc.scalar.copy(out=W2, in_=p3[0:64, :]), r=[("ps", b3)],
                              w=[("Y", 1, sl)])
                        yield
                        b4, p4 = nps()
                        mm8(p4, Pc, W2, [("P", cur, sl), ("Y", 1, sl)], b4)
                        tr.op("dve", lambda: nc.vector.tensor_tensor(out=Qn, in0=Qc, in1=p4[0:64, :],
                                                                     op=ALU.subtract),
                              r=[("ps", b4), ("Q", cur, sl)], w=[("Q", nxt, sl)])
                    cur = nxt
                TT = PQ[cur][:, 0, :]
                TK = ("P", cur, sl)
                for half in range(2):
                    yield
                    bk, pk_ = nps()
                    bv, pv_ = nps()
                    for hh in range(4):
                        h = half * 4 + hh
                        tr.op("pe", lambda h=h, hh=hh: nc.tensor.matmul(
                            pk_[0:64, hh * 128:(hh + 1) * 128], qt[:, 8 + h, tok], ID128B, start=True,
                            stop=True), r=[qk_, "csb"], w=[("ps", bk)], inc=(hh == 3))
                    for hh in range(4):
                        h = half * 4 + hh
                        tr.op("pe", lambda h=h, hh=hh: nc.tensor.matmul(
                            pv_[0:64, hh * 128:(hh + 1) * 128], qt[:, 16 + h, tok], ID128B, start=True,
                            stop=True), r=[qk_, "csb"], w=[("ps", bv)], inc=(hh == 3))
                    hsl = slice(half * 4, half * 4 + 4)
                    pk3 = pk_[0:64, :].rearrange("p (h d) -> p h d", h=4)
                    pv3 = pv_[0:64, :].rearrange("p (h d) -> p h d", h=4)
                    tr.op("dve", lambda pk3=pk3, hsl=hsl: nc.vector.tensor_tensor(
                        out=kbg[:, hsl, :], in0=pk3,
                        in1=sm[0:64, 4, hsl].unsqueeze(2).to_broadcast([64, 4, 128]), op=ALU.mult),
                        r=[("ps", bk), ("sm", sl)], w=[("kbg", sl)])
                    tr.op("dve", lambda pk3=pk3, hsl=hsl: nc.vector.tensor_tensor(
                        out=kdec[:, hsl, :], in0=pk3,
                        in1=sm[0:64, 3, hsl].unsqueeze(2).to_broadcast([64, 4, 128]), op=ALU.mult),
                        r=[("ps", bk), ("sm", sl)], w=[("kdec", sl)])
                    tr.op("dve", lambda pv3=pv3, hsl=hsl: nc.vector.tensor_tensor(
                        out=vb[:, hsl, :], in0=pv3,
                        in1=beta[:, c, hsl].unsqueeze(2).to_broadcast([64, 4, 128]), op=ALU.mult),
                        r=[("ps", bv), ("beta", "t", n % 2)], w=[("vb", sl)])
                for half in range(2):
                    yield
                    bu, pu = nps()
                    for hh in range(4):
                        h = half * 4 + hh
                        tr.op("pe", lambda h=h, hh=hh: nc.tensor.matmul(
                            pu[0:64, hh * 128:(hh + 1) * 128], TT[:, h * 64:(h + 1) * 64], vb[:, h, :],
                            start=True, stop=True), r=[TK, ("vb", sl)], w=[("ps", bu)], inc=(hh == 3))
                    tr.op("act", lambda half=half, pu=pu: nc.scalar.copy(
                        out=usb[:, half * 4:half * 4 + 4, :].rearrange("p h d -> p (h d)"), in_=pu[0:64, :]),
                        r=[("ps", bu)], w=[("usb", sl)])
                yield
                bw, pw = nps()
                for h in range(H):
                    tr.op("pe", lambda h=h: nc.tensor.matmul(pw[:, h * 64:(h + 1) * 64], kbg[:, h, :],
                                                             TT[:, h * 64:(h + 1) * 64], start=True, stop=True),
                          r=[TK, ("kbg", sl)], w=[("ps", bw)], inc=(h == H - 1))
                tr.op("act", lambda: nc.scalar.copy(out=wT[:].rearrange("p h i -> p (h i)"), in_=pw[:, :]),
                      r=[("ps", bw)], w=[("wT", sl)])
                for half in range(2):
                    bws, pws = nps()
                    for hh in range(4):
                        h = half * 4 + hh
                        tr.op("pe", lambda h=h, hh=hh: nc.tensor.matmul(
                            pws[0:64, hh * 128:(hh + 1) * 128], wT[:, h, :], Sb[:, h, :], start=True,
                            stop=True), r=[("wT", sl), "Sb"], w=[("ps", bws)], inc=(hh == 3))
                    tr.op("dve", lambda half=half, pws=pws: nc.vector.tensor_tensor(
                        out=vnew[:, half * 4:half * 4 + 4, :].rearrange("p h d -> p (h d)"),
                        in0=usb[:, half * 4:half * 4 + 4, :].rearrange("p h d -> p (h d)"), in1=pws[0:64, :],
                        op=ALU.subtract), r=[("ps", bws), ("usb", sl)], w=[("vnew", half, sl)])
                for half in range(2):
                    bo, po = nps()
                    for hh in range(4):
                        h = half * 4 + hh
                        tr.op("pe", lambda h=h, hh=hh: nc.tensor.matmul(
                            po[0:64, hh * 128:(hh + 1) * 128], qg[:, h, :], Sb[:, h, :], start=True,
                            stop=False), r=[("qg", sl), "Sb"], w=[("ps", bo)], inc=False)
                        tr.op("pe", lambda h=h, hh=hh: nc.tensor.matmul(
                            po[0:64, hh * 128:(hh + 1) * 128], attn[:, h * 64:(h + 1) * 64], vnew[:, h, :],
                            start=False, stop=True), r=[("attn", sl), ("vnew", half, sl)], w=[("ps", bo)],
                            inc=(hh == 3))
                    osl = osb[:, half * 4:half * 4 + 4, :].rearrange("p h d -> p (h d)")
                    if dr == 0:
                        tr.op("act", lambda osl=osl, po=po: nc.scalar.copy(out=osl, in_=po[0:64, :]),
                              r=[("ps", bo)], w=[("osb", half, sl)])
                    else:
                        if half == 0:
                            tr.dma("sp", ofw[:].rearrange("p h d -> p (h d)"), OF[gtok:gtok + C, :],
                                   r=[("OF", gtok)], w=[("ofw", sl)], sem=("ofw", sl))
                            tr.dma("sp", zsb[:].rearrange("p h d -> p (h d)"), ZS[gtok:gtok + C, :],
                                   w=[("zsb", sl)], sem=("zsb", sl))
                        tr.op("dve", lambda osl=osl, po=po, half=half: nc.vector.tensor_tensor(
                            out=osl, in0=po[0:64, :],
                            in1=ofw[:, half * 4:half * 4 + 4, :].rearrange("p h d -> p (h d)"), op=ALU.add),
                            r=[("ps", bo), ("ofw", sl)], w=[("osb", half, sl)])
                for half in range(2):
                    bs, pS = nps()
                    for hh in range(4):
                        h = half * 4 + hh
                        tr.op("pe", lambda h=h, hh=hh: nc.tensor.matmul(
                            pS[:, hh * 128:(hh + 1) * 128], kdec[:, h, :], vnew[:, h, :], start=True,
                            stop=True), r=[("kdec", sl), ("vnew", half, sl)], w=[("ps", bs)], inc=(hh == 3))
                    hsl = slice(half * 4, half * 4 + 4)
                    tr.op("pool", lambda hsl=hsl: nc.gpsimd.tensor_tensor(
                        out=S[:, hsl, :], in0=S[:, hsl, :],
                        in1=sm[:, 5, hsl].unsqueeze(2).to_broadcast([128, 4, 128]), op=ALU.mult),
                        r=[("sm", sl), "S"], w=["S"])
                    tr.op("dve", lambda hsl=hsl, pS=pS: nc.vector.tensor_tensor(
                        out=S[:, hsl, :].rearrange("p h d -> p (h d)"),
                        in0=S[:, hsl, :].rearrange("p h d -> p (h d)"), in1=pS[:, :], op=ALU.add),
                        r=[("ps", bs), "S"], w=["S"])
                tr.op("act", lambda: nc.scalar.copy(out=Sb[:], in_=S[:]), r=["S"], w=["Sb"])
                if dr == 0:
                    tr.dma("sp", OF[gtok:gtok + C, :], osb[:].rearrange("p h d -> p (h d)"),
                           r=[("osb", 0, sl), ("osb", 1, sl)], w=[("OF", gtok)], sem=("osb", sl))
                else:
                    for h in range(H):
                        hf = h // 4
                        tr.op("pool", lambda h=h: nc.gpsimd.tensor_tensor(
                            out=ysq[:, h, :], in0=osb[:, h, :], in1=osb[:, h, :], op=ALU.mult),
                            r=[("osb", hf, sl)], w=[("ysq", h, sl)])
                        tr.op("dve", lambda h=h: nc.vector.tensor_reduce(
                            out=ss[:, 8 + h:9 + h], in_=ysq[:, h, :], axis=AX.X, op=ALU.add),
                            r=[("ysq", h, sl)], w=[("ss", sl)])
                    tr.op("act", lambda: nc.scalar.activation(out=ss2[:, 8:16], in_=ss[:, 8:16], func=AF.Sqrt,
                                                              bias=epsd[0:64, 0:1], scale=1.0 / 128),
                          r=[("ss", sl), "epsd"], w=[("ss2", sl)])
                    tr.op("dve", lambda: nc.vector.reciprocal(out=ss3[:, 8:16], in_=ss2[:, 8:16]),
                          r=[("ss2", sl)], w=[("ss3", sl)])
                    for h in range(H):
                        hf = h // 4
                        tr.op("dve", lambda h=h: nc.vector.scalar_tensor_tensor(
                            out=ysq[:, h, :], in0=osb[:, h, :], scalar=ss3[:, 8 + h:9 + h], in1=tp[0:64, 32:160],
                            op0=ALU.mult, op1=ALU.mult), r=[("osb", hf, sl), ("ss3", sl), "tp"], w=[("ysq", h, sl)],
                            strict=True)
                        tr.op("dve", lambda h=h: nc.vector.tensor_tensor(
                            out=yb[:, h, :], in0=ysq[:, h, :], in1=zsb[:, h, :], op=ALU.mult),
                            r=[("ysq", h, sl), ("zsb", sl)], w=[("yb", hf, sl)])
                    yield
                    by, py = nps()
                    for h in range(H):
                        tr.op("pe", lambda h=h: nc.tensor.matmul(py[:, h * 64:(h + 1) * 64], yb[:, h, :],
                                                                 ID64B, start=True, stop=True),
                              r=[("yb", 0, sl), ("yb", 1, sl), "csb"], w=[("ps", by)], inc=(h == H - 1))
                    tr.op("act", lambda: nc.scalar.copy(out=ybT[:, :, tok],
                                                        in_=py[:, :].rearrange("p (h i) -> p h i", h=8)),
                          r=[("ps", by)], w=["ybT"])
                if last and dr == 1:
                    tr.dma("sp", YBT[:, n * NT:(n + 1) * NT].rearrange("(h p) n -> p h n", p=128), ybT[:],
                           r=["ybT"], sem="ybT")

            for dr in range(2):
                tiles = list(range(ntile)) if dr == 0 else list(range(ntile - 1, -1, -1))
                facs = []
                for n in tiles:
                    chunks = list(range(nck)) if dr == 0 else list(range(nck - 1, -1, -1))
                    for ci, c in enumerate(chunks):
                        facs.append(lambda sl, dr=dr, n=n, c=c, ci=ci: chunk_gen(dr, n, c, ci == 0, ci == nck - 1, sl))
                run_streams(facs, WG, stagger=14)
                if dr == 0:
                    tr.dma("sp", sx_in.ap(), S[:].rearrange("p h d -> p (h d)"), r=["S"], w=["sx_in"], sem="sx")
                    tr.cc(lambda: nc.gpsimd.collective_compute("AllGather", ALU.bypass, replica_groups=PAIRS,
                                                                ins=[sx_in.ap()], outs=[sx_out.ap()]),
                          r=["sx_in"], w=["sx_out"], sem="sxcc")
                    tr.dma("sp", SX, sx_out.ap().rearrange("(r p) n -> p r n", p=128), r=["sx_out"], w=["SX", "ybT"],
                           sem="sx")
                    Sf = S[:].rearrange("p h d -> p (h d)")
                    tr.op("dve", lambda: nc.vector.tensor_scalar(out=Sf, in0=SX[:, 0, :], scalar1=selt[:, 0:1],
                                                                 scalar2=None, op0=ALU.mult),
                          r=["SX", "selt"], w=["S"])
                    tr.op("dve", lambda: nc.vector.scalar_tensor_tensor(out=Sf, in0=SX[:, 1, :], scalar=selt[:, 1:2],
                                                                        in1=Sf, op0=ALU.mult, op1=ALU.add),
                          r=["SX", "selt", "S"], w=["S"])
                    tr.op("act", lambda: nc.scalar.copy(out=Sb[:], in_=S[:]), r=["S"], w=["Sb"])
            tr.barrier()

        with ExitStack() as st5:
            ring_phase(st5, [b for _ in range(ntile) for b in range(OFF_UP2, OFF_UP2 + NB_UP)])
            P = ffn_pool(st5, wdn2, nxt=1)
            wb = sb("wb", [128, 8, D], BF16, st5)
            wo = sb("wo", [128, 8, D], BF16, st5)
            ybt = P["u"]
            mrg = P["hid"]
            gsb = [sb("gsb%d" % i, [128, NT], F32, st5) for i in range(2)]
            pab = [sb("pab%d" % i, [128, NT], F32, st5) for i in range(2)]
            tr.dma("pool", wb[:], wbout.rearrange("(k p) n -> p k n", p=128), w=["wb"], sem="wb")
            tr.dma("pool", wo[:], wout.rearrange("(k p) n -> p k n", p=128), w=["wo"], sem="wo")
            for n in range(ntile):
                xt = P["xt"][0]
                xkey = "xt0"
                c0 = n * NT
                load_tile(xt, H1T, c0, xkey)
                tr.dma("sp", ybt[:], YBT[:, c0:c0 + NT].rearrange("(h p) n -> p h n", p=128),
                       w=[("u", k) for k in range(8)], sem="ybt")
                for m in range(8):
                    g_ = gsb[m % 2]
                    gk = ("gsb", m % 2)
                    pa_ = pab[m % 2]
                    pk2 = ("pab", m % 2)
                    tr.dma("sp", g_[:], PT[(44 + m) * 128:(45 + m) * 128, c0 + 2:c0 + 2 + NT], w=[gk], sem=gk)
                    tr.dma("sp", pa_[:], PAT[m * 128:(m + 1) * 128, c0:c0 + NT], w=[pk2], sem=pk2)
                    tr.op("act", lambda g_=g_: nc.scalar.activation(out=g_[:], in_=g_[:], func=AF.Sigmoid),
                          r=[gk], w=[gk])
                    bi, pt = nps()
                    for kk in range(8):
                        tr.op("pe", lambda kk=kk, m=m, pt=pt: nc.tensor.matmul(
                            pt[:], wb[:, kk, m * 128:(m + 1) * 128], ybt[:, kk, :], start=(kk == 0), stop=(kk == 7)),
                            r=["wb", ("u", kk)], w=[("ps", bi)], inc=(kk == 7))
                    tr.op("dve", lambda g_=g_, pt=pt: nc.vector.tensor_tensor(out=g_[:], in0=pt[:], in1=g_[:],
                                                                              op=ALU.mult),
                          r=[("ps", bi), gk], w=[gk])
                    tr.op("pool", lambda g_=g_, pa_=pa_, m=m: nc.gpsimd.tensor_tensor(out=mrg[:, m, :], in0=g_[:],
                                                                                      in1=pa_[:], op=ALU.add),
                          r=[gk, pk2], w=[("hid", m)])
                for m in range(8):
                    bi, pt = nps()
                    for kk in range(8):
                        tr.op("pe", lambda kk=kk, m=m, pt=pt: nc.tensor.matmul(
                            pt[:], wo[:, kk, m * 128:(m + 1) * 128], mrg[:, kk, :], start=(kk == 0), stop=(kk == 7)),
                            r=["wo", ("hid", kk)], w=[("ps", bi)], inc=(kk == 7))
                    tr.op("dve", lambda m=m, pt=pt: nc.vector.scalar_tensor_tensor(
                        out=xt[:, m, :], in0=pt[:], scalar=der[:, 32 + m:33 + m], in1=xt[:, m, :], op0=ALU.mult,
                        op1=ALU.add), r=[("ps", bi), "der"], w=[(xkey, m)])
                ffn_body(P, xt, xkey, 2, P["wdn"], True, outT, c0)
            tr.barrier()
        tr.finish()
    return nc


def _blocks(w, cols_list):
    out = np.empty((len(cols_list), 128, 2048), np.float32)
    w3 = w.reshape(8, 128, -1)
    for b, cols in enumerate(cols_list):
        out[b] = w3[:, :, cols].transpose(1, 0, 2).reshape(128, 2048)
    return out


def _consts():
    c = np.zeros((128, 4096), np.float32)
    c[:, 0:128] = 1.0
    c[0:64, 128:192] = np.eye(64)
    j = np.arange(64)[:, None]
    i = np.arange(64)[None, :]
    c[0:64, 192:256] = (j <= i)
    c[0:64, 256:320] = (j >= i)
    capS_F = np.where(j < i, 0.0, NEG)
    capI_F = np.where(j <= i, 0.0, NEG)
    capS_B = np.where(j > i, 0.0, NEG)
    capI_B = np.where(j >= i, 0.0, NEG)
    c[0:64, 320:832] = np.tile(capS_F, (1, 8))
    c[0:64, 832:1344] = np.tile(capI_F, (1, 8))
    c[0:64, 1344:1856] = np.tile(capS_B, (1, 8))
    c[0:64, 1856:2368] = np.tile(capI_B, (1, 8))
    c[0:64, 2368:2880] = np.tile(np.eye(64), (1, 8))
    e8 = np.zeros((8, 8, 64), np.float32)
    for h in range(8):
        e8[h, h, :] = 1.0
    c[0:8, 2880:3392] = e8.reshape(8, 512)
    c[:, 3392:3520] = np.eye(128)
    return c


_PROG = {}


def _run(inputs, T, n_cores):
    f = lambda a: np.ascontiguousarray(np.asarray(a, dtype=np.float32))
    x = f(inputs["x"])
    B = x.shape[0]
    cvec = f(inputs["c"])
    w_ada = f(inputs["w_ada"])[0]
    b_ada = f(inputs["b_ada"])[0]
    w_in = f(inputs["w_in"])[0]
    wup1 = f(inputs["w_ffn1_up"])[0]
    wup2 = f(inputs["w_ffn2_up"])[0]
    conv_a = f(inputs["conv_a"])[0]
    conv_dn = f(inputs["conv_dn"])[0]
    cst = _consts()
    ii = np.arange(64)[:, None]
    jj = np.arange(64)[None, :]
    mks = [(ii // 8 == jj // 8)]
    for bsz in (8, 16, 32):
        mks.append((ii // (2 * bsz) == jj // (2 * bsz)) & (ii // bsz != jj // bsz))
    cst2 = np.concatenate([np.tile(m.astype(np.float32), (1, 8)) for m in mks], axis=1)

    ada_cols = [np.arange(b * 256, (b + 1) * 256) for b in range(NB_ADA)]
    up_cols = [np.concatenate([np.arange(j * 128, (j + 1) * 128), np.arange(DFF + j * 128, DFF + (j + 1) * 128)])
               for j in range(NJ)]
    fm = np.concatenate([np.arange(512, 1024), np.arange(1024, 1536), np.arange(1536, 4608),
                         np.arange(0, 512), np.arange(5664, 7712)])
    in_cols = [fm[b * 256:(b + 1) * 256] for b in range(NB_IN)]
    z_cols = [np.arange(4608 + b * 256, 4608 + (b + 1) * 256) for b in range(NB_Z)]
    wr_common = np.concatenate([_blocks(w_ada, ada_cols), _blocks(wup1, up_cols), _blocks(w_in, in_cols + z_cols),
                                _blocks(wup2, up_cols)], axis=0)
    assert wr_common.shape[0] == NB_TOT

    ba_f = np.arange(5632, 5664)
    ba_b = np.concatenate([np.arange(5640, 5648), np.arange(5632, 5640), np.arange(5656, 5664),
                           np.arange(5648, 5656)])
    norms = np.stack([f(inputs["norm_ffn1"])[0], f(inputs["norm_mix"])[0], f(inputs["norm_ffn2"])[0],
                      f(inputs["norm_final"])], 0)
    in_maps = []
    for core in range(n_cores):
        b, half = core // 2, core % 2
        xs = x[b, half * T:(half + 1) * T, :]
        if half == 1:
            xs = xs[::-1]
        vecs = np.zeros((128, 120), np.float32)
        vecs[:, 0:8] = cvec[b].reshape(8, 128).T
        vecs[:, 8:80] = b_ada.reshape(72, 128).T
        vecs[:, 80:112] = norms.reshape(4, 8, 128).transpose(2, 0, 1).reshape(128, 32)
        cdn = conv_dn if half == 0 else conv_dn[::-1]
        ca = conv_a if half == 0 else conv_a[::-1]
        convw = np.zeros((128, 132), np.float32)
        convw[:, 0:120] = cdn.T.reshape(24, 128, 5).transpose(1, 0, 2).reshape(128, 120)
        convw[:, 120:132] = ca.T.reshape(4, 128, 3).transpose(1, 0, 2).reshape(128, 12)
        names = ["a_log_fwd", "a_log_bwd", "dt_bias_fwd", "dt_bias_bwd"]
        if half == 1:
            names = ["a_log_bwd", "a_log_fwd", "dt_bias_bwd", "dt_bias_fwd"]
        tokp = np.zeros((128, 160), np.float32)
        for q_, nm in enumerate(names):
            tokp[:, q_ * 8:(q_ + 1) * 8] = f(inputs[nm])[0][None, :]
        tokp[:, 32:160] = f(inputs["dn_norm"])[0][None, :]
        selv = np.zeros((128, 2), np.float32)
        selv[:, 1 - half] = 1.0
        in_maps.append({
            "xT": np.ascontiguousarray(xs.T),
            "wr": wr_common,
            "wdn1": f(inputs["w_ffn1_down"])[0],
            "wdn2": f(inputs["w_ffn2_down"])[0],
            "wba": np.ascontiguousarray(w_in[:, ba_f if half == 0 else ba_b]),
            "waout": f(inputs["w_a_out"])[0],
            "wbout": f(inputs["w_b_out"])[0],
            "wout": f(inputs["w_out"])[0],
            "vecs": vecs, "convw": convw, "tokp": tokp, "cst": cst, "cst2": cst2, "sel": selv,
        })
    if T not in _PROG:
        _PROG[T] = build_program(T)
    res = run_bass_kernel_spmd(_PROG[T], in_maps, core_ids=list(range(n_cores)))
    if DEBUG:
        DBG["res"] = res.results
    out = np.empty((B, 2 * T, D), np.float32)
    for core in range(n_cores):
        b, half = core // 2, core % 2
        o = res.results[core]["outT"].T
        if half == 1:
            o = o[::-1]
        out[b, half * T:(half + 1) * T, :] = o
    return out


def kernel(**inputs):
    x = np.asarray(inputs["x"])
    B, S, _ = x.shape
    return _run(inputs, S // 2, 2 * B)
```
